# Optimizing a Trainium2 kernel written in Bass

```python
import jax
import jax.numpy as jnp
from jax import lax
import numpy as np

D_MODEL = 1024
BATCH = 8
SEQ = 2048
DEPTH = 2

GRID_W = 64
CTX_LEN = 256
HG_DK = 128
HG_WIDTH = D_MODEL // 2
HG_HEADS = HG_WIDTH // HG_DK
HG_CHUNK = 64
SC_WIDTH = D_MODEL // 4
SC_TAPS = 3
SG_WIDTH = D_MODEL // 4
SG_HEADS = 4
SG_HEAD_DIM = SG_WIDTH // SG_HEADS
SG_CHUNK = 128
MIX_WIDTH = HG_WIDTH + SC_WIDTH + SG_WIDTH
HG_STATE_COLS = 3 * HG_WIDTH
IN_COLS = 5 * HG_WIDTH + 3 * SC_WIDTH + 2 * SG_WIDTH
D_FF = 2816
N_MOD = 6
ALPHA = (2 * DEPTH) ** 0.25
BETA = (8 * DEPTH) ** -0.25
EPS = 1e-6
TINY = 1e-30

O_Q = 3 * HG_WIDTH
O_G = 4 * HG_WIDTH
O_SC = 5 * HG_WIDTH
O_SG = O_SC + 3 * SC_WIDTH

kernel_name = 'hybrid_hgrn2_shortconv_chunkmlp_dit'

F32 = jnp.float32


def layer_norm(x, g=None, b=None):
    xf = x.astype(F32)
    xc = xf - jnp.mean(xf, axis=-1, keepdims=True)
    y = xc * lax.rsqrt(jnp.mean(xc * xc, axis=-1, keepdims=True) + EPS)
    if g is not None:
        y = y * g.astype(F32) + b.astype(F32)
    return y.astype(x.dtype)


def modulate(x, shift, scale):
    return layer_norm(x) * (1.0 + scale) + shift


def hg_lower_bounds(lb_logits):
    p = jax.nn.softmax(lb_logits.astype(F32), axis=0)
    return jnp.cumsum(p, axis=0) - p[:1]


def hg_forget(z, lb):
    z = z.astype(F32)
    k = (1.0 - lb) * jax.nn.sigmoid(-z)
    logf = jnp.log(jnp.maximum(lb + (1.0 - lb) * jax.nn.sigmoid(z), TINY))
    return k, logf


def split_heads(a):
    return a.reshape(a.shape[0], a.shape[1], HG_HEADS, HG_DK)


def flip(a):
    return a[:, ::-1]


def hg_state_inputs(p_state, lb_f, lb_b):
    i, z_f, z_b = jnp.split(p_state.astype(F32), 3, axis=-1)
    k_f, lf_f = hg_forget(z_f, lb_f)
    k_b, lf_b = hg_forget(z_b, lb_b)
    return (split_heads(i), split_heads(k_f), split_heads(k_b), split_heads(lf_f), split_heads(lf_b))


def gla_chunk_scan(q, k, v, logf, s0):
    bsz, t, h, dk = q.shape
    dv = v.shape[-1]
    n = t // HG_CHUNK

    def to_chunks(a):
        return a.astype(F32).reshape(bsz, n, HG_CHUNK, h, a.shape[-1]).transpose(1, 0, 3, 2, 4)

    causal = jnp.tril(jnp.ones((HG_CHUNK, HG_CHUNK), bool))[:, :, None]

    def step(s, inp):
        qc, kc, vc, gc = inp
        b = jnp.cumsum(gc, axis=2)
        inter = jnp.einsum('bhtk,bhkv->bhtv', qc * jnp.exp(b), s)
        diff = b[:, :, :, None, :] - b[:, :, None, :, :]
        decay = jnp.where(causal, jnp.exp(jnp.minimum(diff, 0.0)), 0.0)
        scores = jnp.einsum('bhtk,bhtsk,bhsk->bhts', qc, decay, kc)
        intra = jnp.einsum('bhts,bhsv->bhtv', scores, vc)
        b_last = b[:, :, -1:, :]
        s_new = jnp.exp(b_last[:, :, 0, :, None]) * s + jnp.einsum('bhsk,bhsv->bhkv', kc * jnp.exp(jnp.minimum(b_last - b, 0.0)), vc)
        return s_new, inter + intra

    _, o = lax.scan(step, s0.astype(F32), (to_chunks(q), to_chunks(k), to_chunks(v), to_chunks(logf)))
    return o.transpose(1, 0, 3, 2, 4).reshape(bsz, t, h, dv)


def gla_final_state(k, v, logf):
    suffix = jnp.minimum(lax.cumsum(logf, axis=1, reverse=True) - logf, 0.0)
    return jnp.einsum('bthk,bthv->bhkv', k * jnp.exp(suffix), v)


def hgrn2_bidir(q, v, k_f, k_b, lf_f, lf_b, s0_f, s0_b):
    o_f = gla_chunk_scan(q, k_f, v, lf_f, s0_f)
    o_b = flip(gla_chunk_scan(flip(q), flip(k_b), flip(v), flip(lf_b), s0_b))
    return o_f + o_b


def conv1d_centred(x, w):
    xp = jnp.pad(x, ((0, 0), (1, 1), (0, 0)))
    return xp[:, :-2] * w[0] + xp[:, 1:-1] * w[1] + xp[:, 2:] * w[2]


def chunk_spatial_gate(u, v, ln_g, ln_b, w_s, b_s):
    bsz, t, _ = v.shape
    n = t // SG_CHUNK
    vh = layer_norm(v.reshape(bsz, t, SG_HEADS, SG_HEAD_DIM),
                    ln_g.reshape(SG_HEADS, SG_HEAD_DIM), ln_b.reshape(SG_HEADS, SG_HEAD_DIM))
    vh = vh.reshape(bsz, n, SG_CHUNK, SG_HEADS, SG_HEAD_DIM)
    mixed = jnp.einsum('gts,bnsgc->bntgc', w_s, vh) + b_s.T[:, :, None]
    return u * mixed.reshape(bsz, t, SG_WIDTH)


def token_mixers(p, hg_in, s0_f, s0_b, hg_norm_g, sc_w, sg_ln_g, sg_ln_b, sg_w, sg_b):
    v, k_f, k_b, lf_f, lf_b = hg_in
    q = jax.nn.silu(split_heads(p[..., O_Q:O_G].astype(F32))) * HG_DK ** -0.5
    o = hgrn2_bidir(q, v, k_f, k_b, lf_f, lf_b, s0_f, s0_b)
    o = o * lax.rsqrt(jnp.mean(o * o, axis=-1, keepdims=True) + EPS) * hg_norm_g.astype(F32)
    hg = (o.reshape(o.shape[0], o.shape[1], HG_WIDTH) * jax.nn.silu(p[..., O_G:O_SC].astype(F32))).astype(p.dtype)
    gate_b, gate_c, h_sc = jnp.split(p[..., O_SC:O_SG], 3, axis=-1)
    sc = gate_b * conv1d_centred(gate_c * h_sc, sc_w)
    u, v_sg = jnp.split(p[..., O_SG:], 2, axis=-1)
    sg = chunk_spatial_gate(u, v_sg, sg_ln_g, sg_ln_b, sg_w, sg_b)
    return jnp.concatenate([hg, sc, sg], axis=-1)


def dwconv3x3(img, w):
    rows, cols = img.shape[1], img.shape[2]
    pad = jnp.pad(img, ((0, 0), (1, 1), (1, 1), (0, 0)))
    out = pad[:, 0:rows, 0:cols] * w[0, 0]
    for di in range(3):
        for dj in range(3):
            if di or dj:
                out = out + pad[:, di:di + rows, dj:dj + cols] * w[di, dj]
    return out


def conv_ffn(h, w_up, conv_w, conv_b, w_down, grid_rows):
    a, g = jnp.split(h @ w_up, 2, axis=-1)
    bsz, t, f = a.shape
    a = dwconv3x3(a.reshape(bsz, grid_rows, t // grid_rows, f), conv_w).reshape(bsz, t, f) + conv_b
    return (jax.nn.gelu(a, approximate=False) * g) @ w_down


def _normal(k, shape, scale):
    return jax.random.normal(k, shape, F32) * scale


def setup_inputs(seed: int = 0) -> dict:
    key = jax.random.key(seed)
    ks = jax.random.split(key, 23)
    return {
        'x': _normal(ks[0], (BATCH, SEQ, D_MODEL), 1.0),
        'c': _normal(ks[1], (BATCH, D_MODEL), 1.0),
        'ctx': _normal(ks[2], (BATCH, CTX_LEN, D_MODEL), 1.0),
        'c_ctx': _normal(ks[3], (D_MODEL,), 1.0),
        'ada_w': _normal(ks[4], (DEPTH, D_MODEL, N_MOD * D_MODEL), 0.5 * D_MODEL ** -0.5),
        'ada_b': _normal(ks[5], (DEPTH, N_MOD * D_MODEL), 0.02),
        'w_in': _normal(ks[6], (DEPTH, D_MODEL, IN_COLS), D_MODEL ** -0.5),
        'hg_lb': _normal(ks[7], (DEPTH, 2, HG_WIDTH), 1.0),
        'hg_norm_g': 1.0 + _normal(ks[8], (DEPTH, HG_DK), 0.02),
        'sc_conv_w': _normal(ks[9], (DEPTH, SC_TAPS, SC_WIDTH), SC_TAPS ** -0.5),
        'sg_ln_g': 1.0 + _normal(ks[10], (DEPTH, SG_WIDTH), 0.02),
        'sg_ln_b': _normal(ks[11], (DEPTH, SG_WIDTH), 0.02),
        'sg_w': _normal(ks[12], (DEPTH, SG_HEADS, SG_CHUNK, SG_CHUNK), 0.5 * SG_CHUNK ** -0.5),
        'sg_b': 1.0 + _normal(ks[13], (DEPTH, SG_HEADS, SG_CHUNK), 0.02),
        'w_out': _normal(ks[14], (DEPTH, MIX_WIDTH, D_MODEL), BETA * MIX_WIDTH ** -0.5),
        'ln1_g': 1.0 + _normal(ks[15], (DEPTH, D_MODEL), 0.02),
        'ln1_b': _normal(ks[16], (DEPTH, D_MODEL), 0.02),
        'ffn_up': _normal(ks[17], (DEPTH, D_MODEL, 2 * D_FF), D_MODEL ** -0.5),
        'ffn_conv_w': _normal(ks[18], (DEPTH, 3, 3, D_FF), 1.0 / 3.0),
        'ffn_conv_b': _normal(ks[19], (DEPTH, D_FF), 0.02),
        'ffn_down': _normal(ks[20], (DEPTH, D_FF, D_MODEL), BETA * D_FF ** -0.5),
        'ln2_g': 1.0 + _normal(ks[21], (DEPTH, D_MODEL), 0.02),
        'ln2_b': _normal(ks[22], (DEPTH, D_MODEL), 0.02),
    }


def reference(x, c, ctx, c_ctx, ada_w, ada_b, w_in, hg_lb, hg_norm_g, sc_conv_w, sg_ln_g, sg_ln_b,
              sg_w, sg_b, w_out, ln1_g, ln1_b, ffn_up, ffn_conv_w, ffn_conv_b, ffn_down, ln2_g, ln2_b):
    rows = x.shape[1] // GRID_W
    lower = hg_lower_bounds(hg_lb)
    silu_c = jax.nn.silu(c)
    silu_cc = jax.nn.silu(c_ctx)
    for l in range(DEPTH):
        last = l == DEPTH - 1
        mx = jnp.split((silu_c @ ada_w[l] + ada_b[l])[:, None, :], N_MOD, axis=-1)
        mc = jnp.split(silu_cc @ ada_w[l] + ada_b[l], N_MOD, axis=-1)
        lb_f, lb_b = lower[l, 0], lower[l, 1]
        px = modulate(x, mx[0], mx[1]) @ w_in[l]
        w_in_ctx = w_in[l][:, :HG_STATE_COLS] if last else w_in[l]
        pc = modulate(ctx, mc[0], mc[1]) @ w_in_ctx
        hg_x = hg_state_inputs(px[..., :HG_STATE_COLS], lb_f, lb_b)
        hg_c = hg_state_inputs(pc[..., :HG_STATE_COLS], lb_f, lb_b)
        v_c, kf_c, kb_c, lff_c, lfb_c = hg_c
        s_f = gla_final_state(kf_c, v_c, lff_c)
        s_b = gla_final_state(flip(kb_c), flip(v_c), flip(lfb_c))
        mix_args = (hg_norm_g[l], sc_conv_w[l], sg_ln_g[l], sg_ln_b[l], sg_w[l], sg_b[l])
        mix_x = token_mixers(px, hg_x, s_f, s_b, *mix_args)
        x_new = layer_norm(ALPHA * x + mx[2] * (mix_x @ w_out[l]), ln1_g[l], ln1_b[l])
        ffn_x = conv_ffn(modulate(x_new, mx[3], mx[4]), ffn_up[l], ffn_conv_w[l], ffn_conv_b[l], ffn_down[l], rows)
        x_new = layer_norm(ALPHA * x_new + mx[5] * ffn_x, ln2_g[l], ln2_b[l])
        if not last:
            zero = jnp.zeros_like(s_f)
            mix_c = token_mixers(pc, hg_c, zero, zero, *mix_args)
            ctx = layer_norm(ALPHA * ctx + mc[2] * (mix_c @ w_out[l]), ln1_g[l], ln1_b[l])
            ffn_c = conv_ffn(modulate(ctx, mc[3], mc[4]), ffn_up[l], ffn_conv_w[l], ffn_conv_b[l], ffn_down[l], 1)
            ctx = layer_norm(ALPHA * ctx + mc[5] * ffn_c, ln2_g[l], ln2_b[l])
        x = x_new
    return x
```

```python
import numpy as np
import concourse.bass as bass
import concourse.mybir as mybir

F32 = mybir.dt.float32
BF16 = mybir.dt.bfloat16
ALU = mybir.AluOpType
AF = mybir.ActivationFunctionType


class Buf:
    __slots__ = ("name", "w", "r", "excl")

    def __init__(self, name="", excl=False):
        self.name = name
        self.excl = excl
        self.w = None
        self.r = []


class Sync:
    ENGS = ("pe", "act", "dve", "pool", "sp")

    SEM_LIMIT = 1900

    def __init__(self, nc, stack, n_dma_sems=20):
        self.nc = nc
        self.stack = stack
        self.owner = {}
        self.cur = {}
        self.nsem = 0
        self.q = {e: [] for e in self.ENGS}
        self.sems = {}
        self.cnt = {}
        self.known = {e: {} for e in self.ENGS}
        for e in ("pe", "act", "dve", "pool"):
            self.cur[e] = None
            self._new_sem(e)
        self.dpool = {}
        self.dk = {}
        for e in ("sp", "act", "pool"):
            self.dpool[e] = []
            for i in range(n_dma_sems):
                key = "d_%s_%d" % (e, i)
                self.sems[key] = stack.enter_context(nc.semaphore(key)); self.nsem += 1
                self.cnt[key] = 0
                self.owner[key] = "dma_" + e
                self.dpool[e].append(key)
            self.dk[e] = 0
        self.pe_pending = False
        self.ninstr = 0

    def _new_sem(self, eng):
        n = sum(1 for k in self.owner if self.owner[k] == eng)
        key = "%s#%d" % (eng, n)
        self.sems[key] = self.stack.enter_context(self.nc.semaphore("s_%s_%d" % (eng, n))); self.nsem += 1
        self.cnt[key] = 0
        self.owner[key] = eng
        self.prev = getattr(self, "prev", {})
        self.prev[eng] = self.cur[eng]
        self.cur[eng] = key

    def _latest(self, eng):
        k = self.cur[eng]
        if self.cnt[k]:
            return (k, self.cnt[k])
        p = self.prev.get(eng)
        return (p, self.cnt[p]) if p and self.cnt[p] else None

    def _wait(self, eng, tok):
        if tok is None:
            return
        key, val = tok
        if self.known[eng].get(key, 0) >= val:
            return
        self.known[eng][key] = val
        sem = self.sems[key]
        self.q[eng].append(lambda e, sem=sem, val=val: e.wait_ge(sem, val))
        self.ninstr += 1

    def _deps(self, eng, reads, writes, skip_self=False):
        for b in reads:
            if b.w is not None and not (skip_self and self.owner[b.w[0]] == eng):
                self._wait(eng, b.w)
            if b.excl:
                for t in b.r:
                    if self.owner[t[0]] != eng:
                        self._wait(eng, t)
        for b in writes:
            if b.w is not None and not (skip_self and self.owner[b.w[0]] == eng):
                self._wait(eng, b.w)
            for t in b.r:
                if not (skip_self and self.owner[t[0]] == eng):
                    self._wait(eng, t)

    def _commit(self, tok, reads, writes):
        for b in reads:
            b.r.append(tok)
            if len(b.r) > 64:
                best = {}
                for k, v in b.r:
                    if best.get(k, 0) < v:
                        best[k] = v
                b.r = list(best.items())
        for b in writes:
            b.w = tok
            b.r = []

    def op(self, eng, fn, reads=(), writes=(), inc=True):
        pe = eng == "pe"
        self._deps(eng, reads, writes, skip_self=pe)
        if self.cnt[self.cur[eng]] >= self.SEM_LIMIT and not (pe and self.pe_pending):
            self._new_sem(eng)
        key = self.cur[eng]
        if inc:
            self.cnt[key] += 1
            tok = (key, self.cnt[key])
            sem = self.sems[key]
            self.q[eng].append(lambda e, fn=fn, sem=sem: fn(e).then_inc(sem, 1))
            if pe:
                self.pe_pending = False
        else:
            assert pe
            tok = (key, self.cnt[key] + 1)
            self.q[eng].append(lambda e, fn=fn: fn(e))
            self.pe_pending = True
        self.ninstr += 1
        self._commit(tok, reads, writes)
        return tok

    def dma(self, eng, out, in_, reads=(), writes=(), **kw):
        self._deps(eng, reads, writes)
        pool = self.dpool[eng]
        slot = self.dk[eng] % len(pool)
        key = pool[slot]
        self.dk[eng] += 1
        if self.cnt[key] + 16 > self.SEM_LIMIT:
            self._wait(eng, (key, self.cnt[key]))
            n = sum(1 for k in self.owner if self.owner[k] == "dma_" + eng)
            nk = "d_%s_%d" % (eng, n)
            self.sems[nk] = self.stack.enter_context(self.nc.semaphore(nk)); self.nsem += 1
            self.cnt[nk] = 0; self.owner[nk] = "dma_" + eng
            pool[slot] = nk; key = nk
            self.retired = getattr(self, "retired", []) + [key]
        if self.cnt[key] > 0:
            self._wait(eng, (key, self.cnt[key]))
        self.cnt[key] += 16
        tok = (key, self.cnt[key])
        sem = self.sems[key]
        self.q[eng].append(
            lambda e, out=out, in_=in_, sem=sem, kw=kw: e.dma_start(out=out, in_=in_, **kw).then_inc(sem, 16))
        self.ninstr += 1
        self._commit(tok, reads, writes)
        return tok

    def wait_all(self, eng, toks):
        for t in toks:
            self._wait(eng, t)

    def barrier(self):
        toks = [t for t in (self._latest(e) for e in ("pe", "act", "dve", "pool")) if t]
        assert not self.pe_pending
        for q in self.dpool:
            for key in self.dpool[q]:
                if self.cnt[key]:
                    toks.append((key, self.cnt[key]))
        for eng in self.ENGS:
            for t in toks:
                if self.owner[t[0]] != eng:
                    self._wait(eng, t)

    def drain_all(self, eng="sp"):
        for q in self.dpool:
            for key in self.dpool[q]:
                if self.cnt[key]:
                    self._wait(eng, (key, self.cnt[key]))

    def emit(self):
        assert not self.pe_pending, "last PE op must carry inc"
        nc = self.nc
        q = self.q
        with nc.Block() as block:
            @block.sync
            def _(e):
                for f in q["sp"]:
                    f(e)

            @block.tensor
            def _(e):
                for f in q["pe"]:
                    f(e)

            @block.scalar
            def _(e):
                for f in q["act"]:
                    f(e)

            @block.vector
            def _(e):
                for f in q["dve"]:
                    f(e)

            @block.gpsimd
            def _(e):
                for f in q["pool"]:
                    f(e)


def run_rr(chains):
    live = list(chains)
    while live:
        for g in list(live):
            try:
                next(g)
            except StopIteration:
                live.remove(g)


from contextlib import ExitStack
from concourse.bass_utils import run_bass_kernel_spmd

FF = 2816; NFC = FF // 128
D = 1024; NT = 18; N = NT * 128; DEPTH = 2; NMOD = 6; EPS = 1e-6
KC = D // 128


def build(debug=None):
    nc = bass.Bass("TRN2", target_bir_lowering=False)
    dram = lambda n, s, dt, kind: nc.dram_tensor(n, s, dt, kind=kind).ap()
    xin = dram("xin", [N, D], F32, "ExternalInput")
    c2 = dram("c2", [128, KC * 2], F32, "ExternalInput")
    ada_w = dram("ada_w", [DEPTH, D, NMOD * D], F32, "ExternalInput")
    ada_b = dram("ada_b_fm", [128, DEPTH * 48], F32, "ExternalInput")
    ident = dram("ident", [128, 128], F32, "ExternalInput")
    w_in = dram("w_in", [DEPTH, D, 3840], F32, "ExternalInput")
    lbl = dram("lbl", [128, 16], F32, "ExternalInput")
    hgng = dram("hg_norm_g", [DEPTH, 128], F32, "ExternalInput")
    scw = dram("scw_fm", [128, DEPTH * 6], F32, "ExternalInput")
    sgln = dram("sgln", [DEPTH * 2, 256], F32, "ExternalInput")
    sgw = dram("sg_w", [DEPTH * 4, 128, 128], F32, "ExternalInput")
    sgb = dram("sgb_fm", [DEPTH * 2, 128, 128], F32, "ExternalInput")
    w_out = dram("w_out", [DEPTH, D, D], F32, "ExternalInput")
    ffn_up = dram("ffn_up", [DEPTH, D, 2 * FF], F32, "ExternalInput")
    ffn_down = dram("ffn_down", [DEPTH, FF, D], F32, "ExternalInput")
    fcw = dram("fcw_fm", [128, DEPTH * NFC * 9], F32, "ExternalInput")
    fcb = dram("fcb_fm", [128, DEPTH * NFC], F32, "ExternalInput")
    lnp = dram("lnp", [DEPTH * 4, D], F32, "ExternalInput")
    cmask = dram("cmask", [128, 256], F32, "ExternalInput")
    rmask = dram("rmask", [128, 512], F32, "ExternalInput")
    outs = {}
    if debug is None:
        outs["out"] = dram("out", [N - 256, D], F32, "ExternalOutput")
    if debug and debug.startswith("hg") and len(debug) == 3:
        outs["dbg_cut"] = dram("dbg_cut", [128, 2 * N], F32, "ExternalOutput")
    if debug == "h":
        outs["dbg_hT"] = dram("dbg_hT", [FF, N], F32, "ExternalOutput")
    if debug == "x2":
        outs["dbg_x2"] = dram("dbg_x2", [N, D], F32, "ExternalOutput")
    if debug == "x1":
        outs["dbg_x1"] = dram("dbg_x1", [N, D], F32, "ExternalOutput")
        outs["dbg_xm2T"] = dram("dbg_xm2T", [128, KC * N], F32, "ExternalOutput")
    if debug in ("hg", "mix"):
        outs["dbg_hgT"] = dram("dbg_hgT", [D if debug == "mix" else 512, N], F32, "ExternalOutput")
    if debug == "p1":
        outs["dbg_mod"] = dram("dbg_mod", [128, DEPTH * 96], F32, "ExternalOutput")
        outs["dbg_xmT"] = dram("dbg_xmT", [128, KC * N], F32, "ExternalOutput")
    with ExitStack() as st:
        S = Sync(nc, st)
        sb = lambda n, s, dt: st.enter_context(nc.sbuf_tensor(n, s, dt))
        pbank = [st.enter_context(nc.psum_tensor("pb%d" % i, [128, 512], F32)) for i in range(8)]
        b_pb = [Buf("pb%d" % i, excl=True) for i in range(8)]
        pbf = lambda i, n: pbank[i][:, 0:n]
        pbb = lambda i, n: pbank[i][:].bitcast(BF16)[:, 0:n]
        id_f = sb("id_f", [128, 128], F32); id_bf = sb("id_bf", [128, 128], BF16)
        c2_f = sb("c2_f", [128, KC * 2], F32); c2_bf = sb("c2_bf", [128, KC * 2], BF16)
        adab = sb("adab", [128, DEPTH * 48], F32)
        mod = sb("mod", [128, DEPTH * 96], F32)
        mod1p = sb("mod1p", [128, DEPTH * 96], F32)
        b_id_f, b_id_bf, b_c2f, b_c2bf, b_adab, b_mod, b_mod1p = (Buf(n) for n in "idf idbf c2f c2bf adab mod mod1p".split())
        S.dma("sp", id_f[:], ident, writes=[b_id_f])
        S.dma("pool", id_bf[:], ident, writes=[b_id_bf])
        S.dma("sp", c2_f[:], c2, writes=[b_c2f])
        S.dma("sp", adab[:], ada_b, writes=[b_adab])
        S.op("act", lambda e: e.activation(c2_bf[:], c2_f[:], AF.Silu), reads=[b_c2f], writes=[b_c2bf])
        st0 = ExitStack()
        aw = [st0.enter_context(nc.sbuf_tensor("aw%d" % i, [128, KC * 512], BF16)) for i in range(2)]
        b_aw = [Buf("aw0"), Buf("aw1")]
        p_mod = pbf(0, 96); b_pmod = b_pb[0]
        def mod_chunk_load(l, g, buf, bb):
            src = ada_w[l].rearrange("(k p) n -> p k n", p=128)[:, :, g * 512:(g + 1) * 512]
            S.dma("pool", buf[:].rearrange("p (k n) -> p k n", k=KC), src, writes=[bb])

        def mod_chunk_mm(l, g, buf, bb, pm_ap, b_pm):
            for jj in range(4):
                j = g * 4 + jj
                for k in range(KC):
                    last = (k == KC - 1)
                    S.op("pe", lambda e, buf=buf, jj=jj, k=k, j=j, last=last: e.matmul(
                        pm_ap[:, 2 * j:2 * j + 2], buf[:, k * 512 + jj * 128:k * 512 + (jj + 1) * 128],
                        c2_bf[:, 2 * k:2 * k + 2], start=(k == 0), stop=last),
                        reads=[bb, b_c2bf], writes=[b_pm], inc=(last and jj == 3))

        def mod_finish(l, pm_ap, b_pm):
            pm = pm_ap.rearrange("p (j s) -> p j s", s=2)
            mv = mod[:, l * 96:(l + 1) * 96].rearrange("p (j s) -> p j s", s=2)
            for s_ in range(2):
                S.op("dve", lambda e, pm=pm, mv=mv, s_=s_, l=l: e.tensor_tensor(
                    mv[:, :, s_], pm[:, :, s_], adab[:, l * 48:(l + 1) * 48], ALU.add),
                    reads=[b_pm, b_adab], writes=[b_mod])
            S.op("dve", lambda e, l=l: e.tensor_scalar_add(mod1p[:, l * 96:(l + 1) * 96], mod[:, l * 96:(l + 1) * 96], 1.0), reads=[b_mod], writes=[b_mod1p])

        for g in range(12):
            mod_chunk_load(0, g, aw[g % 2], b_aw[g % 2])
            mod_chunk_mm(0, g, aw[g % 2], b_aw[g % 2], p_mod, b_pmod)
        mod_finish(0, p_mod, b_pmod)
        S.barrier(); st0.close()
        if debug == "p1":
            S.dma("sp", outs["dbg_mod"], mod[:], reads=[b_mod])

        def modv(l, slot, kc, src, one_plus=False):
            t = mod1p if one_plus else mod
            col = l * 96 + (slot * 8 + kc) * 2 + src
            return t[:, col:col + 1]

        xmT = sb("xmT", [128, KC * N], BF16)
        b_xmT = [Buf("xmT%d" % i) for i in range(NT)]
        if debug in ("x1", "x2"):
            xt = [sb("xt%d" % i, [128, D], F32) for i in range(2)]; b_xt = [Buf("xt0"), Buf("xt1")]

        def mk_sets(alloc, n, tag, with_y=False, with_h=False, plan=None):
            sets = []
            for g in range(n):
                B_ = dict(xt=alloc("xt%s%d" % (tag, g), [128, D], F32), xn=alloc("xn%s%d" % (tag, g), [128, D], BF16), st=alloc("st%s%d" % (tag, g), [128, 16], F32),
                          b_xt=Buf("xt%d" % g), b_xn=Buf("xn%d" % g), b_st=Buf("st%d" % g),
                          tr=((1, 2) if g % 2 == 0 else (7, 0)), mm=((3, 4) if g % 2 == 0 else (5, 6)), tr1=None)
                if plan is not None:
                    p_ = plan[g % len(plan)]
                    B_.update(mm=(p_[0], p_[1]), tr1=p_[2])
                if with_y:
                    B_.update(xq=alloc("xq%s%d" % (tag, g), [128, D], F32), yt=alloc("yt%s%d" % (tag, g), [128, D], F32), b_xq=Buf("xq%d" % g), b_yt=Buf("yt%d" % g))
                if with_h:
                    B_.update(ht=alloc("ht%s%d" % (tag, g), [128, NFC * 128], BF16), b_ht=Buf("ht%d" % g))
                sets.append(B_)
            return sets

        def ln_stats_gen(src, b_src, B_):
            st_t, b_s = B_["st"], B_["b_st"]
            yield
            S.op("dve", lambda e: e.bn_stats(st_t[:, 0:6], src[:, 0:512]), reads=[b_src], writes=[b_s])
            yield
            S.op("dve", lambda e: e.bn_stats(st_t[:, 6:12], src[:, 512:1024]), reads=[b_src], writes=[b_s])
            yield
            S.op("dve", lambda e: e.bn_aggr(st_t[:, 12:14], st_t[:, 0:12]), reads=[b_s], writes=[b_s])
            yield
            S.op("act", lambda e: e.activation(st_t[:, 14:15], st_t[:, 13:14], AF.Sqrt, bias=EPS), reads=[b_s], writes=[b_s])
            yield
            S.op("dve", lambda e: e.reciprocal(st_t[:, 15:16], st_t[:, 14:15]), reads=[b_s], writes=[b_s])

        def ln_mod_T_gen(l, i, slot_shift, slot_scale, B_):
            srcm = 1 if i < 2 else 0
            xt_t, xn_t, st_t = B_["xt"], B_["xn"], B_["st"]
            yield from ln_stats_gen(xt_t, B_["b_xt"], B_)
            yield
            S.op("dve", lambda e: e.tensor_scalar(xn_t[:], xt_t[:], st_t[:, 12:13], st_t[:, 15:16], ALU.subtract, ALU.mult),
                 reads=[B_["b_xt"], B_["b_st"]], writes=[B_["b_xn"]])
            for kc in range(KC):
                if B_["tr1"] is not None:
                    pbk = B_["tr1"]; pv_ = pbank[pbk][:].bitcast(BF16)[:, kc * 128:(kc + 1) * 128]
                else:
                    pbk = B_["tr"][kc % 2]; pv_ = pbb(pbk, 128)
                yield
                S.op("pe", lambda e, kc=kc, pv_=pv_: e.transpose(pv_, xn_t[:, kc * 128:(kc + 1) * 128], id_bf[:]),
                     reads=[B_["b_xn"], b_id_bf], writes=[b_pb[pbk]])
                dst = xmT[:, kc * N + i * 128: kc * N + (i + 1) * 128]
                yield
                S.op("act", lambda e, dst=dst, pv_=pv_, kc=kc: e.activation(
                    dst, pv_, AF.Identity, bias=modv(l, slot_shift, kc, srcm), scale=modv(l, slot_scale, kc, srcm, True)),
                    reads=[b_pb[pbk], b_mod, b_mod1p], writes=[b_xmT[i]])

        GRP = 2

        def run_groups(tiles, sets, load, chain, GRP=GRP):
            tl_ = list(tiles)

            def stream(k):
                mine = tl_[k::GRP]
                for n_, i in enumerate(mine):
                    if n_ == 0:
                        load(i, sets[k])
                    if n_ + 1 < len(mine):
                        load(mine[n_ + 1], sets[k + GRP * ((n_ + 1) % 2)])
                    yield from chain(i, sets[k + GRP * (n_ % 2)])

            run_rr([stream(k) for k in range(GRP)])

        def phase_ln_mod_T(l, src_ap, tiles, slot_shift, slot_scale, src_bufs=None):
            p1 = ExitStack()
            sets = mk_sets(lambda n, s_, dt: p1.enter_context(nc.sbuf_tensor("%s_p1L%d" % (n, l), s_, dt)), 2 * GRP, "a")
            load = lambda i, B_: S.dma("sp", B_["xt"][:], src_ap[i * 128:(i + 1) * 128, :], reads=([src_bufs[i]] if src_bufs else []), writes=[B_["b_xt"]])
            run_groups(tiles, sets, load, lambda i, B_: ln_mod_T_gen(l, i, slot_shift, slot_scale, B_))
            S.barrier(); p1.close()

        phase_ln_mod_T(0, xin, range(NT), 0, 1)
        cut = debug[2] if (debug and debug.startswith("hg") and len(debug) == 3) else None
        dmp = sb("dmp", [128, 512], F32); b_dmp = Buf("dmp")

        def cut_dump(src_ap, src_bufs):
            S.barrier()
            S.op("act", lambda e: e.copy(dmp[:], src_ap), reads=src_bufs, writes=[b_dmp])
            S.dma("sp", outs["dbg_cut"][:, 0:512], dmp[:], reads=[b_dmp])
            S.drain_all("sp"); S.emit(); build.ninstr = S.ninstr

        if cut == "0":
            cut_dump(xmT[:, 0:512], b_xmT); return nc
        if debug == "p1":
            xm_f = sb("xm_f", [128, N], F32); b_xmf = Buf("xmf")
            for kc in range(KC):
                S.op("act", lambda e, kc=kc: e.copy(xm_f[:], xmT[:, kc * N:(kc + 1) * N]), reads=b_xmT, writes=[b_xmf])
                S.dma("sp", outs["dbg_xmT"][:, kc * N:(kc + 1) * N], xm_f[:], reads=[b_xmf])

        mixT = nc.dram_tensor("mixT", [D, N], BF16, kind="Internal").ap()
        b_mixT = [Buf("mixT%d" % i) for i in range(8)]

        def mixer(l, scan_tiles, out_tiles):
            ms = ExitStack()
            sb = lambda n, s_, dt: ms.enter_context(nc.sbuf_tensor("%s_L%d" % (n, l), s_, dt))
            DK = 128; QS = DK ** -0.5; NCH = N // 64
            BLKS = [(0, 512), (512, 512), (1024, 512), (1536, 512), (2048, 256)]
            lb_f = sb("lb_f", [128, 16], F32); lbv = sb("lbv", [128, 16], F32); oml = sb("oml", [128, 16], F32)
            b_lb = Buf("lb")
            cm = sb("cm", [128, 256], F32); rmk = sb("rmk", [128, 512], F32); ngb = sb("ngb", [128, DEPTH * 128], F32)
            b_cm, b_rmk, b_ngb = Buf("cm"), Buf("rmk"), Buf("ngb")
            cm_u = sb("cm_u", [128, 256], mybir.dt.uint32); b_cmu = Buf("cmu")
            S.dma("sp", lb_f[:], lbl, writes=[b_lb]); S.dma("sp", cm[:], cmask, writes=[b_cm]); S.dma("sp", rmk[:], rmask, writes=[b_rmk])
            for l2 in range(DEPTH):
                S.dma("sp", ngb[:, l2 * 128:(l2 + 1) * 128], hgng[l2:l2 + 1, :].partition_broadcast(128), writes=[b_ngb])
            S.op("dve", lambda e: e.tensor_copy(cm_u[:], cm[:]), reads=[b_cm], writes=[b_cmu])
            lt = sb("lt", [128, 32], F32)
            S.op("dve", lambda e: e.memset(lbv[:, 0:8], 0.0), writes=[b_lb], reads=[b_lb])
            S.op("dve", lambda e: e.tensor_max(lt[:, 0:8], lb_f[:, 0:8], lb_f[:, 8:16]), reads=[b_lb], writes=[b_lb])
            S.op("dve", lambda e: e.tensor_sub(lt[:, 8:16], lb_f[:, 0:8], lt[:, 0:8]), reads=[b_lb], writes=[b_lb])
            S.op("dve", lambda e: e.tensor_sub(lt[:, 16:24], lb_f[:, 8:16], lt[:, 0:8]), reads=[b_lb], writes=[b_lb])
            S.op("act", lambda e: e.activation(lt[:, 8:24], lt[:, 8:24], AF.Exp), reads=[b_lb], writes=[b_lb])
            S.op("dve", lambda e: e.tensor_add(lt[:, 24:32], lt[:, 8:16], lt[:, 16:24]), reads=[b_lb], writes=[b_lb])
            S.op("dve", lambda e: e.reciprocal(lt[:, 24:32], lt[:, 24:32]), reads=[b_lb], writes=[b_lb])
            S.op("dve", lambda e: e.tensor_mul(lbv[:, 8:16], lt[:, 16:24], lt[:, 24:32]), reads=[b_lb], writes=[b_lb])
            S.op("dve", lambda e: e.tensor_scalar(oml[:], lbv[:], -1.0, 1.0, ALU.mult, ALU.add), reads=[b_lb], writes=[b_lb])

            wvg = [sb("wvg%d" % i, [128, KC * 256], BF16) for i in range(2)]; b_wvg = [Buf("wvg0"), Buf("wvg1")]
            wzq = [sb("wzq%d" % i, [128, KC * 384], BF16) for i in range(2)]; b_wzq = [Buf("wzq0"), Buf("wzq1")]
            VG = [dict(V=sb("Vh%d" % p_, [128, N], BF16), G=sb("Gh%d" % p_, [128, N], BF16), bV=Buf("V%d" % p_), bG=Buf("G%d" % p_)) for p_ in range(2)]
            Vh, Gh, b_V, b_G = VG[0]["V"], VG[0]["G"], VG[0]["bV"], VG[0]["bG"]
            QT = [sb("QT%d" % d, [128, N], BF16) for d in range(2)]; KT = [sb("KT%d" % d, [128, N], BF16) for d in range(2)]
            QS_T = [sb("QST%d" % d, [128, N], BF16) for d in range(2)]; b_QST = [Buf("QST0"), Buf("QST1")]
            KH = [sb("KH%d" % d, [128, N], BF16) for d in range(2)]; KHt = [sb("KHt%d" % d, [128, N], BF16) for d in range(2)]
            EB = [sb("EB%d" % d, [128, NCH], F32) for d in range(2)]
            b_QT = [Buf("QT0"), Buf("QT1")]; b_KT = [Buf("KT0"), Buf("KT1")]; b_KH = [Buf("KH0"), Buf("KH1")]
            b_KHt = [Buf("KHt0"), Buf("KHt1")]; b_EB = [Buf("EB0"), Buf("EB1")]
            Oacc = [sb("Oacc%d" % d, [128, N], F32) for d in range(2)]; b_O = [Buf("O0"), Buf("O1")]
            HGT = sb("HGT", [128, N], BF16); b_HGT = Buf("HGT")
            NTMP = 9
            tmp = [[sb("tm%d_%d" % (a, i), [128, 512], F32) for i in range(NTMP)] for a in range(2)]
            b_tmp = [[Buf("tm%d_%d" % (a, i)) for i in range(NTMP)] for a in range(2)]
            qs_ = [sb("qs%d" % a, [128, 512], F32) for a in range(2)]; b_qs = [Buf("qs0"), Buf("qs1")]
            Sst = [[sb("S%d_%d" % (d, i), [128, 128], F32) for i in range(2)] for d in range(2)]
            b_S = [[Buf("S%d_%d" % (d, i)) for i in range(2)] for d in range(2)]
            Sbf_f32 = [sb("S3_%d" % d, [128, 128], F32) for d in range(2)]; b_Sx = [Buf("S3_0"), Buf("S3_1")]
            AT = [sb("AT%d" % d, [128, 128], BF16) for d in range(2)]; b_AT = [Buf("AT0"), Buf("AT1")]
            fin = [sb("fin%d" % i, [128, 128], F32) for i in range(2)]; fsq = sb("fsq", [128, 128], F32)
            b_fin = [Buf("fin0"), Buf("fin1")]
            finW = [dict(fsq=(fsq if p_ == 0 else sb("fsq1", [128, 128], F32)), fst=sb("fst%d" % p_, [128, 8], F32), ngg=sb("ngg%d" % p_, [128, 128], F32), hgt=sb("hgt%d" % p_, [128, 128], BF16),
                         b_fsq=Buf("fsq%d" % p_), b_fst=Buf("fst%d" % p_), b_ngg=Buf("ngg%d" % p_), b_hgt=Buf("hgt%d" % p_)) for p_ in range(2)]

            if cut == "L":
                cut_dump(oml[:, 0:16].to_broadcast([128, 16]) if False else xmT[:, 0:512], b_xmT + [b_lb, b_cm, b_rmk, b_ngb]); return "cut"

            e_, one_e, sig, kk, lf, bb, cc_, E1, dd = range(9)

            def hgrn_layer(l, scan_tiles, out_tiles):
                wi = w_in[l].rearrange("(k p) n -> p k n", p=128)
                def load_vg_w(h_):
                    a_ = h_ % 2
                    for ci, c0 in enumerate((h_ * 128, 2048 + h_ * 128)):
                        S.dma("pool", wvg[a_][:].rearrange("p (k n) -> p k n", k=KC)[:, :, ci * 128:(ci + 1) * 128], wi[:, :, c0:c0 + 128], writes=[b_wvg[a_]])

                def load_zq_w(h_):
                    a_ = h_ % 2
                    for ci, c0 in enumerate((512 + h_ * 128, 1024 + h_ * 128, 1536 + h_ * 128)):
                        S.dma("pool", wzq[a_][:].rearrange("p (k n) -> p k n", k=KC)[:, :, ci * 128:(ci + 1) * 128], wi[:, :, c0:c0 + 128], writes=[b_wzq[a_]])

                def A_gen(h_):
                    a_ = h_ % 2; W_ = VG[a_]
                    for i in scan_tiles:
                        pbi = 3 + (i % 2)
                        yield
                        for kc in range(KC):
                            S.op("pe", lambda e, i=i, kc=kc, pbi=pbi: e.matmul(pbf(pbi, 256), xmT[:, kc * N + i * 128:kc * N + (i + 1) * 128],
                                 wvg[a_][:, kc * 256:(kc + 1) * 256], start=(kc == 0), stop=(kc == KC - 1)),
                                 reads=[b_xmT[i], b_wvg[a_]], writes=[b_pb[pbi]], inc=(kc == KC - 1))
                        yield
                        S.op("dve", lambda e, i=i, pbi=pbi: e.tensor_copy(W_["V"][:, i * 128:(i + 1) * 128], pbank[pbi][:, 0:128]), reads=[b_pb[pbi]], writes=[W_["bV"]])
                        yield
                        S.op("act", lambda e, i=i, pbi=pbi: e.activation(W_["G"][:, i * 128:(i + 1) * 128], pbank[pbi][:, 128:256], AF.Silu), reads=[b_pb[pbi]], writes=[W_["bG"]])

                nheads = 1 if cut else 4
                load_vg_w(0)
                if cut == "W":
                    cut_dump(wvg[0][:, 0:512], [b_wvg[0]]); return "cut"
                run_rr([A_gen(0)])
                for h in range(nheads):
                    a = h % 2
                    Vh, Gh, b_V, b_G = VG[a]["V"], VG[a]["G"], VG[a]["bV"], VG[a]["bG"]
                    if h == 0:
                        load_zq_w(0)
                    if h + 1 < nheads:
                        load_zq_w(h + 1)
                        load_vg_w(h + 1)
                    if cut == "A":
                        cut_dump(Vh[:, 0:512], [b_V, b_G]); return "cut"
                    for bi, (t0, nb) in enumerate(BLKS):
                        if t0 // 128 not in scan_tiles:
                            continue
                        tiles_in = [i for i in range(t0 // 128, (t0 + nb) // 128)]
                        for ci in range(3):
                            for kc in range(KC):
                                S.op("pe", lambda e, ci=ci, kc=kc, t0=t0, nb=nb, a=a: e.matmul(pbf(5 + ci, nb), wzq[a][:, kc * 384 + ci * 128:kc * 384 + (ci + 1) * 128],
                                     xmT[:, kc * N + t0:kc * N + t0 + nb], start=(kc == 0), stop=(kc == KC - 1)),
                                     reads=[b_wzq[a]] + [b_xmT[i] for i in tiles_in], writes=[b_pb[5 + ci]], inc=(kc == KC - 1))
                        qa = bi % 2
                        S.op("act", lambda e, nb=nb, qa=qa: e.activation(qs_[qa][:, 0:nb], pbf(7, nb), AF.Silu), reads=[b_pb[7]], writes=[b_qs[qa]])
                        nck = nb // 64; c0 = t0 // 64
                        def gate_chain(d, bi=bi, t0=t0, nb=nb, qa=qa, nck=nck, c0=c0):
                            ta = (bi * 2 + d) % 2
                            T = [t[:, 0:nb] for t in tmp[ta]]; bT = b_tmp[ta]
                            col = l * 8 + d * 4 + h
                            lbc, omc = lbv[:, col:col + 1], oml[:, col:col + 1]
                            yield
                            S.op("act", lambda e, T=T, d=d, nb=nb: e.activation(T[e_], pbf(5 + d, nb), AF.Exp, scale=-1.0), reads=[b_pb[5 + d]], writes=[bT[e_]])
                            yield
                            S.op("act", lambda e, T=T: e.activation(T[one_e], T[e_], AF.Ln, bias=1.0), reads=[bT[e_]], writes=[bT[one_e]])
                            yield
                            S.op("act", lambda e, T=T: e.activation(T[sig], T[one_e], AF.Exp, scale=-1.0), reads=[bT[one_e]], writes=[bT[sig]])
                            yield
                            S.op("dve", lambda e, T=T, omc=omc: e.scalar_tensor_tensor(T[kk], T[e_], omc, T[sig], ALU.mult, ALU.mult), reads=[bT[e_], bT[sig], b_lb], writes=[bT[kk]])
                            yield
                            S.op("act", lambda e, T=T, omc=omc, lbc=lbc: e.activation(T[lf], T[sig], AF.Ln, bias=lbc, scale=omc), reads=[bT[sig], b_lb], writes=[bT[lf]])
                            yield
                            S.op("dve", lambda e, T=T, nb=nb: e.tensor_tensor_scan(T[bb], rmk[:, 0:nb], T[lf], 0.0, ALU.mult, ALU.add), reads=[b_rmk, bT[lf]], writes=[bT[bb]])
                            b3 = T[bb].rearrange("p (c t) -> p c t", t=64)
                            btot = b3[:, :, 63:64]
                            if d == 0:
                                cview, bc_ = T[bb], bT[bb]
                            else:
                                yield
                                S.op("dve", lambda e, T=T: e.tensor_sub(T[cc_], T[lf], T[bb]), reads=[bT[lf], bT[bb]], writes=[bT[cc_]])
                                c3 = T[cc_].rearrange("p (c t) -> p c t", t=64)
                                yield
                                S.op("dve", lambda e, c3=c3, btot=btot, nck=nck: e.tensor_add(c3, c3, btot.to_broadcast([128, nck, 64])), reads=[bT[cc_], bT[bb]], writes=[bT[cc_]])
                                cview, bc_ = T[cc_], bT[cc_]
                            sl = slice(t0, t0 + nb)
                            yield
                            S.op("act", lambda e, T=T, cview=cview: e.activation(T[E1], cview, AF.Exp), reads=[bc_], writes=[bT[E1]])
                            yield
                            S.op("dve", lambda e, T=T, qa=qa, d=d, sl=sl, nb=nb: e.scalar_tensor_tensor(QT[d][:, sl], qs_[qa][:, 0:nb], QS, T[E1], ALU.mult, ALU.mult),
                                 reads=[b_qs[qa], bT[E1]], writes=[b_QT[d]])
                            MID = 31 if d == 0 else 32
                            cm3 = T[one_e].rearrange("p (c t) -> p c t", t=64); cv3m = cview.rearrange("p (c t) -> p c t", t=64)
                            yield
                            S.op("pool", lambda e, cm3=cm3, cv3m=cv3m, nck=nck, MID=MID: e.tensor_sub(cm3, cv3m, cv3m[:, :, MID:MID + 1].to_broadcast([128, nck, 64])),
                                 reads=[bc_, bT[E1], b_QT[d]], writes=[bT[one_e]])
                            yield
                            S.op("act", lambda e, T=T: e.activation(T[E1], T[one_e], AF.Exp), reads=[bT[one_e], b_QT[d]], writes=[bT[E1]])
                            yield
                            S.op("dve", lambda e, T=T, qa=qa, d=d, sl=sl, nb=nb: e.scalar_tensor_tensor(QS_T[d][:, sl], qs_[qa][:, 0:nb], QS, T[E1], ALU.mult, ALU.mult),
                                 reads=[b_qs[qa], bT[E1]], writes=[b_QST[d]])
                            yield
                            S.op("act", lambda e, T=T: e.activation(T[E1], T[one_e], AF.Exp, scale=-1.0), reads=[bT[one_e], b_QST[d]], writes=[bT[E1]])
                            yield
                            S.op("pool", lambda e, T=T, d=d, sl=sl: e.tensor_tensor(KT[d][:, sl], T[kk], T[E1], ALU.mult), reads=[bT[kk], bT[E1]], writes=[b_KT[d]])
                            d3 = T[dd].rearrange("p (c t) -> p c t", t=64); cv3 = cview.rearrange("p (c t) -> p c t", t=64)
                            yield
                            S.op("pool", lambda e, d3=d3, cv3=cv3, btot=btot, nck=nck: e.tensor_sub(d3, btot.to_broadcast([128, nck, 64]), cv3), reads=[bc_, bT[bb]], writes=[bT[dd]])
                            yield
                            S.op("act", lambda e, T=T: e.activation(T[dd], T[dd], AF.Exp), reads=[bT[dd]], writes=[bT[dd]])
                            yield
                            S.op("pool", lambda e, T=T, d=d, sl=sl: e.tensor_tensor(KH[d][:, sl], T[kk], T[dd], ALU.mult), reads=[bT[kk], bT[dd]], writes=[b_KH[d]])
                            yield
                            S.op("act", lambda e, d=d, c0=c0, nck=nck, btot=btot: e.activation(EB[d][:, c0:c0 + nck], btot.rearrange("p c o -> p (c o)"), AF.Exp), reads=[bT[bb]], writes=[b_EB[d]])
                        run_rr([gate_chain(0), gate_chain(1)])
                    if cut == "B":
                        return
                    for d in range(2):
                        for i in scan_tiles:
                            pbi = 1 + (i % 2)
                            S.op("pe", lambda e, d=d, i=i, pbi=pbi: e.transpose(pbb(pbi, 128), KH[d][:, i * 128:(i + 1) * 128], id_bf[:]), reads=[b_KH[d], b_id_bf], writes=[b_pb[pbi]])
                            S.op("act", lambda e, d=d, i=i, pbi=pbi: e.copy(KHt[d][:, i * 128:(i + 1) * 128], pbb(pbi, 128)), reads=[b_pb[pbi]], writes=[b_KHt[d]])
                    if cut == "C":
                        return
                    order = [list(scan_tiles), [t for t in (1, 0) if t in scan_tiles] + [t for t in range(NT - 1, 1, -1) if t in scan_tiles]]
                    TB = [tmp[a_][j_] for a_ in range(2) for j_ in range(NTMP)]; bTB = [b_tmp[a_][j_] for a_ in range(2) for j_ in range(NTMP)]
                    import os as _os2
                    d_stage = int(_os2.environ.get("HG_D_STAGE", "0")) if cut else 0
                    seqs = []
                    for d in range(1 if d_stage else 2):
                        seq = [(i, cpos) for i in order[d] for cpos in ((0, 1) if d == 0 else (1, 0))]
                        seqs.append(seq)
                        nstep = len(order[d])
                        for cpos in (0, 1):
                            base = cpos * nstep
                            s_ = base
                            while s_ < base + nstep:
                                g_end = min(base + nstep, (s_ // 4 + 1) * 4)
                                pbk = 3 + 2 * cpos + ((s_ // 4) % 2)
                                for sl_ in range(s_, g_end):
                                    i = order[d][sl_ - base]; q = sl_ % 4
                                    ps_ = slice(cpos * 64, cpos * 64 + 64); ts_ = slice(i * 128, (i + 1) * 128)
                                    S.op("pe", lambda e, d=d, ts_=ts_, ps_=ps_, pbk=pbk, q=q, Vh=Vh: e.matmul(pbank[pbk][:, q * 128:(q + 1) * 128], KHt[d][ps_, ts_], Vh[ps_, ts_], start=True, stop=True),
                                         reads=[b_KHt[d], b_V], writes=[b_pb[pbk]], inc=(sl_ == g_end - 1))
                                tb = 9 * d + s_ // 4; c_lo, c_hi = (s_ % 4) * 128, ((g_end - 1) % 4 + 1) * 128
                                S.op("act", lambda e, tb=tb, pbk=pbk, c_lo=c_lo, c_hi=c_hi, TB=TB: e.copy(TB[tb][:, c_lo:c_hi], pbank[pbk][:, c_lo:c_hi]), reads=[b_pb[pbk]], writes=[bTB[tb]])
                                s_ = g_end
                    if d_stage == 1:
                        cut_dump(TB[0][:, 0:512], bTB[0:9]); return "cut"
                    RING = 18
                    b_slot = [[Buf("st%d_%d" % (d_, r_)) for r_ in range(RING)] for d_ in range(2)]
                    sslot = lambda d_, k: (KH[d_][:, (k % RING) * 128:(k % RING + 1) * 128], b_slot[d_][k % RING])
                    S3 = [Sst[d_] + [Sbf_f32[d_]] for d_ in range(2)]; b_S3 = [b_S[d_] + [b_Sx[d_]] for d_ in range(2)]

                    produced = [0, 0]
                    consumed = [0, 0]

                    def chain_gen(d):
                        seq = seqs[d]; nstep = len(order[d])
                        yield
                        S.op("dve", lambda e, d=d, S3=S3: e.memset(S3[d][0][:], 0.0), writes=[b_S3[d][0]])
                        for k, (i, cpos) in enumerate(seq[:-1]):
                            sl_ = cpos * nstep + k // 2
                            ch = i * 2 + cpos; tb = 9 * d + sl_ // 4; q = sl_ % 4
                            si, so = k % 3, (k + 1) % 3
                            yield
                            S.op("dve", lambda e, d=d, si=si, so=so, ch=ch, tb=tb, q=q, TB=TB, S3=S3: e.scalar_tensor_tensor(S3[d][so][:], S3[d][si][:], EB[d][:, ch:ch + 1], TB[tb][:, q * 128:(q + 1) * 128], ALU.mult, ALU.add),
                                 reads=[b_S3[d][si], b_EB[d], bTB[tb]], writes=[b_S3[d][so]])
                            dst, bdst = sslot(d, k + 1)
                            yield
                            while (k + 1) - consumed[d] >= RING:
                                yield
                            S.op("act", lambda e, d=d, so=so, dst=dst, S3=S3: e.copy(dst, S3[d][so][:]), reads=[b_S3[d][so]], writes=[bdst])
                            produced[d] = k + 1

                    def out_gen(d):
                        yield
                        S.op("dve", lambda e, d=d: e.memset(AT[d][:], 0.0), writes=[b_AT[d]])
                        for step, i in enumerate(order[d]):
                            if i not in out_tiles:
                                consumed[d] = 2 * (step + 1)
                                continue
                            ts_ = slice(i * 128, (i + 1) * 128); par = step % 2
                            p_sc, p_o = (5, 6)[d], ((7, 0)[d])
                            yield
                            S.op("pe", lambda e, d=d, ts_=ts_, p_sc=p_sc: e.matmul(pbf(p_sc, 128), KT[d][:, ts_], QS_T[d][:, ts_], start=True, stop=True),
                                 reads=[b_KT[d], b_QST[d]], writes=[b_pb[p_sc]])
                            yield
                            S.op("dve", lambda e, d=d, p_sc=p_sc: e.copy_predicated(AT[d][:], cm_u[:, d * 128:(d + 1) * 128], pbf(p_sc, 128)),
                                 reads=[b_pb[p_sc], b_cmu, b_AT[d]], writes=[b_AT[d]])
                            cps = (0, 1) if d == 0 else (1, 0)
                            need = [(cp, k) for cp, k in zip(cps, (2 * step, 2 * step + 1)) if k > 0]
                            yield
                            while need and produced[d] < max(k_ for _, k_ in need):
                                yield
                            S.op("pe", lambda e, d=d, ts_=ts_, p_o=p_o, nn=len(need), Vh=Vh: e.matmul(pbf(p_o, 128), AT[d][:], Vh[:, ts_], start=True, stop=(nn == 0)),
                                 reads=[b_AT[d], b_V], writes=[b_pb[p_o]], inc=(len(need) == 0))
                            for j, (cp, k) in enumerate(need):
                                ps_ = slice(cp * 64, cp * 64 + 64); tsc = slice(i * 128 + cp * 64, i * 128 + cp * 64 + 64)
                                src, bsrc = sslot(d, k)
                                lastj = j == len(need) - 1
                                if not lastj:
                                    pass
                                S.op("pe", lambda e, d=d, tsc=tsc, ps_=ps_, p_o=p_o, src=src, lastj=lastj: e.matmul(pbank[p_o][ps_, 0:128], QT[d][:, tsc], src, start=False, stop=lastj),
                                     reads=[b_QT[d], bsrc], writes=[b_pb[p_o]], inc=lastj)
                            consumed[d] = 2 * (step + 1)
                            yield
                            S.op("act", lambda e, d=d, ts_=ts_, p_o=p_o: e.copy(Oacc[d][:, ts_], pbf(p_o, 128)), reads=[b_pb[p_o]], writes=[b_O[d]])

                    if h == nheads - 1 and not cut and debug != "hg":
                        sc_load(l, 0); sc_load(l, 1); sg_load(l)
                    if d_stage:
                        run_rr([chain_gen(0), out_gen(0)])
                    else:
                        run_rr([chain_gen(0), chain_gen(1), out_gen(0), out_gen(1)] + ([A_gen(h + 1)] if h + 1 < nheads else []))
                    if d_stage == 3:
                        cut_dump(Oacc[0][:, 0:512], [b_O[0]]); return "cut"
                    if cut == "D":
                        return
                    def fin_chain(i, fa):
                        ts_ = slice(i * 128, (i + 1) * 128); pbi = 1 + fa
                        W = finW[fa]
                        yield
                        S.op("dve", lambda e: e.tensor_add(fin[fa][:], Oacc[0][:, ts_], Oacc[1][:, ts_]), reads=[b_O[0], b_O[1]], writes=[b_fin[fa]])
                        yield
                        S.op("act", lambda e: e.activation(W["fsq"][:], fin[fa][:], AF.Square, accum_out=W["fst"][:, 0:1]), reads=[b_fin[fa]], writes=[W["b_fsq"], W["b_fst"]])
                        yield
                        S.op("act", lambda e: e.activation(W["fst"][:, 1:2], W["fst"][:, 0:1], AF.Sqrt, bias=EPS, scale=1.0 / 128), reads=[W["b_fst"]], writes=[W["b_fst"]])
                        yield
                        S.op("dve", lambda e: e.reciprocal(W["fst"][:, 2:3], W["fst"][:, 1:2]), reads=[W["b_fst"]], writes=[W["b_fst"]])
                        yield
                        S.op("pool", lambda e, Gcur=Gcur: e.tensor_tensor(W["ngg"][:], ngb[:, l * 128:(l + 1) * 128], Gcur[:, ts_], ALU.mult), reads=[b_ngb, b_Gcur], writes=[W["b_ngg"]])
                        yield
                        S.op("dve", lambda e: e.scalar_tensor_tensor(W["hgt"][:], fin[fa][:], W["fst"][:, 2:3], W["ngg"][:], ALU.mult, ALU.mult), reads=[b_fin[fa], W["b_fst"], W["b_ngg"]], writes=[W["b_hgt"]])
                        yield
                        S.op("pe", lambda e: e.transpose(pbb(pbi, 128), W["hgt"][:], id_bf[:]), reads=[W["b_hgt"], b_id_bf], writes=[b_pb[pbi]])
                        yield
                        S.op("act", lambda e: e.copy(HGT[:, ts_], pbb(pbi, 128)), reads=[b_pb[pbi]], writes=[b_HGT])

                    Gcur, b_Gcur = Gh, b_G
                    ot_ = list(out_tiles)

                    def fin_stream(k, ot_=ot_):
                        for i in ot_[k::2]:
                            yield from fin_chain(i, k)

                    run_rr([fin_stream(0), fin_stream(1)])
                    S.dma("sp", mixT[h * 128:(h + 1) * 128, :], HGT[:], reads=[b_HGT], writes=[b_mixT[h]])

            if cut:
                S.barrier()
                S.op("act", lambda e, Vh=Vh: e.copy(Oacc[1][:], Vh[:]), reads=[b_V], writes=[b_O[1]])
                S.dma("sp", outs["dbg_cut"][:, 0:N], Oacc[1][:], reads=[b_O[1]])
                if cut in "DE":
                    S.dma("sp", outs["dbg_cut"][:, N:2 * N], Oacc[0][:], reads=[b_O[0]])

            scw_s = sb("scw_s", [128, DEPTH * 6], F32); b_scw = Buf("scw")
            S.dma("sp", scw_s[:], scw, writes=[b_scw])
            lngb = sb("lngb", [128, 512], F32); b_lngb = Buf("lngb")
            WsT = sb("WsT", [128, 4 * 128], BF16); b_WsT = Buf("WsT")
            wsn = sb("wsn", [128, 128], BF16); b_wsn = Buf("wsn")
            BS = sb("BS", [128, 2 * 128], F32); b_BS = Buf("BS")
            SEQS = [(0, 256), (256, N)]
            SEGS = [(0, 256), (256, 512), (512, 1024), (1024, 1536), (1536, 2048), (2048, N)]

            pre_done = {}

            def sc_load(l, cc):
                wi_ = w_in[l].rearrange("(k p) n -> p k n", p=128); a_ = cc % 2
                for ci, c0 in enumerate((2560 + cc * 128, 2816 + cc * 128, 3072 + cc * 128)):
                    S.dma("pool", wzq[a_][:].rearrange("p (k n) -> p k n", k=KC)[:, :, ci * 128:(ci + 1) * 128], wi_[:, :, c0:c0 + 128], writes=[b_wzq[a_]])
                pre_done[("sc", l, cc)] = True

            def sg_load(l):
                wi_ = w_in[l].rearrange("(k p) n -> p k n", p=128)
                S.dma("pool", wvg[0][:].rearrange("p (k n) -> p k n", k=KC), wi_[:, :, 3328:3584], writes=[b_wvg[0]])
                S.dma("pool", wvg[1][:].rearrange("p (k n) -> p k n", k=KC), wi_[:, :, 3584:3840], writes=[b_wvg[1]])
                S.dma("sp", lngb[:, 0:256], sgln[2 * l:2 * l + 1, :].partition_broadcast(128), writes=[b_lngb])
                S.dma("sp", lngb[:, 256:512], sgln[2 * l + 1:2 * l + 2, :].partition_broadcast(128), writes=[b_lngb])
                for cc in range(2):
                    S.dma("sp", BS[:, cc * 128:(cc + 1) * 128], sgb[2 * l + cc], writes=[b_BS])
                pre_done[("sg", l)] = True

            def sc_layer(l, tiles):
                wi = w_in[l].rearrange("(k p) n -> p k n", p=128)
                tmax = (max(tiles) + 1) * 128; tmin = min(tiles) * 128
                Pf, GBf, b_P, b_GB = Oacc[0], Oacc[1], b_O[0], b_O[1]
                for cc in range(2):
                    a = cc % 2
                    if not pre_done.get(("sc", l, cc)):
                        sc_load(l, cc)
                    sc_blks = [(t0, min(512, tmax - t0)) for t0 in range(tmin, tmax, 512)]
                    for (t0, nb) in sc_blks:
                        tiles_in = list(range(t0 // 128, (t0 + nb) // 128))
                        for ci in range(3):
                            for kc in range(KC):
                                S.op("pe", lambda e, ci=ci, kc=kc, t0=t0, nb=nb, a=a: e.matmul(pbf(5 + ci, nb), wzq[a][:, kc * 384 + ci * 128:kc * 384 + (ci + 1) * 128],
                                     xmT[:, kc * N + t0:kc * N + t0 + nb], start=(kc == 0), stop=(kc == KC - 1)),
                                     reads=[b_wzq[a]] + [b_xmT[i] for i in tiles_in], writes=[b_pb[5 + ci]], inc=(kc == KC - 1))
                        T0 = tmp[0][0][:, 0:nb]
                        S.op("act", lambda e, t0=t0, nb=nb: e.copy(GBf[:, t0:t0 + nb], pbf(5, nb)), reads=[b_pb[5]], writes=[b_GB])
                        S.op("act", lambda e, T0=T0, nb=nb: e.copy(T0, pbf(6, nb)), reads=[b_pb[6]], writes=[b_tmp[0][0]])
                        S.op("dve", lambda e, T0=T0, t0=t0, nb=nb: e.tensor_tensor(Pf[:, t0:t0 + nb], T0, pbf(7, nb), ALU.mult), reads=[b_tmp[0][0], b_pb[7]], writes=[b_P])
                    wb = l * 6 + cc * 3
                    w0_, w1_, w2_ = scw_s[:, wb:wb + 1], scw_s[:, wb + 1:wb + 2], scw_s[:, wb + 2:wb + 3]
                    for (a0, a1) in SEGS:
                        if a0 < tmin or a0 >= tmax:
                            continue
                        s0, s1 = [sq for sq in SEQS if sq[0] <= a0 < sq[1]][0]
                        Y = tmp[1][0]; bY = b_tmp[1][0]; n_ = a1 - a0
                        S.op("dve", lambda e, Y=Y, a0=a0, a1=a1, n_=n_, w1_=w1_: e.tensor_scalar(Y[:, 0:n_], Pf[:, a0:a1], w1_, None, ALU.mult), reads=[b_P, b_scw], writes=[bY])
                        lo = max(a0, s0 + 1)
                        S.op("dve", lambda e, Y=Y, a0=a0, a1=a1, lo=lo, w0_=w0_: e.scalar_tensor_tensor(Y[:, lo - a0:a1 - a0], Pf[:, lo - 1:a1 - 1], w0_, Y[:, lo - a0:a1 - a0], ALU.mult, ALU.add),
                             reads=[b_P, b_scw, bY], writes=[bY])
                        hi = min(a1, s1 - 1)
                        S.op("dve", lambda e, Y=Y, a0=a0, hi=hi, w2_=w2_: e.scalar_tensor_tensor(Y[:, 0:hi - a0], Pf[:, a0 + 1:hi + 1], w2_, Y[:, 0:hi - a0], ALU.mult, ALU.add),
                             reads=[b_P, b_scw, bY], writes=[bY])
                        S.op("pool", lambda e, Y=Y, a0=a0, a1=a1, n_=n_: e.tensor_tensor(HGT[:, a0:a1], GBf[:, a0:a1], Y[:, 0:n_], ALU.mult), reads=[b_GB, bY], writes=[b_HGT])
                    S.dma("sp", mixT[(4 + cc) * 128:(5 + cc) * 128, tmin:tmax], HGT[:, tmin:tmax], reads=[b_HGT], writes=[b_mixT[4 + cc]])

            def sg_layer(l, tiles):
                wi = w_in[l].rearrange("(k p) n -> p k n", p=128)
                tmax = (max(tiles) + 1) * 128; tmin = min(tiles) * 128
                if not pre_done.get(("sg", l)):
                    sg_load(l)
                for g in range(4):
                    S.dma("pool", wsn[:], sgw[4 * l + g], writes=[b_wsn])
                    S.op("pe", lambda e: e.transpose(pbb(1, 128), wsn[:], id_bf[:]), reads=[b_wsn, b_id_bf], writes=[b_pb[1]])
                    S.op("act", lambda e, g=g: e.copy(WsT[:, g * 128:(g + 1) * 128], pbb(1, 128)), reads=[b_pb[1]], writes=[b_WsT])
                SGT, b_SGT = KH, b_KH
                sgW = [dict(vn=sb("sgvn%d" % p_, [128, 256], F32), vhb=sb("sgvh%d" % p_, [128, 256], BF16), st=sb("sgst%d" % p_, [128, 40], F32),
                            b_vn=Buf("sgvn%d" % p_), b_vhb=Buf("sgvh%d" % p_), b_st=Buf("sgst%d" % p_), banks=((3, 4, 5), (6, 7, 0))[p_], par=p_) for p_ in range(2)]

                def sg_chain(i, W):
                    ts_ = slice(i * 128, (i + 1) * 128); pv, pu, pm = W["banks"]; par = W["par"]
                    vn_t, vhb_t, st_t = W["vn"], W["vhb"], W["st"]
                    yield
                    for kc in range(KC):
                        S.op("pe", lambda e, kc=kc: e.matmul(pbf(pv, 256), xmT[:, kc * N + i * 128:kc * N + (i + 1) * 128], wvg[1][:, kc * 256:(kc + 1) * 256],
                             start=(kc == 0), stop=(kc == KC - 1)), reads=[b_xmT[i], b_wvg[1]], writes=[b_pb[pv]], inc=(kc == KC - 1))
                    for g in range(4):
                        yield
                        S.op("dve", lambda e, g=g: e.bn_stats(st_t[:, g * 6:(g + 1) * 6], pbank[pv][:, g * 64:(g + 1) * 64]), reads=[b_pb[pv]], writes=[W["b_st"]])
                    for g in range(4):
                        yield
                        S.op("dve", lambda e, g=g: e.bn_aggr(st_t[:, 24 + 2 * g:26 + 2 * g], st_t[:, g * 6:(g + 1) * 6]), reads=[W["b_st"]], writes=[W["b_st"]])
                    mv = st_t[:, 24:32].rearrange("p (g two) -> p g two", two=2)
                    yield
                    S.op("act", lambda e: e.activation(st_t[:, 32:36], mv[:, :, 1], AF.Sqrt, bias=EPS), reads=[W["b_st"]], writes=[W["b_st"]])
                    yield
                    S.op("dve", lambda e: e.reciprocal(st_t[:, 36:40], st_t[:, 32:36]), reads=[W["b_st"]], writes=[W["b_st"]])
                    for g in range(4):
                        yield
                        S.op("dve", lambda e, g=g: e.tensor_scalar(vn_t[:, g * 64:(g + 1) * 64], pbank[pv][:, g * 64:(g + 1) * 64], st_t[:, 24 + 2 * g:25 + 2 * g], st_t[:, 36 + g:37 + g],
                             ALU.subtract, ALU.mult), reads=[b_pb[pv], W["b_st"]], writes=[W["b_vn"]])
                    yield
                    S.op("dve", lambda e: e.tensor_mul(vn_t[:], vn_t[:], lngb[:, 0:256]), reads=[W["b_vn"], b_lngb], writes=[W["b_vn"]])
                    yield
                    S.op("dve", lambda e: e.tensor_add(vhb_t[:], vn_t[:], lngb[:, 256:512]), reads=[W["b_vn"], b_lngb], writes=[W["b_vhb"]])
                    yield
                    for cc in range(2):
                        for kc in range(KC):
                            S.op("pe", lambda e, cc=cc, kc=kc: e.matmul(pbank[pu][:, cc * 128:(cc + 1) * 128], wvg[0][:, kc * 256 + cc * 128:kc * 256 + (cc + 1) * 128], xmT[:, kc * N + i * 128:kc * N + (i + 1) * 128],
                                 start=(kc == 0), stop=(kc == KC - 1)), reads=[b_wvg[0], b_xmT[i]], writes=[b_pb[pu]], inc=(kc == KC - 1 and cc == 1))
                    yield
                    for cc in range(2):
                        for gg in range(2):
                            g = 2 * cc + gg
                            S.op("pe", lambda e, g=g, gg=gg, cc=cc: e.matmul(pbank[pm][gg * 64:(gg + 1) * 64, cc * 128:(cc + 1) * 128], vhb_t[:, g * 64:(g + 1) * 64], WsT[:, g * 128:(g + 1) * 128], start=True, stop=True),
                                 reads=[W["b_vhb"], b_WsT], writes=[b_pb[pm]], inc=(gg == 1 and cc == 1))
                    for cc in range(2):
                        T1, T2 = tmp[cc][1 + 2 * par][:, 0:128], tmp[cc][2 + 2 * par][:, 0:128]
                        bT1, bT2 = b_tmp[cc][1 + 2 * par], b_tmp[cc][2 + 2 * par]
                        yield
                        S.op("dve", lambda e, T1=T1, cc=cc: e.tensor_tensor(T1, pbank[pm][:, cc * 128:(cc + 1) * 128], BS[:, cc * 128:(cc + 1) * 128], ALU.add), reads=[b_pb[pm], b_BS], writes=[bT1])
                        yield
                        S.op("act", lambda e, T2=T2, cc=cc: e.copy(T2, pbank[pu][:, cc * 128:(cc + 1) * 128]), reads=[b_pb[pu]], writes=[bT2])
                        yield
                        S.op("pool", lambda e, T1=T1, T2=T2, cc=cc: e.tensor_tensor(SGT[cc][:, ts_], T1, T2, ALU.mult), reads=[bT1, bT2], writes=[b_SGT[cc]])

                tl_ = list(tiles)

                def sg_stream(k):
                    for i in tl_[k::2]:
                        yield from sg_chain(i, sgW[k])

                run_rr([sg_stream(0), sg_stream(1)])
                for cc in range(2):
                    S.dma("sp", mixT[(6 + cc) * 128:(7 + cc) * 128, tmin:tmax], SGT[cc][:, tmin:tmax], reads=[b_SGT[cc]], writes=[b_mixT[6 + cc]])

            r = hgrn_layer(l, scan_tiles, out_tiles)
            if r == "cut":
                st.enter_context(ms)
                return "cut"
            if debug != "hg":
                sc_layer(l, out_tiles)
                sg_layer(l, out_tiles)
            if debug in ("hg", "mix"):
                mixer.dbg = (Oacc[0], b_O[0], HGT, b_HGT)
                st.enter_context(ms)
                return None
            S.barrier(); ms.close()
            return None

        if mixer(0, list(range(NT)), list(range(NT))) == "cut":
            return nc

        ALPHA = (2 * DEPTH) ** 0.25
        x1d = nc.dram_tensor("x1d", [N, D], F32, kind="Internal").ap()
        b_x1d = [Buf("x1d%d" % i) for i in range(NT)]

        def gate_bcast(dst, b_dst, l, slot, srcm, scr, b_scr):
            for kc in range(KC):
                g = modv(l, slot, kc, srcm)
                S.op("dve", lambda e, g=g: e.tensor_scalar(scr[:], id_f[:], 0.0, g, ALU.mult, ALU.add), reads=[b_id_f, b_mod], writes=[b_scr])
                S.op("pe", lambda e: e.matmul(pbf(0, 128), scr[:], id_f[:], start=True, stop=True), reads=[b_scr, b_id_f], writes=[b_pb[0]])
                S.op("act", lambda e, kc=kc: e.copy(dst[:, kc * 128:(kc + 1) * 128], pbf(0, 128)), reads=[b_pb[0]], writes=[b_dst])

        def phase_wout_ln1(l, src_ap, tiles, src_bufs=None):
            ps4 = ExitStack()
            sb4 = lambda n, s_, dt: ps4.enter_context(nc.sbuf_tensor("%s_p4L%d" % (n, l), s_, dt))
            mixS = sb4("mixS", [128, KC * N], BF16); b_mixS = [Buf("mixS%d" % k) for k in range(KC)]
            wo = sb4("wo", [128, KC * D], BF16); b_wo = Buf("wo")
            gbc = [sb4("gbc%d" % i, [128, D], F32) for i in range(2)]; b_gbc = [Buf("gbc0"), Buf("gbc1")]
            lg = sb4("lg", [128, D], F32); lb_ = sb4("lb_", [128, D], F32); b_lg, b_lbb = Buf("lg"), Buf("lbb")
            scr = sb4("scr", [128, 128], F32); b_scr = Buf("scr")
            for k in range(KC):
                S.dma("sp", mixS[:, k * N:(k + 1) * N], mixT[k * 128:(k + 1) * 128, :], reads=[b_mixT[k]], writes=[b_mixS[k]])
            S.dma("pool", wo[:].rearrange("p (k n) -> p k n", k=KC), w_out[l].rearrange("(k p) n -> p k n", p=128), writes=[b_wo])
            S.dma("sp", lg[:], lnp[4 * l:4 * l + 1, :].partition_broadcast(128), writes=[b_lg])
            S.dma("sp", lb_[:], lnp[4 * l + 1:4 * l + 2, :].partition_broadcast(128), writes=[b_lbb])
            gate_bcast(gbc[0], b_gbc[0], l, 2, 0, scr, b_scr)
            if any(i < 2 for i in tiles):
                gate_bcast(gbc[1], b_gbc[1], l, 2, 1, scr, b_scr)
            G4 = 3
            sets = mk_sets(sb4, 2 * G4, "b", with_y=True, plan=[(3, 3, 1), (4, 4, 2), (5, 5, 6)])
            load = lambda i, B_: S.dma("sp", B_["xq"][:], src_ap[i * 128:(i + 1) * 128, :], reads=([src_bufs[i]] if src_bufs else []), writes=[B_["b_xq"]])

            def chain4(i, B_):
                srcm = 1 if i < 2 else 0
                xq_t, yt_t, xt_t, st_t = B_["xq"], B_["yt"], B_["xt"], B_["st"]
                for hf in range(2):
                    pbi = B_["mm"][hf]; hs = slice(hf * 512, (hf + 1) * 512)
                    yield
                    for kc in range(KC):
                        S.op("pe", lambda e, kc=kc, hf=hf, pbi=pbi: e.matmul(pbf(pbi, 512), mixS[:, kc * N + i * 128:kc * N + (i + 1) * 128],
                             wo[:, kc * D + hf * 512:kc * D + (hf + 1) * 512], start=(kc == 0), stop=(kc == KC - 1)),
                             reads=[b_mixS[kc], b_wo], writes=[b_pb[pbi]], inc=(kc == KC - 1))
                    yield
                    S.op("dve", lambda e, hs=hs, pbi=pbi: e.tensor_tensor(yt_t[:, hs], pbf(pbi, 512), gbc[srcm][:, hs], ALU.mult),
                         reads=[b_pb[pbi], b_gbc[srcm]], writes=[B_["b_yt"]])
                    yield
                    S.op("dve", lambda e, hs=hs: e.scalar_tensor_tensor(yt_t[:, hs], xq_t[:, hs], ALPHA, yt_t[:, hs], ALU.mult, ALU.add),
                         reads=[B_["b_xq"], B_["b_yt"]], writes=[B_["b_yt"]])
                yield from ln_stats_gen(yt_t, B_["b_yt"], B_)
                yield
                S.op("dve", lambda e: e.scalar_tensor_tensor(yt_t[:], yt_t[:], st_t[:, 12:13], lg[:], ALU.subtract, ALU.mult),
                     reads=[B_["b_yt"], B_["b_st"], b_lg], writes=[B_["b_yt"]])
                yield
                S.op("dve", lambda e: e.scalar_tensor_tensor(xt_t[:], yt_t[:], st_t[:, 15:16], lb_[:], ALU.mult, ALU.add),
                     reads=[B_["b_yt"], B_["b_st"], b_lbb, B_["b_xt"]], writes=[B_["b_xt"]])
                yield
                S.dma("sp", x1d[i * 128:(i + 1) * 128, :], xt_t[:], reads=[B_["b_xt"]], writes=[b_x1d[i]])
                yield from ln_mod_T_gen(l, i, 3, 4, B_)

            run_groups(tiles, sets, load, chain4, GRP=G4)
            S.barrier(); ps4.close()

        if debug in ("x1", "h", "x2", None):
            phase_wout_ln1(0, xin, list(range(NT)))
        if debug == "x1":
            for i in range(NT):
                a = i % 2
                S.dma("sp", xt[a][:], x1d[i * 128:(i + 1) * 128, :], reads=[b_x1d[i]], writes=[b_xt[a]])
                S.dma("sp", outs["dbg_x1"][i * 128:(i + 1) * 128, :], xt[a][:], reads=[b_xt[a]])
            xm_f = sb("xm2_f", [128, N], F32); b_xmf = Buf("xm2f")
            for kc in range(KC):
                S.op("act", lambda e, kc=kc: e.copy(xm_f[:], xmT[:, kc * N:(kc + 1) * N]), reads=b_xmT, writes=[b_xmf])
                S.dma("sp", outs["dbg_xm2T"][:, kc * N:(kc + 1) * N], xm_f[:], reads=[b_xmf])

        hTd = nc.dram_tensor("hTd", [FF, N], BF16, kind="Internal").ap()
        b_hTd = [Buf("hTd%d" % i) for i in range(NFC)]
        x2d = [nc.dram_tensor("x2d%d" % i, [N, D], F32, kind="Internal").ap() for i in range(DEPTH - 1)]
        b_x2d = [Buf("x2d%d" % i) for i in range(NT)]
        GW = 64

        def phase_ffn_up(l, with_ctx):
            p5 = ExitStack()
            sb5 = lambda n, s_, dt: p5.enter_context(nc.sbuf_tensor("%s_p5L%d" % (n, l), s_, dt))
            fcw_s = sb5("fcw_s", [128, NFC * 9], F32); fcb_s = sb5("fcb_s", [128, NFC], F32); b_fcw, b_fcb = Buf("fcw"), Buf("fcb")
            S.dma("sp", fcw_s[:], fcw[:, l * NFC * 9:(l + 1) * NFC * 9], writes=[b_fcw])
            S.dma("sp", fcb_s[:], fcb[:, l * NFC:(l + 1) * NFC], writes=[b_fcb])
            wag = [sb5("wag%d" % i, [128, KC * 256], BF16) for i in range(2)]; b_wag = [Buf("wag0"), Buf("wag1")]
            apx = [sb5("apx%d" % i, [128, 34 * 66], BF16) for i in range(2)]; apc = [sb5("apc%d" % i, [128, 258], BF16) for i in range(2)]
            b_ap = [Buf("ap0"), Buf("ap1")]
            dg = [sb5("dg%d" % i, [128, 9 * 128], BF16) for i in range(2)]; b_dg = [Buf("dg0"), Buf("dg1")]
            gel = [sb5("gel%d" % i, [128, 512], F32) for i in range(2)]; b_gel = [Buf("gel0"), Buf("gel1")]
            htc = [sb5("htc%d" % i, [128, N], BF16) for i in range(2)]; b_htc = [Buf("htc0"), Buf("htc1")]
            for i in range(2):
                S.op("pool", lambda e, i=i: e.memset(apx[i][:], 0.0), writes=[b_ap[i]])
                S.op("pool", lambda e, i=i: e.memset(apc[i][:], 0.0), writes=[b_ap[i]])
            FB = ([("c", 0, 256, 0)] if with_ctx else []) + [("x", 256 + 512 * j, 512, j) for j in range(4)]
            wu = ffn_up[l].rearrange("(k p) n -> p k n", p=128)
            defer_l = l + 1 if (l + 1 < DEPTH) else None
            if defer_l is not None:
                awd = [sb5("awd%d" % i, [128, KC * 512], BF16) for i in range(2)]; b_awd = [Buf("awd0"), Buf("awd1")]
                pm_d, b_pmd = pbf(1, 96), b_pb[1]
            for fc in range(NFC):
                a = fc % 2
                if defer_l is not None:
                    if fc < 12:
                        mod_chunk_load(defer_l, fc, awd[fc % 2], b_awd[fc % 2])
                    if 1 <= fc <= 12:
                        mod_chunk_mm(defer_l, fc - 1, awd[(fc - 1) % 2], b_awd[(fc - 1) % 2], pm_d, b_pmd)
                    if fc == 13:
                        mod_finish(defer_l, pm_d, b_pmd)
                for ci, c0 in enumerate((fc * 128, FF + fc * 128)):
                    S.dma("pool", wag[a][:].rearrange("p (k n) -> p k n", k=KC)[:, :, ci * 128:(ci + 1) * 128], wu[:, :, c0:c0 + 128], writes=[b_wag[a]])
                for tap in range(9):
                    wcol = fcw_s[:, fc * 9 + tap:fc * 9 + tap + 1]
                    S.op("dve", lambda e, a=a, tap=tap, wcol=wcol: e.tensor_scalar(dg[a][:, tap * 128:(tap + 1) * 128], id_f[:], wcol, None, ALU.mult),
                         reads=[b_id_f, b_fcw], writes=[b_dg[a]])
                apx3 = apx[a][:].rearrange("p (r c) -> p r c", c=66)
                for bn, (kind, t0, nb, j) in enumerate(FB):
                    pa = (3, 6)[bn % 2]
                    tl = list(range(t0 // 128, (t0 + nb) // 128))
                    for kc in range(KC):
                        S.op("pe", lambda e, a=a, kc=kc, t0=t0, nb=nb, pa=pa: e.matmul(pbf(pa, nb), wag[a][:, kc * 256:kc * 256 + 128], xmT[:, kc * N + t0:kc * N + t0 + nb],
                             start=(kc == 0), stop=(kc == KC - 1)), reads=[b_wag[a]] + [b_xmT[i] for i in tl], writes=[b_pb[pa]], inc=(kc == KC - 1))
                    if kind == "c":
                        S.op("act", lambda e, a=a, pa=pa: e.copy(apc[a][:, 1:257], pbf(pa, 256)), reads=[b_pb[pa]], writes=[b_ap[a]])
                    else:
                        S.op("act", lambda e, a=a, pa=pa, j=j, apx3=apx3: e.copy(apx3[:, 1 + 8 * j:9 + 8 * j, 1:65], pbf(pa, 512).rearrange("p (r c) -> p r c", c=GW)),
                             reads=[b_pb[pa]], writes=[b_ap[a]])
                for bn, (kind, t0, nb, j) in enumerate(FB):
                    pc, pg = (4, 7)[bn % 2], (5, 0)[bn % 2]
                    ga = bn % 2
                    tl = list(range(t0 // 128, (t0 + nb) // 128))
                    if kind == "c":
                        for n_, dj in enumerate(range(3)):
                            tap = 3 + dj
                            S.op("pe", lambda e, a=a, tap=tap, dj=dj, pc=pc, n_=n_: e.matmul(pbf(pc, 256), dg[a][:, tap * 128:(tap + 1) * 128], apc[a][:, dj:dj + 256], start=(n_ == 0), stop=(n_ == 2)),
                                 reads=[b_dg[a], b_ap[a]], writes=[b_pb[pc]], inc=(n_ == 2))
                    else:
                        for tap in range(9):
                            di, dj = tap // 3, tap % 3
                            mv_ = apx3[:, di + 8 * j:di + 8 * j + 8, dj:dj + GW]
                            S.op("pe", lambda e, a=a, tap=tap, mv_=mv_, pc=pc: e.matmul(pbf(pc, 512), dg[a][:, tap * 128:(tap + 1) * 128], mv_, start=(tap == 0), stop=(tap == 8)),
                                 reads=[b_dg[a], b_ap[a]], writes=[b_pb[pc]], inc=(tap == 8))
                    for kc in range(KC):
                        S.op("pe", lambda e, a=a, kc=kc, t0=t0, nb=nb, pg=pg: e.matmul(pbf(pg, nb), wag[a][:, kc * 256 + 128:kc * 256 + 256], xmT[:, kc * N + t0:kc * N + t0 + nb],
                             start=(kc == 0), stop=(kc == KC - 1)), reads=[b_wag[a]] + [b_xmT[i] for i in tl], writes=[b_pb[pg]], inc=(kc == KC - 1))
                    bcol = fcb_s[:, fc:fc + 1]
                    S.op("act", lambda e, ga=ga, nb=nb, pc=pc, bcol=bcol: e.activation(gel[ga][:, 0:nb], pbf(pc, nb), AF.Gelu, bias=bcol), reads=[b_pb[pc], b_fcb], writes=[b_gel[ga]])
                    S.op("dve", lambda e, a=a, ga=ga, t0=t0, nb=nb, pg=pg: e.tensor_tensor(htc[a][:, t0:t0 + nb], gel[ga][:, 0:nb], pbf(pg, nb), ALU.mult),
                         reads=[b_gel[ga], b_pb[pg]], writes=[b_htc[a]])
                lo = FB[0][1]
                S.dma("sp", hTd[fc * 128:(fc + 1) * 128, lo:N], htc[a][:, lo:N], reads=[b_htc[a]], writes=[b_hTd[fc]])
            S.barrier(); p5.close()

        def phase_ffn_down(l, tiles, dst_ap, dst_row0):
            p6 = ExitStack()
            sb6 = lambda n, s_, dt: p6.enter_context(nc.sbuf_tensor("%s_p6L%d" % (n, l), s_, dt))
            wd = sb6("wd", [128, NFC * D], BF16); b_wdp = {(hf_, q_): Buf("wd%d%d" % (hf_, q_)) for hf_ in range(2) for q_ in range(2)}
            gbc = [sb6("gbc%d" % i, [128, D], F32) for i in range(2)]; b_gbc = [Buf("gbc0"), Buf("gbc1")]
            lg = sb6("lg", [128, D], F32); lb_ = sb6("lb_", [128, D], F32); b_lg, b_lbb = Buf("lg"), Buf("lbb")
            scr = sb6("scr", [128, 128], F32); b_scr = Buf("scr")
            wdv = ffn_down[l].rearrange("(f p) n -> p f n", p=128)
            for hf_ in range(2):
                for q_ in range(2):
                    S.dma("pool", wd[:].rearrange("p (f n) -> p f n", f=NFC)[:, q_ * 11:(q_ + 1) * 11, hf_ * 512:(hf_ + 1) * 512],
                          wdv[:, q_ * 11:(q_ + 1) * 11, hf_ * 512:(hf_ + 1) * 512], writes=[b_wdp[(hf_, q_)]])
            S.dma("sp", lg[:], lnp[4 * l + 2:4 * l + 3, :].partition_broadcast(128), writes=[b_lg])
            S.dma("sp", lb_[:], lnp[4 * l + 3:4 * l + 4, :].partition_broadcast(128), writes=[b_lbb])
            gate_bcast(gbc[0], b_gbc[0], l, 5, 0, scr, b_scr)
            if any(i < 2 for i in tiles):
                gate_bcast(gbc[1], b_gbc[1], l, 5, 1, scr, b_scr)
            hv = hTd.rearrange("(f p) n -> p f n", p=128)
            sets = mk_sets(sb6, 2 * GRP, "c", with_y=True, with_h=True)

            def load(i, B_):
                S.dma("sp", B_["ht"][:].rearrange("p (f n) -> p f n", f=NFC), hv[:, :, i * 128:(i + 1) * 128], reads=b_hTd, writes=[B_["b_ht"]])
                S.dma("sp", B_["xq"][:], x1d[i * 128:(i + 1) * 128, :], reads=[b_x1d[i]], writes=[B_["b_xq"]])

            def chain6(i, B_):
                srcm = 1 if i < 2 else 0
                xq_t, yt_t, xt_t, st_t, ht_t = B_["xq"], B_["yt"], B_["xt"], B_["st"], B_["ht"]
                for hf in range(2):
                    pbi = B_["mm"][hf]; hs = slice(hf * 512, (hf + 1) * 512)
                    yield
                    for f_ in range(NFC):
                        S.op("pe", lambda e, f_=f_, hf=hf, pbi=pbi: e.matmul(pbf(pbi, 512), ht_t[:, f_ * 128:(f_ + 1) * 128], wd[:, f_ * D + hf * 512:f_ * D + (hf + 1) * 512],
                             start=(f_ == 0), stop=(f_ == NFC - 1)), reads=[B_["b_ht"], b_wdp[(hf, f_ // 11)]], writes=[b_pb[pbi]], inc=(f_ == NFC - 1))
                    yield
                    S.op("dve", lambda e, hs=hs, pbi=pbi: e.tensor_tensor(yt_t[:, hs], pbf(pbi, 512), gbc[srcm][:, hs], ALU.mult),
                         reads=[b_pb[pbi], b_gbc[srcm]], writes=[B_["b_yt"]])
                    yield
                    S.op("dve", lambda e, hs=hs: e.scalar_tensor_tensor(yt_t[:, hs], xq_t[:, hs], ALPHA, yt_t[:, hs], ALU.mult, ALU.add),
                         reads=[B_["b_xq"], B_["b_yt"]], writes=[B_["b_yt"]])
                yield from ln_stats_gen(yt_t, B_["b_yt"], B_)
                yield
                S.op("dve", lambda e: e.scalar_tensor_tensor(yt_t[:], yt_t[:], st_t[:, 12:13], lg[:], ALU.subtract, ALU.mult),
                     reads=[B_["b_yt"], B_["b_st"], b_lg], writes=[B_["b_yt"]])
                yield
                S.op("dve", lambda e: e.scalar_tensor_tensor(xt_t[:], yt_t[:], st_t[:, 15:16], lb_[:], ALU.mult, ALU.add),
                     reads=[B_["b_yt"], B_["b_st"], b_lbb, B_["b_xt"]], writes=[B_["b_xt"]])
                r0 = i * 128 - dst_row0
                yield
                S.dma("sp", dst_ap[r0:r0 + 128, :], xt_t[:], reads=[B_["b_xt"]], writes=[b_x2d[i]])

            run_groups(tiles, sets, load, chain6)
            S.barrier(); p6.close()

        if debug in ("h", "x2", None):
            phase_ffn_up(0, True)
        if debug == "h":
            hst_b = sb("hst_b", [128, N], BF16); hst_f = sb("hst_f", [128, N], F32); b_hsb, b_hsf = Buf("hsb"), Buf("hsf")
            for fc in range(NFC):
                S.dma("sp", hst_b[:], hTd[fc * 128:(fc + 1) * 128, :], reads=[b_hTd[fc]], writes=[b_hsb])
                S.op("act", lambda e: e.copy(hst_f[:], hst_b[:]), reads=[b_hsb], writes=[b_hsf])
                S.dma("sp", outs["dbg_hT"][fc * 128:(fc + 1) * 128, :], hst_f[:], reads=[b_hsf])
        if debug in ("x2", None):
            phase_ffn_down(0, list(range(NT)), x2d[0], 0)
        if debug == "x2":
            for i in range(NT):
                a = i % 2
                S.dma("sp", xt[a][:], x2d[0][i * 128:(i + 1) * 128, :], reads=[b_x2d[i]], writes=[b_xt[a]])
                S.dma("sp", outs["dbg_x2"][i * 128:(i + 1) * 128, :], xt[a][:], reads=[b_xt[a]])

        if debug is None:
            XT = list(range(2, NT))
            phase_ln_mod_T(1, x2d[0], range(NT), 0, 1, src_bufs=b_x2d)
            mixer(1, list(range(NT)), XT)
            phase_wout_ln1(1, x2d[0], XT, src_bufs=b_x2d)
            phase_ffn_up(1, False)
            phase_ffn_down(1, XT, outs["out"], 256)
        if debug in ("hg", "mix"):
            hg_f, b_hgf, hg_b, b_hgb = mixer.dbg
            for h in range(8 if debug == "mix" else 4):
                S.dma("sp", hg_b[:], mixT[h * 128:(h + 1) * 128, :], reads=[b_mixT[h]], writes=[b_hgb])
                S.op("act", lambda e: e.copy(hg_f[:], hg_b[:]), reads=[b_hgb], writes=[b_hgf])
                S.dma("sp", outs["dbg_hgT"][h * 128:(h + 1) * 128, :], hg_f[:], reads=[b_hgf])
        S.drain_all("sp")
        S.emit()
        build.ninstr = S.ninstr
    return nc


def _prep(inputs, b):
    f = lambda a: np.ascontiguousarray(np.asarray(a, dtype=np.float32))
    m = {}
    m["xin"] = f(np.concatenate([inputs["ctx"][b], inputs["x"][b]], axis=0))
    cc = np.stack([np.asarray(inputs["c"][b]).reshape(KC, 128).T, np.asarray(inputs["c_ctx"]).reshape(KC, 128).T], axis=-1)
    m["c2"] = f(cc.reshape(128, KC * 2))
    m["ada_w"] = f(inputs["ada_w"])
    m["ada_b_fm"] = f(np.asarray(inputs["ada_b"]).reshape(DEPTH, 48, 128).transpose(2, 0, 1).reshape(128, DEPTH * 48))
    m["ident"] = np.eye(128, dtype=np.float32)
    m["w_in"] = f(inputs["w_in"])
    m["w_out"] = f(inputs["w_out"])
    m["ffn_up"] = f(inputs["ffn_up"]); m["ffn_down"] = f(inputs["ffn_down"])
    m["fcw_fm"] = f(np.asarray(inputs["ffn_conv_w"]).reshape(DEPTH, 9, NFC, 128).transpose(3, 0, 2, 1).reshape(128, DEPTH * NFC * 9))
    m["fcb_fm"] = f(np.asarray(inputs["ffn_conv_b"]).reshape(DEPTH, NFC, 128).transpose(2, 0, 1).reshape(128, DEPTH * NFC))
    m["lnp"] = f(np.stack([np.asarray(inputs[k]) for k in ("ln1_g", "ln1_b", "ln2_g", "ln2_b")], axis=1).reshape(DEPTH * 4, D))
    m["scw_fm"] = f(np.asarray(inputs["sc_conv_w"]).reshape(DEPTH, 3, 2, 128).transpose(3, 0, 2, 1).reshape(128, DEPTH * 6))
    m["sgln"] = f(np.stack([np.asarray(inputs["sg_ln_g"]), np.asarray(inputs["sg_ln_b"])], axis=1).reshape(DEPTH * 2, 256))
    m["sg_w"] = f(np.asarray(inputs["sg_w"]).reshape(DEPTH * 4, 128, 128))
    sb_ = np.asarray(inputs["sg_b"]).reshape(DEPTH, 2, 2, 1, 128)
    m["sgb_fm"] = f(np.broadcast_to(sb_, (DEPTH, 2, 2, 64, 128)).reshape(DEPTH * 2, 128, 128))
    m["lbl"] = f(np.asarray(inputs["hg_lb"]).reshape(DEPTH, 2, 4, 128).transpose(3, 0, 1, 2).reshape(128, 16))
    m["hg_norm_g"] = f(inputs["hg_norm_g"])
    ii = np.arange(128)
    same = (ii[:, None] // 64) == (ii[None, :] // 64)
    m["cmask"] = f(np.concatenate([same & (ii[:, None] <= ii[None, :]), same & (ii[:, None] >= ii[None, :])], axis=1))
    rm = np.ones((128, 512), np.float32); rm[:, ::64] = 0.0
    m["rmask"] = rm
    return m


_NC = None


def kernel(**inputs):
    global _NC
    if _NC is None:
        _NC = build()
    shared = None
    maps = []
    for b in range(8):
        m = _prep(inputs, b)
        if shared is None:
            shared = {k: m[k] for k in m if k not in ("xin", "c2")}
        else:
            m.update(shared)
        maps.append(m)
    res = run_bass_kernel_spmd(_NC, maps, core_ids=list(range(8)))
    return np.stack([np.asarray(r["out"], dtype=np.float32) for r in res.results], axis=0)
```

```python
import numpy as np
import concourse.bass as bass
import concourse.mybir as mybir

F32 = mybir.dt.float32
BF16 = mybir.dt.bfloat16
ALU = mybir.AluOpType
AF = mybir.ActivationFunctionType


class Buf:
    __slots__ = ("name", "w", "r", "excl")

    def __init__(self, name="", excl=False):
        self.name = name
        self.excl = excl
        self.w = None
        self.r = []


class Sync:
    ENGS = ("pe", "act", "dve", "pool", "sp")

    SEM_LIMIT = 1900

    def __init__(self, nc, stack, n_dma_sems=20):
        self.nc = nc
        self.stack = stack
        self.owner = {}
        self.cur = {}
        self.nsem = 0
        self.q = {e: [] for e in self.ENGS}
        self.sems = {}
        self.cnt = {}
        self.known = {e: {} for e in self.ENGS}
        for e in ("pe", "act", "dve", "pool"):
            self.cur[e] = None
            self._new_sem(e)
        self.dpool = {}
        self.dk = {}
        for e in ("sp", "act", "pool"):
            self.dpool[e] = []
            for i in range(n_dma_sems):
                key = "d_%s_%d" % (e, i)
                self.sems[key] = stack.enter_context(nc.semaphore(key)); self.nsem += 1
                self.cnt[key] = 0
                self.owner[key] = "dma_" + e
                self.dpool[e].append(key)
            self.dk[e] = 0
        self.pe_pending = False
        self.ninstr = 0

    def _new_sem(self, eng):
        n = sum(1 for k in self.owner if self.owner[k] == eng)
        key = "%s#%d" % (eng, n)
        self.sems[key] = self.stack.enter_context(self.nc.semaphore("s_%s_%d" % (eng, n))); self.nsem += 1
        self.cnt[key] = 0
        self.owner[key] = eng
        self.prev = getattr(self, "prev", {})
        self.prev[eng] = self.cur[eng]
        self.cur[eng] = key

    def _latest(self, eng):
        k = self.cur[eng]
        if self.cnt[k]:
            return (k, self.cnt[k])
        p = self.prev.get(eng)
        return (p, self.cnt[p]) if p and self.cnt[p] else None

    def _wait(self, eng, tok):
        if tok is None:
            return
        key, val = tok
        if self.known[eng].get(key, 0) >= val:
            return
        self.known[eng][key] = val
        sem = self.sems[key]
        self.q[eng].append(lambda e, sem=sem, val=val: e.wait_ge(sem, val))
        self.ninstr += 1

    def _deps(self, eng, reads, writes, skip_self=False):
        for b in reads:
            if b.w is not None and not (skip_self and self.owner[b.w[0]] == eng):
                self._wait(eng, b.w)
            if b.excl:
                for t in b.r:
                    if self.owner[t[0]] != eng:
                        self._wait(eng, t)
        for b in writes:
            if b.w is not None and not (skip_self and self.owner[b.w[0]] == eng):
                self._wait(eng, b.w)
            for t in b.r:
                if not (skip_self and self.owner[t[0]] == eng):
                    self._wait(eng, t)

    def _commit(self, tok, reads, writes):
        for b in reads:
            b.r.append(tok)
            if len(b.r) > 64:
                best = {}
                for k, v in b.r:
                    if best.get(k, 0) < v:
                        best[k] = v
                b.r = list(best.items())
        for b in writes:
            b.w = tok
            b.r = []

    def op(self, eng, fn, reads=(), writes=(), inc=True):
        pe = eng == "pe"
        self._deps(eng, reads, writes, skip_self=pe)
        if self.cnt[self.cur[eng]] >= self.SEM_LIMIT and not (pe and self.pe_pending):
            self._new_sem(eng)
        key = self.cur[eng]
        if inc:
            self.cnt[key] += 1
            tok = (key, self.cnt[key])
            sem = self.sems[key]
            self.q[eng].append(lambda e, fn=fn, sem=sem: fn(e).then_inc(sem, 1))
            if pe:
                self.pe_pending = False
        else:
            assert pe
            tok = (key, self.cnt[key] + 1)
            self.q[eng].append(lambda e, fn=fn: fn(e))
            self.pe_pending = True
        self.ninstr += 1
        self._commit(tok, reads, writes)
        return tok

    def dma(self, eng, out, in_, reads=(), writes=(), **kw):
        self._deps(eng, reads, writes)
        pool = self.dpool[eng]
        slot = self.dk[eng] % len(pool)
        key = pool[slot]
        self.dk[eng] += 1
        if self.cnt[key] + 16 > self.SEM_LIMIT:
            self._wait(eng, (key, self.cnt[key]))
            n = sum(1 for k in self.owner if self.owner[k] == "dma_" + eng)
            nk = "d_%s_%d" % (eng, n)
            self.sems[nk] = self.stack.enter_context(self.nc.semaphore(nk)); self.nsem += 1
            self.cnt[nk] = 0; self.owner[nk] = "dma_" + eng
            pool[slot] = nk; key = nk
            self.retired = getattr(self, "retired", []) + [key]
        if self.cnt[key] > 0:
            self._wait(eng, (key, self.cnt[key]))
        self.cnt[key] += 16
        tok = (key, self.cnt[key])
        sem = self.sems[key]
        self.q[eng].append(
            lambda e, out=out, in_=in_, sem=sem, kw=kw: e.dma_start(out=out, in_=in_, **kw).then_inc(sem, 16))
        self.ninstr += 1
        self._commit(tok, reads, writes)
        return tok

    def wait_all(self, eng, toks):
        for t in toks:
            self._wait(eng, t)

    def barrier(self):
        toks = [t for t in (self._latest(e) for e in ("pe", "act", "dve", "pool")) if t]
        assert not self.pe_pending
        for q in self.dpool:
            for key in self.dpool[q]:
                if self.cnt[key]:
                    toks.append((key, self.cnt[key]))
        for eng in self.ENGS:
            for t in toks:
                if self.owner[t[0]] != eng:
                    self._wait(eng, t)

    def drain_all(self, eng="sp"):
        for q in self.dpool:
            for key in self.dpool[q]:
                if self.cnt[key]:
                    self._wait(eng, (key, self.cnt[key]))

    def emit(self):
        assert not self.pe_pending, "last PE op must carry inc"
        nc = self.nc
        q = self.q
        with nc.Block() as block:
            @block.sync
            def _(e):
                for f in q["sp"]:
                    f(e)

            @block.tensor
            def _(e):
                for f in q["pe"]:
                    f(e)

            @block.scalar
            def _(e):
                for f in q["act"]:
                    f(e)

            @block.vector
            def _(e):
                for f in q["dve"]:
                    f(e)

            @block.gpsimd
            def _(e):
                for f in q["pool"]:
                    f(e)


def run_rr(chains):
    live = list(chains)
    while live:
        for g in list(live):
            try:
                next(g)
            except StopIteration:
                live.remove(g)


from contextlib import ExitStack
from concourse.bass_utils import run_bass_kernel_spmd

FF = 2816; NFC = FF // 128
D = 1024; NT = 18; N = NT * 128; DEPTH = 2; NMOD = 6; EPS = 1e-6
KC = D // 128


def build(debug=None):
    nc = bass.Bass("TRN2", target_bir_lowering=False)
    dram = lambda n, s, dt, kind: nc.dram_tensor(n, s, dt, kind=kind).ap()
    xin = dram("xin", [N, D], F32, "ExternalInput")
    c2 = dram("c2", [128, KC * 2], F32, "ExternalInput")
    ada_w = dram("ada_w", [DEPTH, D, NMOD * D], F32, "ExternalInput")
    ada_b = dram("ada_b_fm", [128, DEPTH * 48], F32, "ExternalInput")
    ident = dram("ident", [128, 128], F32, "ExternalInput")
    w_in = dram("w_in", [DEPTH, D, 3840], F32, "ExternalInput")
    lbl = dram("lbl", [128, 16], F32, "ExternalInput")
    hgng = dram("hg_norm_g", [DEPTH, 128], F32, "ExternalInput")
    scw = dram("scw_fm", [128, DEPTH * 6], F32, "ExternalInput")
    sgln = dram("sgln", [DEPTH * 2, 256], F32, "ExternalInput")
    sgw = dram("sg_w", [DEPTH * 4, 128, 128], F32, "ExternalInput")
    sgb = dram("sgb_fm", [DEPTH * 2, 128, 128], F32, "ExternalInput")
    w_out = dram("w_out", [DEPTH, D, D], F32, "ExternalInput")
    ffn_up = dram("ffn_up", [DEPTH, D, 2 * FF], F32, "ExternalInput")
    ffn_down = dram("ffn_down", [DEPTH, FF, D], F32, "ExternalInput")
    fcw = dram("fcw_fm", [128, DEPTH * NFC * 9], F32, "ExternalInput")
    fcb = dram("fcb_fm", [128, DEPTH * NFC], F32, "ExternalInput")
    lnp = dram("lnp", [DEPTH * 4, D], F32, "ExternalInput")
    cmask = dram("cmask", [128, 256], F32, "ExternalInput")
    rmask = dram("rmask", [128, 512], F32, "ExternalInput")
    outs = {}
    if debug is None:
        outs["out"] = dram("out", [N - 256, D], F32, "ExternalOutput")
    if debug and debug.startswith("hg") and len(debug) == 3:
        outs["dbg_cut"] = dram("dbg_cut", [128, 2 * N], F32, "ExternalOutput")
    if debug == "h":
        outs["dbg_hT"] = dram("dbg_hT", [FF, N], F32, "ExternalOutput")
    if debug == "x2":
        outs["dbg_x2"] = dram("dbg_x2", [N, D], F32, "ExternalOutput")
    if debug == "x1":
        outs["dbg_x1"] = dram("dbg_x1", [N, D], F32, "ExternalOutput")
        outs["dbg_xm2T"] = dram("dbg_xm2T", [128, KC * N], F32, "ExternalOutput")
    if debug in ("hg", "mix"):
        outs["dbg_hgT"] = dram("dbg_hgT", [D if debug == "mix" else 512, N], F32, "ExternalOutput")
    if debug == "p1":
        outs["dbg_mod"] = dram("dbg_mod", [128, DEPTH * 96], F32, "ExternalOutput")
        outs["dbg_xmT"] = dram("dbg_xmT", [128, KC * N], F32, "ExternalOutput")
    with ExitStack() as st:
        S = Sync(nc, st)
        sb = lambda n, s, dt: st.enter_context(nc.sbuf_tensor(n, s, dt))
        pbank = [st.enter_context(nc.psum_tensor("pb%d" % i, [128, 512], F32)) for i in range(8)]
        b_pb = [Buf("pb%d" % i, excl=True) for i in range(8)]
        pbf = lambda i, n: pbank[i][:, 0:n]
        pbb = lambda i, n: pbank[i][:].bitcast(BF16)[:, 0:n]
        id_f = sb("id_f", [128, 128], F32); id_bf = sb("id_bf", [128, 128], BF16)
        c2_f = sb("c2_f", [128, KC * 2], F32); c2_bf = sb("c2_bf", [128, KC * 2], BF16)
        adab = sb("adab", [128, DEPTH * 48], F32)
        mod = sb("mod", [128, DEPTH * 96], F32)
        mod1p = sb("mod1p", [128, DEPTH * 96], F32)
        b_id_f, b_id_bf, b_c2f, b_c2bf, b_adab, b_mod, b_mod1p = (Buf(n) for n in "idf idbf c2f c2bf adab mod mod1p".split())
        S.dma("sp", id_f[:], ident, writes=[b_id_f])
        S.dma("pool", id_bf[:], ident, writes=[b_id_bf])
        S.dma("sp", c2_f[:], c2, writes=[b_c2f])
        S.dma("sp", adab[:], ada_b, writes=[b_adab])
        S.op("act", lambda e: e.activation(c2_bf[:], c2_f[:], AF.Silu), reads=[b_c2f], writes=[b_c2bf])
        st0 = ExitStack()
        aw = [st0.enter_context(nc.sbuf_tensor("aw%d" % i, [128, KC * 512], BF16)) for i in range(2)]
        b_aw = [Buf("aw0"), Buf("aw1")]
        p_mod = pbf(0, 96); b_pmod = b_pb[0]
        def mod_chunk_load(l, g, buf, bb):
            src = ada_w[l].rearrange("(k p) n -> p k n", p=128)[:, :, g * 512:(g + 1) * 512]
            S.dma("pool", buf[:].rearrange("p (k n) -> p k n", k=KC), src, writes=[bb])

        def mod_chunk_mm(l, g, buf, bb, pm_ap, b_pm):
            for jj in range(4):
                j = g * 4 + jj
                for k in range(KC):
                    last = (k == KC - 1)
                    S.op("pe", lambda e, buf=buf, jj=jj, k=k, j=j, last=last: e.matmul(
                        pm_ap[:, 2 * j:2 * j + 2], buf[:, k * 512 + jj * 128:k * 512 + (jj + 1) * 128],
                        c2_bf[:, 2 * k:2 * k + 2], start=(k == 0), stop=last),
                        reads=[bb, b_c2bf], writes=[b_pm], inc=(last and jj == 3))

        def mod_finish(l, pm_ap, b_pm):
            pm = pm_ap.rearrange("p (j s) -> p j s", s=2)
            mv = mod[:, l * 96:(l + 1) * 96].rearrange("p (j s) -> p j s", s=2)
            for s_ in range(2):
                S.op("dve", lambda e, pm=pm, mv=mv, s_=s_, l=l: e.tensor_tensor(
                    mv[:, :, s_], pm[:, :, s_], adab[:, l * 48:(l + 1) * 48], ALU.add),
                    reads=[b_pm, b_adab], writes=[b_mod])
            S.op("dve", lambda e, l=l: e.tensor_scalar_add(mod1p[:, l * 96:(l + 1) * 96], mod[:, l * 96:(l + 1) * 96], 1.0), reads=[b_mod], writes=[b_mod1p])

        for g in range(12):
            mod_chunk_load(0, g, aw[g % 2], b_aw[g % 2])
            mod_chunk_mm(0, g, aw[g % 2], b_aw[g % 2], p_mod, b_pmod)
        mod_finish(0, p_mod, b_pmod)
        S.barrier(); st0.close()
        if debug == "p1":
            S.dma("sp", outs["dbg_mod"], mod[:], reads=[b_mod])

        def modv(l, slot, kc, src, one_plus=False):
            t = mod1p if one_plus else mod
            col = l * 96 + (slot * 8 + kc) * 2 + src
            return t[:, col:col + 1]

        xmT = sb("xmT", [128, KC * N], BF16)
        b_xmT = [Buf("xmT%d" % i) for i in range(NT)]
        if debug in ("x1", "x2"):
            xt = [sb("xt%d" % i, [128, D], F32) for i in range(2)]; b_xt = [Buf("xt0"), Buf("xt1")]

        def mk_sets(alloc, n, tag, with_y=False, with_h=False, plan=None):
            sets = []
            for g in range(n):
                B_ = dict(xt=alloc("xt%s%d" % (tag, g), [128, D], F32), xn=alloc("xn%s%d" % (tag, g), [128, D], BF16), st=alloc("st%s%d" % (tag, g), [128, 16], F32),
                          b_xt=Buf("xt%d" % g), b_xn=Buf("xn%d" % g), b_st=Buf("st%d" % g),
                          tr=((1, 2) if g % 2 == 0 else (7, 0)), mm=((3, 4) if g % 2 == 0 else (5, 6)), tr1=None)
                if plan is not None:
                    p_ = plan[g % len(plan)]
                    B_.update(mm=(p_[0], p_[1]), tr1=p_[2])
                if with_y:
                    B_.update(xq=alloc("xq%s%d" % (tag, g), [128, D], F32), yt=alloc("yt%s%d" % (tag, g), [128, D], F32), b_xq=Buf("xq%d" % g), b_yt=Buf("yt%d" % g))
                if with_h:
                    B_.update(ht=alloc("ht%s%d" % (tag, g), [128, NFC * 128], BF16), b_ht=Buf("ht%d" % g))
                sets.append(B_)
            return sets

        def ln_stats_gen(src, b_src, B_):
            st_t, b_s = B_["st"], B_["b_st"]
            yield
            S.op("dve", lambda e: e.bn_stats(st_t[:, 0:6], src[:, 0:512]), reads=[b_src], writes=[b_s])
            yield
            S.op("dve", lambda e: e.bn_stats(st_t[:, 6:12], src[:, 512:1024]), reads=[b_src], writes=[b_s])
            yield
            S.op("dve", lambda e: e.bn_aggr(st_t[:, 12:14], st_t[:, 0:12]), reads=[b_s], writes=[b_s])
            yield
            S.op("act", lambda e: e.activation(st_t[:, 14:15], st_t[:, 13:14], AF.Sqrt, bias=EPS), reads=[b_s], writes=[b_s])
            yield
            S.op("dve", lambda e: e.reciprocal(st_t[:, 15:16], st_t[:, 14:15]), reads=[b_s], writes=[b_s])

        def ln_mod_T_gen(l, i, slot_shift, slot_scale, B_):
            srcm = 1 if i < 2 else 0
            xt_t, xn_t, st_t = B_["xt"], B_["xn"], B_["st"]
            yield from ln_stats_gen(xt_t, B_["b_xt"], B_)
            yield
            S.op("dve", lambda e: e.tensor_scalar(xn_t[:], xt_t[:], st_t[:, 12:13], st_t[:, 15:16], ALU.subtract, ALU.mult),
                 reads=[B_["b_xt"], B_["b_st"]], writes=[B_["b_xn"]])
            for kc in range(KC):
                if B_["tr1"] is not None:
                    pbk = B_["tr1"]; pv_ = pbank[pbk][:].bitcast(BF16)[:, kc * 128:(kc + 1) * 128]
                else:
                    pbk = B_["tr"][kc % 2]; pv_ = pbb(pbk, 128)
                yield
                S.op("pe", lambda e, kc=kc, pv_=pv_: e.transpose(pv_, xn_t[:, kc * 128:(kc + 1) * 128], id_bf[:]),
                     reads=[B_["b_xn"], b_id_bf], writes=[b_pb[pbk]])
                dst = xmT[:, kc * N + i * 128: kc * N + (i + 1) * 128]
                yield
                S.op("act", lambda e, dst=dst, pv_=pv_, kc=kc: e.activation(
                    dst, pv_, AF.Identity, bias=modv(l, slot_shift, kc, srcm), scale=modv(l, slot_scale, kc, srcm, True)),
                    reads=[b_pb[pbk], b_mod, b_mod1p], writes=[b_xmT[i]])

        GRP = 2

        def run_groups(tiles, sets, load, chain, GRP=GRP):
            tl_ = list(tiles)

            def stream(k):
                mine = tl_[k::GRP]
                for n_, i in enumerate(mine):
                    if n_ == 0:
                        load(i, sets[k])
                    if n_ + 1 < len(mine):
                        load(mine[n_ + 1], sets[k + GRP * ((n_ + 1) % 2)])
                    yield from chain(i, sets[k + GRP * (n_ % 2)])

            run_rr([stream(k) for k in range(GRP)])

        def phase_ln_mod_T(l, src_ap, tiles, slot_shift, slot_scale, src_bufs=None):
            p1 = ExitStack()
            sets = mk_sets(lambda n, s_, dt: p1.enter_context(nc.sbuf_tensor("%s_p1L%d" % (n, l), s_, dt)), 2 * GRP, "a")
            load = lambda i, B_: S.dma("sp", B_["xt"][:], src_ap[i * 128:(i + 1) * 128, :], reads=([src_bufs[i]] if src_bufs else []), writes=[B_["b_xt"]])
            run_groups(tiles, sets, load, lambda i, B_: ln_mod_T_gen(l, i, slot_shift, slot_scale, B_))
            S.barrier(); p1.close()

        phase_ln_mod_T(0, xin, range(NT), 0, 1)
        cut = debug[2] if (debug and debug.startswith("hg") and len(debug) == 3) else None
        dmp = sb("dmp", [128, 512], F32); b_dmp = Buf("dmp")

        def cut_dump(src_ap, src_bufs):
            S.barrier()
            S.op("act", lambda e: e.copy(dmp[:], src_ap), reads=src_bufs, writes=[b_dmp])
            S.dma("sp", outs["dbg_cut"][:, 0:512], dmp[:], reads=[b_dmp])
            S.drain_all("sp"); S.emit(); build.ninstr = S.ninstr

        if cut == "0":
            cut_dump(xmT[:, 0:512], b_xmT); return nc
        if debug == "p1":
            xm_f = sb("xm_f", [128, N], F32); b_xmf = Buf("xmf")
            for kc in range(KC):
                S.op("act", lambda e, kc=kc: e.copy(xm_f[:], xmT[:, kc * N:(kc + 1) * N]), reads=b_xmT, writes=[b_xmf])
                S.dma("sp", outs["dbg_xmT"][:, kc * N:(kc + 1) * N], xm_f[:], reads=[b_xmf])

        mixT = nc.dram_tensor("mixT", [D, N], BF16, kind="Internal").ap()
        b_mixT = [Buf("mixT%d" % i) for i in range(8)]

        def mixer(l, scan_tiles, out_tiles):
            ms = ExitStack()
            sb = lambda n, s_, dt: ms.enter_context(nc.sbuf_tensor("%s_L%d" % (n, l), s_, dt))
            DK = 128; QS = DK ** -0.5; NCH = N // 64
            BLKS = [(0, 512), (512, 512), (1024, 512), (1536, 512), (2048, 256)]
            lb_f = sb("lb_f", [128, 16], F32); lbv = sb("lbv", [128, 16], F32); oml = sb("oml", [128, 16], F32)
            b_lb = Buf("lb")
            cm = sb("cm", [128, 256], F32); rmk = sb("rmk", [128, 512], F32); ngb = sb("ngb", [128, DEPTH * 128], F32)
            b_cm, b_rmk, b_ngb = Buf("cm"), Buf("rmk"), Buf("ngb")
            cm_u = sb("cm_u", [128, 256], mybir.dt.uint32); b_cmu = Buf("cmu")
            S.dma("sp", lb_f[:], lbl, writes=[b_lb]); S.dma("sp", cm[:], cmask, writes=[b_cm]); S.dma("sp", rmk[:], rmask, writes=[b_rmk])
            for l2 in range(DEPTH):
                S.dma("sp", ngb[:, l2 * 128:(l2 + 1) * 128], hgng[l2:l2 + 1, :].partition_broadcast(128), writes=[b_ngb])
            S.op("dve", lambda e: e.tensor_copy(cm_u[:], cm[:]), reads=[b_cm], writes=[b_cmu])
            lt = sb("lt", [128, 32], F32)
            S.op("dve", lambda e: e.memset(lbv[:, 0:8], 0.0), writes=[b_lb], reads=[b_lb])
            S.op("dve", lambda e: e.tensor_max(lt[:, 0:8], lb_f[:, 0:8], lb_f[:, 8:16]), reads=[b_lb], writes=[b_lb])
            S.op("dve", lambda e: e.tensor_sub(lt[:, 8:16], lb_f[:, 0:8], lt[:, 0:8]), reads=[b_lb], writes=[b_lb])
            S.op("dve", lambda e: e.tensor_sub(lt[:, 16:24], lb_f[:, 8:16], lt[:, 0:8]), reads=[b_lb], writes=[b_lb])
            S.op("act", lambda e: e.activation(lt[:, 8:24], lt[:, 8:24], AF.Exp), reads=[b_lb], writes=[b_lb])
            S.op("dve", lambda e: e.tensor_add(lt[:, 24:32], lt[:, 8:16], lt[:, 16:24]), reads=[b_lb], writes=[b_lb])
            S.op("dve", lambda e: e.reciprocal(lt[:, 24:32], lt[:, 24:32]), reads=[b_lb], writes=[b_lb])
            S.op("dve", lambda e: e.tensor_mul(lbv[:, 8:16], lt[:, 16:24], lt[:, 24:32]), reads=[b_lb], writes=[b_lb])
            S.op("dve", lambda e: e.tensor_scalar(oml[:], lbv[:], -1.0, 1.0, ALU.mult, ALU.add), reads=[b_lb], writes=[b_lb])

            wvg = [sb("wvg%d" % i, [128, KC * 256], BF16) for i in range(2)]; b_wvg = [Buf("wvg0"), Buf("wvg1")]
            wzq = [sb("wzq%d" % i, [128, KC * 384], BF16) for i in range(2)]; b_wzq = [Buf("wzq0"), Buf("wzq1")]
            VG = [dict(V=sb("Vh%d" % p_, [128, N], BF16), G=sb("Gh%d" % p_, [128, N], BF16), bV=Buf("V%d" % p_), bG=Buf("G%d" % p_)) for p_ in range(2)]
            Vh, Gh, b_V, b_G = VG[0]["V"], VG[0]["G"], VG[0]["bV"], VG[0]["bG"]
            QT = [sb("QT%d" % d, [128, N], BF16) for d in range(2)]; KT = [sb("KT%d" % d, [128, N], BF16) for d in range(2)]
            QS_T = [sb("QST%d" % d, [128, N], BF16) for d in range(2)]; b_QST = [Buf("QST0"), Buf("QST1")]
            KH = [sb("KH%d" % d, [128, N], BF16) for d in range(2)]; KHt = [sb("KHt%d" % d, [128, N], BF16) for d in range(2)]
            EB = [sb("EB%d" % d, [128, NCH], F32) for d in range(2)]
            b_QT = [Buf("QT0"), Buf("QT1")]; b_KT = [Buf("KT0"), Buf("KT1")]; b_KH = [Buf("KH0"), Buf("KH1")]
            b_KHt = [Buf("KHt0"), Buf("KHt1")]; b_EB = [Buf("EB0"), Buf("EB1")]
            Oacc = [sb("Oacc%d" % d, [128, N], F32) for d in range(2)]; b_O = [Buf("O0"), Buf("O1")]
            HGT = sb("HGT", [128, N], BF16); b_HGT = Buf("HGT")
            NTMP = 9
            tmp = [[sb("tm%d_%d" % (a, i), [128, 512], F32) for i in range(NTMP)] for a in range(2)]
            b_tmp = [[Buf("tm%d_%d" % (a, i)) for i in range(NTMP)] for a in range(2)]
            qs_ = [sb("qs%d" % a, [128, 512], F32) for a in range(2)]; b_qs = [Buf("qs0"), Buf("qs1")]
            Sst = [[sb("S%d_%d" % (d, i), [128, 128], F32) for i in range(2)] for d in range(2)]
            b_S = [[Buf("S%d_%d" % (d, i)) for i in range(2)] for d in range(2)]
            Sbf_f32 = [sb("S3_%d" % d, [128, 128], F32) for d in range(2)]; b_Sx = [Buf("S3_0"), Buf("S3_1")]
            AT = [sb("AT%d" % d, [128, 128], BF16) for d in range(2)]; b_AT = [Buf("AT0"), Buf("AT1")]
            fin = [sb("fin%d" % i, [128, 128], F32) for i in range(2)]; fsq = sb("fsq", [128, 128], F32)
            b_fin = [Buf("fin0"), Buf("fin1")]
            finW = [dict(fsq=(fsq if p_ == 0 else sb("fsq1", [128, 128], F32)), fst=sb("fst%d" % p_, [128, 8], F32), ngg=sb("ngg%d" % p_, [128, 128], F32), hgt=sb("hgt%d" % p_, [128, 128], BF16),
                         b_fsq=Buf("fsq%d" % p_), b_fst=Buf("fst%d" % p_), b_ngg=Buf("ngg%d" % p_), b_hgt=Buf("hgt%d" % p_)) for p_ in range(2)]

            if cut == "L":
                cut_dump(oml[:, 0:16].to_broadcast([128, 16]) if False else xmT[:, 0:512], b_xmT + [b_lb, b_cm, b_rmk, b_ngb]); return "cut"

            e_, one_e, sig, kk, lf, bb, cc_, E1, dd = range(9)

            def hgrn_layer(l, scan_tiles, out_tiles):
                wi = w_in[l].rearrange("(k p) n -> p k n", p=128)
                def load_vg_w(h_):
                    a_ = h_ % 2
                    for ci, c0 in enumerate((h_ * 128, 2048 + h_ * 128)):
                        S.dma("pool", wvg[a_][:].rearrange("p (k n) -> p k n", k=KC)[:, :, ci * 128:(ci + 1) * 128], wi[:, :, c0:c0 + 128], writes=[b_wvg[a_]])

                def load_zq_w(h_):
                    a_ = h_ % 2
                    for ci, c0 in enumerate((512 + h_ * 128, 1024 + h_ * 128, 1536 + h_ * 128)):
                        S.dma("pool", wzq[a_][:].rearrange("p (k n) -> p k n", k=KC)[:, :, ci * 128:(ci + 1) * 128], wi[:, :, c0:c0 + 128], writes=[b_wzq[a_]])

                def A_gen(h_):
                    a_ = h_ % 2; W_ = VG[a_]
                    for i in scan_tiles:
                        pbi = 3 + (i % 2)
                        yield
                        for kc in range(KC):
                            S.op("pe", lambda e, i=i, kc=kc, pbi=pbi: e.matmul(pbf(pbi, 256), xmT[:, kc * N + i * 128:kc * N + (i + 1) * 128],
                                 wvg[a_][:, kc * 256:(kc + 1) * 256], start=(kc == 0), stop=(kc == KC - 1)),
                                 reads=[b_xmT[i], b_wvg[a_]], writes=[b_pb[pbi]], inc=(kc == KC - 1))
                        yield
                        S.op("dve", lambda e, i=i, pbi=pbi: e.tensor_copy(W_["V"][:, i * 128:(i + 1) * 128], pbank[pbi][:, 0:128]), reads=[b_pb[pbi]], writes=[W_["bV"]])
                        yield
                        S.op("act", lambda e, i=i, pbi=pbi: e.activation(W_["G"][:, i * 128:(i + 1) * 128], pbank[pbi][:, 128:256], AF.Silu), reads=[b_pb[pbi]], writes=[W_["bG"]])

                nheads = 1 if cut else 4
                load_vg_w(0)
                if cut == "W":
                    cut_dump(wvg[0][:, 0:512], [b_wvg[0]]); return "cut"
                run_rr([A_gen(0)])
                for h in range(nheads):
                    a = h % 2
                    Vh, Gh, b_V, b_G = VG[a]["V"], VG[a]["G"], VG[a]["bV"], VG[a]["bG"]
                    if h == 0:
                        load_zq_w(0)
                    if h + 1 < nheads:
                        load_zq_w(h + 1)
                        load_vg_w(h + 1)
                    if cut == "A":
                        cut_dump(Vh[:, 0:512], [b_V, b_G]); return "cut"
                    for bi, (t0, nb) in enumerate(BLKS):
                        if t0 // 128 not in scan_tiles:
                            continue
                        tiles_in = [i for i in range(t0 // 128, (t0 + nb) // 128)]
                        for ci in range(3):
                            for kc in range(KC):
                                S.op("pe", lambda e, ci=ci, kc=kc, t0=t0, nb=nb, a=a: e.matmul(pbf(5 + ci, nb), wzq[a][:, kc * 384 + ci * 128:kc * 384 + (ci + 1) * 128],
                                     xmT[:, kc * N + t0:kc * N + t0 + nb], start=(kc == 0), stop=(kc == KC - 1)),
                                     reads=[b_wzq[a]] + [b_xmT[i] for i in tiles_in], writes=[b_pb[5 + ci]], inc=(kc == KC - 1))
                        qa = bi % 2
                        S.op("act", lambda e, nb=nb, qa=qa: e.activation(qs_[qa][:, 0:nb], pbf(7, nb), AF.Silu), reads=[b_pb[7]], writes=[b_qs[qa]])
                        nck = nb // 64; c0 = t0 // 64
                        def gate_chain(d, bi=bi, t0=t0, nb=nb, qa=qa, nck=nck, c0=c0):
                            ta = (bi * 2 + d) % 2
                            T = [t[:, 0:nb] for t in tmp[ta]]; bT = b_tmp[ta]
                            col = l * 8 + d * 4 + h
                            lbc, omc = lbv[:, col:col + 1], oml[:, col:col + 1]
                            yield
                            S.op("act", lambda e, T=T, d=d, nb=nb: e.activation(T[e_], pbf(5 + d, nb), AF.Exp, scale=-1.0), reads=[b_pb[5 + d]], writes=[bT[e_]])
                            yield
                            S.op("act", lambda e, T=T: e.activation(T[one_e], T[e_], AF.Ln, bias=1.0), reads=[bT[e_]], writes=[bT[one_e]])
                            yield
                            S.op("act", lambda e, T=T: e.activation(T[sig], T[one_e], AF.Exp, scale=-1.0), reads=[bT[one_e]], writes=[bT[sig]])
                            yield
                            S.op("dve", lambda e, T=T, omc=omc: e.scalar_tensor_tensor(T[kk], T[e_], omc, T[sig], ALU.mult, ALU.mult), reads=[bT[e_], bT[sig], b_lb], writes=[bT[kk]])
                            yield
                            S.op("act", lambda e, T=T, omc=omc, lbc=lbc: e.activation(T[lf], T[sig], AF.Ln, bias=lbc, scale=omc), reads=[bT[sig], b_lb], writes=[bT[lf]])
                            yield
                            S.op("dve", lambda e, T=T, nb=nb: e.tensor_tensor_scan(T[bb], rmk[:, 0:nb], T[lf], 0.0, ALU.mult, ALU.add), reads=[b_rmk, bT[lf]], writes=[bT[bb]])
                            b3 = T[bb].rearrange("p (c t) -> p c t", t=64)
                            btot = b3[:, :, 63:64]
                            if d == 0:
                                cview, bc_ = T[bb], bT[bb]
                            else:
                                yield
                                S.op("dve", lambda e, T=T: e.tensor_sub(T[cc_], T[lf], T[bb]), reads=[bT[lf], bT[bb]], writes=[bT[cc_]])
                                c3 = T[cc_].rearrange("p (c t) -> p c t", t=64)
                                yield
                                S.op("dve", lambda e, c3=c3, btot=btot, nck=nck: e.tensor_add(c3, c3, btot.to_broadcast([128, nck, 64])), reads=[bT[cc_], bT[bb]], writes=[bT[cc_]])
                                cview, bc_ = T[cc_], bT[cc_]
                            sl = slice(t0, t0 + nb)
                            yield
                            S.op("act", lambda e, T=T, cview=cview: e.activation(T[E1], cview, AF.Exp), reads=[bc_], writes=[bT[E1]])
                            yield
                            S.op("dve", lambda e, T=T, qa=qa, d=d, sl=sl, nb=nb: e.scalar_tensor_tensor(QT[d][:, sl], qs_[qa][:, 0:nb], QS, T[E1], ALU.mult, ALU.mult),
                                 reads=[b_qs[qa], bT[E1]], writes=[b_QT[d]])
                            MID = 31 if d == 0 else 32
                            cm3 = T[one_e].rearrange("p (c t) -> p c t", t=64); cv3m = cview.rearrange("p (c t) -> p c t", t=64)
                            yield
                            S.op("dve", lambda e, cm3=cm3, cv3m=cv3m, nck=nck, MID=MID: e.tensor_sub(cm3, cv3m, cv3m[:, :, MID:MID + 1].to_broadcast([128, nck, 64])),
                                 reads=[bc_, bT[E1], b_QT[d]], writes=[bT[one_e]])
                            yield
                            S.op("act", lambda e, T=T: e.activation(T[E1], T[one_e], AF.Exp), reads=[bT[one_e], b_QT[d]], writes=[bT[E1]])
                            yield
                            S.op("dve", lambda e, T=T, qa=qa, d=d, sl=sl, nb=nb: e.scalar_tensor_tensor(QS_T[d][:, sl], qs_[qa][:, 0:nb], QS, T[E1], ALU.mult, ALU.mult),
                                 reads=[b_qs[qa], bT[E1]], writes=[b_QST[d]])
                            yield
                            S.op("act", lambda e, T=T: e.activation(T[E1], T[one_e], AF.Exp, scale=-1.0), reads=[bT[one_e], b_QST[d]], writes=[bT[E1]])
                            yield
                            S.op("dve", lambda e, T=T, d=d, sl=sl: e.tensor_tensor(KT[d][:, sl], T[kk], T[E1], ALU.mult), reads=[bT[kk], bT[E1]], writes=[b_KT[d]])
                            d3 = T[dd].rearrange("p (c t) -> p c t", t=64); cv3 = cview.rearrange("p (c t) -> p c t", t=64)
                            yield
                            S.op("dve", lambda e, d3=d3, cv3=cv3, btot=btot, nck=nck: e.tensor_sub(d3, btot.to_broadcast([128, nck, 64]), cv3), reads=[bc_, bT[bb]], writes=[bT[dd]])
                            yield
                            S.op("act", lambda e, T=T: e.activation(T[dd], T[dd], AF.Exp), reads=[bT[dd]], writes=[bT[dd]])
                            yield
                            S.op("dve", lambda e, T=T, d=d, sl=sl: e.tensor_tensor(KH[d][:, sl], T[kk], T[dd], ALU.mult), reads=[bT[kk], bT[dd]], writes=[b_KH[d]])
                            yield
                            S.op("act", lambda e, d=d, c0=c0, nck=nck, btot=btot: e.activation(EB[d][:, c0:c0 + nck], btot.rearrange("p c o -> p (c o)"), AF.Exp), reads=[bT[bb]], writes=[b_EB[d]])
                        run_rr([gate_chain(0), gate_chain(1)])
                    if cut == "B":
                        return
                    for d in range(2):
                        for i in scan_tiles:
                            pbi = 1 + (i % 2)
                            S.op("pe", lambda e, d=d, i=i, pbi=pbi: e.transpose(pbb(pbi, 128), KH[d][:, i * 128:(i + 1) * 128], id_bf[:]), reads=[b_KH[d], b_id_bf], writes=[b_pb[pbi]])
                            S.op("act", lambda e, d=d, i=i, pbi=pbi: e.copy(KHt[d][:, i * 128:(i + 1) * 128], pbb(pbi, 128)), reads=[b_pb[pbi]], writes=[b_KHt[d]])
                    if cut == "C":
                        return
                    order = [list(scan_tiles), [t for t in (1, 0) if t in scan_tiles] + [t for t in range(NT - 1, 1, -1) if t in scan_tiles]]
                    TB = [tmp[a_][j_] for a_ in range(2) for j_ in range(NTMP)]; bTB = [b_tmp[a_][j_] for a_ in range(2) for j_ in range(NTMP)]
                    import os as _os2
                    d_stage = int(_os2.environ.get("HG_D_STAGE", "0")) if cut else 0
                    seqs = []
                    for d in range(1 if d_stage else 2):
                        seq = [(i, cpos) for i in order[d] for cpos in ((0, 1) if d == 0 else (1, 0))]
                        seqs.append(seq)
                        nstep = len(order[d])
                        for cpos in (0, 1):
                            base = cpos * nstep
                            s_ = base
                            while s_ < base + nstep:
                                g_end = min(base + nstep, (s_ // 4 + 1) * 4)
                                pbk = 3 + 2 * cpos + ((s_ // 4) % 2)
                                for sl_ in range(s_, g_end):
                                    i = order[d][sl_ - base]; q = sl_ % 4
                                    ps_ = slice(cpos * 64, cpos * 64 + 64); ts_ = slice(i * 128, (i + 1) * 128)
                                    S.op("pe", lambda e, d=d, ts_=ts_, ps_=ps_, pbk=pbk, q=q, Vh=Vh: e.matmul(pbank[pbk][:, q * 128:(q + 1) * 128], KHt[d][ps_, ts_], Vh[ps_, ts_], start=True, stop=True),
                                         reads=[b_KHt[d], b_V], writes=[b_pb[pbk]], inc=(sl_ == g_end - 1))
                                tb = 9 * d + s_ // 4; c_lo, c_hi = (s_ % 4) * 128, ((g_end - 1) % 4 + 1) * 128
                                S.op("act", lambda e, tb=tb, pbk=pbk, c_lo=c_lo, c_hi=c_hi, TB=TB: e.copy(TB[tb][:, c_lo:c_hi], pbank[pbk][:, c_lo:c_hi]), reads=[b_pb[pbk]], writes=[bTB[tb]])
                                s_ = g_end
                    if d_stage == 1:
                        cut_dump(TB[0][:, 0:512], bTB[0:9]); return "cut"
                    RING = 18
                    b_slot = [[Buf("st%d_%d" % (d_, r_)) for r_ in range(RING)] for d_ in range(2)]
                    sslot = lambda d_, k: (KH[d_][:, (k % RING) * 128:(k % RING + 1) * 128], b_slot[d_][k % RING])
                    S3 = [Sst[d_] + [Sbf_f32[d_]] for d_ in range(2)]; b_S3 = [b_S[d_] + [b_Sx[d_]] for d_ in range(2)]

                    produced = [0, 0]
                    consumed = [0, 0]

                    def chain_gen(d):
                        seq = seqs[d]; nstep = len(order[d])
                        yield
                        S.op("dve", lambda e, d=d, S3=S3: e.memset(S3[d][0][:], 0.0), writes=[b_S3[d][0]])
                        for k, (i, cpos) in enumerate(seq[:-1]):
                            sl_ = cpos * nstep + k // 2
                            ch = i * 2 + cpos; tb = 9 * d + sl_ // 4; q = sl_ % 4
                            si, so = k % 3, (k + 1) % 3
                            yield
                            S.op("dve", lambda e, d=d, si=si, so=so, ch=ch, tb=tb, q=q, TB=TB, S3=S3: e.scalar_tensor_tensor(S3[d][so][:], S3[d][si][:], EB[d][:, ch:ch + 1], TB[tb][:, q * 128:(q + 1) * 128], ALU.mult, ALU.add),
                                 reads=[b_S3[d][si], b_EB[d], bTB[tb]], writes=[b_S3[d][so]])
                            dst, bdst = sslot(d, k + 1)
                            yield
                            while (k + 1) - consumed[d] >= RING:
                                yield
                            S.op("act", lambda e, d=d, so=so, dst=dst, S3=S3: e.copy(dst, S3[d][so][:]), reads=[b_S3[d][so]], writes=[bdst])
                            produced[d] = k + 1

                    def out_gen(d):
                        yield
                        S.op("dve", lambda e, d=d: e.memset(AT[d][:], 0.0), writes=[b_AT[d]])
                        for step, i in enumerate(order[d]):
                            if i not in out_tiles:
                                consumed[d] = 2 * (step + 1)
                                continue
                            ts_ = slice(i * 128, (i + 1) * 128); par = step % 2
                            p_sc, p_o = (5, 6)[d], ((7, 0)[d])
                            yield
                            S.op("pe", lambda e, d=d, ts_=ts_, p_sc=p_sc: e.matmul(pbf(p_sc, 128), KT[d][:, ts_], QS_T[d][:, ts_], start=True, stop=True),
                                 reads=[b_KT[d], b_QST[d]], writes=[b_pb[p_sc]])
                            yield
                            S.op("dve", lambda e, d=d, p_sc=p_sc: e.copy_predicated(AT[d][:], cm_u[:, d * 128:(d + 1) * 128], pbf(p_sc, 128)),
                                 reads=[b_pb[p_sc], b_cmu, b_AT[d]], writes=[b_AT[d]])
                            cps = (0, 1) if d == 0 else (1, 0)
                            need = [(cp, k) for cp, k in zip(cps, (2 * step, 2 * step + 1)) if k > 0]
                            yield
                            while need and produced[d] < max(k_ for _, k_ in need):
                                yield
                            S.op("pe", lambda e, d=d, ts_=ts_, p_o=p_o, nn=len(need), Vh=Vh: e.matmul(pbf(p_o, 128), AT[d][:], Vh[:, ts_], start=True, stop=(nn == 0)),
                                 reads=[b_AT[d], b_V], writes=[b_pb[p_o]], inc=(len(need) == 0))
                            for j, (cp, k) in enumerate(need):
                                ps_ = slice(cp * 64, cp * 64 + 64); tsc = slice(i * 128 + cp * 64, i * 128 + cp * 64 + 64)
                                src, bsrc = sslot(d, k)
                                lastj = j == len(need) - 1
                                if not lastj:
                                    pass
                                S.op("pe", lambda e, d=d, tsc=tsc, ps_=ps_, p_o=p_o, src=src, lastj=lastj: e.matmul(pbank[p_o][ps_, 0:128], QT[d][:, tsc], src, start=False, stop=lastj),
                                     reads=[b_QT[d], bsrc], writes=[b_pb[p_o]], inc=lastj)
                            consumed[d] = 2 * (step + 1)
                            yield
                            S.op("act", lambda e, d=d, ts_=ts_, p_o=p_o: e.copy(Oacc[d][:, ts_], pbf(p_o, 128)), reads=[b_pb[p_o]], writes=[b_O[d]])

                    if h == nheads - 1 and not cut and debug != "hg":
                        sc_load(l, 0); sc_load(l, 1); sg_load(l)
                    if d_stage:
                        run_rr([chain_gen(0), out_gen(0)])
                    else:
                        run_rr([chain_gen(0), chain_gen(1), out_gen(0), out_gen(1)] + ([A_gen(h + 1)] if h + 1 < nheads else []))
                    if d_stage == 3:
                        cut_dump(Oacc[0][:, 0:512], [b_O[0]]); return "cut"
                    if cut == "D":
                        return
                    def fin_chain(i, fa):
                        ts_ = slice(i * 128, (i + 1) * 128); pbi = 1 + fa
                        W = finW[fa]
                        yield
                        S.op("dve", lambda e: e.tensor_add(fin[fa][:], Oacc[0][:, ts_], Oacc[1][:, ts_]), reads=[b_O[0], b_O[1]], writes=[b_fin[fa]])
                        yield
                        S.op("act", lambda e: e.activation(W["fsq"][:], fin[fa][:], AF.Square, accum_out=W["fst"][:, 0:1]), reads=[b_fin[fa]], writes=[W["b_fsq"], W["b_fst"]])
                        yield
                        S.op("act", lambda e: e.activation(W["fst"][:, 1:2], W["fst"][:, 0:1], AF.Sqrt, bias=EPS, scale=1.0 / 128), reads=[W["b_fst"]], writes=[W["b_fst"]])
                        yield
                        S.op("dve", lambda e: e.reciprocal(W["fst"][:, 2:3], W["fst"][:, 1:2]), reads=[W["b_fst"]], writes=[W["b_fst"]])
                        yield
                        S.op("pool", lambda e, Gcur=Gcur: e.tensor_tensor(W["ngg"][:], ngb[:, l * 128:(l + 1) * 128], Gcur[:, ts_], ALU.mult), reads=[b_ngb, b_Gcur], writes=[W["b_ngg"]])
                        yield
                        S.op("dve", lambda e: e.scalar_tensor_tensor(W["hgt"][:], fin[fa][:], W["fst"][:, 2:3], W["ngg"][:], ALU.mult, ALU.mult), reads=[b_fin[fa], W["b_fst"], W["b_ngg"]], writes=[W["b_hgt"]])
                        yield
                        S.op("pe", lambda e: e.transpose(pbb(pbi, 128), W["hgt"][:], id_bf[:]), reads=[W["b_hgt"], b_id_bf], writes=[b_pb[pbi]])
                        yield
                        S.op("act", lambda e: e.copy(HGT[:, ts_], pbb(pbi, 128)), reads=[b_pb[pbi]], writes=[b_HGT])

                    Gcur, b_Gcur = Gh, b_G
                    ot_ = list(out_tiles)

                    def fin_stream(k, ot_=ot_):
                        for i in ot_[k::2]:
                            yield from fin_chain(i, k)

                    run_rr([fin_stream(0), fin_stream(1)])
                    S.dma("sp", mixT[h * 128:(h + 1) * 128, :], HGT[:], reads=[b_HGT], writes=[b_mixT[h]])

            if cut:
                S.barrier()
                S.op("act", lambda e, Vh=Vh: e.copy(Oacc[1][:], Vh[:]), reads=[b_V], writes=[b_O[1]])
                S.dma("sp", outs["dbg_cut"][:, 0:N], Oacc[1][:], reads=[b_O[1]])
                if cut in "DE":
                    S.dma("sp", outs["dbg_cut"][:, N:2 * N], Oacc[0][:], reads=[b_O[0]])

            scw_s = sb("scw_s", [128, DEPTH * 6], F32); b_scw = Buf("scw")
            S.dma("sp", scw_s[:], scw, writes=[b_scw])
            lngb = sb("lngb", [128, 512], F32); b_lngb = Buf("lngb")
            WsT = sb("WsT", [128, 4 * 128], BF16); b_WsT = Buf("WsT")
            wsn = sb("wsn", [128, 128], BF16); b_wsn = Buf("wsn")
            BS = sb("BS", [128, 2 * 128], F32); b_BS = Buf("BS")
            SEQS = [(0, 256), (256, N)]
            SEGS = [(0, 256), (256, 512), (512, 1024), (1024, 1536), (1536, 2048), (2048, N)]

            pre_done = {}

            def sc_load(l, cc):
                wi_ = w_in[l].rearrange("(k p) n -> p k n", p=128); a_ = cc % 2
                for ci, c0 in enumerate((2560 + cc * 128, 2816 + cc * 128, 3072 + cc * 128)):
                    S.dma("pool", wzq[a_][:].rearrange("p (k n) -> p k n", k=KC)[:, :, ci * 128:(ci + 1) * 128], wi_[:, :, c0:c0 + 128], writes=[b_wzq[a_]])
                pre_done[("sc", l, cc)] = True

            def sg_load(l):
                wi_ = w_in[l].rearrange("(k p) n -> p k n", p=128)
                S.dma("pool", wvg[0][:].rearrange("p (k n) -> p k n", k=KC), wi_[:, :, 3328:3584], writes=[b_wvg[0]])
                S.dma("pool", wvg[1][:].rearrange("p (k n) -> p k n", k=KC), wi_[:, :, 3584:3840], writes=[b_wvg[1]])
                S.dma("sp", lngb[:, 0:256], sgln[2 * l:2 * l + 1, :].partition_broadcast(128), writes=[b_lngb])
                S.dma("sp", lngb[:, 256:512], sgln[2 * l + 1:2 * l + 2, :].partition_broadcast(128), writes=[b_lngb])
                for cc in range(2):
                    S.dma("sp", BS[:, cc * 128:(cc + 1) * 128], sgb[2 * l + cc], writes=[b_BS])
                pre_done[("sg", l)] = True

            def sc_layer(l, tiles):
                wi = w_in[l].rearrange("(k p) n -> p k n", p=128)
                tmax = (max(tiles) + 1) * 128; tmin = min(tiles) * 128
                Pf, GBf, b_P, b_GB = Oacc[0], Oacc[1], b_O[0], b_O[1]
                for cc in range(2):
                    a = cc % 2
                    if not pre_done.get(("sc", l, cc)):
                        sc_load(l, cc)
                    sc_blks = [(t0, min(512, tmax - t0)) for t0 in range(tmin, tmax, 512)]
                    for (t0, nb) in sc_blks:
                        tiles_in = list(range(t0 // 128, (t0 + nb) // 128))
                        for ci in range(3):
                            for kc in range(KC):
                                S.op("pe", lambda e, ci=ci, kc=kc, t0=t0, nb=nb, a=a: e.matmul(pbf(5 + ci, nb), wzq[a][:, kc * 384 + ci * 128:kc * 384 + (ci + 1) * 128],
                                     xmT[:, kc * N + t0:kc * N + t0 + nb], start=(kc == 0), stop=(kc == KC - 1)),
                                     reads=[b_wzq[a]] + [b_xmT[i] for i in tiles_in], writes=[b_pb[5 + ci]], inc=(kc == KC - 1))
                        T0 = tmp[0][0][:, 0:nb]
                        S.op("act", lambda e, t0=t0, nb=nb: e.copy(GBf[:, t0:t0 + nb], pbf(5, nb)), reads=[b_pb[5]], writes=[b_GB])
                        S.op("act", lambda e, T0=T0, nb=nb: e.copy(T0, pbf(6, nb)), reads=[b_pb[6]], writes=[b_tmp[0][0]])
                        S.op("dve", lambda e, T0=T0, t0=t0, nb=nb: e.tensor_tensor(Pf[:, t0:t0 + nb], T0, pbf(7, nb), ALU.mult), reads=[b_tmp[0][0], b_pb[7]], writes=[b_P])
                    wb = l * 6 + cc * 3
                    w0_, w1_, w2_ = scw_s[:, wb:wb + 1], scw_s[:, wb + 1:wb + 2], scw_s[:, wb + 2:wb + 3]
                    for (a0, a1) in SEGS:
                        if a0 < tmin or a0 >= tmax:
                            continue
                        s0, s1 = [sq for sq in SEQS if sq[0] <= a0 < sq[1]][0]
                        Y = tmp[1][0]; bY = b_tmp[1][0]; n_ = a1 - a0
                        S.op("dve", lambda e, Y=Y, a0=a0, a1=a1, n_=n_, w1_=w1_: e.tensor_scalar(Y[:, 0:n_], Pf[:, a0:a1], w1_, None, ALU.mult), reads=[b_P, b_scw], writes=[bY])
                        lo = max(a0, s0 + 1)
                        S.op("dve", lambda e, Y=Y, a0=a0, a1=a1, lo=lo, w0_=w0_: e.scalar_tensor_tensor(Y[:, lo - a0:a1 - a0], Pf[:, lo - 1:a1 - 1], w0_, Y[:, lo - a0:a1 - a0], ALU.mult, ALU.add),
                             reads=[b_P, b_scw, bY], writes=[bY])
                        hi = min(a1, s1 - 1)
                        S.op("dve", lambda e, Y=Y, a0=a0, hi=hi, w2_=w2_: e.scalar_tensor_tensor(Y[:, 0:hi - a0], Pf[:, a0 + 1:hi + 1], w2_, Y[:, 0:hi - a0], ALU.mult, ALU.add),
                             reads=[b_P, b_scw, bY], writes=[bY])
                        S.op("pool", lambda e, Y=Y, a0=a0, a1=a1, n_=n_: e.tensor_tensor(HGT[:, a0:a1], GBf[:, a0:a1], Y[:, 0:n_], ALU.mult), reads=[b_GB, bY], writes=[b_HGT])
                    S.dma("sp", mixT[(4 + cc) * 128:(5 + cc) * 128, tmin:tmax], HGT[:, tmin:tmax], reads=[b_HGT], writes=[b_mixT[4 + cc]])

            def sg_layer(l, tiles):
                wi = w_in[l].rearrange("(k p) n -> p k n", p=128)
                tmax = (max(tiles) + 1) * 128; tmin = min(tiles) * 128
                if not pre_done.get(("sg", l)):
                    sg_load(l)
                for g in range(4):
                    S.dma("pool", wsn[:], sgw[4 * l + g], writes=[b_wsn])
                    S.op("pe", lambda e: e.transpose(pbb(1, 128), wsn[:], id_bf[:]), reads=[b_wsn, b_id_bf], writes=[b_pb[1]])
                    S.op("act", lambda e, g=g: e.copy(WsT[:, g * 128:(g + 1) * 128], pbb(1, 128)), reads=[b_pb[1]], writes=[b_WsT])
                SGT, b_SGT = KH, b_KH
                sgW = [dict(vn=sb("sgvn%d" % p_, [128, 256], F32), vhb=sb("sgvh%d" % p_, [128, 256], BF16), st=sb("sgst%d" % p_, [128, 40], F32),
                            b_vn=Buf("sgvn%d" % p_), b_vhb=Buf("sgvh%d" % p_), b_st=Buf("sgst%d" % p_), banks=((3, 4, 5), (6, 7, 0))[p_], par=p_) for p_ in range(2)]

                def sg_chain(i, W):
                    ts_ = slice(i * 128, (i + 1) * 128); pv, pu, pm = W["banks"]; par = W["par"]
                    vn_t, vhb_t, st_t = W["vn"], W["vhb"], W["st"]
                    yield
                    for kc in range(KC):
                        S.op("pe", lambda e, kc=kc: e.matmul(pbf(pv, 256), xmT[:, kc * N + i * 128:kc * N + (i + 1) * 128], wvg[1][:, kc * 256:(kc + 1) * 256],
                             start=(kc == 0), stop=(kc == KC - 1)), reads=[b_xmT[i], b_wvg[1]], writes=[b_pb[pv]], inc=(kc == KC - 1))
                    for g in range(4):
                        yield
                        S.op("dve", lambda e, g=g: e.bn_stats(st_t[:, g * 6:(g + 1) * 6], pbank[pv][:, g * 64:(g + 1) * 64]), reads=[b_pb[pv]], writes=[W["b_st"]])
                    for g in range(4):
                        yield
                        S.op("dve", lambda e, g=g: e.bn_aggr(st_t[:, 24 + 2 * g:26 + 2 * g], st_t[:, g * 6:(g + 1) * 6]), reads=[W["b_st"]], writes=[W["b_st"]])
                    mv = st_t[:, 24:32].rearrange("p (g two) -> p g two", two=2)
                    yield
                    S.op("act", lambda e: e.activation(st_t[:, 32:36], mv[:, :, 1], AF.Sqrt, bias=EPS), reads=[W["b_st"]], writes=[W["b_st"]])
                    yield
                    S.op("dve", lambda e: e.reciprocal(st_t[:, 36:40], st_t[:, 32:36]), reads=[W["b_st"]], writes=[W["b_st"]])
                    for g in range(4):
                        yield
                        S.op("dve", lambda e, g=g: e.tensor_scalar(vn_t[:, g * 64:(g + 1) * 64], pbank[pv][:, g * 64:(g + 1) * 64], st_t[:, 24 + 2 * g:25 + 2 * g], st_t[:, 36 + g:37 + g],
                             ALU.subtract, ALU.mult), reads=[b_pb[pv], W["b_st"]], writes=[W["b_vn"]])
                    yield
                    S.op("dve", lambda e: e.tensor_mul(vn_t[:], vn_t[:], lngb[:, 0:256]), reads=[W["b_vn"], b_lngb], writes=[W["b_vn"]])
                    yield
                    S.op("dve", lambda e: e.tensor_add(vhb_t[:], vn_t[:], lngb[:, 256:512]), reads=[W["b_vn"], b_lngb], writes=[W["b_vhb"]])
                    yield
                    for cc in range(2):
                        for kc in range(KC):
                            S.op("pe", lambda e, cc=cc, kc=kc: e.matmul(pbank[pu][:, cc * 128:(cc + 1) * 128], wvg[0][:, kc * 256 + cc * 128:kc * 256 + (cc + 1) * 128], xmT[:, kc * N + i * 128:kc * N + (i + 1) * 128],
                                 start=(kc == 0), stop=(kc == KC - 1)), reads=[b_wvg[0], b_xmT[i]], writes=[b_pb[pu]], inc=(kc == KC - 1 and cc == 1))
                    yield
                    for cc in range(2):
                        for gg in range(2):
                            g = 2 * cc + gg
                            S.op("pe", lambda e, g=g, gg=gg, cc=cc: e.matmul(pbank[pm][gg * 64:(gg + 1) * 64, cc * 128:(cc + 1) * 128], vhb_t[:, g * 64:(g + 1) * 64], WsT[:, g * 128:(g + 1) * 128], start=True, stop=True),
                                 reads=[W["b_vhb"], b_WsT], writes=[b_pb[pm]], inc=(gg == 1 and cc == 1))
                    for cc in range(2):
                        T1, T2 = tmp[cc][1 + 2 * par][:, 0:128], tmp[cc][2 + 2 * par][:, 0:128]
                        bT1, bT2 = b_tmp[cc][1 + 2 * par], b_tmp[cc][2 + 2 * par]
                        yield
                        S.op("dve", lambda e, T1=T1, cc=cc: e.tensor_tensor(T1, pbank[pm][:, cc * 128:(cc + 1) * 128], BS[:, cc * 128:(cc + 1) * 128], ALU.add), reads=[b_pb[pm], b_BS], writes=[bT1])
                        yield
                        S.op("act", lambda e, T2=T2, cc=cc: e.copy(T2, pbank[pu][:, cc * 128:(cc + 1) * 128]), reads=[b_pb[pu]], writes=[bT2])
                        yield
                        S.op("pool", lambda e, T1=T1, T2=T2, cc=cc: e.tensor_tensor(SGT[cc][:, ts_], T1, T2, ALU.mult), reads=[bT1, bT2], writes=[b_SGT[cc]])

                tl_ = list(tiles)

                def sg_stream(k):
                    for i in tl_[k::2]:
                        yield from sg_chain(i, sgW[k])

                run_rr([sg_stream(0), sg_stream(1)])
                for cc in range(2):
                    S.dma("sp", mixT[(6 + cc) * 128:(7 + cc) * 128, tmin:tmax], SGT[cc][:, tmin:tmax], reads=[b_SGT[cc]], writes=[b_mixT[6 + cc]])

            r = hgrn_layer(l, scan_tiles, out_tiles)
            if r == "cut":
                st.enter_context(ms)
                return "cut"
            if debug != "hg":
                sc_layer(l, out_tiles)
                sg_layer(l, out_tiles)
            if debug in ("hg", "mix"):
                mixer.dbg = (Oacc[0], b_O[0], HGT, b_HGT)
                st.enter_context(ms)
                return None
            S.barrier(); ms.close()
            return None

        if mixer(0, list(range(NT)), list(range(NT))) == "cut":
            return nc

        ALPHA = (2 * DEPTH) ** 0.25
        x1d = nc.dram_tensor("x1d", [N, D], F32, kind="Internal").ap()
        b_x1d = [Buf("x1d%d" % i) for i in range(NT)]

        def gate_bcast(dst, b_dst, l, slot, srcm, scr, b_scr):
            for kc in range(KC):
                g = modv(l, slot, kc, srcm)
                S.op("dve", lambda e, g=g: e.tensor_scalar(scr[:], id_f[:], 0.0, g, ALU.mult, ALU.add), reads=[b_id_f, b_mod], writes=[b_scr])
                S.op("pe", lambda e: e.matmul(pbf(0, 128), scr[:], id_f[:], start=True, stop=True), reads=[b_scr, b_id_f], writes=[b_pb[0]])
                S.op("act", lambda e, kc=kc: e.copy(dst[:, kc * 128:(kc + 1) * 128], pbf(0, 128)), reads=[b_pb[0]], writes=[b_dst])

        def phase_wout_ln1(l, src_ap, tiles, src_bufs=None):
            ps4 = ExitStack()
            sb4 = lambda n, s_, dt: ps4.enter_context(nc.sbuf_tensor("%s_p4L%d" % (n, l), s_, dt))
            mixS = sb4("mixS", [128, KC * N], BF16); b_mixS = [Buf("mixS%d" % k) for k in range(KC)]
            wo = sb4("wo", [128, KC * D], BF16); b_wo = Buf("wo")
            gbc = [sb4("gbc%d" % i, [128, D], F32) for i in range(2)]; b_gbc = [Buf("gbc0"), Buf("gbc1")]
            lg = sb4("lg", [128, D], F32); lb_ = sb4("lb_", [128, D], F32); b_lg, b_lbb = Buf("lg"), Buf("lbb")
            scr = sb4("scr", [128, 128], F32); b_scr = Buf("scr")
            for k in range(KC):
                S.dma("sp", mixS[:, k * N:(k + 1) * N], mixT[k * 128:(k + 1) * 128, :], reads=[b_mixT[k]], writes=[b_mixS[k]])
            S.dma("pool", wo[:].rearrange("p (k n) -> p k n", k=KC), w_out[l].rearrange("(k p) n -> p k n", p=128), writes=[b_wo])
            S.dma("sp", lg[:], lnp[4 * l:4 * l + 1, :].partition_broadcast(128), writes=[b_lg])
            S.dma("sp", lb_[:], lnp[4 * l + 1:4 * l + 2, :].partition_broadcast(128), writes=[b_lbb])
            gate_bcast(gbc[0], b_gbc[0], l, 2, 0, scr, b_scr)
            if any(i < 2 for i in tiles):
                gate_bcast(gbc[1], b_gbc[1], l, 2, 1, scr, b_scr)
            G4 = 3
            sets = mk_sets(sb4, 2 * G4, "b", with_y=True, plan=[(3, 3, 1), (4, 4, 2), (5, 5, 6)])
            load = lambda i, B_: S.dma("sp", B_["xq"][:], src_ap[i * 128:(i + 1) * 128, :], reads=([src_bufs[i]] if src_bufs else []), writes=[B_["b_xq"]])

            def chain4(i, B_):
                srcm = 1 if i < 2 else 0
                xq_t, yt_t, xt_t, st_t = B_["xq"], B_["yt"], B_["xt"], B_["st"]
                for hf in range(2):
                    pbi = B_["mm"][hf]; hs = slice(hf * 512, (hf + 1) * 512)
                    yield
                    for kc in range(KC):
                        S.op("pe", lambda e, kc=kc, hf=hf, pbi=pbi: e.matmul(pbf(pbi, 512), mixS[:, kc * N + i * 128:kc * N + (i + 1) * 128],
                             wo[:, kc * D + hf * 512:kc * D + (hf + 1) * 512], start=(kc == 0), stop=(kc == KC - 1)),
                             reads=[b_mixS[kc], b_wo], writes=[b_pb[pbi]], inc=(kc == KC - 1))
                    yield
                    S.op("dve", lambda e, hs=hs, pbi=pbi: e.tensor_tensor(yt_t[:, hs], pbf(pbi, 512), gbc[srcm][:, hs], ALU.mult),
                         reads=[b_pb[pbi], b_gbc[srcm]], writes=[B_["b_yt"]])
                    yield
                    S.op("dve", lambda e, hs=hs: e.scalar_tensor_tensor(yt_t[:, hs], xq_t[:, hs], ALPHA, yt_t[:, hs], ALU.mult, ALU.add),
                         reads=[B_["b_xq"], B_["b_yt"]], writes=[B_["b_yt"]])
                yield from ln_stats_gen(yt_t, B_["b_yt"], B_)
                yield
                S.op("dve", lambda e: e.scalar_tensor_tensor(yt_t[:], yt_t[:], st_t[:, 12:13], lg[:], ALU.subtract, ALU.mult),
                     reads=[B_["b_yt"], B_["b_st"], b_lg], writes=[B_["b_yt"]])
                yield
                S.op("dve", lambda e: e.scalar_tensor_tensor(xt_t[:], yt_t[:], st_t[:, 15:16], lb_[:], ALU.mult, ALU.add),
                     reads=[B_["b_yt"], B_["b_st"], b_lbb, B_["b_xt"]], writes=[B_["b_xt"]])
                yield
                S.dma("sp", x1d[i * 128:(i + 1) * 128, :], xt_t[:], reads=[B_["b_xt"]], writes=[b_x1d[i]])
                yield from ln_mod_T_gen(l, i, 3, 4, B_)

            run_groups(tiles, sets, load, chain4, GRP=G4)
            S.barrier(); ps4.close()

        if debug in ("x1", "h", "x2", None):
            phase_wout_ln1(0, xin, list(range(NT)))
        if debug == "x1":
            for i in range(NT):
                a = i % 2
                S.dma("sp", xt[a][:], x1d[i * 128:(i + 1) * 128, :], reads=[b_x1d[i]], writes=[b_xt[a]])
                S.dma("sp", outs["dbg_x1"][i * 128:(i + 1) * 128, :], xt[a][:], reads=[b_xt[a]])
            xm_f = sb("xm2_f", [128, N], F32); b_xmf = Buf("xm2f")
            for kc in range(KC):
                S.op("act", lambda e, kc=kc: e.copy(xm_f[:], xmT[:, kc * N:(kc + 1) * N]), reads=b_xmT, writes=[b_xmf])
                S.dma("sp", outs["dbg_xm2T"][:, kc * N:(kc + 1) * N], xm_f[:], reads=[b_xmf])

        hTd = nc.dram_tensor("hTd", [FF, N], BF16, kind="Internal").ap()
        b_hTd = [Buf("hTd%d" % i) for i in range(NFC)]
        x2d = [nc.dram_tensor("x2d%d" % i, [N, D], F32, kind="Internal").ap() for i in range(DEPTH - 1)]
        b_x2d = [Buf("x2d%d" % i) for i in range(NT)]
        GW = 64

        def phase_ffn_up(l, with_ctx):
            p5 = ExitStack()
            sb5 = lambda n, s_, dt: p5.enter_context(nc.sbuf_tensor("%s_p5L%d" % (n, l), s_, dt))
            fcw_s = sb5("fcw_s", [128, NFC * 9], F32); fcb_s = sb5("fcb_s", [128, NFC], F32); b_fcw, b_fcb = Buf("fcw"), Buf("fcb")
            S.dma("sp", fcw_s[:], fcw[:, l * NFC * 9:(l + 1) * NFC * 9], writes=[b_fcw])
            S.dma("sp", fcb_s[:], fcb[:, l * NFC:(l + 1) * NFC], writes=[b_fcb])
            wag = [sb5("wag%d" % i, [128, KC * 256], BF16) for i in range(2)]; b_wag = [Buf("wag0"), Buf("wag1")]
            apx = [sb5("apx%d" % i, [128, 34 * 66], BF16) for i in range(2)]; apc = [sb5("apc%d" % i, [128, 258], BF16) for i in range(2)]
            b_ap = [Buf("ap0"), Buf("ap1")]
            dg = [sb5("dg%d" % i, [128, 9 * 128], BF16) for i in range(2)]; b_dg = [Buf("dg0"), Buf("dg1")]
            gel = [sb5("gel%d" % i, [128, 512], F32) for i in range(2)]; b_gel = [Buf("gel0"), Buf("gel1")]
            htc = [sb5("htc%d" % i, [128, N], BF16) for i in range(2)]; b_htc = [Buf("htc0"), Buf("htc1")]
            for i in range(2):
                S.op("pool", lambda e, i=i: e.memset(apx[i][:], 0.0), writes=[b_ap[i]])
                S.op("pool", lambda e, i=i: e.memset(apc[i][:], 0.0), writes=[b_ap[i]])
            FB = ([("c", 0, 256, 0)] if with_ctx else []) + [("x", 256 + 512 * j, 512, j) for j in range(4)]
            wu = ffn_up[l].rearrange("(k p) n -> p k n", p=128)
            defer_l = l + 1 if (l + 1 < DEPTH) else None
            if defer_l is not None:
                awd = [sb5("awd%d" % i, [128, KC * 512], BF16) for i in range(2)]; b_awd = [Buf("awd0"), Buf("awd1")]
                pm_d, b_pmd = pbf(1, 96), b_pb[1]
            for fc in range(NFC):
                a = fc % 2
                if defer_l is not None:
                    if fc < 12:
                        mod_chunk_load(defer_l, fc, awd[fc % 2], b_awd[fc % 2])
                    if 1 <= fc <= 12:
                        mod_chunk_mm(defer_l, fc - 1, awd[(fc - 1) % 2], b_awd[(fc - 1) % 2], pm_d, b_pmd)
                    if fc == 13:
                        mod_finish(defer_l, pm_d, b_pmd)
                for ci, c0 in enumerate((fc * 128, FF + fc * 128)):
                    S.dma("pool", wag[a][:].rearrange("p (k n) -> p k n", k=KC)[:, :, ci * 128:(ci + 1) * 128], wu[:, :, c0:c0 + 128], writes=[b_wag[a]])
                for tap in range(9):
                    wcol = fcw_s[:, fc * 9 + tap:fc * 9 + tap + 1]
                    S.op("dve", lambda e, a=a, tap=tap, wcol=wcol: e.tensor_scalar(dg[a][:, tap * 128:(tap + 1) * 128], id_f[:], wcol, None, ALU.mult),
                         reads=[b_id_f, b_fcw], writes=[b_dg[a]])
                apx3 = apx[a][:].rearrange("p (r c) -> p r c", c=66)
                for bn, (kind, t0, nb, j) in enumerate(FB):
                    pa = (3, 6)[bn % 2]
                    tl = list(range(t0 // 128, (t0 + nb) // 128))
                    for kc in range(KC):
                        S.op("pe", lambda e, a=a, kc=kc, t0=t0, nb=nb, pa=pa: e.matmul(pbf(pa, nb), wag[a][:, kc * 256:kc * 256 + 128], xmT[:, kc * N + t0:kc * N + t0 + nb],
                             start=(kc == 0), stop=(kc == KC - 1)), reads=[b_wag[a]] + [b_xmT[i] for i in tl], writes=[b_pb[pa]], inc=(kc == KC - 1))
                    if kind == "c":
                        S.op("act", lambda e, a=a, pa=pa: e.copy(apc[a][:, 1:257], pbf(pa, 256)), reads=[b_pb[pa]], writes=[b_ap[a]])
                    else:
                        S.op("act", lambda e, a=a, pa=pa, j=j, apx3=apx3: e.copy(apx3[:, 1 + 8 * j:9 + 8 * j, 1:65], pbf(pa, 512).rearrange("p (r c) -> p r c", c=GW)),
                             reads=[b_pb[pa]], writes=[b_ap[a]])
                for bn, (kind, t0, nb, j) in enumerate(FB):
                    pc, pg = (4, 7)[bn % 2], (5, 0)[bn % 2]
                    ga = bn % 2
                    tl = list(range(t0 // 128, (t0 + nb) // 128))
                    if kind == "c":
                        for n_, dj in enumerate(range(3)):
                            tap = 3 + dj
                            S.op("pe", lambda e, a=a, tap=tap, dj=dj, pc=pc, n_=n_: e.matmul(pbf(pc, 256), dg[a][:, tap * 128:(tap + 1) * 128], apc[a][:, dj:dj + 256], start=(n_ == 0), stop=(n_ == 2)),
                                 reads=[b_dg[a], b_ap[a]], writes=[b_pb[pc]], inc=(n_ == 2))
                    else:
                        for tap in range(9):
                            di, dj = tap // 3, tap % 3
                            mv_ = apx3[:, di + 8 * j:di + 8 * j + 8, dj:dj + GW]
                            S.op("pe", lambda e, a=a, tap=tap, mv_=mv_, pc=pc: e.matmul(pbf(pc, 512), dg[a][:, tap * 128:(tap + 1) * 128], mv_, start=(tap == 0), stop=(tap == 8)),
                                 reads=[b_dg[a], b_ap[a]], writes=[b_pb[pc]], inc=(tap == 8))
                    for kc in range(KC):
                        S.op("pe", lambda e, a=a, kc=kc, t0=t0, nb=nb, pg=pg: e.matmul(pbf(pg, nb), wag[a][:, kc * 256 + 128:kc * 256 + 256], xmT[:, kc * N + t0:kc * N + t0 + nb],
                             start=(kc == 0), stop=(kc == KC - 1)), reads=[b_wag[a]] + [b_xmT[i] for i in tl], writes=[b_pb[pg]], inc=(kc == KC - 1))
                    bcol = fcb_s[:, fc:fc + 1]
                    S.op("act", lambda e, ga=ga, nb=nb, pc=pc, bcol=bcol: e.activation(gel[ga][:, 0:nb], pbf(pc, nb), AF.Gelu, bias=bcol), reads=[b_pb[pc], b_fcb], writes=[b_gel[ga]])
                    S.op("dve", lambda e, a=a, ga=ga, t0=t0, nb=nb, pg=pg: e.tensor_tensor(htc[a][:, t0:t0 + nb], gel[ga][:, 0:nb], pbf(pg, nb), ALU.mult),
                         reads=[b_gel[ga], b_pb[pg]], writes=[b_htc[a]])
                lo = FB[0][1]
                S.dma("sp", hTd[fc * 128:(fc + 1) * 128, lo:N], htc[a][:, lo:N], reads=[b_htc[a]], writes=[b_hTd[fc]])
            S.barrier(); p5.close()

        def phase_ffn_down(l, tiles, dst_ap, dst_row0):
            p6 = ExitStack()
            sb6 = lambda n, s_, dt: p6.enter_context(nc.sbuf_tensor("%s_p6L%d" % (n, l), s_, dt))
            wd = sb6("wd", [128, NFC * D], BF16); b_wdp = {(hf_, q_): Buf("wd%d%d" % (hf_, q_)) for hf_ in range(2) for q_ in range(2)}
            gbc = [sb6("gbc%d" % i, [128, D], F32) for i in range(2)]; b_gbc = [Buf("gbc0"), Buf("gbc1")]
            lg = sb6("lg", [128, D], F32); lb_ = sb6("lb_", [128, D], F32); b_lg, b_lbb = Buf("lg"), Buf("lbb")
            scr = sb6("scr", [128, 128], F32); b_scr = Buf("scr")
            wdv = ffn_down[l].rearrange("(f p) n -> p f n", p=128)
            for hf_ in range(2):
                for q_ in range(2):
                    S.dma("pool", wd[:].rearrange("p (f n) -> p f n", f=NFC)[:, q_ * 11:(q_ + 1) * 11, hf_ * 512:(hf_ + 1) * 512],
                          wdv[:, q_ * 11:(q_ + 1) * 11, hf_ * 512:(hf_ + 1) * 512], writes=[b_wdp[(hf_, q_)]])
            S.dma("sp", lg[:], lnp[4 * l + 2:4 * l + 3, :].partition_broadcast(128), writes=[b_lg])
            S.dma("sp", lb_[:], lnp[4 * l + 3:4 * l + 4, :].partition_broadcast(128), writes=[b_lbb])
            gate_bcast(gbc[0], b_gbc[0], l, 5, 0, scr, b_scr)
            if any(i < 2 for i in tiles):
                gate_bcast(gbc[1], b_gbc[1], l, 5, 1, scr, b_scr)
            hv = hTd.rearrange("(f p) n -> p f n", p=128)
            sets = mk_sets(sb6, 2 * GRP, "c", with_y=True, with_h=True)

            def load(i, B_):
                S.dma("sp", B_["ht"][:].rearrange("p (f n) -> p f n", f=NFC), hv[:, :, i * 128:(i + 1) * 128], reads=b_hTd, writes=[B_["b_ht"]])
                S.dma("sp", B_["xq"][:], x1d[i * 128:(i + 1) * 128, :], reads=[b_x1d[i]], writes=[B_["b_xq"]])

            def chain6(i, B_):
                srcm = 1 if i < 2 else 0
                xq_t, yt_t, xt_t, st_t, ht_t = B_["xq"], B_["yt"], B_["xt"], B_["st"], B_["ht"]
                for hf in range(2):
                    pbi = B_["mm"][hf]; hs = slice(hf * 512, (hf + 1) * 512)
                    yield
                    for f_ in range(NFC):
                        S.op("pe", lambda e, f_=f_, hf=hf, pbi=pbi: e.matmul(pbf(pbi, 512), ht_t[:, f_ * 128:(f_ + 1) * 128], wd[:, f_ * D + hf * 512:f_ * D + (hf + 1) * 512],
                             start=(f_ == 0), stop=(f_ == NFC - 1)), reads=[B_["b_ht"], b_wdp[(hf, f_ // 11)]], writes=[b_pb[pbi]], inc=(f_ == NFC - 1))
                    yield
                    S.op("dve", lambda e, hs=hs, pbi=pbi: e.tensor_tensor(yt_t[:, hs], pbf(pbi, 512), gbc[srcm][:, hs], ALU.mult),
                         reads=[b_pb[pbi], b_gbc[srcm]], writes=[B_["b_yt"]])
                    yield
                    S.op("dve", lambda e, hs=hs: e.scalar_tensor_tensor(yt_t[:, hs], xq_t[:, hs], ALPHA, yt_t[:, hs], ALU.mult, ALU.add),
                         reads=[B_["b_xq"], B_["b_yt"]], writes=[B_["b_yt"]])
                yield from ln_stats_gen(yt_t, B_["b_yt"], B_)
                yield
                S.op("dve", lambda e: e.scalar_tensor_tensor(yt_t[:], yt_t[:], st_t[:, 12:13], lg[:], ALU.subtract, ALU.mult),
                     reads=[B_["b_yt"], B_["b_st"], b_lg], writes=[B_["b_yt"]])
                yield
                S.op("dve", lambda e: e.scalar_tensor_tensor(xt_t[:], yt_t[:], st_t[:, 15:16], lb_[:], ALU.mult, ALU.add),
                     reads=[B_["b_yt"], B_["b_st"], b_lbb, B_["b_xt"]], writes=[B_["b_xt"]])
                r0 = i * 128 - dst_row0
                yield
                S.dma("sp", dst_ap[r0:r0 + 128, :], xt_t[:], reads=[B_["b_xt"]], writes=[b_x2d[i]])

            run_groups(tiles, sets, load, chain6)
            S.barrier(); p6.close()

        if debug in ("h", "x2", None):
            phase_ffn_up(0, True)
        if debug == "h":
            hst_b = sb("hst_b", [128, N], BF16); hst_f = sb("hst_f", [128, N], F32); b_hsb, b_hsf = Buf("hsb"), Buf("hsf")
            for fc in range(NFC):
                S.dma("sp", hst_b[:], hTd[fc * 128:(fc + 1) * 128, :], reads=[b_hTd[fc]], writes=[b_hsb])
                S.op("act", lambda e: e.copy(hst_f[:], hst_b[:]), reads=[b_hsb], writes=[b_hsf])
                S.dma("sp", outs["dbg_hT"][fc * 128:(fc + 1) * 128, :], hst_f[:], reads=[b_hsf])
        if debug in ("x2", None):
            phase_ffn_down(0, list(range(NT)), x2d[0], 0)
        if debug == "x2":
            for i in range(NT):
                a = i % 2
                S.dma("sp", xt[a][:], x2d[0][i * 128:(i + 1) * 128, :], reads=[b_x2d[i]], writes=[b_xt[a]])
                S.dma("sp", outs["dbg_x2"][i * 128:(i + 1) * 128, :], xt[a][:], reads=[b_xt[a]])

        if debug is None:
            XT = list(range(2, NT))
            phase_ln_mod_T(1, x2d[0], range(NT), 0, 1, src_bufs=b_x2d)
            mixer(1, list(range(NT)), XT)
            phase_wout_ln1(1, x2d[0], XT, src_bufs=b_x2d)
            phase_ffn_up(1, False)
            phase_ffn_down(1, XT, outs["out"], 256)
        if debug in ("hg", "mix"):
            hg_f, b_hgf, hg_b, b_hgb = mixer.dbg
            for h in range(8 if debug == "mix" else 4):
                S.dma("sp", hg_b[:], mixT[h * 128:(h + 1) * 128, :], reads=[b_mixT[h]], writes=[b_hgb])
                S.op("act", lambda e: e.copy(hg_f[:], hg_b[:]), reads=[b_hgb], writes=[b_hgf])
                S.dma("sp", outs["dbg_hgT"][h * 128:(h + 1) * 128, :], hg_f[:], reads=[b_hgf])
        S.drain_all("sp")
        S.emit()
        build.ninstr = S.ninstr
    return nc


def _prep(inputs, b):
    f = lambda a: np.ascontiguousarray(np.asarray(a, dtype=np.float32))
    m = {}
    m["xin"] = f(np.concatenate([inputs["ctx"][b], inputs["x"][b]], axis=0))
    cc = np.stack([np.asarray(inputs["c"][b]).reshape(KC, 128).T, np.asarray(inputs["c_ctx"]).reshape(KC, 128).T], axis=-1)
    m["c2"] = f(cc.reshape(128, KC * 2))
    m["ada_w"] = f(inputs["ada_w"])
    m["ada_b_fm"] = f(np.asarray(inputs["ada_b"]).reshape(DEPTH, 48, 128).transpose(2, 0, 1).reshape(128, DEPTH * 48))
    m["ident"] = np.eye(128, dtype=np.float32)
    m["w_in"] = f(inputs["w_in"])
    m["w_out"] = f(inputs["w_out"])
    m["ffn_up"] = f(inputs["ffn_up"]); m["ffn_down"] = f(inputs["ffn_down"])
    m["fcw_fm"] = f(np.asarray(inputs["ffn_conv_w"]).reshape(DEPTH, 9, NFC, 128).transpose(3, 0, 2, 1).reshape(128, DEPTH * NFC * 9))
    m["fcb_fm"] = f(np.asarray(inputs["ffn_conv_b"]).reshape(DEPTH, NFC, 128).transpose(2, 0, 1).reshape(128, DEPTH * NFC))
    m["lnp"] = f(np.stack([np.asarray(inputs[k]) for k in ("ln1_g", "ln1_b", "ln2_g", "ln2_b")], axis=1).reshape(DEPTH * 4, D))
    m["scw_fm"] = f(np.asarray(inputs["sc_conv_w"]).reshape(DEPTH, 3, 2, 128).transpose(3, 0, 2, 1).reshape(128, DEPTH * 6))
    m["sgln"] = f(np.stack([np.asarray(inputs["sg_ln_g"]), np.asarray(inputs["sg_ln_b"])], axis=1).reshape(DEPTH * 2, 256))
    m["sg_w"] = f(np.asarray(inputs["sg_w"]).reshape(DEPTH * 4, 128, 128))
    sb_ = np.asarray(inputs["sg_b"]).reshape(DEPTH, 2, 2, 1, 128)
    m["sgb_fm"] = f(np.broadcast_to(sb_, (DEPTH, 2, 2, 64, 128)).reshape(DEPTH * 2, 128, 128))
    m["lbl"] = f(np.asarray(inputs["hg_lb"]).reshape(DEPTH, 2, 4, 128).transpose(3, 0, 1, 2).reshape(128, 16))
    m["hg_norm_g"] = f(inputs["hg_norm_g"])
    ii = np.arange(128)
    same = (ii[:, None] // 64) == (ii[None, :] // 64)
    m["cmask"] = f(np.concatenate([same & (ii[:, None] <= ii[None, :]), same & (ii[:, None] >= ii[None, :])], axis=1))
    rm = np.ones((128, 512), np.float32); rm[:, ::64] = 0.0
    m["rmask"] = rm
    return m


_NC = None


def kernel(**inputs):
    global _NC
    if _NC is None:
        _NC = build()
    shared = None
    maps = []
    for b in range(8):
        m = _prep(inputs, b)
        if shared is None:
            shared = {k: m[k] for k in m if k not in ("xin", "c2")}
        else:
            m.update(shared)
        maps.append(m)
    res = run_bass_kernel_spmd(_NC, maps, core_ids=list(range(8)))
    return np.stack([np.asarray(r["out"], dtype=np.float32) for r in res.results], axis=0)
```

```python
import numpy as np
import concourse.bass as bass
import concourse.mybir as mybir

F32 = mybir.dt.float32
BF16 = mybir.dt.bfloat16
ALU = mybir.AluOpType
AF = mybir.ActivationFunctionType


class Buf:
    __slots__ = ("name", "w", "r", "excl")

    def __init__(self, name="", excl=False):
        self.name = name
        self.excl = excl
        self.w = None
        self.r = []


class Sync:
    ENGS = ("pe", "act", "dve", "pool", "sp")

    SEM_LIMIT = 1900

    def __init__(self, nc, stack, n_dma_sems=20):
        self.nc = nc
        self.stack = stack
        self.owner = {}
        self.cur = {}
        self.nsem = 0
        self.q = {e: [] for e in self.ENGS}
        self.sems = {}
        self.cnt = {}
        self.known = {e: {} for e in self.ENGS}
        for e in ("pe", "act", "dve", "pool"):
            self.cur[e] = None
            self._new_sem(e)
        self.dpool = {}
        self.dk = {}
        for e in ("sp", "act", "pool"):
            self.dpool[e] = []
            for i in range(n_dma_sems):
                key = "d_%s_%d" % (e, i)
                self.sems[key] = stack.enter_context(nc.semaphore(key)); self.nsem += 1
                self.cnt[key] = 0
                self.owner[key] = "dma_" + e
                self.dpool[e].append(key)
            self.dk[e] = 0
        self.pe_pending = False
        self.ninstr = 0

    def _new_sem(self, eng):
        n = sum(1 for k in self.owner if self.owner[k] == eng)
        key = "%s#%d" % (eng, n)
        self.sems[key] = self.stack.enter_context(self.nc.semaphore("s_%s_%d" % (eng, n))); self.nsem += 1
        self.cnt[key] = 0
        self.owner[key] = eng
        self.prev = getattr(self, "prev", {})
        self.prev[eng] = self.cur[eng]
        self.cur[eng] = key

    def _latest(self, eng):
        k = self.cur[eng]
        if self.cnt[k]:
            return (k, self.cnt[k])
        p = self.prev.get(eng)
        return (p, self.cnt[p]) if p and self.cnt[p] else None

    def _wait(self, eng, tok):
        if tok is None:
            return
        key, val = tok
        if self.known[eng].get(key, 0) >= val:
            return
        self.known[eng][key] = val
        sem = self.sems[key]
        self.q[eng].append(lambda e, sem=sem, val=val: e.wait_ge(sem, val))
        self.ninstr += 1

    def _deps(self, eng, reads, writes, skip_self=False):
        for b in reads:
            if b.w is not None and not (skip_self and self.owner[b.w[0]] == eng):
                self._wait(eng, b.w)
            if b.excl:
                for t in b.r:
                    if self.owner[t[0]] != eng:
                        self._wait(eng, t)
        for b in writes:
            if b.w is not None and not (skip_self and self.owner[b.w[0]] == eng):
                self._wait(eng, b.w)
            for t in b.r:
                if not (skip_self and self.owner[t[0]] == eng):
                    self._wait(eng, t)

    def _commit(self, tok, reads, writes):
        for b in reads:
            b.r.append(tok)
            if len(b.r) > 64:
                best = {}
                for k, v in b.r:
                    if best.get(k, 0) < v:
                        best[k] = v
                b.r = list(best.items())
        for b in writes:
            b.w = tok
            b.r = []

    def op(self, eng, fn, reads=(), writes=(), inc=True):
        pe = eng == "pe"
        self._deps(eng, reads, writes, skip_self=pe)
        if self.cnt[self.cur[eng]] >= self.SEM_LIMIT and not (pe and self.pe_pending):
            self._new_sem(eng)
        key = self.cur[eng]
        if inc:
            self.cnt[key] += 1
            tok = (key, self.cnt[key])
            sem = self.sems[key]
            self.q[eng].append(lambda e, fn=fn, sem=sem: fn(e).then_inc(sem, 1))
            if pe:
                self.pe_pending = False
        else:
            assert pe
            tok = (key, self.cnt[key] + 1)
            self.q[eng].append(lambda e, fn=fn: fn(e))
            self.pe_pending = True
        self.ninstr += 1
        self._commit(tok, reads, writes)
        return tok

    def dma(self, eng, out, in_, reads=(), writes=(), **kw):
        self._deps(eng, reads, writes)
        pool = self.dpool[eng]
        slot = self.dk[eng] % len(pool)
        key = pool[slot]
        self.dk[eng] += 1
        if self.cnt[key] + 16 > self.SEM_LIMIT:
            self._wait(eng, (key, self.cnt[key]))
            n = sum(1 for k in self.owner if self.owner[k] == "dma_" + eng)
            nk = "d_%s_%d" % (eng, n)
            self.sems[nk] = self.stack.enter_context(self.nc.semaphore(nk)); self.nsem += 1
            self.cnt[nk] = 0; self.owner[nk] = "dma_" + eng
            pool[slot] = nk; key = nk
            self.retired = getattr(self, "retired", []) + [key]
        if self.cnt[key] > 0:
            self._wait(eng, (key, self.cnt[key]))
        self.cnt[key] += 16
        tok = (key, self.cnt[key])
        sem = self.sems[key]
        self.q[eng].append(
            lambda e, out=out, in_=in_, sem=sem, kw=kw: e.dma_start(out=out, in_=in_, **kw).then_inc(sem, 16))
        self.ninstr += 1
        self._commit(tok, reads, writes)
        return tok

    def wait_all(self, eng, toks):
        for t in toks:
            self._wait(eng, t)

    def barrier(self):
        toks = [t for t in (self._latest(e) for e in ("pe", "act", "dve", "pool")) if t]
        assert not self.pe_pending
        for q in self.dpool:
            for key in self.dpool[q]:
                if self.cnt[key]:
                    toks.append((key, self.cnt[key]))
        for eng in self.ENGS:
            for t in toks:
                if self.owner[t[0]] != eng:
                    self._wait(eng, t)

    def drain_all(self, eng="sp"):
        for q in self.dpool:
            for key in self.dpool[q]:
                if self.cnt[key]:
                    self._wait(eng, (key, self.cnt[key]))

    def emit(self):
        assert not self.pe_pending, "last PE op must carry inc"
        nc = self.nc
        q = self.q
        with nc.Block() as block:
            @block.sync
            def _(e):
                for f in q["sp"]:
                    f(e)

            @block.tensor
            def _(e):
                for f in q["pe"]:
                    f(e)

            @block.scalar
            def _(e):
                for f in q["act"]:
                    f(e)

            @block.vector
            def _(e):
                for f in q["dve"]:
                    f(e)

            @block.gpsimd
            def _(e):
                for f in q["pool"]:
                    f(e)


def run_rr(chains):
    live = list(chains)
    while live:
        for g in list(live):
            try:
                next(g)
            except StopIteration:
                live.remove(g)


from contextlib import ExitStack
from concourse.bass_utils import run_bass_kernel_spmd

FF = 2816; NFC = FF // 128
D = 1024; NT = 18; N = NT * 128; DEPTH = 2; NMOD = 6; EPS = 1e-6
KC = D // 128


def build(debug=None):
    nc = bass.Bass("TRN2", target_bir_lowering=False)
    dram = lambda n, s, dt, kind: nc.dram_tensor(n, s, dt, kind=kind).ap()
    xin = dram("xin", [N, D], F32, "ExternalInput")
    c2 = dram("c2", [128, KC * 2], F32, "ExternalInput")
    ada_w = dram("ada_w", [DEPTH, D, NMOD * D], F32, "ExternalInput")
    ada_b = dram("ada_b_fm", [128, DEPTH * 48], F32, "ExternalInput")
    ident = dram("ident", [128, 128], F32, "ExternalInput")
    w_in = dram("w_in", [DEPTH, D, 3840], F32, "ExternalInput")
    lbl = dram("lbl", [128, 16], F32, "ExternalInput")
    hgng = dram("hg_norm_g", [DEPTH, 128], F32, "ExternalInput")
    scw = dram("scw_fm", [128, DEPTH * 6], F32, "ExternalInput")
    sgln = dram("sgln", [DEPTH * 2, 256], F32, "ExternalInput")
    sgw = dram("sg_w", [DEPTH * 4, 128, 128], F32, "ExternalInput")
    sgb = dram("sgb_fm", [DEPTH * 2, 128, 128], F32, "ExternalInput")
    w_out = dram("w_out", [DEPTH, D, D], F32, "ExternalInput")
    ffn_up = dram("ffn_up", [DEPTH, D, 2 * FF], F32, "ExternalInput")
    ffn_down = dram("ffn_down", [DEPTH, FF, D], F32, "ExternalInput")
    fcw = dram("fcw_fm", [128, DEPTH * NFC * 9], F32, "ExternalInput")
    fcb = dram("fcb_fm", [128, DEPTH * NFC], F32, "ExternalInput")
    lnp = dram("lnp", [DEPTH * 4, D], F32, "ExternalInput")
    cmask = dram("cmask", [128, 256], F32, "ExternalInput")
    rmask = dram("rmask", [128, 512], F32, "ExternalInput")
    outs = {}
    if debug is None:
        outs["out"] = dram("out", [N - 256, D], F32, "ExternalOutput")
    if debug and debug.startswith("hg") and len(debug) == 3:
        outs["dbg_cut"] = dram("dbg_cut", [128, 2 * N], F32, "ExternalOutput")
    if debug == "h":
        outs["dbg_hT"] = dram("dbg_hT", [FF, N], F32, "ExternalOutput")
    if debug == "x2":
        outs["dbg_x2"] = dram("dbg_x2", [N, D], F32, "ExternalOutput")
    if debug == "x1":
        outs["dbg_x1"] = dram("dbg_x1", [N, D], F32, "ExternalOutput")
        outs["dbg_xm2T"] = dram("dbg_xm2T", [128, KC * N], F32, "ExternalOutput")
    if debug in ("hg", "mix"):
        outs["dbg_hgT"] = dram("dbg_hgT", [D if debug == "mix" else 512, N], F32, "ExternalOutput")
    if debug == "p1":
        outs["dbg_mod"] = dram("dbg_mod", [128, DEPTH * 96], F32, "ExternalOutput")
        outs["dbg_xmT"] = dram("dbg_xmT", [128, KC * N], F32, "ExternalOutput")
    with ExitStack() as st:
        S = Sync(nc, st)
        sb = lambda n, s, dt: st.enter_context(nc.sbuf_tensor(n, s, dt))
        pbank = [st.enter_context(nc.psum_tensor("pb%d" % i, [128, 512], F32)) for i in range(8)]
        b_pb = [Buf("pb%d" % i, excl=True) for i in range(8)]
        pbf = lambda i, n: pbank[i][:, 0:n]
        pbb = lambda i, n: pbank[i][:].bitcast(BF16)[:, 0:n]
        id_f = sb("id_f", [128, 128], F32); id_bf = sb("id_bf", [128, 128], BF16)
        c2_f = sb("c2_f", [128, KC * 2], F32); c2_bf = sb("c2_bf", [128, KC * 2], BF16)
        adab = sb("adab", [128, DEPTH * 48], F32)
        mod = sb("mod", [128, DEPTH * 96], F32)
        mod1p = sb("mod1p", [128, DEPTH * 96], F32)
        b_id_f, b_id_bf, b_c2f, b_c2bf, b_adab, b_mod, b_mod1p = (Buf(n) for n in "idf idbf c2f c2bf adab mod mod1p".split())
        S.dma("sp", id_f[:], ident, writes=[b_id_f])
        S.dma("pool", id_bf[:], ident, writes=[b_id_bf])
        S.dma("sp", c2_f[:], c2, writes=[b_c2f])
        S.dma("sp", adab[:], ada_b, writes=[b_adab])
        S.op("act", lambda e: e.activation(c2_bf[:], c2_f[:], AF.Silu), reads=[b_c2f], writes=[b_c2bf])
        st0 = ExitStack()
        aw = [st0.enter_context(nc.sbuf_tensor("aw%d" % i, [128, KC * 512], BF16)) for i in range(2)]
        b_aw = [Buf("aw0"), Buf("aw1")]
        p_mod = pbf(0, 96); b_pmod = b_pb[0]
        def mod_chunk_load(l, g, buf, bb):
            src = ada_w[l].rearrange("(k p) n -> p k n", p=128)[:, :, g * 512:(g + 1) * 512]
            S.dma("pool", buf[:].rearrange("p (k n) -> p k n", k=KC), src, writes=[bb])

        def mod_chunk_mm(l, g, buf, bb, pm_ap, b_pm):
            for jj in range(4):
                j = g * 4 + jj
                for k in range(KC):
                    last = (k == KC - 1)
                    S.op("pe", lambda e, buf=buf, jj=jj, k=k, j=j, last=last: e.matmul(
                        pm_ap[:, 2 * j:2 * j + 2], buf[:, k * 512 + jj * 128:k * 512 + (jj + 1) * 128],
                        c2_bf[:, 2 * k:2 * k + 2], start=(k == 0), stop=last),
                        reads=[bb, b_c2bf], writes=[b_pm], inc=(last and jj == 3))

        def mod_finish(l, pm_ap, b_pm):
            pm = pm_ap.rearrange("p (j s) -> p j s", s=2)
            mv = mod[:, l * 96:(l + 1) * 96].rearrange("p (j s) -> p j s", s=2)
            for s_ in range(2):
                S.op("dve", lambda e, pm=pm, mv=mv, s_=s_, l=l: e.tensor_tensor(
                    mv[:, :, s_], pm[:, :, s_], adab[:, l * 48:(l + 1) * 48], ALU.add),
                    reads=[b_pm, b_adab], writes=[b_mod])
            S.op("dve", lambda e, l=l: e.tensor_scalar_add(mod1p[:, l * 96:(l + 1) * 96], mod[:, l * 96:(l + 1) * 96], 1.0), reads=[b_mod], writes=[b_mod1p])

        for g in range(12):
            mod_chunk_load(0, g, aw[g % 2], b_aw[g % 2])
            mod_chunk_mm(0, g, aw[g % 2], b_aw[g % 2], p_mod, b_pmod)
        mod_finish(0, p_mod, b_pmod)
        S.barrier(); st0.close()
        if debug == "p1":
            S.dma("sp", outs["dbg_mod"], mod[:], reads=[b_mod])

        def modv(l, slot, kc, src, one_plus=False):
            t = mod1p if one_plus else mod
            col = l * 96 + (slot * 8 + kc) * 2 + src
            return t[:, col:col + 1]

        xmT = sb("xmT", [128, KC * N], BF16)
        b_xmT = [Buf("xmT%d" % i) for i in range(NT)]
        if debug in ("x1", "x2"):
            xt = [sb("xt%d" % i, [128, D], F32) for i in range(2)]; b_xt = [Buf("xt0"), Buf("xt1")]

        def mk_sets(alloc, n, tag, with_y=False, with_h=False, plan=None):
            sets = []
            for g in range(n):
                B_ = dict(xt=alloc("xt%s%d" % (tag, g), [128, D], F32), xn=alloc("xn%s%d" % (tag, g), [128, D], BF16), st=alloc("st%s%d" % (tag, g), [128, 16], F32),
                          b_xt=Buf("xt%d" % g), b_xn=Buf("xn%d" % g), b_st=Buf("st%d" % g),
                          tr=((1, 2) if g % 2 == 0 else (7, 0)), mm=((3, 4) if g % 2 == 0 else (5, 6)), tr1=None)
                if plan is not None:
                    p_ = plan[g % len(plan)]
                    B_.update(mm=(p_[0], p_[1]), tr1=p_[2])
                if with_y:
                    B_.update(xq=alloc("xq%s%d" % (tag, g), [128, D], F32), yt=alloc("yt%s%d" % (tag, g), [128, D], F32), b_xq=Buf("xq%d" % g), b_yt=Buf("yt%d" % g))
                if with_h:
                    B_.update(ht=alloc("ht%s%d" % (tag, g), [128, NFC * 128], BF16), b_ht=Buf("ht%d" % g))
                sets.append(B_)
            return sets

        def ln_stats_gen(src, b_src, B_):
            st_t, b_s = B_["st"], B_["b_st"]
            yield
            S.op("dve", lambda e: e.bn_stats(st_t[:, 0:6], src[:, 0:512]), reads=[b_src], writes=[b_s])
            yield
            S.op("dve", lambda e: e.bn_stats(st_t[:, 6:12], src[:, 512:1024]), reads=[b_src], writes=[b_s])
            yield
            S.op("dve", lambda e: e.bn_aggr(st_t[:, 12:14], st_t[:, 0:12]), reads=[b_s], writes=[b_s])
            yield
            S.op("act", lambda e: e.activation(st_t[:, 14:15], st_t[:, 13:14], AF.Sqrt, bias=EPS), reads=[b_s], writes=[b_s])
            yield
            S.op("dve", lambda e: e.reciprocal(st_t[:, 15:16], st_t[:, 14:15]), reads=[b_s], writes=[b_s])

        def ln_mod_T_gen(l, i, slot_shift, slot_scale, B_):
            srcm = 1 if i < 2 else 0
            xt_t, xn_t, st_t = B_["xt"], B_["xn"], B_["st"]
            yield from ln_stats_gen(xt_t, B_["b_xt"], B_)
            yield
            S.op("dve", lambda e: e.tensor_scalar(xn_t[:], xt_t[:], st_t[:, 12:13], st_t[:, 15:16], ALU.subtract, ALU.mult),
                 reads=[B_["b_xt"], B_["b_st"]], writes=[B_["b_xn"]])
            for kc in range(KC):
                if B_["tr1"] is not None:
                    pbk = B_["tr1"]; pv_ = pbank[pbk][:].bitcast(BF16)[:, kc * 128:(kc + 1) * 128]
                else:
                    pbk = B_["tr"][kc % 2]; pv_ = pbb(pbk, 128)
                yield
                S.op("pe", lambda e, kc=kc, pv_=pv_: e.transpose(pv_, xn_t[:, kc * 128:(kc + 1) * 128], id_bf[:]),
                     reads=[B_["b_xn"], b_id_bf], writes=[b_pb[pbk]])
                dst = xmT[:, kc * N + i * 128: kc * N + (i + 1) * 128]
                yield
                S.op("act", lambda e, dst=dst, pv_=pv_, kc=kc: e.activation(
                    dst, pv_, AF.Identity, bias=modv(l, slot_shift, kc, srcm), scale=modv(l, slot_scale, kc, srcm, True)),
                    reads=[b_pb[pbk], b_mod, b_mod1p], writes=[b_xmT[i]])

        GRP = 2

        def run_groups(tiles, sets, load, chain, GRP=GRP):
            tl_ = list(tiles)

            def stream(k):
                mine = tl_[k::GRP]
                for n_, i in enumerate(mine):
                    if n_ == 0:
                        load(i, sets[k])
                    if n_ + 1 < len(mine):
                        load(mine[n_ + 1], sets[k + GRP * ((n_ + 1) % 2)])
                    yield from chain(i, sets[k + GRP * (n_ % 2)])

            run_rr([stream(k) for k in range(GRP)])

        def phase_ln_mod_T(l, src_ap, tiles, slot_shift, slot_scale, src_bufs=None):
            p1 = ExitStack()
            sets = mk_sets(lambda n, s_, dt: p1.enter_context(nc.sbuf_tensor("%s_p1L%d" % (n, l), s_, dt)), 2 * GRP, "a")
            load = lambda i, B_: S.dma("sp", B_["xt"][:], src_ap[i * 128:(i + 1) * 128, :], reads=([src_bufs[i]] if src_bufs else []), writes=[B_["b_xt"]])
            run_groups(tiles, sets, load, lambda i, B_: ln_mod_T_gen(l, i, slot_shift, slot_scale, B_))
            S.barrier(); p1.close()

        phase_ln_mod_T(0, xin, range(NT), 0, 1)
        cut = debug[2] if (debug and debug.startswith("hg") and len(debug) == 3) else None
        dmp = sb("dmp", [128, 512], F32); b_dmp = Buf("dmp")

        def cut_dump(src_ap, src_bufs):
            S.barrier()
            S.op("act", lambda e: e.copy(dmp[:], src_ap), reads=src_bufs, writes=[b_dmp])
            S.dma("sp", outs["dbg_cut"][:, 0:512], dmp[:], reads=[b_dmp])
            S.drain_all("sp"); S.emit(); build.ninstr = S.ninstr

        if cut == "0":
            cut_dump(xmT[:, 0:512], b_xmT); return nc
        if debug == "p1":
            xm_f = sb("xm_f", [128, N], F32); b_xmf = Buf("xmf")
            for kc in range(KC):
                S.op("act", lambda e, kc=kc: e.copy(xm_f[:], xmT[:, kc * N:(kc + 1) * N]), reads=b_xmT, writes=[b_xmf])
                S.dma("sp", outs["dbg_xmT"][:, kc * N:(kc + 1) * N], xm_f[:], reads=[b_xmf])

        mixT = nc.dram_tensor("mixT", [D, N], BF16, kind="Internal").ap()
        b_mixT = [Buf("mixT%d" % i) for i in range(8)]

        def mixer(l, scan_tiles, out_tiles):
            ms = ExitStack()
            sb = lambda n, s_, dt: ms.enter_context(nc.sbuf_tensor("%s_L%d" % (n, l), s_, dt))
            DK = 128; QS = DK ** -0.5; NCH = N // 64
            BLKS = [(0, 512), (512, 512), (1024, 512), (1536, 512), (2048, 256)]
            lb_f = sb("lb_f", [128, 16], F32); lbv = sb("lbv", [128, 16], F32); oml = sb("oml", [128, 16], F32)
            b_lb = Buf("lb")
            cm = sb("cm", [128, 256], F32); rmk = sb("rmk", [128, 512], F32); ngb = sb("ngb", [128, DEPTH * 128], F32)
            b_cm, b_rmk, b_ngb = Buf("cm"), Buf("rmk"), Buf("ngb")
            cm_u = sb("cm_u", [128, 256], mybir.dt.uint32); b_cmu = Buf("cmu")
            S.dma("sp", lb_f[:], lbl, writes=[b_lb]); S.dma("sp", cm[:], cmask, writes=[b_cm]); S.dma("sp", rmk[:], rmask, writes=[b_rmk])
            for l2 in range(DEPTH):
                S.dma("sp", ngb[:, l2 * 128:(l2 + 1) * 128], hgng[l2:l2 + 1, :].partition_broadcast(128), writes=[b_ngb])
            S.op("dve", lambda e: e.tensor_copy(cm_u[:], cm[:]), reads=[b_cm], writes=[b_cmu])
            lt = sb("lt", [128, 32], F32)
            S.op("dve", lambda e: e.memset(lbv[:, 0:8], 0.0), writes=[b_lb], reads=[b_lb])
            S.op("dve", lambda e: e.tensor_max(lt[:, 0:8], lb_f[:, 0:8], lb_f[:, 8:16]), reads=[b_lb], writes=[b_lb])
            S.op("dve", lambda e: e.tensor_sub(lt[:, 8:16], lb_f[:, 0:8], lt[:, 0:8]), reads=[b_lb], writes=[b_lb])
            S.op("dve", lambda e: e.tensor_sub(lt[:, 16:24], lb_f[:, 8:16], lt[:, 0:8]), reads=[b_lb], writes=[b_lb])
            S.op("act", lambda e: e.activation(lt[:, 8:24], lt[:, 8:24], AF.Exp), reads=[b_lb], writes=[b_lb])
            S.op("dve", lambda e: e.tensor_add(lt[:, 24:32], lt[:, 8:16], lt[:, 16:24]), reads=[b_lb], writes=[b_lb])
            S.op("dve", lambda e: e.reciprocal(lt[:, 24:32], lt[:, 24:32]), reads=[b_lb], writes=[b_lb])
            S.op("dve", lambda e: e.tensor_mul(lbv[:, 8:16], lt[:, 16:24], lt[:, 24:32]), reads=[b_lb], writes=[b_lb])
            S.op("dve", lambda e: e.tensor_scalar(oml[:], lbv[:], -1.0, 1.0, ALU.mult, ALU.add), reads=[b_lb], writes=[b_lb])

            wvg = [sb("wvg%d" % i, [128, KC * 256], BF16) for i in range(2)]; b_wvg = [Buf("wvg0"), Buf("wvg1")]
            wzq = [sb("wzq%d" % i, [128, KC * 384], BF16) for i in range(2)]; b_wzq = [Buf("wzq0"), Buf("wzq1")]
            VG = [dict(V=sb("Vh%d" % p_, [128, N], BF16), G=sb("Gh%d" % p_, [128, N], BF16), bV=Buf("V%d" % p_), bG=Buf("G%d" % p_)) for p_ in range(2)]
            Vh, Gh, b_V, b_G = VG[0]["V"], VG[0]["G"], VG[0]["bV"], VG[0]["bG"]
            QT = [sb("QT%d" % d, [128, N], BF16) for d in range(2)]; KT = [sb("KT%d" % d, [128, N], BF16) for d in range(2)]
            QS_T = [sb("QST%d" % d, [128, N], BF16) for d in range(2)]; b_QST = [Buf("QST0"), Buf("QST1")]
            KH = [sb("KH%d" % d, [128, N], BF16) for d in range(2)]; KHt = [sb("KHt%d" % d, [128, N], BF16) for d in range(2)]
            EB = [sb("EB%d" % d, [128, NCH], F32) for d in range(2)]
            b_QT = [Buf("QT0"), Buf("QT1")]; b_KT = [Buf("KT0"), Buf("KT1")]; b_KH = [Buf("KH0"), Buf("KH1")]
            b_KHt = [Buf("KHt0"), Buf("KHt1")]; b_EB = [Buf("EB0"), Buf("EB1")]
            Oacc = [sb("Oacc%d" % d, [128, N], F32) for d in range(2)]; b_O = [Buf("O0"), Buf("O1")]
            HGT = sb("HGT", [128, N], BF16); b_HGT = Buf("HGT")
            NTMP = 9
            tmp = [[sb("tm%d_%d" % (a, i), [128, 512], F32) for i in range(NTMP)] for a in range(2)]
            b_tmp = [[Buf("tm%d_%d" % (a, i)) for i in range(NTMP)] for a in range(2)]
            qs_ = [sb("qs%d" % a, [128, 512], F32) for a in range(2)]; b_qs = [Buf("qs0"), Buf("qs1")]
            Sst = [[sb("S%d_%d" % (d, i), [128, 128], F32) for i in range(2)] for d in range(2)]
            b_S = [[Buf("S%d_%d" % (d, i)) for i in range(2)] for d in range(2)]
            Sbf_f32 = [sb("S3_%d" % d, [128, 128], F32) for d in range(2)]; b_Sx = [Buf("S3_0"), Buf("S3_1")]
            AT = [sb("AT%d" % d, [128, 128], BF16) for d in range(2)]; b_AT = [Buf("AT0"), Buf("AT1")]
            fin = [sb("fin%d" % i, [128, 128], F32) for i in range(2)]; fsq = sb("fsq", [128, 128], F32)
            b_fin = [Buf("fin0"), Buf("fin1")]
            finW = [dict(fsq=(fsq if p_ == 0 else sb("fsq1", [128, 128], F32)), fst=sb("fst%d" % p_, [128, 8], F32), ngg=sb("ngg%d" % p_, [128, 128], F32), hgt=sb("hgt%d" % p_, [128, 128], BF16),
                         b_fsq=Buf("fsq%d" % p_), b_fst=Buf("fst%d" % p_), b_ngg=Buf("ngg%d" % p_), b_hgt=Buf("hgt%d" % p_)) for p_ in range(2)]

            if cut == "L":
                cut_dump(oml[:, 0:16].to_broadcast([128, 16]) if False else xmT[:, 0:512], b_xmT + [b_lb, b_cm, b_rmk, b_ngb]); return "cut"

            e_, one_e, sig, kk, lf, bb, cc_, E1, dd = range(9)

            def hgrn_layer(l, scan_tiles, out_tiles):
                wi = w_in[l].rearrange("(k p) n -> p k n", p=128)
                def load_vg_w(h_):
                    a_ = h_ % 2
                    for ci, c0 in enumerate((h_ * 128, 2048 + h_ * 128)):
                        S.dma("pool", wvg[a_][:].rearrange("p (k n) -> p k n", k=KC)[:, :, ci * 128:(ci + 1) * 128], wi[:, :, c0:c0 + 128], writes=[b_wvg[a_]])

                def load_zq_w(h_):
                    a_ = h_ % 2
                    for ci, c0 in enumerate((512 + h_ * 128, 1024 + h_ * 128, 1536 + h_ * 128)):
                        S.dma("pool", wzq[a_][:].rearrange("p (k n) -> p k n", k=KC)[:, :, ci * 128:(ci + 1) * 128], wi[:, :, c0:c0 + 128], writes=[b_wzq[a_]])

                def A_gen(h_):
                    a_ = h_ % 2; W_ = VG[a_]
                    for i in scan_tiles:
                        pbi = 3 + (i % 2)
                        yield
                        for kc in range(KC):
                            S.op("pe", lambda e, i=i, kc=kc, pbi=pbi: e.matmul(pbf(pbi, 256), xmT[:, kc * N + i * 128:kc * N + (i + 1) * 128],
                                 wvg[a_][:, kc * 256:(kc + 1) * 256], start=(kc == 0), stop=(kc == KC - 1)),
                                 reads=[b_xmT[i], b_wvg[a_]], writes=[b_pb[pbi]], inc=(kc == KC - 1))
                        yield
                        S.op("dve", lambda e, i=i, pbi=pbi: e.tensor_copy(W_["V"][:, i * 128:(i + 1) * 128], pbank[pbi][:, 0:128]), reads=[b_pb[pbi]], writes=[W_["bV"]])
                        yield
                        S.op("act", lambda e, i=i, pbi=pbi: e.activation(W_["G"][:, i * 128:(i + 1) * 128], pbank[pbi][:, 128:256], AF.Silu), reads=[b_pb[pbi]], writes=[W_["bG"]])

                nheads = 1 if cut else 4
                load_vg_w(0)
                if cut == "W":
                    cut_dump(wvg[0][:, 0:512], [b_wvg[0]]); return "cut"
                run_rr([A_gen(0)])
                for h in range(nheads):
                    a = h % 2
                    Vh, Gh, b_V, b_G = VG[a]["V"], VG[a]["G"], VG[a]["bV"], VG[a]["bG"]
                    if h == 0:
                        load_zq_w(0)
                    if h + 1 < nheads:
                        load_zq_w(h + 1)
                        load_vg_w(h + 1)
                    if cut == "A":
                        cut_dump(Vh[:, 0:512], [b_V, b_G]); return "cut"
                    for bi, (t0, nb) in enumerate(BLKS):
                        if t0 // 128 not in scan_tiles:
                            continue
                        tiles_in = [i for i in range(t0 // 128, (t0 + nb) // 128)]
                        for ci in range(3):
                            for kc in range(KC):
                                S.op("pe", lambda e, ci=ci, kc=kc, t0=t0, nb=nb, a=a: e.matmul(pbf(5 + ci, nb), wzq[a][:, kc * 384 + ci * 128:kc * 384 + (ci + 1) * 128],
                                     xmT[:, kc * N + t0:kc * N + t0 + nb], start=(kc == 0), stop=(kc == KC - 1)),
                                     reads=[b_wzq[a]] + [b_xmT[i] for i in tiles_in], writes=[b_pb[5 + ci]], inc=(kc == KC - 1))
                        qa = bi % 2
                        S.op("act", lambda e, nb=nb, qa=qa: e.activation(qs_[qa][:, 0:nb], pbf(7, nb), AF.Silu), reads=[b_pb[7]], writes=[b_qs[qa]])
                        nck = nb // 64; c0 = t0 // 64
                        def gate_chain(d, bi=bi, t0=t0, nb=nb, qa=qa, nck=nck, c0=c0):
                            ta = (bi * 2 + d) % 2
                            T = [t[:, 0:nb] for t in tmp[ta]]; bT = b_tmp[ta]
                            col = l * 8 + d * 4 + h
                            lbc, omc = lbv[:, col:col + 1], oml[:, col:col + 1]
                            yield
                            S.op("act", lambda e, T=T, d=d, nb=nb: e.activation(T[e_], pbf(5 + d, nb), AF.Exp, scale=-1.0), reads=[b_pb[5 + d]], writes=[bT[e_]])
                            yield
                            S.op("act", lambda e, T=T: e.activation(T[one_e], T[e_], AF.Ln, bias=1.0), reads=[bT[e_]], writes=[bT[one_e]])
                            yield
                            S.op("act", lambda e, T=T: e.activation(T[sig], T[one_e], AF.Exp, scale=-1.0), reads=[bT[one_e]], writes=[bT[sig]])
                            yield
                            S.op("dve", lambda e, T=T, omc=omc: e.scalar_tensor_tensor(T[kk], T[e_], omc, T[sig], ALU.mult, ALU.mult), reads=[bT[e_], bT[sig], b_lb], writes=[bT[kk]])
                            yield
                            S.op("act", lambda e, T=T, omc=omc, lbc=lbc: e.activation(T[lf], T[sig], AF.Ln, bias=lbc, scale=omc), reads=[bT[sig], b_lb], writes=[bT[lf]])
                            yield
                            S.op("dve", lambda e, T=T, nb=nb: e.tensor_tensor_scan(T[bb], rmk[:, 0:nb], T[lf], 0.0, ALU.mult, ALU.add), reads=[b_rmk, bT[lf]], writes=[bT[bb]])
                            b3 = T[bb].rearrange("p (c t) -> p c t", t=64)
                            btot = b3[:, :, 63:64]
                            if d == 0:
                                cview, bc_ = T[bb], bT[bb]
                            else:
                                yield
                                S.op("dve", lambda e, T=T: e.tensor_sub(T[cc_], T[lf], T[bb]), reads=[bT[lf], bT[bb]], writes=[bT[cc_]])
                                c3 = T[cc_].rearrange("p (c t) -> p c t", t=64)
                                yield
                                S.op("dve", lambda e, c3=c3, btot=btot, nck=nck: e.tensor_add(c3, c3, btot.to_broadcast([128, nck, 64])), reads=[bT[cc_], bT[bb]], writes=[bT[cc_]])
                                cview, bc_ = T[cc_], bT[cc_]
                            sl = slice(t0, t0 + nb)
                            yield
                            S.op("act", lambda e, T=T, cview=cview: e.activation(T[E1], cview, AF.Exp), reads=[bc_], writes=[bT[E1]])
                            yield
                            S.op("dve", lambda e, T=T, qa=qa, d=d, sl=sl, nb=nb: e.scalar_tensor_tensor(QT[d][:, sl], qs_[qa][:, 0:nb], QS, T[E1], ALU.mult, ALU.mult),
                                 reads=[b_qs[qa], bT[E1]], writes=[b_QT[d]])
                            MID = 31 if d == 0 else 32
                            cm3 = T[one_e].rearrange("p (c t) -> p c t", t=64); cv3m = cview.rearrange("p (c t) -> p c t", t=64)
                            yield
                            S.op("dve", lambda e, cm3=cm3, cv3m=cv3m, nck=nck, MID=MID: e.tensor_sub(cm3, cv3m, cv3m[:, :, MID:MID + 1].to_broadcast([128, nck, 64])),
                                 reads=[bc_, bT[E1], b_QT[d]], writes=[bT[one_e]])
                            yield
                            S.op("act", lambda e, T=T: e.activation(T[E1], T[one_e], AF.Exp), reads=[bT[one_e], b_QT[d]], writes=[bT[E1]])
                            yield
                            S.op("dve", lambda e, T=T, qa=qa, d=d, sl=sl, nb=nb: e.scalar_tensor_tensor(QS_T[d][:, sl], qs_[qa][:, 0:nb], QS, T[E1], ALU.mult, ALU.mult),
                                 reads=[b_qs[qa], bT[E1]], writes=[b_QST[d]])
                            yield
                            S.op("act", lambda e, T=T: e.activation(T[E1], T[one_e], AF.Exp, scale=-1.0), reads=[bT[one_e], b_QST[d]], writes=[bT[E1]])
                            yield
                            S.op("dve", lambda e, T=T, d=d, sl=sl: e.tensor_tensor(KT[d][:, sl], T[kk], T[E1], ALU.mult), reads=[bT[kk], bT[E1]], writes=[b_KT[d]])
                            d3 = T[dd].rearrange("p (c t) -> p c t", t=64); cv3 = cview.rearrange("p (c t) -> p c t", t=64)
                            yield
                            S.op("dve", lambda e, d3=d3, cv3=cv3, btot=btot, nck=nck: e.tensor_sub(d3, btot.to_broadcast([128, nck, 64]), cv3), reads=[bc_, bT[bb]], writes=[bT[dd]])
                            yield
                            S.op("act", lambda e, T=T: e.activation(T[dd], T[dd], AF.Exp), reads=[bT[dd]], writes=[bT[dd]])
                            yield
                            S.op("dve", lambda e, T=T, d=d, sl=sl: e.tensor_tensor(KH[d][:, sl], T[kk], T[dd], ALU.mult), reads=[bT[kk], bT[dd]], writes=[b_KH[d]])
                            yield
                            S.op("act", lambda e, d=d, c0=c0, nck=nck, btot=btot: e.activation(EB[d][:, c0:c0 + nck], btot.rearrange("p c o -> p (c o)"), AF.Exp), reads=[bT[bb]], writes=[b_EB[d]])
                        run_rr([gate_chain(0), gate_chain(1)])
                    if cut == "B":
                        return
                    for d in range(2):
                        for i in scan_tiles:
                            pbi = 1 + (i % 2)
                            S.op("pe", lambda e, d=d, i=i, pbi=pbi: e.transpose(pbb(pbi, 128), KH[d][:, i * 128:(i + 1) * 128], id_bf[:]), reads=[b_KH[d], b_id_bf], writes=[b_pb[pbi]])
                            S.op("act", lambda e, d=d, i=i, pbi=pbi: e.copy(KHt[d][:, i * 128:(i + 1) * 128], pbb(pbi, 128)), reads=[b_pb[pbi]], writes=[b_KHt[d]])
                    if cut == "C":
                        return
                    order = [list(scan_tiles), [t for t in (1, 0) if t in scan_tiles] + [t for t in range(NT - 1, 1, -1) if t in scan_tiles]]
                    TB = [tmp[a_][j_] for a_ in range(2) for j_ in range(NTMP)]; bTB = [b_tmp[a_][j_] for a_ in range(2) for j_ in range(NTMP)]
                    import os as _os2
                    d_stage = int(_os2.environ.get("HG_D_STAGE", "0")) if cut else 0
                    seqs = []
                    for d in range(1 if d_stage else 2):
                        seq = [(i, cpos) for i in order[d] for cpos in ((0, 1) if d == 0 else (1, 0))]
                        seqs.append(seq)
                        nstep = len(order[d])
                        for cpos in (0, 1):
                            base = cpos * nstep
                            s_ = base
                            while s_ < base + nstep:
                                g_end = min(base + nstep, (s_ // 4 + 1) * 4)
                                pbk = 3 + 2 * cpos + ((s_ // 4) % 2)
                                for sl_ in range(s_, g_end):
                                    i = order[d][sl_ - base]; q = sl_ % 4
                                    ps_ = slice(cpos * 64, cpos * 64 + 64); ts_ = slice(i * 128, (i + 1) * 128)
                                    S.op("pe", lambda e, d=d, ts_=ts_, ps_=ps_, pbk=pbk, q=q, Vh=Vh: e.matmul(pbank[pbk][:, q * 128:(q + 1) * 128], KHt[d][ps_, ts_], Vh[ps_, ts_], start=True, stop=True),
                                         reads=[b_KHt[d], b_V], writes=[b_pb[pbk]], inc=(sl_ == g_end - 1))
                                tb = 9 * d + s_ // 4; c_lo, c_hi = (s_ % 4) * 128, ((g_end - 1) % 4 + 1) * 128
                                S.op("act", lambda e, tb=tb, pbk=pbk, c_lo=c_lo, c_hi=c_hi, TB=TB: e.copy(TB[tb][:, c_lo:c_hi], pbank[pbk][:, c_lo:c_hi]), reads=[b_pb[pbk]], writes=[bTB[tb]])
                                s_ = g_end
                    if d_stage == 1:
                        cut_dump(TB[0][:, 0:512], bTB[0:9]); return "cut"
                    RING = 18
                    b_slot = [[Buf("st%d_%d" % (d_, r_)) for r_ in range(RING)] for d_ in range(2)]
                    sslot = lambda d_, k: (KH[d_][:, (k % RING) * 128:(k % RING + 1) * 128], b_slot[d_][k % RING])
                    S3 = [Sst[d_] + [Sbf_f32[d_]] for d_ in range(2)]; b_S3 = [b_S[d_] + [b_Sx[d_]] for d_ in range(2)]

                    produced = [0, 0]
                    consumed = [0, 0]

                    def chain_gen(d):
                        seq = seqs[d]; nstep = len(order[d])
                        yield
                        S.op("dve", lambda e, d=d, S3=S3: e.memset(S3[d][0][:], 0.0), writes=[b_S3[d][0]])
                        for k, (i, cpos) in enumerate(seq[:-1]):
                            sl_ = cpos * nstep + k // 2
                            ch = i * 2 + cpos; tb = 9 * d + sl_ // 4; q = sl_ % 4
                            si, so = k % 3, (k + 1) % 3
                            yield
                            S.op("dve", lambda e, d=d, si=si, so=so, ch=ch, tb=tb, q=q, TB=TB, S3=S3: e.scalar_tensor_tensor(S3[d][so][:], S3[d][si][:], EB[d][:, ch:ch + 1], TB[tb][:, q * 128:(q + 1) * 128], ALU.mult, ALU.add),
                                 reads=[b_S3[d][si], b_EB[d], bTB[tb]], writes=[b_S3[d][so]])
                            dst, bdst = sslot(d, k + 1)
                            yield
                            while (k + 1) - consumed[d] >= RING:
                                yield
                            S.op("act", lambda e, d=d, so=so, dst=dst, S3=S3: e.copy(dst, S3[d][so][:]), reads=[b_S3[d][so]], writes=[bdst])
                            produced[d] = k + 1

                    def out_gen(d):
                        yield
                        S.op("dve", lambda e, d=d: e.memset(AT[d][:], 0.0), writes=[b_AT[d]])
                        for step, i in enumerate(order[d]):
                            if i not in out_tiles:
                                consumed[d] = 2 * (step + 1)
                                continue
                            ts_ = slice(i * 128, (i + 1) * 128); par = step % 2
                            p_sc, p_o = (5, 6)[d], ((7, 0)[d])
                            yield
                            S.op("pe", lambda e, d=d, ts_=ts_, p_sc=p_sc: e.matmul(pbf(p_sc, 128), KT[d][:, ts_], QS_T[d][:, ts_], start=True, stop=True),
                                 reads=[b_KT[d], b_QST[d]], writes=[b_pb[p_sc]])
                            yield
                            S.op("dve", lambda e, d=d, p_sc=p_sc: e.copy_predicated(AT[d][:], cm_u[:, d * 128:(d + 1) * 128], pbf(p_sc, 128)),
                                 reads=[b_pb[p_sc], b_cmu, b_AT[d]], writes=[b_AT[d]])
                            cps = (0, 1) if d == 0 else (1, 0)
                            need = [(cp, k) for cp, k in zip(cps, (2 * step, 2 * step + 1)) if k > 0]
                            yield
                            while need and produced[d] < max(k_ for _, k_ in need):
                                yield
                            S.op("pe", lambda e, d=d, ts_=ts_, p_o=p_o, nn=len(need), Vh=Vh: e.matmul(pbf(p_o, 128), AT[d][:], Vh[:, ts_], start=True, stop=(nn == 0)),
                                 reads=[b_AT[d], b_V], writes=[b_pb[p_o]], inc=(len(need) == 0))
                            for j, (cp, k) in enumerate(need):
                                ps_ = slice(cp * 64, cp * 64 + 64); tsc = slice(i * 128 + cp * 64, i * 128 + cp * 64 + 64)
                                src, bsrc = sslot(d, k)
                                lastj = j == len(need) - 1
                                if not lastj:
                                    pass
                                S.op("pe", lambda e, d=d, tsc=tsc, ps_=ps_, p_o=p_o, src=src, lastj=lastj: e.matmul(pbank[p_o][ps_, 0:128], QT[d][:, tsc], src, start=False, stop=lastj),
                                     reads=[b_QT[d], bsrc], writes=[b_pb[p_o]], inc=lastj)
                            consumed[d] = 2 * (step + 1)
                            yield
                            S.op("act", lambda e, d=d, ts_=ts_, p_o=p_o: e.copy(Oacc[d][:, ts_], pbf(p_o, 128)), reads=[b_pb[p_o]], writes=[b_O[d]])

                    if h == nheads - 1 and not cut and debug != "hg":
                        sc_load(l, 0); sc_load(l, 1); sg_load(l)
                    if d_stage:
                        run_rr([chain_gen(0), out_gen(0)])
                    else:
                        run_rr([chain_gen(0), chain_gen(1), out_gen(0), out_gen(1)] + ([A_gen(h + 1)] if h + 1 < nheads else []))
                    if d_stage == 3:
                        cut_dump(Oacc[0][:, 0:512], [b_O[0]]); return "cut"
                    if cut == "D":
                        return
                    def fin_chain(i, fa):
                        ts_ = slice(i * 128, (i + 1) * 128); pbi = 1 + fa
                        W = finW[fa]
                        yield
                        S.op("dve", lambda e: e.tensor_add(fin[fa][:], Oacc[0][:, ts_], Oacc[1][:, ts_]), reads=[b_O[0], b_O[1]], writes=[b_fin[fa]])
                        yield
                        S.op("act", lambda e: e.activation(W["fsq"][:], fin[fa][:], AF.Square, accum_out=W["fst"][:, 0:1]), reads=[b_fin[fa]], writes=[W["b_fsq"], W["b_fst"]])
                        yield
                        S.op("act", lambda e: e.activation(W["fst"][:, 1:2], W["fst"][:, 0:1], AF.Sqrt, bias=EPS, scale=1.0 / 128), reads=[W["b_fst"]], writes=[W["b_fst"]])
                        yield
                        S.op("dve", lambda e: e.reciprocal(W["fst"][:, 2:3], W["fst"][:, 1:2]), reads=[W["b_fst"]], writes=[W["b_fst"]])
                        yield
                        S.op("dve", lambda e, Gcur=Gcur: e.tensor_tensor(W["ngg"][:], ngb[:, l * 128:(l + 1) * 128], Gcur[:, ts_], ALU.mult), reads=[b_ngb, b_Gcur], writes=[W["b_ngg"]])
                        yield
                        S.op("dve", lambda e: e.scalar_tensor_tensor(W["hgt"][:], fin[fa][:], W["fst"][:, 2:3], W["ngg"][:], ALU.mult, ALU.mult), reads=[b_fin[fa], W["b_fst"], W["b_ngg"]], writes=[W["b_hgt"]])
                        yield
                        S.op("pe", lambda e: e.transpose(pbb(pbi, 128), W["hgt"][:], id_bf[:]), reads=[W["b_hgt"], b_id_bf], writes=[b_pb[pbi]])
                        yield
                        S.op("act", lambda e: e.copy(HGT[:, ts_], pbb(pbi, 128)), reads=[b_pb[pbi]], writes=[b_HGT])

                    Gcur, b_Gcur = Gh, b_G
                    ot_ = list(out_tiles)

                    def fin_stream(k, ot_=ot_):
                        for i in ot_[k::2]:
                            yield from fin_chain(i, k)

                    run_rr([fin_stream(0), fin_stream(1)])
                    S.dma("sp", mixT[h * 128:(h + 1) * 128, :], HGT[:], reads=[b_HGT], writes=[b_mixT[h]])

            if cut:
                S.barrier()
                S.op("act", lambda e, Vh=Vh: e.copy(Oacc[1][:], Vh[:]), reads=[b_V], writes=[b_O[1]])
                S.dma("sp", outs["dbg_cut"][:, 0:N], Oacc[1][:], reads=[b_O[1]])
                if cut in "DE":
                    S.dma("sp", outs["dbg_cut"][:, N:2 * N], Oacc[0][:], reads=[b_O[0]])

            scw_s = sb("scw_s", [128, DEPTH * 6], F32); b_scw = Buf("scw")
            S.dma("sp", scw_s[:], scw, writes=[b_scw])
            lngb = sb("lngb", [128, 512], F32); b_lngb = Buf("lngb")
            WsT = sb("WsT", [128, 4 * 128], BF16); b_WsT = Buf("WsT")
            wsn = sb("wsn", [128, 128], BF16); b_wsn = Buf("wsn")
            BS = sb("BS", [128, 2 * 128], F32); b_BS = Buf("BS")
            SEQS = [(0, 256), (256, N)]
            SEGS = [(0, 256), (256, 512), (512, 1024), (1024, 1536), (1536, 2048), (2048, N)]

            pre_done = {}

            def sc_load(l, cc):
                wi_ = w_in[l].rearrange("(k p) n -> p k n", p=128); a_ = cc % 2
                for ci, c0 in enumerate((2560 + cc * 128, 2816 + cc * 128, 3072 + cc * 128)):
                    S.dma("pool", wzq[a_][:].rearrange("p (k n) -> p k n", k=KC)[:, :, ci * 128:(ci + 1) * 128], wi_[:, :, c0:c0 + 128], writes=[b_wzq[a_]])
                pre_done[("sc", l, cc)] = True

            def sg_load(l):
                wi_ = w_in[l].rearrange("(k p) n -> p k n", p=128)
                S.dma("pool", wvg[0][:].rearrange("p (k n) -> p k n", k=KC), wi_[:, :, 3328:3584], writes=[b_wvg[0]])
                S.dma("pool", wvg[1][:].rearrange("p (k n) -> p k n", k=KC), wi_[:, :, 3584:3840], writes=[b_wvg[1]])
                S.dma("sp", lngb[:, 0:256], sgln[2 * l:2 * l + 1, :].partition_broadcast(128), writes=[b_lngb])
                S.dma("sp", lngb[:, 256:512], sgln[2 * l + 1:2 * l + 2, :].partition_broadcast(128), writes=[b_lngb])
                for cc in range(2):
                    S.dma("sp", BS[:, cc * 128:(cc + 1) * 128], sgb[2 * l + cc], writes=[b_BS])
                pre_done[("sg", l)] = True

            def sc_layer(l, tiles):
                wi = w_in[l].rearrange("(k p) n -> p k n", p=128)
                tmax = (max(tiles) + 1) * 128; tmin = min(tiles) * 128
                Pf, GBf, b_P, b_GB = Oacc[0], Oacc[1], b_O[0], b_O[1]
                for cc in range(2):
                    a = cc % 2
                    if not pre_done.get(("sc", l, cc)):
                        sc_load(l, cc)
                    sc_blks = [(t0, min(512, tmax - t0)) for t0 in range(tmin, tmax, 512)]
                    for (t0, nb) in sc_blks:
                        tiles_in = list(range(t0 // 128, (t0 + nb) // 128))
                        for ci in range(3):
                            for kc in range(KC):
                                S.op("pe", lambda e, ci=ci, kc=kc, t0=t0, nb=nb, a=a: e.matmul(pbf(5 + ci, nb), wzq[a][:, kc * 384 + ci * 128:kc * 384 + (ci + 1) * 128],
                                     xmT[:, kc * N + t0:kc * N + t0 + nb], start=(kc == 0), stop=(kc == KC - 1)),
                                     reads=[b_wzq[a]] + [b_xmT[i] for i in tiles_in], writes=[b_pb[5 + ci]], inc=(kc == KC - 1))
                        T0 = tmp[0][0][:, 0:nb]
                        S.op("act", lambda e, t0=t0, nb=nb: e.copy(GBf[:, t0:t0 + nb], pbf(5, nb)), reads=[b_pb[5]], writes=[b_GB])
                        S.op("act", lambda e, T0=T0, nb=nb: e.copy(T0, pbf(6, nb)), reads=[b_pb[6]], writes=[b_tmp[0][0]])
                        S.op("dve", lambda e, T0=T0, t0=t0, nb=nb: e.tensor_tensor(Pf[:, t0:t0 + nb], T0, pbf(7, nb), ALU.mult), reads=[b_tmp[0][0], b_pb[7]], writes=[b_P])
                    wb = l * 6 + cc * 3
                    w0_, w1_, w2_ = scw_s[:, wb:wb + 1], scw_s[:, wb + 1:wb + 2], scw_s[:, wb + 2:wb + 3]
                    for (a0, a1) in SEGS:
                        if a0 < tmin or a0 >= tmax:
                            continue
                        s0, s1 = [sq for sq in SEQS if sq[0] <= a0 < sq[1]][0]
                        Y = tmp[1][0]; bY = b_tmp[1][0]; n_ = a1 - a0
                        S.op("dve", lambda e, Y=Y, a0=a0, a1=a1, n_=n_, w1_=w1_: e.tensor_scalar(Y[:, 0:n_], Pf[:, a0:a1], w1_, None, ALU.mult), reads=[b_P, b_scw], writes=[bY])
                        lo = max(a0, s0 + 1)
                        S.op("dve", lambda e, Y=Y, a0=a0, a1=a1, lo=lo, w0_=w0_: e.scalar_tensor_tensor(Y[:, lo - a0:a1 - a0], Pf[:, lo - 1:a1 - 1], w0_, Y[:, lo - a0:a1 - a0], ALU.mult, ALU.add),
                             reads=[b_P, b_scw, bY], writes=[bY])
                        hi = min(a1, s1 - 1)
                        S.op("dve", lambda e, Y=Y, a0=a0, hi=hi, w2_=w2_: e.scalar_tensor_tensor(Y[:, 0:hi - a0], Pf[:, a0 + 1:hi + 1], w2_, Y[:, 0:hi - a0], ALU.mult, ALU.add),
                             reads=[b_P, b_scw, bY], writes=[bY])
                        S.op("dve", lambda e, Y=Y, a0=a0, a1=a1, n_=n_: e.tensor_tensor(HGT[:, a0:a1], GBf[:, a0:a1], Y[:, 0:n_], ALU.mult), reads=[b_GB, bY], writes=[b_HGT])
                    S.dma("sp", mixT[(4 + cc) * 128:(5 + cc) * 128, tmin:tmax], HGT[:, tmin:tmax], reads=[b_HGT], writes=[b_mixT[4 + cc]])

            def sg_layer(l, tiles):
                wi = w_in[l].rearrange("(k p) n -> p k n", p=128)
                tmax = (max(tiles) + 1) * 128; tmin = min(tiles) * 128
                if not pre_done.get(("sg", l)):
                    sg_load(l)
                for g in range(4):
                    S.dma("pool", wsn[:], sgw[4 * l + g], writes=[b_wsn])
                    S.op("pe", lambda e: e.transpose(pbb(1, 128), wsn[:], id_bf[:]), reads=[b_wsn, b_id_bf], writes=[b_pb[1]])
                    S.op("act", lambda e, g=g: e.copy(WsT[:, g * 128:(g + 1) * 128], pbb(1, 128)), reads=[b_pb[1]], writes=[b_WsT])
                SGT, b_SGT = KH, b_KH
                sgW = [dict(vn=sb("sgvn%d" % p_, [128, 256], F32), vhb=sb("sgvh%d" % p_, [128, 256], BF16), st=sb("sgst%d" % p_, [128, 40], F32),
                            b_vn=Buf("sgvn%d" % p_), b_vhb=Buf("sgvh%d" % p_), b_st=Buf("sgst%d" % p_), banks=((3, 4, 5), (6, 7, 0))[p_], par=p_) for p_ in range(2)]

                def sg_chain(i, W):
                    ts_ = slice(i * 128, (i + 1) * 128); pv, pu, pm = W["banks"]; par = W["par"]
                    vn_t, vhb_t, st_t = W["vn"], W["vhb"], W["st"]
                    yield
                    for kc in range(KC):
                        S.op("pe", lambda e, kc=kc: e.matmul(pbf(pv, 256), xmT[:, kc * N + i * 128:kc * N + (i + 1) * 128], wvg[1][:, kc * 256:(kc + 1) * 256],
                             start=(kc == 0), stop=(kc == KC - 1)), reads=[b_xmT[i], b_wvg[1]], writes=[b_pb[pv]], inc=(kc == KC - 1))
                    for g in range(4):
                        yield
                        S.op("dve", lambda e, g=g: e.bn_stats(st_t[:, g * 6:(g + 1) * 6], pbank[pv][:, g * 64:(g + 1) * 64]), reads=[b_pb[pv]], writes=[W["b_st"]])
                    for g in range(4):
                        yield
                        S.op("dve", lambda e, g=g: e.bn_aggr(st_t[:, 24 + 2 * g:26 + 2 * g], st_t[:, g * 6:(g + 1) * 6]), reads=[W["b_st"]], writes=[W["b_st"]])
                    mv = st_t[:, 24:32].rearrange("p (g two) -> p g two", two=2)
                    yield
                    S.op("act", lambda e: e.activation(st_t[:, 32:36], mv[:, :, 1], AF.Sqrt, bias=EPS), reads=[W["b_st"]], writes=[W["b_st"]])
                    yield
                    S.op("dve", lambda e: e.reciprocal(st_t[:, 36:40], st_t[:, 32:36]), reads=[W["b_st"]], writes=[W["b_st"]])
                    for g in range(4):
                        yield
                        S.op("dve", lambda e, g=g: e.tensor_scalar(vn_t[:, g * 64:(g + 1) * 64], pbank[pv][:, g * 64:(g + 1) * 64], st_t[:, 24 + 2 * g:25 + 2 * g], st_t[:, 36 + g:37 + g],
                             ALU.subtract, ALU.mult), reads=[b_pb[pv], W["b_st"]], writes=[W["b_vn"]])
                    yield
                    S.op("dve", lambda e: e.tensor_mul(vn_t[:], vn_t[:], lngb[:, 0:256]), reads=[W["b_vn"], b_lngb], writes=[W["b_vn"]])
                    yield
                    S.op("dve", lambda e: e.tensor_add(vhb_t[:], vn_t[:], lngb[:, 256:512]), reads=[W["b_vn"], b_lngb], writes=[W["b_vhb"]])
                    yield
                    for cc in range(2):
                        for kc in range(KC):
                            S.op("pe", lambda e, cc=cc, kc=kc: e.matmul(pbank[pu][:, cc * 128:(cc + 1) * 128], wvg[0][:, kc * 256 + cc * 128:kc * 256 + (cc + 1) * 128], xmT[:, kc * N + i * 128:kc * N + (i + 1) * 128],
                                 start=(kc == 0), stop=(kc == KC - 1)), reads=[b_wvg[0], b_xmT[i]], writes=[b_pb[pu]], inc=(kc == KC - 1 and cc == 1))
                    yield
                    for cc in range(2):
                        for gg in range(2):
                            g = 2 * cc + gg
                            S.op("pe", lambda e, g=g, gg=gg, cc=cc: e.matmul(pbank[pm][gg * 64:(gg + 1) * 64, cc * 128:(cc + 1) * 128], vhb_t[:, g * 64:(g + 1) * 64], WsT[:, g * 128:(g + 1) * 128], start=True, stop=True),
                                 reads=[W["b_vhb"], b_WsT], writes=[b_pb[pm]], inc=(gg == 1 and cc == 1))
                    for cc in range(2):
                        T1, T2 = tmp[cc][1 + 2 * par][:, 0:128], tmp[cc][2 + 2 * par][:, 0:128]
                        bT1, bT2 = b_tmp[cc][1 + 2 * par], b_tmp[cc][2 + 2 * par]
                        yield
                        S.op("dve", lambda e, T1=T1, cc=cc: e.tensor_tensor(T1, pbank[pm][:, cc * 128:(cc + 1) * 128], BS[:, cc * 128:(cc + 1) * 128], ALU.add), reads=[b_pb[pm], b_BS], writes=[bT1])
                        yield
                        S.op("act", lambda e, T2=T2, cc=cc: e.copy(T2, pbank[pu][:, cc * 128:(cc + 1) * 128]), reads=[b_pb[pu]], writes=[bT2])
                        yield
                        S.op("dve", lambda e, T1=T1, T2=T2, cc=cc: e.tensor_tensor(SGT[cc][:, ts_], T1, T2, ALU.mult), reads=[bT1, bT2], writes=[b_SGT[cc]])

                tl_ = list(tiles)

                def sg_stream(k):
                    for i in tl_[k::2]:
                        yield from sg_chain(i, sgW[k])

                run_rr([sg_stream(0), sg_stream(1)])
                for cc in range(2):
                    S.dma("sp", mixT[(6 + cc) * 128:(7 + cc) * 128, tmin:tmax], SGT[cc][:, tmin:tmax], reads=[b_SGT[cc]], writes=[b_mixT[6 + cc]])

            r = hgrn_layer(l, scan_tiles, out_tiles)
            if r == "cut":
                st.enter_context(ms)
                return "cut"
            if debug != "hg":
                sc_layer(l, out_tiles)
                sg_layer(l, out_tiles)
            if debug in ("hg", "mix"):
                mixer.dbg = (Oacc[0], b_O[0], HGT, b_HGT)
                st.enter_context(ms)
                return None
            S.barrier(); ms.close()
            return None

        if mixer(0, list(range(NT)), list(range(NT))) == "cut":
            return nc

        ALPHA = (2 * DEPTH) ** 0.25
        x1d = nc.dram_tensor("x1d", [N, D], F32, kind="Internal").ap()
        b_x1d = [Buf("x1d%d" % i) for i in range(NT)]

        def gate_bcast(dst, b_dst, l, slot, srcm, scr, b_scr):
            for kc in range(KC):
                g = modv(l, slot, kc, srcm)
                S.op("dve", lambda e, g=g: e.tensor_scalar(scr[:], id_f[:], 0.0, g, ALU.mult, ALU.add), reads=[b_id_f, b_mod], writes=[b_scr])
                S.op("pe", lambda e: e.matmul(pbf(0, 128), scr[:], id_f[:], start=True, stop=True), reads=[b_scr, b_id_f], writes=[b_pb[0]])
                S.op("act", lambda e, kc=kc: e.copy(dst[:, kc * 128:(kc + 1) * 128], pbf(0, 128)), reads=[b_pb[0]], writes=[b_dst])

        def phase_wout_ln1(l, src_ap, tiles, src_bufs=None):
            ps4 = ExitStack()
            sb4 = lambda n, s_, dt: ps4.enter_context(nc.sbuf_tensor("%s_p4L%d" % (n, l), s_, dt))
            mixS = sb4("mixS", [128, KC * N], BF16); b_mixS = [Buf("mixS%d" % k) for k in range(KC)]
            wo = sb4("wo", [128, KC * D], BF16); b_wo = Buf("wo")
            gbc = [sb4("gbc%d" % i, [128, D], F32) for i in range(2)]; b_gbc = [Buf("gbc0"), Buf("gbc1")]
            lg = sb4("lg", [128, D], F32); lb_ = sb4("lb_", [128, D], F32); b_lg, b_lbb = Buf("lg"), Buf("lbb")
            scr = sb4("scr", [128, 128], F32); b_scr = Buf("scr")
            for k in range(KC):
                S.dma("sp", mixS[:, k * N:(k + 1) * N], mixT[k * 128:(k + 1) * 128, :], reads=[b_mixT[k]], writes=[b_mixS[k]])
            S.dma("pool", wo[:].rearrange("p (k n) -> p k n", k=KC), w_out[l].rearrange("(k p) n -> p k n", p=128), writes=[b_wo])
            S.dma("sp", lg[:], lnp[4 * l:4 * l + 1, :].partition_broadcast(128), writes=[b_lg])
            S.dma("sp", lb_[:], lnp[4 * l + 1:4 * l + 2, :].partition_broadcast(128), writes=[b_lbb])
            gate_bcast(gbc[0], b_gbc[0], l, 2, 0, scr, b_scr)
            if any(i < 2 for i in tiles):
                gate_bcast(gbc[1], b_gbc[1], l, 2, 1, scr, b_scr)
            G4 = 3
            sets = mk_sets(sb4, 2 * G4, "b", with_y=True, plan=[(3, 3, 1), (4, 4, 2), (5, 5, 6)])
            load = lambda i, B_: S.dma("sp", B_["xq"][:], src_ap[i * 128:(i + 1) * 128, :], reads=([src_bufs[i]] if src_bufs else []), writes=[B_["b_xq"]])

            def chain4(i, B_):
                srcm = 1 if i < 2 else 0
                xq_t, yt_t, xt_t, st_t = B_["xq"], B_["yt"], B_["xt"], B_["st"]
                for hf in range(2):
                    pbi = B_["mm"][hf]; hs = slice(hf * 512, (hf + 1) * 512)
                    yield
                    for kc in range(KC):
                        S.op("pe", lambda e, kc=kc, hf=hf, pbi=pbi: e.matmul(pbf(pbi, 512), mixS[:, kc * N + i * 128:kc * N + (i + 1) * 128],
                             wo[:, kc * D + hf * 512:kc * D + (hf + 1) * 512], start=(kc == 0), stop=(kc == KC - 1)),
                             reads=[b_mixS[kc], b_wo], writes=[b_pb[pbi]], inc=(kc == KC - 1))
                    yield
                    S.op("dve", lambda e, hs=hs, pbi=pbi: e.tensor_tensor(yt_t[:, hs], pbf(pbi, 512), gbc[srcm][:, hs], ALU.mult),
                         reads=[b_pb[pbi], b_gbc[srcm]], writes=[B_["b_yt"]])
                    yield
                    S.op("dve", lambda e, hs=hs: e.scalar_tensor_tensor(yt_t[:, hs], xq_t[:, hs], ALPHA, yt_t[:, hs], ALU.mult, ALU.add),
                         reads=[B_["b_xq"], B_["b_yt"]], writes=[B_["b_yt"]])
                yield from ln_stats_gen(yt_t, B_["b_yt"], B_)
                yield
                S.op("dve", lambda e: e.scalar_tensor_tensor(yt_t[:], yt_t[:], st_t[:, 12:13], lg[:], ALU.subtract, ALU.mult),
                     reads=[B_["b_yt"], B_["b_st"], b_lg], writes=[B_["b_yt"]])
                yield
                S.op("dve", lambda e: e.scalar_tensor_tensor(xt_t[:], yt_t[:], st_t[:, 15:16], lb_[:], ALU.mult, ALU.add),
                     reads=[B_["b_yt"], B_["b_st"], b_lbb, B_["b_xt"]], writes=[B_["b_xt"]])
                yield
                S.dma("sp", x1d[i * 128:(i + 1) * 128, :], xt_t[:], reads=[B_["b_xt"]], writes=[b_x1d[i]])
                yield from ln_mod_T_gen(l, i, 3, 4, B_)

            run_groups(tiles, sets, load, chain4, GRP=G4)
            S.barrier(); ps4.close()

        if debug in ("x1", "h", "x2", None):
            phase_wout_ln1(0, xin, list(range(NT)))
        if debug == "x1":
            for i in range(NT):
                a = i % 2
                S.dma("sp", xt[a][:], x1d[i * 128:(i + 1) * 128, :], reads=[b_x1d[i]], writes=[b_xt[a]])
                S.dma("sp", outs["dbg_x1"][i * 128:(i + 1) * 128, :], xt[a][:], reads=[b_xt[a]])
            xm_f = sb("xm2_f", [128, N], F32); b_xmf = Buf("xm2f")
            for kc in range(KC):
                S.op("act", lambda e, kc=kc: e.copy(xm_f[:], xmT[:, kc * N:(kc + 1) * N]), reads=b_xmT, writes=[b_xmf])
                S.dma("sp", outs["dbg_xm2T"][:, kc * N:(kc + 1) * N], xm_f[:], reads=[b_xmf])

        hTd = nc.dram_tensor("hTd", [FF, N], BF16, kind="Internal").ap()
        b_hTd = [Buf("hTd%d" % i) for i in range(NFC)]
        x2d = [nc.dram_tensor("x2d%d" % i, [N, D], F32, kind="Internal").ap() for i in range(DEPTH - 1)]
        b_x2d = [Buf("x2d%d" % i) for i in range(NT)]
        GW = 64

        def phase_ffn_up(l, with_ctx):
            p5 = ExitStack()
            sb5 = lambda n, s_, dt: p5.enter_context(nc.sbuf_tensor("%s_p5L%d" % (n, l), s_, dt))
            fcw_s = sb5("fcw_s", [128, NFC * 9], F32); fcb_s = sb5("fcb_s", [128, NFC], F32); b_fcw, b_fcb = Buf("fcw"), Buf("fcb")
            S.dma("sp", fcw_s[:], fcw[:, l * NFC * 9:(l + 1) * NFC * 9], writes=[b_fcw])
            S.dma("sp", fcb_s[:], fcb[:, l * NFC:(l + 1) * NFC], writes=[b_fcb])
            wag = [sb5("wag%d" % i, [128, KC * 256], BF16) for i in range(2)]; b_wag = [Buf("wag0"), Buf("wag1")]
            apx = [sb5("apx%d" % i, [128, 34 * 66], BF16) for i in range(2)]; apc = [sb5("apc%d" % i, [128, 258], BF16) for i in range(2)]
            b_ap = [Buf("ap0"), Buf("ap1")]
            dg = [sb5("dg%d" % i, [128, 9 * 128], BF16) for i in range(2)]; b_dg = [Buf("dg0"), Buf("dg1")]
            gel = [sb5("gel%d" % i, [128, 512], F32) for i in range(2)]; b_gel = [Buf("gel0"), Buf("gel1")]
            htc = [sb5("htc%d" % i, [128, N], BF16) for i in range(2)]; b_htc = [Buf("htc0"), Buf("htc1")]
            for i in range(2):
                S.op("pool", lambda e, i=i: e.memset(apx[i][:], 0.0), writes=[b_ap[i]])
                S.op("pool", lambda e, i=i: e.memset(apc[i][:], 0.0), writes=[b_ap[i]])
            FB = ([("c", 0, 256, 0)] if with_ctx else []) + [("x", 256 + 512 * j, 512, j) for j in range(4)]
            wu = ffn_up[l].rearrange("(k p) n -> p k n", p=128)
            defer_l = l + 1 if (l + 1 < DEPTH) else None
            if defer_l is not None:
                awd = [sb5("awd%d" % i, [128, KC * 512], BF16) for i in range(2)]; b_awd = [Buf("awd0"), Buf("awd1")]
                pm_d, b_pmd = pbf(1, 96), b_pb[1]
            for fc in range(NFC):
                a = fc % 2
                if defer_l is not None:
                    if fc < 12:
                        mod_chunk_load(defer_l, fc, awd[fc % 2], b_awd[fc % 2])
                    if 1 <= fc <= 12:
                        mod_chunk_mm(defer_l, fc - 1, awd[(fc - 1) % 2], b_awd[(fc - 1) % 2], pm_d, b_pmd)
                    if fc == 13:
                        mod_finish(defer_l, pm_d, b_pmd)
                for ci, c0 in enumerate((fc * 128, FF + fc * 128)):
                    S.dma("pool", wag[a][:].rearrange("p (k n) -> p k n", k=KC)[:, :, ci * 128:(ci + 1) * 128], wu[:, :, c0:c0 + 128], writes=[b_wag[a]])
                for tap in range(9):
                    wcol = fcw_s[:, fc * 9 + tap:fc * 9 + tap + 1]
                    S.op("dve", lambda e, a=a, tap=tap, wcol=wcol: e.tensor_scalar(dg[a][:, tap * 128:(tap + 1) * 128], id_f[:], wcol, None, ALU.mult),
                         reads=[b_id_f, b_fcw], writes=[b_dg[a]])
                apx3 = apx[a][:].rearrange("p (r c) -> p r c", c=66)
                for bn, (kind, t0, nb, j) in enumerate(FB):
                    pa = (3, 6)[bn % 2]
                    tl = list(range(t0 // 128, (t0 + nb) // 128))
                    for kc in range(KC):
                        S.op("pe", lambda e, a=a, kc=kc, t0=t0, nb=nb, pa=pa: e.matmul(pbf(pa, nb), wag[a][:, kc * 256:kc * 256 + 128], xmT[:, kc * N + t0:kc * N + t0 + nb],
                             start=(kc == 0), stop=(kc == KC - 1)), reads=[b_wag[a]] + [b_xmT[i] for i in tl], writes=[b_pb[pa]], inc=(kc == KC - 1))
                    if kind == "c":
                        S.op("act", lambda e, a=a, pa=pa: e.copy(apc[a][:, 1:257], pbf(pa, 256)), reads=[b_pb[pa]], writes=[b_ap[a]])
                    else:
                        S.op("act", lambda e, a=a, pa=pa, j=j, apx3=apx3: e.copy(apx3[:, 1 + 8 * j:9 + 8 * j, 1:65], pbf(pa, 512).rearrange("p (r c) -> p r c", c=GW)),
                             reads=[b_pb[pa]], writes=[b_ap[a]])
                for bn, (kind, t0, nb, j) in enumerate(FB):
                    pc, pg = (4, 7)[bn % 2], (5, 0)[bn % 2]
                    ga = bn % 2
                    tl = list(range(t0 // 128, (t0 + nb) // 128))
                    if kind == "c":
                        for n_, dj in enumerate(range(3)):
                            tap = 3 + dj
                            S.op("pe", lambda e, a=a, tap=tap, dj=dj, pc=pc, n_=n_: e.matmul(pbf(pc, 256), dg[a][:, tap * 128:(tap + 1) * 128], apc[a][:, dj:dj + 256], start=(n_ == 0), stop=(n_ == 2)),
                                 reads=[b_dg[a], b_ap[a]], writes=[b_pb[pc]], inc=(n_ == 2))
                    else:
                        for tap in range(9):
                            di, dj = tap // 3, tap % 3
                            mv_ = apx3[:, di + 8 * j:di + 8 * j + 8, dj:dj + GW]
                            S.op("pe", lambda e, a=a, tap=tap, mv_=mv_, pc=pc: e.matmul(pbf(pc, 512), dg[a][:, tap * 128:(tap + 1) * 128], mv_, start=(tap == 0), stop=(tap == 8)),
                                 reads=[b_dg[a], b_ap[a]], writes=[b_pb[pc]], inc=(tap == 8))
                    for kc in range(KC):
                        S.op("pe", lambda e, a=a, kc=kc, t0=t0, nb=nb, pg=pg: e.matmul(pbf(pg, nb), wag[a][:, kc * 256 + 128:kc * 256 + 256], xmT[:, kc * N + t0:kc * N + t0 + nb],
                             start=(kc == 0), stop=(kc == KC - 1)), reads=[b_wag[a]] + [b_xmT[i] for i in tl], writes=[b_pb[pg]], inc=(kc == KC - 1))
                    bcol = fcb_s[:, fc:fc + 1]
                    S.op("act", lambda e, ga=ga, nb=nb, pc=pc, bcol=bcol: e.activation(gel[ga][:, 0:nb], pbf(pc, nb), AF.Gelu, bias=bcol), reads=[b_pb[pc], b_fcb], writes=[b_gel[ga]])
                    S.op("dve", lambda e, a=a, ga=ga, t0=t0, nb=nb, pg=pg: e.tensor_tensor(htc[a][:, t0:t0 + nb], gel[ga][:, 0:nb], pbf(pg, nb), ALU.mult),
                         reads=[b_gel[ga], b_pb[pg]], writes=[b_htc[a]])
                lo = FB[0][1]
                S.dma("sp", hTd[fc * 128:(fc + 1) * 128, lo:N], htc[a][:, lo:N], reads=[b_htc[a]], writes=[b_hTd[fc]])
            S.barrier(); p5.close()

        def phase_ffn_down(l, tiles, dst_ap, dst_row0):
            p6 = ExitStack()
            sb6 = lambda n, s_, dt: p6.enter_context(nc.sbuf_tensor("%s_p6L%d" % (n, l), s_, dt))
            wd = sb6("wd", [128, NFC * D], BF16); b_wdp = {(hf_, q_): Buf("wd%d%d" % (hf_, q_)) for hf_ in range(2) for q_ in range(2)}
            gbc = [sb6("gbc%d" % i, [128, D], F32) for i in range(2)]; b_gbc = [Buf("gbc0"), Buf("gbc1")]
            lg = sb6("lg", [128, D], F32); lb_ = sb6("lb_", [128, D], F32); b_lg, b_lbb = Buf("lg"), Buf("lbb")
            scr = sb6("scr", [128, 128], F32); b_scr = Buf("scr")
            wdv = ffn_down[l].rearrange("(f p) n -> p f n", p=128)
            for hf_ in range(2):
                for q_ in range(2):
                    S.dma("pool", wd[:].rearrange("p (f n) -> p f n", f=NFC)[:, q_ * 11:(q_ + 1) * 11, hf_ * 512:(hf_ + 1) * 512],
                          wdv[:, q_ * 11:(q_ + 1) * 11, hf_ * 512:(hf_ + 1) * 512], writes=[b_wdp[(hf_, q_)]])
            S.dma("sp", lg[:], lnp[4 * l + 2:4 * l + 3, :].partition_broadcast(128), writes=[b_lg])
            S.dma("sp", lb_[:], lnp[4 * l + 3:4 * l + 4, :].partition_broadcast(128), writes=[b_lbb])
            gate_bcast(gbc[0], b_gbc[0], l, 5, 0, scr, b_scr)
            if any(i < 2 for i in tiles):
                gate_bcast(gbc[1], b_gbc[1], l, 5, 1, scr, b_scr)
            hv = hTd.rearrange("(f p) n -> p f n", p=128)
            sets = mk_sets(sb6, 2 * GRP, "c", with_y=True, with_h=True)

            def load(i, B_):
                S.dma("sp", B_["ht"][:].rearrange("p (f n) -> p f n", f=NFC), hv[:, :, i * 128:(i + 1) * 128], reads=b_hTd, writes=[B_["b_ht"]])
                S.dma("sp", B_["xq"][:], x1d[i * 128:(i + 1) * 128, :], reads=[b_x1d[i]], writes=[B_["b_xq"]])

            def chain6(i, B_):
                srcm = 1 if i < 2 else 0
                xq_t, yt_t, xt_t, st_t, ht_t = B_["xq"], B_["yt"], B_["xt"], B_["st"], B_["ht"]
                for hf in range(2):
                    pbi = B_["mm"][hf]; hs = slice(hf * 512, (hf + 1) * 512)
                    yield
                    for f_ in range(NFC):
                        S.op("pe", lambda e, f_=f_, hf=hf, pbi=pbi: e.matmul(pbf(pbi, 512), ht_t[:, f_ * 128:(f_ + 1) * 128], wd[:, f_ * D + hf * 512:f_ * D + (hf + 1) * 512],
                             start=(f_ == 0), stop=(f_ == NFC - 1)), reads=[B_["b_ht"], b_wdp[(hf, f_ // 11)]], writes=[b_pb[pbi]], inc=(f_ == NFC - 1))
                    yield
                    S.op("dve", lambda e, hs=hs, pbi=pbi: e.tensor_tensor(yt_t[:, hs], pbf(pbi, 512), gbc[srcm][:, hs], ALU.mult),
                         reads=[b_pb[pbi], b_gbc[srcm]], writes=[B_["b_yt"]])
                    yield
                    S.op("dve", lambda e, hs=hs: e.scalar_tensor_tensor(yt_t[:, hs], xq_t[:, hs], ALPHA, yt_t[:, hs], ALU.mult, ALU.add),
                         reads=[B_["b_xq"], B_["b_yt"]], writes=[B_["b_yt"]])
                yield from ln_stats_gen(yt_t, B_["b_yt"], B_)
                yield
                S.op("dve", lambda e: e.scalar_tensor_tensor(yt_t[:], yt_t[:], st_t[:, 12:13], lg[:], ALU.subtract, ALU.mult),
                     reads=[B_["b_yt"], B_["b_st"], b_lg], writes=[B_["b_yt"]])
                yield
                S.op("dve", lambda e: e.scalar_tensor_tensor(xt_t[:], yt_t[:], st_t[:, 15:16], lb_[:], ALU.mult, ALU.add),
                     reads=[B_["b_yt"], B_["b_st"], b_lbb, B_["b_xt"]], writes=[B_["b_xt"]])
                r0 = i * 128 - dst_row0
                yield
                S.dma("sp", dst_ap[r0:r0 + 128, :], xt_t[:], reads=[B_["b_xt"]], writes=[b_x2d[i]])

            run_groups(tiles, sets, load, chain6)
            S.barrier(); p6.close()

        if debug in ("h", "x2", None):
            phase_ffn_up(0, True)
        if debug == "h":
            hst_b = sb("hst_b", [128, N], BF16); hst_f = sb("hst_f", [128, N], F32); b_hsb, b_hsf = Buf("hsb"), Buf("hsf")
            for fc in range(NFC):
                S.dma("sp", hst_b[:], hTd[fc * 128:(fc + 1) * 128, :], reads=[b_hTd[fc]], writes=[b_hsb])
                S.op("act", lambda e: e.copy(hst_f[:], hst_b[:]), reads=[b_hsb], writes=[b_hsf])
                S.dma("sp", outs["dbg_hT"][fc * 128:(fc + 1) * 128, :], hst_f[:], reads=[b_hsf])
        if debug in ("x2", None):
            phase_ffn_down(0, list(range(NT)), x2d[0], 0)
        if debug == "x2":
            for i in range(NT):
                a = i % 2
                S.dma("sp", xt[a][:], x2d[0][i * 128:(i + 1) * 128, :], reads=[b_x2d[i]], writes=[b_xt[a]])
                S.dma("sp", outs["dbg_x2"][i * 128:(i + 1) * 128, :], xt[a][:], reads=[b_xt[a]])

        if debug is None:
            XT = list(range(2, NT))
            phase_ln_mod_T(1, x2d[0], range(NT), 0, 1, src_bufs=b_x2d)
            mixer(1, list(range(NT)), XT)
            phase_wout_ln1(1, x2d[0], XT, src_bufs=b_x2d)
            phase_ffn_up(1, False)
            phase_ffn_down(1, XT, outs["out"], 256)
        if debug in ("hg", "mix"):
            hg_f, b_hgf, hg_b, b_hgb = mixer.dbg
            for h in range(8 if debug == "mix" else 4):
                S.dma("sp", hg_b[:], mixT[h * 128:(h + 1) * 128, :], reads=[b_mixT[h]], writes=[b_hgb])
                S.op("act", lambda e: e.copy(hg_f[:], hg_b[:]), reads=[b_hgb], writes=[b_hgf])
                S.dma("sp", outs["dbg_hgT"][h * 128:(h + 1) * 128, :], hg_f[:], reads=[b_hgf])
        S.drain_all("sp")
        S.emit()
        build.ninstr = S.ninstr
    return nc


def _prep(inputs, b):
    f = lambda a: np.ascontiguousarray(np.asarray(a, dtype=np.float32))
    m = {}
    m["xin"] = f(np.concatenate([inputs["ctx"][b], inputs["x"][b]], axis=0))
    cc = np.stack([np.asarray(inputs["c"][b]).reshape(KC, 128).T, np.asarray(inputs["c_ctx"]).reshape(KC, 128).T], axis=-1)
    m["c2"] = f(cc.reshape(128, KC * 2))
    m["ada_w"] = f(inputs["ada_w"])
    m["ada_b_fm"] = f(np.asarray(inputs["ada_b"]).reshape(DEPTH, 48, 128).transpose(2, 0, 1).reshape(128, DEPTH * 48))
    m["ident"] = np.eye(128, dtype=np.float32)
    m["w_in"] = f(inputs["w_in"])
    m["w_out"] = f(inputs["w_out"])
    m["ffn_up"] = f(inputs["ffn_up"]); m["ffn_down"] = f(inputs["ffn_down"])
    m["fcw_fm"] = f(np.asarray(inputs["ffn_conv_w"]).reshape(DEPTH, 9, NFC, 128).transpose(3, 0, 2, 1).reshape(128, DEPTH * NFC * 9))
    m["fcb_fm"] = f(np.asarray(inputs["ffn_conv_b"]).reshape(DEPTH, NFC, 128).transpose(2, 0, 1).reshape(128, DEPTH * NFC))
    m["lnp"] = f(np.stack([np.asarray(inputs[k]) for k in ("ln1_g", "ln1_b", "ln2_g", "ln2_b")], axis=1).reshape(DEPTH * 4, D))
    m["scw_fm"] = f(np.asarray(inputs["sc_conv_w"]).reshape(DEPTH, 3, 2, 128).transpose(3, 0, 2, 1).reshape(128, DEPTH * 6))
    m["sgln"] = f(np.stack([np.asarray(inputs["sg_ln_g"]), np.asarray(inputs["sg_ln_b"])], axis=1).reshape(DEPTH * 2, 256))
    m["sg_w"] = f(np.asarray(inputs["sg_w"]).reshape(DEPTH * 4, 128, 128))
    sb_ = np.asarray(inputs["sg_b"]).reshape(DEPTH, 2, 2, 1, 128)
    m["sgb_fm"] = f(np.broadcast_to(sb_, (DEPTH, 2, 2, 64, 128)).reshape(DEPTH * 2, 128, 128))
    m["lbl"] = f(np.asarray(inputs["hg_lb"]).reshape(DEPTH, 2, 4, 128).transpose(3, 0, 1, 2).reshape(128, 16))
    m["hg_norm_g"] = f(inputs["hg_norm_g"])
    ii = np.arange(128)
    same = (ii[:, None] // 64) == (ii[None, :] // 64)
    m["cmask"] = f(np.concatenate([same & (ii[:, None] <= ii[None, :]), same & (ii[:, None] >= ii[None, :])], axis=1))
    rm = np.ones((128, 512), np.float32); rm[:, ::64] = 0.0
    m["rmask"] = rm
    return m


_NC = None


def kernel(**inputs):
    global _NC
    if _NC is None:
        _NC = build()
    shared = None
    maps = []
    for b in range(8):
        m = _prep(inputs, b)
        if shared is None:
            shared = {k: m[k] for k in m if k not in ("xin", "c2")}
        else:
            m.update(shared)
        maps.append(m)
    res = run_bass_kernel_spmd(_NC, maps, core_ids=list(range(8)))
    return np.stack([np.asarray(r["out"], dtype=np.float32) for r in res.results], axis=0)
```

```python
import numpy as np
import concourse.bass as bass
import concourse.mybir as mybir

F32 = mybir.dt.float32
BF16 = mybir.dt.bfloat16
ALU = mybir.AluOpType
AF = mybir.ActivationFunctionType


class Buf:
    __slots__ = ("name", "w", "r", "excl")

    def __init__(self, name="", excl=False):
        self.name = name
        self.excl = excl
        self.w = None
        self.r = []


class Sync:
    ENGS = ("pe", "act", "dve", "pool", "sp")

    SEM_LIMIT = 1900

    def __init__(self, nc, stack, n_dma_sems=20):
        self.nc = nc
        self.stack = stack
        self.owner = {}
        self.cur = {}
        self.nsem = 0
        self.q = {e: [] for e in self.ENGS}
        self.sems = {}
        self.cnt = {}
        self.known = {e: {} for e in self.ENGS}
        for e in ("pe", "act", "dve", "pool"):
            self.cur[e] = None
            self._new_sem(e)
        self.dpool = {}
        self.dk = {}
        for e in ("sp", "act", "pool"):
            self.dpool[e] = []
            for i in range(n_dma_sems):
                key = "d_%s_%d" % (e, i)
                self.sems[key] = stack.enter_context(nc.semaphore(key)); self.nsem += 1
                self.cnt[key] = 0
                self.owner[key] = "dma_" + e
                self.dpool[e].append(key)
            self.dk[e] = 0
        self.pe_pending = False
        self.ninstr = 0

    def _new_sem(self, eng):
        n = sum(1 for k in self.owner if self.owner[k] == eng)
        key = "%s#%d" % (eng, n)
        self.sems[key] = self.stack.enter_context(self.nc.semaphore("s_%s_%d" % (eng, n))); self.nsem += 1
        self.cnt[key] = 0
        self.owner[key] = eng
        self.prev = getattr(self, "prev", {})
        self.prev[eng] = self.cur[eng]
        self.cur[eng] = key

    def _latest(self, eng):
        k = self.cur[eng]
        if self.cnt[k]:
            return (k, self.cnt[k])
        p = self.prev.get(eng)
        return (p, self.cnt[p]) if p and self.cnt[p] else None

    def _wait(self, eng, tok):
        if tok is None:
            return
        key, val = tok
        if self.known[eng].get(key, 0) >= val:
            return
        self.known[eng][key] = val
        sem = self.sems[key]
        self.q[eng].append(lambda e, sem=sem, val=val: e.wait_ge(sem, val))
        self.ninstr += 1

    def _deps(self, eng, reads, writes, skip_self=False):
        for b in reads:
            if b.w is not None and not (skip_self and self.owner[b.w[0]] == eng):
                self._wait(eng, b.w)
            if b.excl:
                for t in b.r:
                    if self.owner[t[0]] != eng:
                        self._wait(eng, t)
        for b in writes:
            if b.w is not None and not (skip_self and self.owner[b.w[0]] == eng):
                self._wait(eng, b.w)
            for t in b.r:
                if not (skip_self and self.owner[t[0]] == eng):
                    self._wait(eng, t)

    def _commit(self, tok, reads, writes):
        for b in reads:
            b.r.append(tok)
            if len(b.r) > 64:
                best = {}
                for k, v in b.r:
                    if best.get(k, 0) < v:
                        best[k] = v
                b.r = list(best.items())
        for b in writes:
            b.w = tok
            b.r = []

    def op(self, eng, fn, reads=(), writes=(), inc=True):
        pe = eng == "pe"
        self._deps(eng, reads, writes, skip_self=pe)
        if self.cnt[self.cur[eng]] >= self.SEM_LIMIT and not (pe and self.pe_pending):
            self._new_sem(eng)
        key = self.cur[eng]
        if inc:
            self.cnt[key] += 1
            tok = (key, self.cnt[key])
            sem = self.sems[key]
            self.q[eng].append(lambda e, fn=fn, sem=sem: fn(e).then_inc(sem, 1))
            if pe:
                self.pe_pending = False
        else:
            assert pe
            tok = (key, self.cnt[key] + 1)
            self.q[eng].append(lambda e, fn=fn: fn(e))
            self.pe_pending = True
        self.ninstr += 1
        self._commit(tok, reads, writes)
        return tok

    def dma(self, eng, out, in_, reads=(), writes=(), **kw):
        self._deps(eng, reads, writes)
        pool = self.dpool[eng]
        slot = self.dk[eng] % len(pool)
        key = pool[slot]
        self.dk[eng] += 1
        if self.cnt[key] + 16 > self.SEM_LIMIT:
            self._wait(eng, (key, self.cnt[key]))
            n = sum(1 for k in self.owner if self.owner[k] == "dma_" + eng)
            nk = "d_%s_%d" % (eng, n)
            self.sems[nk] = self.stack.enter_context(self.nc.semaphore(nk)); self.nsem += 1
            self.cnt[nk] = 0; self.owner[nk] = "dma_" + eng
            pool[slot] = nk; key = nk
            self.retired = getattr(self, "retired", []) + [key]
        if self.cnt[key] > 0:
            self._wait(eng, (key, self.cnt[key]))
        self.cnt[key] += 16
        tok = (key, self.cnt[key])
        sem = self.sems[key]
        self.q[eng].append(
            lambda e, out=out, in_=in_, sem=sem, kw=kw: e.dma_start(out=out, in_=in_, **kw).then_inc(sem, 16))
        self.ninstr += 1
        self._commit(tok, reads, writes)
        return tok

    def wait_all(self, eng, toks):
        for t in toks:
            self._wait(eng, t)

    def barrier(self):
        toks = [t for t in (self._latest(e) for e in ("pe", "act", "dve", "pool")) if t]
        assert not self.pe_pending
        for q in self.dpool:
            for key in self.dpool[q]:
                if self.cnt[key]:
                    toks.append((key, self.cnt[key]))
        for eng in self.ENGS:
            for t in toks:
                if self.owner[t[0]] != eng:
                    self._wait(eng, t)

    def drain_all(self, eng="sp"):
        for q in self.dpool:
            for key in self.dpool[q]:
                if self.cnt[key]:
                    self._wait(eng, (key, self.cnt[key]))

    def emit(self):
        assert not self.pe_pending, "last PE op must carry inc"
        nc = self.nc
        q = self.q
        with nc.Block() as block:
            @block.sync
            def _(e):
                for f in q["sp"]:
                    f(e)

            @block.tensor
            def _(e):
                for f in q["pe"]:
                    f(e)

            @block.scalar
            def _(e):
                for f in q["act"]:
                    f(e)

            @block.vector
            def _(e):
                for f in q["dve"]:
                    f(e)

            @block.gpsimd
            def _(e):
                for f in q["pool"]:
                    f(e)


def run_rr(chains):
    live = list(chains)
    while live:
        for g in list(live):
            try:
                next(g)
            except StopIteration:
                live.remove(g)


from contextlib import ExitStack
from concourse.bass_utils import run_bass_kernel_spmd

FF = 2816; NFC = FF // 128
D = 1024; NT = 18; N = NT * 128; DEPTH = 2; NMOD = 6; EPS = 1e-6
KC = D // 128


def build(debug=None):
    nc = bass.Bass("TRN2", target_bir_lowering=False)
    dram = lambda n, s, dt, kind: nc.dram_tensor(n, s, dt, kind=kind).ap()
    xin = dram("xin", [N, D], F32, "ExternalInput")
    c2 = dram("c2", [128, KC * 2], F32, "ExternalInput")
    ada_w = dram("ada_w", [DEPTH, D, NMOD * D], F32, "ExternalInput")
    ada_b = dram("ada_b_fm", [128, DEPTH * 48], F32, "ExternalInput")
    ident = dram("ident", [128, 128], F32, "ExternalInput")
    w_in = dram("w_in", [DEPTH, D, 3840], F32, "ExternalInput")
    lbl = dram("lbl", [128, 16], F32, "ExternalInput")
    hgng = dram("hg_norm_g", [DEPTH, 128], F32, "ExternalInput")
    scw = dram("scw_fm", [128, DEPTH * 6], F32, "ExternalInput")
    sgln = dram("sgln", [DEPTH * 2, 256], F32, "ExternalInput")
    sgw = dram("sg_w", [DEPTH * 4, 128, 128], F32, "ExternalInput")
    sgb = dram("sgb_fm", [DEPTH * 2, 128, 128], F32, "ExternalInput")
    w_out = dram("w_out", [DEPTH, D, D], F32, "ExternalInput")
    ffn_up = dram("ffn_up", [DEPTH, D, 2 * FF], F32, "ExternalInput")
    ffn_down = dram("ffn_down", [DEPTH, FF, D], F32, "ExternalInput")
    fcw = dram("fcw_fm", [128, DEPTH * NFC * 9], F32, "ExternalInput")
    fcb = dram("fcb_fm", [128, DEPTH * NFC], F32, "ExternalInput")
    lnp = dram("lnp", [DEPTH * 4, D], F32, "ExternalInput")
    cmask = dram("cmask", [128, 256], F32, "ExternalInput")
    rmask = dram("rmask", [128, 512], F32, "ExternalInput")
    outs = {}
    if debug is None:
        outs["out"] = dram("out", [N - 256, D], F32, "ExternalOutput")
    if debug and debug.startswith("hg") and len(debug) == 3:
        outs["dbg_cut"] = dram("dbg_cut", [128, 2 * N], F32, "ExternalOutput")
    if debug == "h":
        outs["dbg_hT"] = dram("dbg_hT", [FF, N], F32, "ExternalOutput")
    if debug == "x2":
        outs["dbg_x2"] = dram("dbg_x2", [N, D], F32, "ExternalOutput")
    if debug == "x1":
        outs["dbg_x1"] = dram("dbg_x1", [N, D], F32, "ExternalOutput")
        outs["dbg_xm2T"] = dram("dbg_xm2T", [128, KC * N], F32, "ExternalOutput")
    if debug in ("hg", "mix"):
        outs["dbg_hgT"] = dram("dbg_hgT", [D if debug == "mix" else 512, N], F32, "ExternalOutput")
    if debug == "p1":
        outs["dbg_mod"] = dram("dbg_mod", [128, DEPTH * 96], F32, "ExternalOutput")
        outs["dbg_xmT"] = dram("dbg_xmT", [128, KC * N], F32, "ExternalOutput")
    with ExitStack() as st:
        S = Sync(nc, st)
        sb = lambda n, s, dt: st.enter_context(nc.sbuf_tensor(n, s, dt))
        pbank = [st.enter_context(nc.psum_tensor("pb%d" % i, [128, 512], F32)) for i in range(8)]
        b_pb = [Buf("pb%d" % i, excl=True) for i in range(8)]
        pbf = lambda i, n: pbank[i][:, 0:n]
        pbb = lambda i, n: pbank[i][:].bitcast(BF16)[:, 0:n]
        id_f = sb("id_f", [128, 128], F32); id_bf = sb("id_bf", [128, 128], BF16)
        c2_f = sb("c2_f", [128, KC * 2], F32); c2_bf = sb("c2_bf", [128, KC * 2], BF16)
        adab = sb("adab", [128, DEPTH * 48], F32)
        mod = sb("mod", [128, DEPTH * 96], F32)
        mod1p = sb("mod1p", [128, DEPTH * 96], F32)
        b_id_f, b_id_bf, b_c2f, b_c2bf, b_adab, b_mod, b_mod1p = (Buf(n) for n in "idf idbf c2f c2bf adab mod mod1p".split())
        S.dma("sp", id_f[:], ident, writes=[b_id_f])
        S.dma("pool", id_bf[:], ident, writes=[b_id_bf])
        S.dma("sp", c2_f[:], c2, writes=[b_c2f])
        S.dma("sp", adab[:], ada_b, writes=[b_adab])
        S.op("act", lambda e: e.activation(c2_bf[:], c2_f[:], AF.Silu), reads=[b_c2f], writes=[b_c2bf])
        st0 = ExitStack()
        aw = [st0.enter_context(nc.sbuf_tensor("aw%d" % i, [128, KC * 512], BF16)) for i in range(2)]
        b_aw = [Buf("aw0"), Buf("aw1")]
        p_mod = pbf(0, 96); b_pmod = b_pb[0]
        def mod_chunk_load(l, g, buf, bb):
            src = ada_w[l].rearrange("(k p) n -> p k n", p=128)[:, :, g * 512:(g + 1) * 512]
            S.dma("pool", buf[:].rearrange("p (k n) -> p k n", k=KC), src, writes=[bb])

        def mod_chunk_mm(l, g, buf, bb, pm_ap, b_pm):
            for jj in range(4):
                j = g * 4 + jj
                for k in range(KC):
                    last = (k == KC - 1)
                    S.op("pe", lambda e, buf=buf, jj=jj, k=k, j=j, last=last: e.matmul(
                        pm_ap[:, 2 * j:2 * j + 2], buf[:, k * 512 + jj * 128:k * 512 + (jj + 1) * 128],
                        c2_bf[:, 2 * k:2 * k + 2], start=(k == 0), stop=last),
                        reads=[bb, b_c2bf], writes=[b_pm], inc=(last and jj == 3))

        def mod_finish(l, pm_ap, b_pm):
            pm = pm_ap.rearrange("p (j s) -> p j s", s=2)
            mv = mod[:, l * 96:(l + 1) * 96].rearrange("p (j s) -> p j s", s=2)
            for s_ in range(2):
                S.op("dve", lambda e, pm=pm, mv=mv, s_=s_, l=l: e.tensor_tensor(
                    mv[:, :, s_], pm[:, :, s_], adab[:, l * 48:(l + 1) * 48], ALU.add),
                    reads=[b_pm, b_adab], writes=[b_mod])
            S.op("dve", lambda e, l=l: e.tensor_scalar_add(mod1p[:, l * 96:(l + 1) * 96], mod[:, l * 96:(l + 1) * 96], 1.0), reads=[b_mod], writes=[b_mod1p])

        for g in range(12):
            mod_chunk_load(0, g, aw[g % 2], b_aw[g % 2])
            mod_chunk_mm(0, g, aw[g % 2], b_aw[g % 2], p_mod, b_pmod)
        mod_finish(0, p_mod, b_pmod)
        S.barrier(); st0.close()
        if debug == "p1":
            S.dma("sp", outs["dbg_mod"], mod[:], reads=[b_mod])

        def modv(l, slot, kc, src, one_plus=False):
            t = mod1p if one_plus else mod
            col = l * 96 + (slot * 8 + kc) * 2 + src
            return t[:, col:col + 1]

        xmT = sb("xmT", [128, KC * N], BF16)
        b_xmT = [Buf("xmT%d" % i) for i in range(NT)]
        if debug in ("x1", "x2"):
            xt = [sb("xt%d" % i, [128, D], F32) for i in range(2)]; b_xt = [Buf("xt0"), Buf("xt1")]

        def mk_sets(alloc, n, tag, with_y=False, with_h=False, plan=None):
            sets = []
            for g in range(n):
                B_ = dict(xt=alloc("xt%s%d" % (tag, g), [128, D], F32), xn=alloc("xn%s%d" % (tag, g), [128, D], BF16), st=alloc("st%s%d" % (tag, g), [128, 16], F32),
                          b_xt=Buf("xt%d" % g), b_xn=Buf("xn%d" % g), b_st=Buf("st%d" % g),
                          tr=((1, 2) if g % 2 == 0 else (7, 0)), mm=((3, 4) if g % 2 == 0 else (5, 6)), tr1=None)
                if plan is not None:
                    p_ = plan[g % len(plan)]
                    B_.update(mm=(p_[0], p_[1]), tr1=p_[2])
                if with_y:
                    B_.update(xq=alloc("xq%s%d" % (tag, g), [128, D], F32), yt=alloc("yt%s%d" % (tag, g), [128, D], F32), b_xq=Buf("xq%d" % g), b_yt=Buf("yt%d" % g))
                if with_h:
                    B_.update(ht=alloc("ht%s%d" % (tag, g), [128, NFC * 128], BF16), b_ht=Buf("ht%d" % g))
                sets.append(B_)
            return sets

        def ln_stats_gen(src, b_src, B_):
            st_t, b_s = B_["st"], B_["b_st"]
            yield
            S.op("dve", lambda e: e.bn_stats(st_t[:, 0:6], src[:, 0:512]), reads=[b_src], writes=[b_s])
            yield
            S.op("dve", lambda e: e.bn_stats(st_t[:, 6:12], src[:, 512:1024]), reads=[b_src], writes=[b_s])
            yield
            S.op("dve", lambda e: e.bn_aggr(st_t[:, 12:14], st_t[:, 0:12]), reads=[b_s], writes=[b_s])
            yield
            S.op("act", lambda e: e.activation(st_t[:, 14:15], st_t[:, 13:14], AF.Sqrt, bias=EPS), reads=[b_s], writes=[b_s])
            yield
            S.op("dve", lambda e: e.reciprocal(st_t[:, 15:16], st_t[:, 14:15]), reads=[b_s], writes=[b_s])

        def ln_mod_T_gen(l, i, slot_shift, slot_scale, B_):
            srcm = 1 if i < 2 else 0
            xt_t, xn_t, st_t = B_["xt"], B_["xn"], B_["st"]
            yield from ln_stats_gen(xt_t, B_["b_xt"], B_)
            yield
            S.op("dve", lambda e: e.tensor_scalar(xn_t[:], xt_t[:], st_t[:, 12:13], st_t[:, 15:16], ALU.subtract, ALU.mult),
                 reads=[B_["b_xt"], B_["b_st"]], writes=[B_["b_xn"]])
            for kc in range(KC):
                if B_["tr1"] is not None:
                    pbk = B_["tr1"]; pv_ = pbank[pbk][:].bitcast(BF16)[:, kc * 128:(kc + 1) * 128]
                else:
                    pbk = B_["tr"][kc % 2]; pv_ = pbb(pbk, 128)
                yield
                S.op("pe", lambda e, kc=kc, pv_=pv_: e.transpose(pv_, xn_t[:, kc * 128:(kc + 1) * 128], id_bf[:]),
                     reads=[B_["b_xn"], b_id_bf], writes=[b_pb[pbk]])
                dst = xmT[:, kc * N + i * 128: kc * N + (i + 1) * 128]
                yield
                S.op("act", lambda e, dst=dst, pv_=pv_, kc=kc: e.activation(
                    dst, pv_, AF.Identity, bias=modv(l, slot_shift, kc, srcm), scale=modv(l, slot_scale, kc, srcm, True)),
                    reads=[b_pb[pbk], b_mod, b_mod1p], writes=[b_xmT[i]])

        GRP = 2

        def run_groups(tiles, sets, load, chain, GRP=GRP):
            tl_ = list(tiles)

            def stream(k):
                mine = tl_[k::GRP]
                for n_, i in enumerate(mine):
                    if n_ == 0:
                        load(i, sets[k])
                    if n_ + 1 < len(mine):
                        load(mine[n_ + 1], sets[k + GRP * ((n_ + 1) % 2)])
                    yield from chain(i, sets[k + GRP * (n_ % 2)])

            run_rr([stream(k) for k in range(GRP)])

        def phase_ln_mod_T(l, src_ap, tiles, slot_shift, slot_scale, src_bufs=None):
            p1 = ExitStack()
            sets = mk_sets(lambda n, s_, dt: p1.enter_context(nc.sbuf_tensor("%s_p1L%d" % (n, l), s_, dt)), 2 * GRP, "a")
            load = lambda i, B_: S.dma("sp", B_["xt"][:], src_ap[i * 128:(i + 1) * 128, :], reads=([src_bufs[i]] if src_bufs else []), writes=[B_["b_xt"]])
            run_groups(tiles, sets, load, lambda i, B_: ln_mod_T_gen(l, i, slot_shift, slot_scale, B_))
            S.barrier(); p1.close()

        phase_ln_mod_T(0, xin, range(NT), 0, 1)
        cut = debug[2] if (debug and debug.startswith("hg") and len(debug) == 3) else None
        dmp = sb("dmp", [128, 512], F32); b_dmp = Buf("dmp")

        def cut_dump(src_ap, src_bufs):
            S.barrier()
            S.op("act", lambda e: e.copy(dmp[:], src_ap), reads=src_bufs, writes=[b_dmp])
            S.dma("sp", outs["dbg_cut"][:, 0:512], dmp[:], reads=[b_dmp])
            S.drain_all("sp"); S.emit(); build.ninstr = S.ninstr

        if cut == "0":
            cut_dump(xmT[:, 0:512], b_xmT); return nc
        if debug == "p1":
            xm_f = sb("xm_f", [128, N], F32); b_xmf = Buf("xmf")
            for kc in range(KC):
                S.op("act", lambda e, kc=kc: e.copy(xm_f[:], xmT[:, kc * N:(kc + 1) * N]), reads=b_xmT, writes=[b_xmf])
                S.dma("sp", outs["dbg_xmT"][:, kc * N:(kc + 1) * N], xm_f[:], reads=[b_xmf])

        mixT = nc.dram_tensor("mixT", [D, N], BF16, kind="Internal").ap()
        b_mixT = [Buf("mixT%d" % i) for i in range(8)]

        def mixer(l, scan_tiles, out_tiles):
            ms = ExitStack()
            sb = lambda n, s_, dt: ms.enter_context(nc.sbuf_tensor("%s_L%d" % (n, l), s_, dt))
            DK = 128; QS = DK ** -0.5; NCH = N // 64
            BLKS = [(0, 512), (512, 512), (1024, 512), (1536, 512), (2048, 256)]
            lb_f = sb("lb_f", [128, 16], F32); lbv = sb("lbv", [128, 16], F32); oml = sb("oml", [128, 16], F32)
            b_lb = Buf("lb")
            cm = sb("cm", [128, 256], F32); rmk = sb("rmk", [128, 512], F32); ngb = sb("ngb", [128, DEPTH * 128], F32)
            b_cm, b_rmk, b_ngb = Buf("cm"), Buf("rmk"), Buf("ngb")
            cm_u = sb("cm_u", [128, 256], mybir.dt.uint32); b_cmu = Buf("cmu")
            S.dma("sp", lb_f[:], lbl, writes=[b_lb]); S.dma("sp", cm[:], cmask, writes=[b_cm]); S.dma("sp", rmk[:], rmask, writes=[b_rmk])
            for l2 in range(DEPTH):
                S.dma("sp", ngb[:, l2 * 128:(l2 + 1) * 128], hgng[l2:l2 + 1, :].partition_broadcast(128), writes=[b_ngb])
            S.op("dve", lambda e: e.tensor_copy(cm_u[:], cm[:]), reads=[b_cm], writes=[b_cmu])
            lt = sb("lt", [128, 32], F32)
            S.op("dve", lambda e: e.memset(lbv[:, 0:8], 0.0), writes=[b_lb], reads=[b_lb])
            S.op("dve", lambda e: e.tensor_max(lt[:, 0:8], lb_f[:, 0:8], lb_f[:, 8:16]), reads=[b_lb], writes=[b_lb])
            S.op("dve", lambda e: e.tensor_sub(lt[:, 8:16], lb_f[:, 0:8], lt[:, 0:8]), reads=[b_lb], writes=[b_lb])
            S.op("dve", lambda e: e.tensor_sub(lt[:, 16:24], lb_f[:, 8:16], lt[:, 0:8]), reads=[b_lb], writes=[b_lb])
            S.op("act", lambda e: e.activation(lt[:, 8:24], lt[:, 8:24], AF.Exp), reads=[b_lb], writes=[b_lb])
            S.op("dve", lambda e: e.tensor_add(lt[:, 24:32], lt[:, 8:16], lt[:, 16:24]), reads=[b_lb], writes=[b_lb])
            S.op("dve", lambda e: e.reciprocal(lt[:, 24:32], lt[:, 24:32]), reads=[b_lb], writes=[b_lb])
            S.op("dve", lambda e: e.tensor_mul(lbv[:, 8:16], lt[:, 16:24], lt[:, 24:32]), reads=[b_lb], writes=[b_lb])
            S.op("dve", lambda e: e.tensor_scalar(oml[:], lbv[:], -1.0, 1.0, ALU.mult, ALU.add), reads=[b_lb], writes=[b_lb])

            wvg = [sb("wvg%d" % i, [128, KC * 256], BF16) for i in range(2)]; b_wvg = [Buf("wvg0"), Buf("wvg1")]
            wzq = [sb("wzq%d" % i, [128, KC * 384], BF16) for i in range(2)]; b_wzq = [Buf("wzq0"), Buf("wzq1")]
            VG = [dict(V=sb("Vh%d" % p_, [128, N], BF16), G=sb("Gh%d" % p_, [128, N], BF16), bV=Buf("V%d" % p_), bG=Buf("G%d" % p_)) for p_ in range(2)]
            Vh, Gh, b_V, b_G = VG[0]["V"], VG[0]["G"], VG[0]["bV"], VG[0]["bG"]
            QT = [sb("QT%d" % d, [128, N], BF16) for d in range(2)]; KT = [sb("KT%d" % d, [128, N], BF16) for d in range(2)]
            QS_T = [sb("QST%d" % d, [128, N], BF16) for d in range(2)]; b_QST = [Buf("QST0"), Buf("QST1")]
            KH = [sb("KH%d" % d, [128, N], BF16) for d in range(2)]; KHt = [sb("KHt%d" % d, [128, N], BF16) for d in range(2)]
            EB = [sb("EB%d" % d, [128, NCH], F32) for d in range(2)]
            EM = [sb("EM%d" % d, [128, NCH], F32) for d in range(2)]; b_EM = [Buf("EM0"), Buf("EM1")]
            b_QT = [Buf("QT0"), Buf("QT1")]; b_KT = [Buf("KT0"), Buf("KT1")]; b_KH = [Buf("KH0"), Buf("KH1")]
            b_KHt = [Buf("KHt0"), Buf("KHt1")]; b_EB = [Buf("EB0"), Buf("EB1")]
            Oacc = [sb("Oacc%d" % d, [128, N], F32) for d in range(2)]; b_O = [Buf("O0"), Buf("O1")]
            HGT = sb("HGT", [128, N], BF16); b_HGT = Buf("HGT")
            NTMP = 9
            tmp = [[sb("tm%d_%d" % (a, i), [128, 512], F32) for i in range(NTMP)] for a in range(2)]
            b_tmp = [[Buf("tm%d_%d" % (a, i)) for i in range(NTMP)] for a in range(2)]
            qs_ = [sb("qs%d" % a, [128, 512], F32) for a in range(2)]; b_qs = [Buf("qs0"), Buf("qs1")]
            Sst = [[sb("S%d_%d" % (d, i), [128, 128], F32) for i in range(2)] for d in range(2)]
            b_S = [[Buf("S%d_%d" % (d, i)) for i in range(2)] for d in range(2)]
            Sbf_f32 = [sb("S3_%d" % d, [128, 128], F32) for d in range(2)]; b_Sx = [Buf("S3_0"), Buf("S3_1")]
            AT = [sb("AT%d" % d, [128, 128], BF16) for d in range(2)]; b_AT = [Buf("AT0"), Buf("AT1")]
            fin = [sb("fin%d" % i, [128, 128], F32) for i in range(2)]; fsq = sb("fsq", [128, 128], F32)
            b_fin = [Buf("fin0"), Buf("fin1")]
            finW = [dict(fsq=(fsq if p_ == 0 else sb("fsq1", [128, 128], F32)), fst=sb("fst%d" % p_, [128, 8], F32), ngg=sb("ngg%d" % p_, [128, 128], F32), hgt=sb("hgt%d" % p_, [128, 128], BF16),
                         b_fsq=Buf("fsq%d" % p_), b_fst=Buf("fst%d" % p_), b_ngg=Buf("ngg%d" % p_), b_hgt=Buf("hgt%d" % p_)) for p_ in range(2)]

            if cut == "L":
                cut_dump(oml[:, 0:16].to_broadcast([128, 16]) if False else xmT[:, 0:512], b_xmT + [b_lb, b_cm, b_rmk, b_ngb]); return "cut"

            e_, one_e, sig, kk, lf, bb, cc_, E1, dd = range(9)

            def hgrn_layer(l, scan_tiles, out_tiles):
                wi = w_in[l].rearrange("(k p) n -> p k n", p=128)
                def load_vg_w(h_):
                    a_ = h_ % 2
                    for ci, c0 in enumerate((h_ * 128, 2048 + h_ * 128)):
                        S.dma("pool", wvg[a_][:].rearrange("p (k n) -> p k n", k=KC)[:, :, ci * 128:(ci + 1) * 128], wi[:, :, c0:c0 + 128], writes=[b_wvg[a_]])

                def load_zq_w(h_):
                    a_ = h_ % 2
                    for ci, c0 in enumerate((512 + h_ * 128, 1024 + h_ * 128, 1536 + h_ * 128)):
                        S.dma("pool", wzq[a_][:].rearrange("p (k n) -> p k n", k=KC)[:, :, ci * 128:(ci + 1) * 128], wi[:, :, c0:c0 + 128], writes=[b_wzq[a_]])

                def A_gen(h_):
                    a_ = h_ % 2; W_ = VG[a_]
                    for i in scan_tiles:
                        pbi = 3 + (i % 2)
                        yield
                        for kc in range(KC):
                            S.op("pe", lambda e, i=i, kc=kc, pbi=pbi: e.matmul(pbf(pbi, 256), xmT[:, kc * N + i * 128:kc * N + (i + 1) * 128],
                                 wvg[a_][:, kc * 256:(kc + 1) * 256], start=(kc == 0), stop=(kc == KC - 1)),
                                 reads=[b_xmT[i], b_wvg[a_]], writes=[b_pb[pbi]], inc=(kc == KC - 1))
                        yield
                        S.op("dve", lambda e, i=i, pbi=pbi: e.tensor_copy(W_["V"][:, i * 128:(i + 1) * 128], pbank[pbi][:, 0:128]), reads=[b_pb[pbi]], writes=[W_["bV"]])
                        yield
                        S.op("act", lambda e, i=i, pbi=pbi: e.activation(W_["G"][:, i * 128:(i + 1) * 128], pbank[pbi][:, 128:256], AF.Silu), reads=[b_pb[pbi]], writes=[W_["bG"]])

                nheads = 1 if cut else 4
                load_vg_w(0)
                if cut == "W":
                    cut_dump(wvg[0][:, 0:512], [b_wvg[0]]); return "cut"
                run_rr([A_gen(0)])
                for h in range(nheads):
                    a = h % 2
                    Vh, Gh, b_V, b_G = VG[a]["V"], VG[a]["G"], VG[a]["bV"], VG[a]["bG"]
                    if h == 0:
                        load_zq_w(0)
                    if h + 1 < nheads:
                        load_zq_w(h + 1)
                        load_vg_w(h + 1)
                    if cut == "A":
                        cut_dump(Vh[:, 0:512], [b_V, b_G]); return "cut"
                    for bi, (t0, nb) in enumerate(BLKS):
                        if t0 // 128 not in scan_tiles:
                            continue
                        tiles_in = [i for i in range(t0 // 128, (t0 + nb) // 128)]
                        for ci in range(3):
                            for kc in range(KC):
                                S.op("pe", lambda e, ci=ci, kc=kc, t0=t0, nb=nb, a=a: e.matmul(pbf(5 + ci, nb), wzq[a][:, kc * 384 + ci * 128:kc * 384 + (ci + 1) * 128],
                                     xmT[:, kc * N + t0:kc * N + t0 + nb], start=(kc == 0), stop=(kc == KC - 1)),
                                     reads=[b_wzq[a]] + [b_xmT[i] for i in tiles_in], writes=[b_pb[5 + ci]], inc=(kc == KC - 1))
                        qa = bi % 2
                        S.op("act", lambda e, nb=nb, qa=qa: e.activation(qs_[qa][:, 0:nb], pbf(7, nb), AF.Silu), reads=[b_pb[7]], writes=[b_qs[qa]])
                        nck = nb // 64; c0 = t0 // 64
                        def gate_chain(d, bi=bi, t0=t0, nb=nb, qa=qa, nck=nck, c0=c0):
                            ta = (bi * 2 + d) % 2
                            T = [t[:, 0:nb] for t in tmp[ta]]; bT = b_tmp[ta]
                            col = l * 8 + d * 4 + h
                            lbc, omc = lbv[:, col:col + 1], oml[:, col:col + 1]
                            yield
                            S.op("act", lambda e, T=T, d=d, nb=nb: e.activation(T[e_], pbf(5 + d, nb), AF.Exp, scale=-1.0), reads=[b_pb[5 + d]], writes=[bT[e_]])
                            yield
                            S.op("act", lambda e, T=T: e.activation(T[one_e], T[e_], AF.Ln, bias=1.0), reads=[bT[e_]], writes=[bT[one_e]])
                            yield
                            S.op("act", lambda e, T=T: e.activation(T[sig], T[one_e], AF.Exp, scale=-1.0), reads=[bT[one_e]], writes=[bT[sig]])
                            yield
                            S.op("dve", lambda e, T=T, omc=omc: e.scalar_tensor_tensor(T[kk], T[e_], omc, T[sig], ALU.mult, ALU.mult), reads=[bT[e_], bT[sig], b_lb], writes=[bT[kk]])
                            yield
                            S.op("act", lambda e, T=T, omc=omc, lbc=lbc: e.activation(T[lf], T[sig], AF.Ln, bias=lbc, scale=omc), reads=[bT[sig], b_lb], writes=[bT[lf]])
                            yield
                            S.op("dve", lambda e, T=T, nb=nb: e.tensor_tensor_scan(T[bb], rmk[:, 0:nb], T[lf], 0.0, ALU.mult, ALU.add), reads=[b_rmk, bT[lf]], writes=[bT[bb]])
                            b3 = T[bb].rearrange("p (c t) -> p c t", t=64)
                            btot = b3[:, :, 63:64]
                            if d == 0:
                                cview, bc_ = T[bb], bT[bb]
                            else:
                                yield
                                S.op("dve", lambda e, T=T: e.tensor_sub(T[cc_], T[lf], T[bb]), reads=[bT[lf], bT[bb]], writes=[bT[cc_]])
                                c3 = T[cc_].rearrange("p (c t) -> p c t", t=64)
                                yield
                                S.op("dve", lambda e, c3=c3, btot=btot, nck=nck: e.tensor_add(c3, c3, btot.to_broadcast([128, nck, 64])), reads=[bT[cc_], bT[bb]], writes=[bT[cc_]])
                                cview, bc_ = T[cc_], bT[cc_]
                            sl = slice(t0, t0 + nb)
                            MID = 31 if d == 0 else 32
                            cm3 = T[one_e].rearrange("p (c t) -> p c t", t=64); cv3m = cview.rearrange("p (c t) -> p c t", t=64)
                            yield
                            S.op("dve", lambda e, cm3=cm3, cv3m=cv3m, nck=nck, MID=MID: e.tensor_sub(cm3, cv3m, cv3m[:, :, MID:MID + 1].to_broadcast([128, nck, 64])),
                                 reads=[bc_, bT[E1]], writes=[bT[one_e]])
                            yield
                            S.op("act", lambda e, T=T: e.activation(T[E1], T[one_e], AF.Exp), reads=[bT[one_e]], writes=[bT[E1]])
                            yield
                            S.op("dve", lambda e, T=T, qa=qa, d=d, sl=sl, nb=nb: e.scalar_tensor_tensor(QS_T[d][:, sl], qs_[qa][:, 0:nb], QS, T[E1], ALU.mult, ALU.mult),
                                 reads=[b_qs[qa], bT[E1]], writes=[b_QST[d]])
                            yield
                            S.op("act", lambda e, T=T: e.activation(T[E1], T[one_e], AF.Exp, scale=-1.0), reads=[bT[one_e], b_QST[d]], writes=[bT[E1]])
                            yield
                            S.op("dve", lambda e, T=T, d=d, sl=sl: e.tensor_tensor(KT[d][:, sl], T[kk], T[E1], ALU.mult), reads=[bT[kk], bT[E1]], writes=[b_KT[d]])
                            d3 = T[dd].rearrange("p (c t) -> p c t", t=64); cv3 = cview.rearrange("p (c t) -> p c t", t=64)
                            yield
                            S.op("dve", lambda e, d3=d3, cv3=cv3, btot=btot, nck=nck: e.tensor_sub(d3, btot.to_broadcast([128, nck, 64]), cv3), reads=[bc_, bT[bb]], writes=[bT[dd]])
                            yield
                            S.op("act", lambda e, T=T: e.activation(T[dd], T[dd], AF.Exp), reads=[bT[dd]], writes=[bT[dd]])
                            yield
                            S.op("dve", lambda e, T=T, d=d, sl=sl: e.tensor_tensor(KH[d][:, sl], T[kk], T[dd], ALU.mult), reads=[bT[kk], bT[dd]], writes=[b_KH[d]])
                            yield
                            S.op("act", lambda e, d=d, c0=c0, nck=nck, btot=btot: e.activation(EB[d][:, c0:c0 + nck], btot.rearrange("p c o -> p (c o)"), AF.Exp), reads=[bT[bb]], writes=[b_EB[d]])
                            yield
                            S.op("act", lambda e, d=d, c0=c0, nck=nck, cv3m=cv3m, MID=MID: e.activation(EM[d][:, c0:c0 + nck], cv3m[:, :, MID:MID + 1].rearrange("p c o -> p (c o)"), AF.Exp), reads=[bc_], writes=[b_EM[d]])
                        run_rr([gate_chain(0), gate_chain(1)])
                    if cut == "B":
                        return
                    for d in range(2):
                        for i in scan_tiles:
                            pbi = 1 + (i % 2)
                            S.op("pe", lambda e, d=d, i=i, pbi=pbi: e.transpose(pbb(pbi, 128), KH[d][:, i * 128:(i + 1) * 128], id_bf[:]), reads=[b_KH[d], b_id_bf], writes=[b_pb[pbi]])
                            S.op("act", lambda e, d=d, i=i, pbi=pbi: e.copy(KHt[d][:, i * 128:(i + 1) * 128], pbb(pbi, 128)), reads=[b_pb[pbi]], writes=[b_KHt[d]])
                    if cut == "C":
                        return
                    order = [list(scan_tiles), [t for t in (1, 0) if t in scan_tiles] + [t for t in range(NT - 1, 1, -1) if t in scan_tiles]]
                    TB = [tmp[a_][j_] for a_ in range(2) for j_ in range(NTMP)]; bTB = [b_tmp[a_][j_] for a_ in range(2) for j_ in range(NTMP)]
                    import os as _os2
                    d_stage = int(_os2.environ.get("HG_D_STAGE", "0")) if cut else 0
                    seqs = []
                    for d in range(1 if d_stage else 2):
                        seq = [(i, cpos) for i in order[d] for cpos in ((0, 1) if d == 0 else (1, 0))]
                        seqs.append(seq)
                        nstep = len(order[d])
                        for cpos in (0, 1):
                            base = cpos * nstep
                            s_ = base
                            while s_ < base + nstep:
                                g_end = min(base + nstep, (s_ // 4 + 1) * 4)
                                pbk = 3 + 2 * cpos + ((s_ // 4) % 2)
                                for sl_ in range(s_, g_end):
                                    i = order[d][sl_ - base]; q = sl_ % 4
                                    ps_ = slice(cpos * 64, cpos * 64 + 64); ts_ = slice(i * 128, (i + 1) * 128)
                                    S.op("pe", lambda e, d=d, ts_=ts_, ps_=ps_, pbk=pbk, q=q, Vh=Vh: e.matmul(pbank[pbk][:, q * 128:(q + 1) * 128], KHt[d][ps_, ts_], Vh[ps_, ts_], start=True, stop=True),
                                         reads=[b_KHt[d], b_V], writes=[b_pb[pbk]], inc=(sl_ == g_end - 1))
                                tb = 9 * d + s_ // 4; c_lo, c_hi = (s_ % 4) * 128, ((g_end - 1) % 4 + 1) * 128
                                S.op("act", lambda e, tb=tb, pbk=pbk, c_lo=c_lo, c_hi=c_hi, TB=TB: e.copy(TB[tb][:, c_lo:c_hi], pbank[pbk][:, c_lo:c_hi]), reads=[b_pb[pbk]], writes=[bTB[tb]])
                                s_ = g_end
                    if d_stage == 1:
                        cut_dump(TB[0][:, 0:512], bTB[0:9]); return "cut"
                    RING = 18
                    b_slot = [[Buf("st%d_%d" % (d_, r_)) for r_ in range(RING)] for d_ in range(2)]
                    sslot = lambda d_, k: (KH[d_][:, (k % RING) * 128:(k % RING + 1) * 128], b_slot[d_][k % RING])
                    S3 = [Sst[d_] + [Sbf_f32[d_]] for d_ in range(2)]; b_S3 = [b_S[d_] + [b_Sx[d_]] for d_ in range(2)]

                    produced = [0, 0]
                    consumed = [0, 0]

                    def chain_gen(d):
                        seq = seqs[d]; nstep = len(order[d])
                        yield
                        S.op("dve", lambda e, d=d, S3=S3: e.memset(S3[d][0][:], 0.0), writes=[b_S3[d][0]])
                        for k, (i, cpos) in enumerate(seq[:-1]):
                            sl_ = cpos * nstep + k // 2
                            ch = i * 2 + cpos; tb = 9 * d + sl_ // 4; q = sl_ % 4
                            si, so = k % 3, (k + 1) % 3
                            yield
                            S.op("dve", lambda e, d=d, si=si, so=so, ch=ch, tb=tb, q=q, TB=TB, S3=S3: e.scalar_tensor_tensor(S3[d][so][:], S3[d][si][:], EB[d][:, ch:ch + 1], TB[tb][:, q * 128:(q + 1) * 128], ALU.mult, ALU.add),
                                 reads=[b_S3[d][si], b_EB[d], bTB[tb]], writes=[b_S3[d][so]])
                            dst, bdst = sslot(d, k + 1)
                            yield
                            while (k + 1) - consumed[d] >= RING:
                                yield
                            i2, cp2 = seq[k + 1]; ch2 = i2 * 2 + cp2
                            S.op("act", lambda e, d=d, so=so, dst=dst, S3=S3, ch2=ch2: e.activation(dst, S3[d][so][:], AF.Identity, scale=EM[d][:, ch2:ch2 + 1]), reads=[b_S3[d][so], b_EM[d]], writes=[bdst])
                            produced[d] = k + 1

                    def out_gen(d):
                        yield
                        S.op("dve", lambda e, d=d: e.memset(AT[d][:], 0.0), writes=[b_AT[d]])
                        for step, i in enumerate(order[d]):
                            if i not in out_tiles:
                                consumed[d] = 2 * (step + 1)
                                continue
                            ts_ = slice(i * 128, (i + 1) * 128); par = step % 2
                            p_sc, p_o = (5, 6)[d], ((7, 0)[d])
                            yield
                            S.op("pe", lambda e, d=d, ts_=ts_, p_sc=p_sc: e.matmul(pbf(p_sc, 128), KT[d][:, ts_], QS_T[d][:, ts_], start=True, stop=True),
                                 reads=[b_KT[d], b_QST[d]], writes=[b_pb[p_sc]])
                            yield
                            S.op("dve", lambda e, d=d, p_sc=p_sc: e.copy_predicated(AT[d][:], cm_u[:, d * 128:(d + 1) * 128], pbf(p_sc, 128)),
                                 reads=[b_pb[p_sc], b_cmu, b_AT[d]], writes=[b_AT[d]])
                            cps = (0, 1) if d == 0 else (1, 0)
                            need = [(cp, k) for cp, k in zip(cps, (2 * step, 2 * step + 1)) if k > 0]
                            yield
                            while need and produced[d] < max(k_ for _, k_ in need):
                                yield
                            S.op("pe", lambda e, d=d, ts_=ts_, p_o=p_o, nn=len(need), Vh=Vh: e.matmul(pbf(p_o, 128), AT[d][:], Vh[:, ts_], start=True, stop=(nn == 0)),
                                 reads=[b_AT[d], b_V], writes=[b_pb[p_o]], inc=(len(need) == 0))
                            for j, (cp, k) in enumerate(need):
                                ps_ = slice(cp * 64, cp * 64 + 64); tsc = slice(i * 128 + cp * 64, i * 128 + cp * 64 + 64)
                                src, bsrc = sslot(d, k)
                                lastj = j == len(need) - 1
                                if not lastj:
                                    pass
                                S.op("pe", lambda e, d=d, tsc=tsc, ps_=ps_, p_o=p_o, src=src, lastj=lastj: e.matmul(pbank[p_o][ps_, 0:128], QS_T[d][:, tsc], src, start=False, stop=lastj),
                                     reads=[b_QST[d], bsrc], writes=[b_pb[p_o]], inc=lastj)
                            consumed[d] = 2 * (step + 1)
                            yield
                            S.op("act", lambda e, d=d, ts_=ts_, p_o=p_o: e.copy(Oacc[d][:, ts_], pbf(p_o, 128)), reads=[b_pb[p_o]], writes=[b_O[d]])

                    if h == nheads - 1 and not cut and debug != "hg":
                        sc_load(l, 0); sc_load(l, 1); sg_load(l)
                    if d_stage:
                        run_rr([chain_gen(0), out_gen(0)])
                    else:
                        run_rr([chain_gen(0), chain_gen(1), out_gen(0), out_gen(1)] + ([A_gen(h + 1)] if h + 1 < nheads else []))
                    if d_stage == 3:
                        cut_dump(Oacc[0][:, 0:512], [b_O[0]]); return "cut"
                    if cut == "D":
                        return
                    def fin_chain(i, fa):
                        ts_ = slice(i * 128, (i + 1) * 128); pbi = 1 + fa
                        W = finW[fa]
                        yield
                        S.op("dve", lambda e: e.tensor_add(fin[fa][:], Oacc[0][:, ts_], Oacc[1][:, ts_]), reads=[b_O[0], b_O[1]], writes=[b_fin[fa]])
                        yield
                        S.op("act", lambda e: e.activation(W["fsq"][:], fin[fa][:], AF.Square, accum_out=W["fst"][:, 0:1]), reads=[b_fin[fa]], writes=[W["b_fsq"], W["b_fst"]])
                        yield
                        S.op("act", lambda e: e.activation(W["fst"][:, 1:2], W["fst"][:, 0:1], AF.Sqrt, bias=EPS, scale=1.0 / 128), reads=[W["b_fst"]], writes=[W["b_fst"]])
                        yield
                        S.op("dve", lambda e: e.reciprocal(W["fst"][:, 2:3], W["fst"][:, 1:2]), reads=[W["b_fst"]], writes=[W["b_fst"]])
                        yield
                        S.op("dve", lambda e, Gcur=Gcur: e.tensor_tensor(W["ngg"][:], ngb[:, l * 128:(l + 1) * 128], Gcur[:, ts_], ALU.mult), reads=[b_ngb, b_Gcur], writes=[W["b_ngg"]])
                        yield
                        S.op("dve", lambda e: e.scalar_tensor_tensor(W["hgt"][:], fin[fa][:], W["fst"][:, 2:3], W["ngg"][:], ALU.mult, ALU.mult), reads=[b_fin[fa], W["b_fst"], W["b_ngg"]], writes=[W["b_hgt"]])
                        yield
                        S.op("pe", lambda e: e.transpose(pbb(pbi, 128), W["hgt"][:], id_bf[:]), reads=[W["b_hgt"], b_id_bf], writes=[b_pb[pbi]])
                        yield
                        S.op("act", lambda e: e.copy(HGT[:, ts_], pbb(pbi, 128)), reads=[b_pb[pbi]], writes=[b_HGT])

                    Gcur, b_Gcur = Gh, b_G
                    ot_ = list(out_tiles)

                    def fin_stream(k, ot_=ot_):
                        for i in ot_[k::2]:
                            yield from fin_chain(i, k)

                    run_rr([fin_stream(0), fin_stream(1)])
                    S.dma("sp", mixT[h * 128:(h + 1) * 128, :], HGT[:], reads=[b_HGT], writes=[b_mixT[h]])

            if cut:
                S.barrier()
                S.op("act", lambda e, Vh=Vh: e.copy(Oacc[1][:], Vh[:]), reads=[b_V], writes=[b_O[1]])
                S.dma("sp", outs["dbg_cut"][:, 0:N], Oacc[1][:], reads=[b_O[1]])
                if cut in "DE":
                    S.dma("sp", outs["dbg_cut"][:, N:2 * N], Oacc[0][:], reads=[b_O[0]])

            scw_s = sb("scw_s", [128, DEPTH * 6], F32); b_scw = Buf("scw")
            S.dma("sp", scw_s[:], scw, writes=[b_scw])
            lngb = sb("lngb", [128, 512], F32); b_lngb = Buf("lngb")
            WsT = sb("WsT", [128, 4 * 128], BF16); b_WsT = Buf("WsT")
            wsn = sb("wsn", [128, 128], BF16); b_wsn = Buf("wsn")
            BS = sb("BS", [128, 2 * 128], F32); b_BS = Buf("BS")
            SEQS = [(0, 256), (256, N)]
            SEGS = [(0, 256), (256, 512), (512, 1024), (1024, 1536), (1536, 2048), (2048, N)]

            pre_done = {}

            def sc_load(l, cc):
                wi_ = w_in[l].rearrange("(k p) n -> p k n", p=128); a_ = cc % 2
                for ci, c0 in enumerate((2560 + cc * 128, 2816 + cc * 128, 3072 + cc * 128)):
                    S.dma("pool", wzq[a_][:].rearrange("p (k n) -> p k n", k=KC)[:, :, ci * 128:(ci + 1) * 128], wi_[:, :, c0:c0 + 128], writes=[b_wzq[a_]])
                pre_done[("sc", l, cc)] = True

            def sg_load(l):
                wi_ = w_in[l].rearrange("(k p) n -> p k n", p=128)
                S.dma("pool", wvg[0][:].rearrange("p (k n) -> p k n", k=KC), wi_[:, :, 3328:3584], writes=[b_wvg[0]])
                S.dma("pool", wvg[1][:].rearrange("p (k n) -> p k n", k=KC), wi_[:, :, 3584:3840], writes=[b_wvg[1]])
                S.dma("sp", lngb[:, 0:256], sgln[2 * l:2 * l + 1, :].partition_broadcast(128), writes=[b_lngb])
                S.dma("sp", lngb[:, 256:512], sgln[2 * l + 1:2 * l + 2, :].partition_broadcast(128), writes=[b_lngb])
                for cc in range(2):
                    S.dma("sp", BS[:, cc * 128:(cc + 1) * 128], sgb[2 * l + cc], writes=[b_BS])
                pre_done[("sg", l)] = True

            def sc_layer(l, tiles):
                wi = w_in[l].rearrange("(k p) n -> p k n", p=128)
                tmax = (max(tiles) + 1) * 128; tmin = min(tiles) * 128
                Pf, GBf, b_P, b_GB = Oacc[0], Oacc[1], b_O[0], b_O[1]
                for cc in range(2):
                    a = cc % 2
                    if not pre_done.get(("sc", l, cc)):
                        sc_load(l, cc)
                    sc_blks = [(t0, min(512, tmax - t0)) for t0 in range(tmin, tmax, 512)]
                    for (t0, nb) in sc_blks:
                        tiles_in = list(range(t0 // 128, (t0 + nb) // 128))
                        for ci in range(3):
                            for kc in range(KC):
                                S.op("pe", lambda e, ci=ci, kc=kc, t0=t0, nb=nb, a=a: e.matmul(pbf(5 + ci, nb), wzq[a][:, kc * 384 + ci * 128:kc * 384 + (ci + 1) * 128],
                                     xmT[:, kc * N + t0:kc * N + t0 + nb], start=(kc == 0), stop=(kc == KC - 1)),
                                     reads=[b_wzq[a]] + [b_xmT[i] for i in tiles_in], writes=[b_pb[5 + ci]], inc=(kc == KC - 1))
                        T0 = tmp[0][0][:, 0:nb]
                        S.op("act", lambda e, t0=t0, nb=nb: e.copy(GBf[:, t0:t0 + nb], pbf(5, nb)), reads=[b_pb[5]], writes=[b_GB])
                        S.op("act", lambda e, T0=T0, nb=nb: e.copy(T0, pbf(6, nb)), reads=[b_pb[6]], writes=[b_tmp[0][0]])
                        S.op("dve", lambda e, T0=T0, t0=t0, nb=nb: e.tensor_tensor(Pf[:, t0:t0 + nb], T0, pbf(7, nb), ALU.mult), reads=[b_tmp[0][0], b_pb[7]], writes=[b_P])
                    wb = l * 6 + cc * 3
                    w0_, w1_, w2_ = scw_s[:, wb:wb + 1], scw_s[:, wb + 1:wb + 2], scw_s[:, wb + 2:wb + 3]
                    for (a0, a1) in SEGS:
                        if a0 < tmin or a0 >= tmax:
                            continue
                        s0, s1 = [sq for sq in SEQS if sq[0] <= a0 < sq[1]][0]
                        Y = tmp[1][0]; bY = b_tmp[1][0]; n_ = a1 - a0
                        S.op("dve", lambda e, Y=Y, a0=a0, a1=a1, n_=n_, w1_=w1_: e.tensor_scalar(Y[:, 0:n_], Pf[:, a0:a1], w1_, None, ALU.mult), reads=[b_P, b_scw], writes=[bY])
                        lo = max(a0, s0 + 1)
                        S.op("dve", lambda e, Y=Y, a0=a0, a1=a1, lo=lo, w0_=w0_: e.scalar_tensor_tensor(Y[:, lo - a0:a1 - a0], Pf[:, lo - 1:a1 - 1], w0_, Y[:, lo - a0:a1 - a0], ALU.mult, ALU.add),
                             reads=[b_P, b_scw, bY], writes=[bY])
                        hi = min(a1, s1 - 1)
                        S.op("dve", lambda e, Y=Y, a0=a0, hi=hi, w2_=w2_: e.scalar_tensor_tensor(Y[:, 0:hi - a0], Pf[:, a0 + 1:hi + 1], w2_, Y[:, 0:hi - a0], ALU.mult, ALU.add),
                             reads=[b_P, b_scw, bY], writes=[bY])
                        S.op("dve", lambda e, Y=Y, a0=a0, a1=a1, n_=n_: e.tensor_tensor(HGT[:, a0:a1], GBf[:, a0:a1], Y[:, 0:n_], ALU.mult), reads=[b_GB, bY], writes=[b_HGT])
                    S.dma("sp", mixT[(4 + cc) * 128:(5 + cc) * 128, tmin:tmax], HGT[:, tmin:tmax], reads=[b_HGT], writes=[b_mixT[4 + cc]])

            def sg_layer(l, tiles):
                wi = w_in[l].rearrange("(k p) n -> p k n", p=128)
                tmax = (max(tiles) + 1) * 128; tmin = min(tiles) * 128
                if not pre_done.get(("sg", l)):
                    sg_load(l)
                for g in range(4):
                    S.dma("pool", wsn[:], sgw[4 * l + g], writes=[b_wsn])
                    S.op("pe", lambda e: e.transpose(pbb(1, 128), wsn[:], id_bf[:]), reads=[b_wsn, b_id_bf], writes=[b_pb[1]])
                    S.op("act", lambda e, g=g: e.copy(WsT[:, g * 128:(g + 1) * 128], pbb(1, 128)), reads=[b_pb[1]], writes=[b_WsT])
                SGT, b_SGT = KH, b_KH
                sgW = [dict(vn=sb("sgvn%d" % p_, [128, 256], F32), vhb=sb("sgvh%d" % p_, [128, 256], BF16), st=sb("sgst%d" % p_, [128, 40], F32),
                            b_vn=Buf("sgvn%d" % p_), b_vhb=Buf("sgvh%d" % p_), b_st=Buf("sgst%d" % p_), banks=((3, 4, 5), (6, 7, 0))[p_], par=p_) for p_ in range(2)]

                def sg_chain(i, W):
                    ts_ = slice(i * 128, (i + 1) * 128); pv, pu, pm = W["banks"]; par = W["par"]
                    vn_t, vhb_t, st_t = W["vn"], W["vhb"], W["st"]
                    yield
                    for kc in range(KC):
                        S.op("pe", lambda e, kc=kc: e.matmul(pbf(pv, 256), xmT[:, kc * N + i * 128:kc * N + (i + 1) * 128], wvg[1][:, kc * 256:(kc + 1) * 256],
                             start=(kc == 0), stop=(kc == KC - 1)), reads=[b_xmT[i], b_wvg[1]], writes=[b_pb[pv]], inc=(kc == KC - 1))
                    for g in range(4):
                        yield
                        S.op("dve", lambda e, g=g: e.bn_stats(st_t[:, g * 6:(g + 1) * 6], pbank[pv][:, g * 64:(g + 1) * 64]), reads=[b_pb[pv]], writes=[W["b_st"]])
                    for g in range(4):
                        yield
                        S.op("dve", lambda e, g=g: e.bn_aggr(st_t[:, 24 + 2 * g:26 + 2 * g], st_t[:, g * 6:(g + 1) * 6]), reads=[W["b_st"]], writes=[W["b_st"]])
                    mv = st_t[:, 24:32].rearrange("p (g two) -> p g two", two=2)
                    yield
                    S.op("act", lambda e: e.activation(st_t[:, 32:36], mv[:, :, 1], AF.Sqrt, bias=EPS), reads=[W["b_st"]], writes=[W["b_st"]])
                    yield
                    S.op("dve", lambda e: e.reciprocal(st_t[:, 36:40], st_t[:, 32:36]), reads=[W["b_st"]], writes=[W["b_st"]])
                    for g in range(4):
                        yield
                        S.op("dve", lambda e, g=g: e.tensor_scalar(vn_t[:, g * 64:(g + 1) * 64], pbank[pv][:, g * 64:(g + 1) * 64], st_t[:, 24 + 2 * g:25 + 2 * g], st_t[:, 36 + g:37 + g],
                             ALU.subtract, ALU.mult), reads=[b_pb[pv], W["b_st"]], writes=[W["b_vn"]])
                    yield
                    S.op("dve", lambda e: e.tensor_mul(vn_t[:], vn_t[:], lngb[:, 0:256]), reads=[W["b_vn"], b_lngb], writes=[W["b_vn"]])
                    yield
                    S.op("dve", lambda e: e.tensor_add(vhb_t[:], vn_t[:], lngb[:, 256:512]), reads=[W["b_vn"], b_lngb], writes=[W["b_vhb"]])
                    yield
                    for cc in range(2):
                        for kc in range(KC):
                            S.op("pe", lambda e, cc=cc, kc=kc: e.matmul(pbank[pu][:, cc * 128:(cc + 1) * 128], wvg[0][:, kc * 256 + cc * 128:kc * 256 + (cc + 1) * 128], xmT[:, kc * N + i * 128:kc * N + (i + 1) * 128],
                                 start=(kc == 0), stop=(kc == KC - 1)), reads=[b_wvg[0], b_xmT[i]], writes=[b_pb[pu]], inc=(kc == KC - 1 and cc == 1))
                    yield
                    for cc in range(2):
                        for gg in range(2):
                            g = 2 * cc + gg
                            S.op("pe", lambda e, g=g, gg=gg, cc=cc: e.matmul(pbank[pm][gg * 64:(gg + 1) * 64, cc * 128:(cc + 1) * 128], vhb_t[:, g * 64:(g + 1) * 64], WsT[:, g * 128:(g + 1) * 128], start=True, stop=True),
                                 reads=[W["b_vhb"], b_WsT], writes=[b_pb[pm]], inc=(gg == 1 and cc == 1))
                    for cc in range(2):
                        T1, T2 = tmp[cc][1 + 2 * par][:, 0:128], tmp[cc][2 + 2 * par][:, 0:128]
                        bT1, bT2 = b_tmp[cc][1 + 2 * par], b_tmp[cc][2 + 2 * par]
                        yield
                        S.op("dve", lambda e, T1=T1, cc=cc: e.tensor_tensor(T1, pbank[pm][:, cc * 128:(cc + 1) * 128], BS[:, cc * 128:(cc + 1) * 128], ALU.add), reads=[b_pb[pm], b_BS], writes=[bT1])
                        yield
                        S.op("act", lambda e, T2=T2, cc=cc: e.copy(T2, pbank[pu][:, cc * 128:(cc + 1) * 128]), reads=[b_pb[pu]], writes=[bT2])
                        yield
                        S.op("dve", lambda e, T1=T1, T2=T2, cc=cc: e.tensor_tensor(SGT[cc][:, ts_], T1, T2, ALU.mult), reads=[bT1, bT2], writes=[b_SGT[cc]])

                tl_ = list(tiles)

                def sg_stream(k):
                    for i in tl_[k::2]:
                        yield from sg_chain(i, sgW[k])

                run_rr([sg_stream(0), sg_stream(1)])
                for cc in range(2):
                    S.dma("sp", mixT[(6 + cc) * 128:(7 + cc) * 128, tmin:tmax], SGT[cc][:, tmin:tmax], reads=[b_SGT[cc]], writes=[b_mixT[6 + cc]])

            r = hgrn_layer(l, scan_tiles, out_tiles)
            if r == "cut":
                st.enter_context(ms)
                return "cut"
            if debug != "hg":
                sc_layer(l, out_tiles)
                sg_layer(l, out_tiles)
            if debug in ("hg", "mix"):
                mixer.dbg = (Oacc[0], b_O[0], HGT, b_HGT)
                st.enter_context(ms)
                return None
            S.barrier(); ms.close()
            return None

        if mixer(0, list(range(NT)), list(range(NT))) == "cut":
            return nc

        ALPHA = (2 * DEPTH) ** 0.25
        x1d = nc.dram_tensor("x1d", [N, D], F32, kind="Internal").ap()
        b_x1d = [Buf("x1d%d" % i) for i in range(NT)]

        def gate_bcast(dst, b_dst, l, slot, srcm, scr, b_scr):
            for kc in range(KC):
                g = modv(l, slot, kc, srcm)
                S.op("dve", lambda e, g=g: e.tensor_scalar(scr[:], id_f[:], 0.0, g, ALU.mult, ALU.add), reads=[b_id_f, b_mod], writes=[b_scr])
                S.op("pe", lambda e: e.matmul(pbf(0, 128), scr[:], id_f[:], start=True, stop=True), reads=[b_scr, b_id_f], writes=[b_pb[0]])
                S.op("act", lambda e, kc=kc: e.copy(dst[:, kc * 128:(kc + 1) * 128], pbf(0, 128)), reads=[b_pb[0]], writes=[b_dst])

        def phase_wout_ln1(l, src_ap, tiles, src_bufs=None):
            ps4 = ExitStack()
            sb4 = lambda n, s_, dt: ps4.enter_context(nc.sbuf_tensor("%s_p4L%d" % (n, l), s_, dt))
            mixS = sb4("mixS", [128, KC * N], BF16); b_mixS = [Buf("mixS%d" % k) for k in range(KC)]
            wo = sb4("wo", [128, KC * D], BF16); b_wo = Buf("wo")
            gbc = [sb4("gbc%d" % i, [128, D], F32) for i in range(2)]; b_gbc = [Buf("gbc0"), Buf("gbc1")]
            lg = sb4("lg", [128, D], F32); lb_ = sb4("lb_", [128, D], F32); b_lg, b_lbb = Buf("lg"), Buf("lbb")
            scr = sb4("scr", [128, 128], F32); b_scr = Buf("scr")
            for k in range(KC):
                S.dma("sp", mixS[:, k * N:(k + 1) * N], mixT[k * 128:(k + 1) * 128, :], reads=[b_mixT[k]], writes=[b_mixS[k]])
            S.dma("pool", wo[:].rearrange("p (k n) -> p k n", k=KC), w_out[l].rearrange("(k p) n -> p k n", p=128), writes=[b_wo])
            S.dma("sp", lg[:], lnp[4 * l:4 * l + 1, :].partition_broadcast(128), writes=[b_lg])
            S.dma("sp", lb_[:], lnp[4 * l + 1:4 * l + 2, :].partition_broadcast(128), writes=[b_lbb])
            gate_bcast(gbc[0], b_gbc[0], l, 2, 0, scr, b_scr)
            if any(i < 2 for i in tiles):
                gate_bcast(gbc[1], b_gbc[1], l, 2, 1, scr, b_scr)
            G4 = 3
            sets = mk_sets(sb4, 2 * G4, "b", with_y=True, plan=[(3, 3, 1), (4, 4, 2), (5, 5, 6)])
            load = lambda i, B_: S.dma("sp", B_["xq"][:], src_ap[i * 128:(i + 1) * 128, :], reads=([src_bufs[i]] if src_bufs else []), writes=[B_["b_xq"]])

            def chain4(i, B_):
                srcm = 1 if i < 2 else 0
                xq_t, yt_t, xt_t, st_t = B_["xq"], B_["yt"], B_["xt"], B_["st"]
                for hf in range(2):
                    pbi = B_["mm"][hf]; hs = slice(hf * 512, (hf + 1) * 512)
                    yield
                    for kc in range(KC):
                        S.op("pe", lambda e, kc=kc, hf=hf, pbi=pbi: e.matmul(pbf(pbi, 512), mixS[:, kc * N + i * 128:kc * N + (i + 1) * 128],
                             wo[:, kc * D + hf * 512:kc * D + (hf + 1) * 512], start=(kc == 0), stop=(kc == KC - 1)),
                             reads=[b_mixS[kc], b_wo], writes=[b_pb[pbi]], inc=(kc == KC - 1))
                    yield
                    S.op("dve", lambda e, hs=hs, pbi=pbi: e.tensor_tensor(yt_t[:, hs], pbf(pbi, 512), gbc[srcm][:, hs], ALU.mult),
                         reads=[b_pb[pbi], b_gbc[srcm]], writes=[B_["b_yt"]])
                    yield
                    S.op("dve", lambda e, hs=hs: e.scalar_tensor_tensor(yt_t[:, hs], xq_t[:, hs], ALPHA, yt_t[:, hs], ALU.mult, ALU.add),
                         reads=[B_["b_xq"], B_["b_yt"]], writes=[B_["b_yt"]])
                yield from ln_stats_gen(yt_t, B_["b_yt"], B_)
                yield
                S.op("dve", lambda e: e.scalar_tensor_tensor(yt_t[:], yt_t[:], st_t[:, 12:13], lg[:], ALU.subtract, ALU.mult),
                     reads=[B_["b_yt"], B_["b_st"], b_lg], writes=[B_["b_yt"]])
                yield
                S.op("dve", lambda e: e.scalar_tensor_tensor(xt_t[:], yt_t[:], st_t[:, 15:16], lb_[:], ALU.mult, ALU.add),
                     reads=[B_["b_yt"], B_["b_st"], b_lbb, B_["b_xt"]], writes=[B_["b_xt"]])
                yield
                S.dma("sp", x1d[i * 128:(i + 1) * 128, :], xt_t[:], reads=[B_["b_xt"]], writes=[b_x1d[i]])
                yield from ln_mod_T_gen(l, i, 3, 4, B_)

            run_groups(tiles, sets, load, chain4, GRP=G4)
            S.barrier(); ps4.close()

        if debug in ("x1", "h", "x2", None):
            phase_wout_ln1(0, xin, list(range(NT)))
        if debug == "x1":
            for i in range(NT):
                a = i % 2
                S.dma("sp", xt[a][:], x1d[i * 128:(i + 1) * 128, :], reads=[b_x1d[i]], writes=[b_xt[a]])
                S.dma("sp", outs["dbg_x1"][i * 128:(i + 1) * 128, :], xt[a][:], reads=[b_xt[a]])
            xm_f = sb("xm2_f", [128, N], F32); b_xmf = Buf("xm2f")
            for kc in range(KC):
                S.op("act", lambda e, kc=kc: e.copy(xm_f[:], xmT[:, kc * N:(kc + 1) * N]), reads=b_xmT, writes=[b_xmf])
                S.dma("sp", outs["dbg_xm2T"][:, kc * N:(kc + 1) * N], xm_f[:], reads=[b_xmf])

        hTd = nc.dram_tensor("hTd", [FF, N], BF16, kind="Internal").ap()
        b_hTd = [Buf("hTd%d" % i) for i in range(NFC)]
        x2d = [nc.dram_tensor("x2d%d" % i, [N, D], F32, kind="Internal").ap() for i in range(DEPTH - 1)]
        b_x2d = [Buf("x2d%d" % i) for i in range(NT)]
        GW = 64

        def phase_ffn_up(l, with_ctx):
            p5 = ExitStack()
            sb5 = lambda n, s_, dt: p5.enter_context(nc.sbuf_tensor("%s_p5L%d" % (n, l), s_, dt))
            fcw_s = sb5("fcw_s", [128, NFC * 9], F32); fcb_s = sb5("fcb_s", [128, NFC], F32); b_fcw, b_fcb = Buf("fcw"), Buf("fcb")
            S.dma("sp", fcw_s[:], fcw[:, l * NFC * 9:(l + 1) * NFC * 9], writes=[b_fcw])
            S.dma("sp", fcb_s[:], fcb[:, l * NFC:(l + 1) * NFC], writes=[b_fcb])
            wag = [sb5("wag%d" % i, [128, KC * 256], BF16) for i in range(2)]; b_wag = [Buf("wag0"), Buf("wag1")]
            apx = [sb5("apx%d" % i, [128, 34 * 66], BF16) for i in range(2)]; apc = [sb5("apc%d" % i, [128, 258], BF16) for i in range(2)]
            b_ap = [Buf("ap0"), Buf("ap1")]
            dg = [sb5("dg%d" % i, [128, 9 * 128], BF16) for i in range(2)]; b_dg = [Buf("dg0"), Buf("dg1")]
            gel = [sb5("gel%d" % i, [128, 512], F32) for i in range(2)]; b_gel = [Buf("gel0"), Buf("gel1")]
            htc = [sb5("htc%d" % i, [128, N], BF16) for i in range(2)]; b_htc = [Buf("htc0"), Buf("htc1")]
            for i in range(2):
                S.op("pool", lambda e, i=i: e.memset(apx[i][:], 0.0), writes=[b_ap[i]])
                S.op("pool", lambda e, i=i: e.memset(apc[i][:], 0.0), writes=[b_ap[i]])
            FB = ([("c", 0, 256, 0)] if with_ctx else []) + [("x", 256 + 512 * j, 512, j) for j in range(4)]
            wu = ffn_up[l].rearrange("(k p) n -> p k n", p=128)
            defer_l = l + 1 if (l + 1 < DEPTH) else None
            if defer_l is not None:
                awd = [sb5("awd%d" % i, [128, KC * 512], BF16) for i in range(2)]; b_awd = [Buf("awd0"), Buf("awd1")]
                pm_d, b_pmd = pbf(1, 96), b_pb[1]
            for fc in range(NFC):
                a = fc % 2
                if defer_l is not None:
                    if fc < 12:
                        mod_chunk_load(defer_l, fc, awd[fc % 2], b_awd[fc % 2])
                    if 1 <= fc <= 12:
                        mod_chunk_mm(defer_l, fc - 1, awd[(fc - 1) % 2], b_awd[(fc - 1) % 2], pm_d, b_pmd)
                    if fc == 13:
                        mod_finish(defer_l, pm_d, b_pmd)
                for ci, c0 in enumerate((fc * 128, FF + fc * 128)):
                    S.dma("pool", wag[a][:].rearrange("p (k n) -> p k n", k=KC)[:, :, ci * 128:(ci + 1) * 128], wu[:, :, c0:c0 + 128], writes=[b_wag[a]])
                for tap in range(9):
                    wcol = fcw_s[:, fc * 9 + tap:fc * 9 + tap + 1]
                    S.op("dve", lambda e, a=a, tap=tap, wcol=wcol: e.tensor_scalar(dg[a][:, tap * 128:(tap + 1) * 128], id_f[:], wcol, None, ALU.mult),
                         reads=[b_id_f, b_fcw], writes=[b_dg[a]])
                apx3 = apx[a][:].rearrange("p (r c) -> p r c", c=66)
                for bn, (kind, t0, nb, j) in enumerate(FB):
                    pa = (3, 6)[bn % 2]
                    tl = list(range(t0 // 128, (t0 + nb) // 128))
                    for kc in range(KC):
                        S.op("pe", lambda e, a=a, kc=kc, t0=t0, nb=nb, pa=pa: e.matmul(pbf(pa, nb), wag[a][:, kc * 256:kc * 256 + 128], xmT[:, kc * N + t0:kc * N + t0 + nb],
                             start=(kc == 0), stop=(kc == KC - 1)), reads=[b_wag[a]] + [b_xmT[i] for i in tl], writes=[b_pb[pa]], inc=(kc == KC - 1))
                    if kind == "c":
                        S.op("act", lambda e, a=a, pa=pa: e.copy(apc[a][:, 1:257], pbf(pa, 256)), reads=[b_pb[pa]], writes=[b_ap[a]])
                    else:
                        S.op("act", lambda e, a=a, pa=pa, j=j, apx3=apx3: e.copy(apx3[:, 1 + 8 * j:9 + 8 * j, 1:65], pbf(pa, 512).rearrange("p (r c) -> p r c", c=GW)),
                             reads=[b_pb[pa]], writes=[b_ap[a]])
                for bn, (kind, t0, nb, j) in enumerate(FB):
                    pc, pg = (4, 7)[bn % 2], (5, 0)[bn % 2]
                    ga = bn % 2
                    tl = list(range(t0 // 128, (t0 + nb) // 128))
                    if kind == "c":
                        for n_, dj in enumerate(range(3)):
                            tap = 3 + dj
                            S.op("pe", lambda e, a=a, tap=tap, dj=dj, pc=pc, n_=n_: e.matmul(pbf(pc, 256), dg[a][:, tap * 128:(tap + 1) * 128], apc[a][:, dj:dj + 256], start=(n_ == 0), stop=(n_ == 2)),
                                 reads=[b_dg[a], b_ap[a]], writes=[b_pb[pc]], inc=(n_ == 2))
                    else:
                        for tap in range(9):
                            di, dj = tap // 3, tap % 3
                            mv_ = apx3[:, di + 8 * j:di + 8 * j + 8, dj:dj + GW]
                            S.op("pe", lambda e, a=a, tap=tap, mv_=mv_, pc=pc: e.matmul(pbf(pc, 512), dg[a][:, tap * 128:(tap + 1) * 128], mv_, start=(tap == 0), stop=(tap == 8)),
                                 reads=[b_dg[a], b_ap[a]], writes=[b_pb[pc]], inc=(tap == 8))
                    for kc in range(KC):
                        S.op("pe", lambda e, a=a, kc=kc, t0=t0, nb=nb, pg=pg: e.matmul(pbf(pg, nb), wag[a][:, kc * 256 + 128:kc * 256 + 256], xmT[:, kc * N + t0:kc * N + t0 + nb],
                             start=(kc == 0), stop=(kc == KC - 1)), reads=[b_wag[a]] + [b_xmT[i] for i in tl], writes=[b_pb[pg]], inc=(kc == KC - 1))
                    bcol = fcb_s[:, fc:fc + 1]
                    S.op("act", lambda e, ga=ga, nb=nb, pc=pc, bcol=bcol: e.activation(gel[ga][:, 0:nb], pbf(pc, nb), AF.Gelu, bias=bcol), reads=[b_pb[pc], b_fcb], writes=[b_gel[ga]])
                    S.op("dve", lambda e, a=a, ga=ga, t0=t0, nb=nb, pg=pg: e.tensor_tensor(htc[a][:, t0:t0 + nb], gel[ga][:, 0:nb], pbf(pg, nb), ALU.mult),
                         reads=[b_gel[ga], b_pb[pg]], writes=[b_htc[a]])
                lo = FB[0][1]
                S.dma("sp", hTd[fc * 128:(fc + 1) * 128, lo:N], htc[a][:, lo:N], reads=[b_htc[a]], writes=[b_hTd[fc]])
            S.barrier(); p5.close()

        def phase_ffn_down(l, tiles, dst_ap, dst_row0):
            p6 = ExitStack()
            sb6 = lambda n, s_, dt: p6.enter_context(nc.sbuf_tensor("%s_p6L%d" % (n, l), s_, dt))
            wd = sb6("wd", [128, NFC * D], BF16); b_wdp = {(hf_, q_): Buf("wd%d%d" % (hf_, q_)) for hf_ in range(2) for q_ in range(2)}
            gbc = [sb6("gbc%d" % i, [128, D], F32) for i in range(2)]; b_gbc = [Buf("gbc0"), Buf("gbc1")]
            lg = sb6("lg", [128, D], F32); lb_ = sb6("lb_", [128, D], F32); b_lg, b_lbb = Buf("lg"), Buf("lbb")
            scr = sb6("scr", [128, 128], F32); b_scr = Buf("scr")
            wdv = ffn_down[l].rearrange("(f p) n -> p f n", p=128)
            for hf_ in range(2):
                for q_ in range(2):
                    S.dma("pool", wd[:].rearrange("p (f n) -> p f n", f=NFC)[:, q_ * 11:(q_ + 1) * 11, hf_ * 512:(hf_ + 1) * 512],
                          wdv[:, q_ * 11:(q_ + 1) * 11, hf_ * 512:(hf_ + 1) * 512], writes=[b_wdp[(hf_, q_)]])
            S.dma("sp", lg[:], lnp[4 * l + 2:4 * l + 3, :].partition_broadcast(128), writes=[b_lg])
            S.dma("sp", lb_[:], lnp[4 * l + 3:4 * l + 4, :].partition_broadcast(128), writes=[b_lbb])
            gate_bcast(gbc[0], b_gbc[0], l, 5, 0, scr, b_scr)
            if any(i < 2 for i in tiles):
                gate_bcast(gbc[1], b_gbc[1], l, 5, 1, scr, b_scr)
            hv = hTd.rearrange("(f p) n -> p f n", p=128)
            sets = mk_sets(sb6, 2 * GRP, "c", with_y=True, with_h=True)

            def load(i, B_):
                S.dma("sp", B_["ht"][:].rearrange("p (f n) -> p f n", f=NFC), hv[:, :, i * 128:(i + 1) * 128], reads=b_hTd, writes=[B_["b_ht"]])
                S.dma("sp", B_["xq"][:], x1d[i * 128:(i + 1) * 128, :], reads=[b_x1d[i]], writes=[B_["b_xq"]])

            def chain6(i, B_):
                srcm = 1 if i < 2 else 0
                xq_t, yt_t, xt_t, st_t, ht_t = B_["xq"], B_["yt"], B_["xt"], B_["st"], B_["ht"]
                for hf in range(2):
                    pbi = B_["mm"][hf]; hs = slice(hf * 512, (hf + 1) * 512)
                    yield
                    for f_ in range(NFC):
                        S.op("pe", lambda e, f_=f_, hf=hf, pbi=pbi: e.matmul(pbf(pbi, 512), ht_t[:, f_ * 128:(f_ + 1) * 128], wd[:, f_ * D + hf * 512:f_ * D + (hf + 1) * 512],
                             start=(f_ == 0), stop=(f_ == NFC - 1)), reads=[B_["b_ht"], b_wdp[(hf, f_ // 11)]], writes=[b_pb[pbi]], inc=(f_ == NFC - 1))
                    yield
                    S.op("dve", lambda e, hs=hs, pbi=pbi: e.tensor_tensor(yt_t[:, hs], pbf(pbi, 512), gbc[srcm][:, hs], ALU.mult),
                         reads=[b_pb[pbi], b_gbc[srcm]], writes=[B_["b_yt"]])
                    yield
                    S.op("dve", lambda e, hs=hs: e.scalar_tensor_tensor(yt_t[:, hs], xq_t[:, hs], ALPHA, yt_t[:, hs], ALU.mult, ALU.add),
                         reads=[B_["b_xq"], B_["b_yt"]], writes=[B_["b_yt"]])
                yield from ln_stats_gen(yt_t, B_["b_yt"], B_)
                yield
                S.op("dve", lambda e: e.scalar_tensor_tensor(yt_t[:], yt_t[:], st_t[:, 12:13], lg[:], ALU.subtract, ALU.mult),
                     reads=[B_["b_yt"], B_["b_st"], b_lg], writes=[B_["b_yt"]])
                yield
                S.op("dve", lambda e: e.scalar_tensor_tensor(xt_t[:], yt_t[:], st_t[:, 15:16], lb_[:], ALU.mult, ALU.add),
                     reads=[B_["b_yt"], B_["b_st"], b_lbb, B_["b_xt"]], writes=[B_["b_xt"]])
                r0 = i * 128 - dst_row0
                yield
                S.dma("sp", dst_ap[r0:r0 + 128, :], xt_t[:], reads=[B_["b_xt"]], writes=[b_x2d[i]])

            run_groups(tiles, sets, load, chain6)
            S.barrier(); p6.close()

        if debug in ("h", "x2", None):
            phase_ffn_up(0, True)
        if debug == "h":
            hst_b = sb("hst_b", [128, N], BF16); hst_f = sb("hst_f", [128, N], F32); b_hsb, b_hsf = Buf("hsb"), Buf("hsf")
            for fc in range(NFC):
                S.dma("sp", hst_b[:], hTd[fc * 128:(fc + 1) * 128, :], reads=[b_hTd[fc]], writes=[b_hsb])
                S.op("act", lambda e: e.copy(hst_f[:], hst_b[:]), reads=[b_hsb], writes=[b_hsf])
                S.dma("sp", outs["dbg_hT"][fc * 128:(fc + 1) * 128, :], hst_f[:], reads=[b_hsf])
        if debug in ("x2", None):
            phase_ffn_down(0, list(range(NT)), x2d[0], 0)
        if debug == "x2":
            for i in range(NT):
                a = i % 2
                S.dma("sp", xt[a][:], x2d[0][i * 128:(i + 1) * 128, :], reads=[b_x2d[i]], writes=[b_xt[a]])
                S.dma("sp", outs["dbg_x2"][i * 128:(i + 1) * 128, :], xt[a][:], reads=[b_xt[a]])

        if debug is None:
            XT = list(range(2, NT))
            phase_ln_mod_T(1, x2d[0], range(NT), 0, 1, src_bufs=b_x2d)
            mixer(1, list(range(NT)), XT)
            phase_wout_ln1(1, x2d[0], XT, src_bufs=b_x2d)
            phase_ffn_up(1, False)
            phase_ffn_down(1, XT, outs["out"], 256)
        if debug in ("hg", "mix"):
            hg_f, b_hgf, hg_b, b_hgb = mixer.dbg
            for h in range(8 if debug == "mix" else 4):
                S.dma("sp", hg_b[:], mixT[h * 128:(h + 1) * 128, :], reads=[b_mixT[h]], writes=[b_hgb])
                S.op("act", lambda e: e.copy(hg_f[:], hg_b[:]), reads=[b_hgb], writes=[b_hgf])
                S.dma("sp", outs["dbg_hgT"][h * 128:(h + 1) * 128, :], hg_f[:], reads=[b_hgf])
        S.drain_all("sp")
        S.emit()
        build.ninstr = S.ninstr
    return nc


def _prep(inputs, b):
    f = lambda a: np.ascontiguousarray(np.asarray(a, dtype=np.float32))
    m = {}
    m["xin"] = f(np.concatenate([inputs["ctx"][b], inputs["x"][b]], axis=0))
    cc = np.stack([np.asarray(inputs["c"][b]).reshape(KC, 128).T, np.asarray(inputs["c_ctx"]).reshape(KC, 128).T], axis=-1)
    m["c2"] = f(cc.reshape(128, KC * 2))
    m["ada_w"] = f(inputs["ada_w"])
    m["ada_b_fm"] = f(np.asarray(inputs["ada_b"]).reshape(DEPTH, 48, 128).transpose(2, 0, 1).reshape(128, DEPTH * 48))
    m["ident"] = np.eye(128, dtype=np.float32)
    m["w_in"] = f(inputs["w_in"])
    m["w_out"] = f(inputs["w_out"])
    m["ffn_up"] = f(inputs["ffn_up"]); m["ffn_down"] = f(inputs["ffn_down"])
    m["fcw_fm"] = f(np.asarray(inputs["ffn_conv_w"]).reshape(DEPTH, 9, NFC, 128).transpose(3, 0, 2, 1).reshape(128, DEPTH * NFC * 9))
    m["fcb_fm"] = f(np.asarray(inputs["ffn_conv_b"]).reshape(DEPTH, NFC, 128).transpose(2, 0, 1).reshape(128, DEPTH * NFC))
    m["lnp"] = f(np.stack([np.asarray(inputs[k]) for k in ("ln1_g", "ln1_b", "ln2_g", "ln2_b")], axis=1).reshape(DEPTH * 4, D))
    m["scw_fm"] = f(np.asarray(inputs["sc_conv_w"]).reshape(DEPTH, 3, 2, 128).transpose(3, 0, 2, 1).reshape(128, DEPTH * 6))
    m["sgln"] = f(np.stack([np.asarray(inputs["sg_ln_g"]), np.asarray(inputs["sg_ln_b"])], axis=1).reshape(DEPTH * 2, 256))
    m["sg_w"] = f(np.asarray(inputs["sg_w"]).reshape(DEPTH * 4, 128, 128))
    sb_ = np.asarray(inputs["sg_b"]).reshape(DEPTH, 2, 2, 1, 128)
    m["sgb_fm"] = f(np.broadcast_to(sb_, (DEPTH, 2, 2, 64, 128)).reshape(DEPTH * 2, 128, 128))
    m["lbl"] = f(np.asarray(inputs["hg_lb"]).reshape(DEPTH, 2, 4, 128).transpose(3, 0, 1, 2).reshape(128, 16))
    m["hg_norm_g"] = f(inputs["hg_norm_g"])
    ii = np.arange(128)
    same = (ii[:, None] // 64) == (ii[None, :] // 64)
    m["cmask"] = f(np.concatenate([same & (ii[:, None] <= ii[None, :]), same & (ii[:, None] >= ii[None, :])], axis=1))
    rm = np.ones((128, 512), np.float32); rm[:, ::64] = 0.0
    m["rmask"] = rm
    return m


_NC = None


def kernel(**inputs):
    global _NC
    if _NC is None:
        _NC = build()
    shared = None
    maps = []
    for b in range(8):
        m = _prep(inputs, b)
        if shared is None:
            shared = {k: m[k] for k in m if k not in ("xin", "c2")}
        else:
            m.update(shared)
        maps.append(m)
    res = run_bass_kernel_spmd(_NC, maps, core_ids=list(range(8)))
    return np.stack([np.asarray(r["out"], dtype=np.float32) for r in res.results], axis=0)
```

```python
import numpy as np
import concourse.bass as bass
import concourse.mybir as mybir

F32 = mybir.dt.float32
BF16 = mybir.dt.bfloat16
ALU = mybir.AluOpType
AF = mybir.ActivationFunctionType


class Buf:
    __slots__ = ("name", "w", "r", "excl")

    def __init__(self, name="", excl=False):
        self.name = name
        self.excl = excl
        self.w = None
        self.r = []


class Sync:
    ENGS = ("pe", "act", "dve", "pool", "sp")

    SEM_LIMIT = 1900

    def __init__(self, nc, stack, n_dma_sems=20):
        self.nc = nc
        self.stack = stack
        self.owner = {}
        self.cur = {}
        self.nsem = 0
        self.q = {e: [] for e in self.ENGS}
        self.sems = {}
        self.cnt = {}
        self.known = {e: {} for e in self.ENGS}
        for e in ("pe", "act", "dve", "pool"):
            self.cur[e] = None
            self._new_sem(e)
        self.dpool = {}
        self.dk = {}
        for e in ("sp", "act", "pool"):
            self.dpool[e] = []
            for i in range(n_dma_sems):
                key = "d_%s_%d" % (e, i)
                self.sems[key] = stack.enter_context(nc.semaphore(key)); self.nsem += 1
                self.cnt[key] = 0
                self.owner[key] = "dma_" + e
                self.dpool[e].append(key)
            self.dk[e] = 0
        self.pe_pending = False
        self.ninstr = 0

    def _new_sem(self, eng):
        n = sum(1 for k in self.owner if self.owner[k] == eng)
        key = "%s#%d" % (eng, n)
        self.sems[key] = self.stack.enter_context(self.nc.semaphore("s_%s_%d" % (eng, n))); self.nsem += 1
        self.cnt[key] = 0
        self.owner[key] = eng
        self.prev = getattr(self, "prev", {})
        self.prev[eng] = self.cur[eng]
        self.cur[eng] = key

    def _latest(self, eng):
        k = self.cur[eng]
        if self.cnt[k]:
            return (k, self.cnt[k])
        p = self.prev.get(eng)
        return (p, self.cnt[p]) if p and self.cnt[p] else None

    def _wait(self, eng, tok):
        if tok is None:
            return
        key, val = tok
        if self.known[eng].get(key, 0) >= val:
            return
        self.known[eng][key] = val
        sem = self.sems[key]
        self.q[eng].append(lambda e, sem=sem, val=val: e.wait_ge(sem, val))
        self.ninstr += 1

    def _deps(self, eng, reads, writes, skip_self=False):
        for b in reads:
            if b.w is not None and not (skip_self and self.owner[b.w[0]] == eng):
                self._wait(eng, b.w)
            if b.excl:
                for t in b.r:
                    if self.owner[t[0]] != eng:
                        self._wait(eng, t)
        for b in writes:
            if b.w is not None and not (skip_self and self.owner[b.w[0]] == eng):
                self._wait(eng, b.w)
            for t in b.r:
                if not (skip_self and self.owner[t[0]] == eng):
                    self._wait(eng, t)

    def _commit(self, tok, reads, writes):
        for b in reads:
            b.r.append(tok)
            if len(b.r) > 64:
                best = {}
                for k, v in b.r:
                    if best.get(k, 0) < v:
                        best[k] = v
                b.r = list(best.items())
        for b in writes:
            b.w = tok
            b.r = []

    def op(self, eng, fn, reads=(), writes=(), inc=True):
        pe = eng == "pe"
        self._deps(eng, reads, writes, skip_self=pe)
        if self.cnt[self.cur[eng]] >= self.SEM_LIMIT and not (pe and self.pe_pending):
            self._new_sem(eng)
        key = self.cur[eng]
        if inc:
            self.cnt[key] += 1
            tok = (key, self.cnt[key])
            sem = self.sems[key]
            self.q[eng].append(lambda e, fn=fn, sem=sem: fn(e).then_inc(sem, 1))
            if pe:
                self.pe_pending = False
        else:
            assert pe
            tok = (key, self.cnt[key] + 1)
            self.q[eng].append(lambda e, fn=fn: fn(e))
            self.pe_pending = True
        self.ninstr += 1
        self._commit(tok, reads, writes)
        return tok

    def dma(self, eng, out, in_, reads=(), writes=(), **kw):
        self._deps(eng, reads, writes)
        pool = self.dpool[eng]
        slot = self.dk[eng] % len(pool)
        key = pool[slot]
        self.dk[eng] += 1
        if self.cnt[key] + 16 > self.SEM_LIMIT:
            self._wait(eng, (key, self.cnt[key]))
            n = sum(1 for k in self.owner if self.owner[k] == "dma_" + eng)
            nk = "d_%s_%d" % (eng, n)
            self.sems[nk] = self.stack.enter_context(self.nc.semaphore(nk)); self.nsem += 1
            self.cnt[nk] = 0; self.owner[nk] = "dma_" + eng
            pool[slot] = nk; key = nk
            self.retired = getattr(self, "retired", []) + [key]
        if self.cnt[key] > 0:
            self._wait(eng, (key, self.cnt[key]))
        self.cnt[key] += 16
        tok = (key, self.cnt[key])
        sem = self.sems[key]
        self.q[eng].append(
            lambda e, out=out, in_=in_, sem=sem, kw=kw: e.dma_start(out=out, in_=in_, **kw).then_inc(sem, 16))
        self.ninstr += 1
        self._commit(tok, reads, writes)
        return tok

    def wait_all(self, eng, toks):
        for t in toks:
            self._wait(eng, t)

    def barrier(self):
        toks = [t for t in (self._latest(e) for e in ("pe", "act", "dve", "pool")) if t]
        assert not self.pe_pending
        for q in self.dpool:
            for key in self.dpool[q]:
                if self.cnt[key]:
                    toks.append((key, self.cnt[key]))
        for eng in self.ENGS:
            for t in toks:
                if self.owner[t[0]] != eng:
                    self._wait(eng, t)

    def drain_all(self, eng="sp"):
        for q in self.dpool:
            for key in self.dpool[q]:
                if self.cnt[key]:
                    self._wait(eng, (key, self.cnt[key]))

    def emit(self):
        assert not self.pe_pending, "last PE op must carry inc"
        nc = self.nc
        q = self.q
        with nc.Block() as block:
            @block.sync
            def _(e):
                for f in q["sp"]:
                    f(e)

            @block.tensor
            def _(e):
                for f in q["pe"]:
                    f(e)

            @block.scalar
            def _(e):
                for f in q["act"]:
                    f(e)

            @block.vector
            def _(e):
                for f in q["dve"]:
                    f(e)

            @block.gpsimd
            def _(e):
                for f in q["pool"]:
                    f(e)


def run_rr(chains):
    live = list(chains)
    while live:
        for g in list(live):
            try:
                next(g)
            except StopIteration:
                live.remove(g)


from contextlib import ExitStack
from concourse.bass_utils import run_bass_kernel_spmd

FF = 2816; NFC = FF // 128
D = 1024; NT = 18; N = NT * 128; DEPTH = 2; NMOD = 6; EPS = 1e-6
KC = D // 128


def build(debug=None):
    nc = bass.Bass("TRN2", target_bir_lowering=False)
    dram = lambda n, s, dt, kind: nc.dram_tensor(n, s, dt, kind=kind).ap()
    xin = dram("xin", [N, D], F32, "ExternalInput")
    c2 = dram("c2", [128, KC * 2], F32, "ExternalInput")
    ada_w = dram("ada_w", [DEPTH, D, NMOD * D], F32, "ExternalInput")
    ada_b = dram("ada_b_fm", [128, DEPTH * 48], F32, "ExternalInput")
    ident = dram("ident", [128, 128], F32, "ExternalInput")
    w_in = dram("w_in", [DEPTH, D, 3840], F32, "ExternalInput")
    lbl = dram("lbl", [128, 16], F32, "ExternalInput")
    hgng = dram("hg_norm_g", [DEPTH, 128], F32, "ExternalInput")
    scw = dram("scw_fm", [128, DEPTH * 6], F32, "ExternalInput")
    sgln = dram("sgln", [DEPTH * 2, 256], F32, "ExternalInput")
    sgw = dram("sg_w", [DEPTH * 4, 128, 128], F32, "ExternalInput")
    sgb = dram("sgb_fm", [DEPTH * 2, 128, 128], F32, "ExternalInput")
    w_out = dram("w_out", [DEPTH, D, D], F32, "ExternalInput")
    ffn_up = dram("ffn_up", [DEPTH, D, 2 * FF], F32, "ExternalInput")
    ffn_down = dram("ffn_down", [DEPTH, FF, D], F32, "ExternalInput")
    fcw = dram("fcw_fm", [128, DEPTH * NFC * 9], F32, "ExternalInput")
    fcb = dram("fcb_fm", [128, DEPTH * NFC], F32, "ExternalInput")
    lnp = dram("lnp", [DEPTH * 4, D], F32, "ExternalInput")
    cmask = dram("cmask", [128, 256], F32, "ExternalInput")
    rmask = dram("rmask", [128, 512], F32, "ExternalInput")
    outs = {}
    if debug is None:
        outs["out"] = dram("out", [N - 256, D], F32, "ExternalOutput")
    if debug and debug.startswith("hg") and len(debug) == 3:
        outs["dbg_cut"] = dram("dbg_cut", [128, 2 * N], F32, "ExternalOutput")
    if debug == "h":
        outs["dbg_hT"] = dram("dbg_hT", [FF, N], F32, "ExternalOutput")
    if debug == "x2":
        outs["dbg_x2"] = dram("dbg_x2", [N, D], F32, "ExternalOutput")
    if debug == "x1":
        outs["dbg_x1"] = dram("dbg_x1", [N, D], F32, "ExternalOutput")
        outs["dbg_xm2T"] = dram("dbg_xm2T", [128, KC * N], F32, "ExternalOutput")
    if debug in ("hg", "mix"):
        outs["dbg_hgT"] = dram("dbg_hgT", [D if debug == "mix" else 512, N], F32, "ExternalOutput")
    if debug == "p1":
        outs["dbg_mod"] = dram("dbg_mod", [128, DEPTH * 96], F32, "ExternalOutput")
        outs["dbg_xmT"] = dram("dbg_xmT", [128, KC * N], F32, "ExternalOutput")
    with ExitStack() as st:
        S = Sync(nc, st)
        sb = lambda n, s, dt: st.enter_context(nc.sbuf_tensor(n, s, dt))
        pbank = [st.enter_context(nc.psum_tensor("pb%d" % i, [128, 512], F32)) for i in range(8)]
        b_pb = [Buf("pb%d" % i, excl=True) for i in range(8)]
        pbf = lambda i, n: pbank[i][:, 0:n]
        pbb = lambda i, n: pbank[i][:].bitcast(BF16)[:, 0:n]
        id_f = sb("id_f", [128, 128], F32); id_bf = sb("id_bf", [128, 128], BF16)
        c2_f = sb("c2_f", [128, KC * 2], F32); c2_bf = sb("c2_bf", [128, KC * 2], BF16)
        adab = sb("adab", [128, DEPTH * 48], F32)
        mod = sb("mod", [128, DEPTH * 96], F32)
        mod1p = sb("mod1p", [128, DEPTH * 96], F32)
        b_id_f, b_id_bf, b_c2f, b_c2bf, b_adab, b_mod, b_mod1p = (Buf(n) for n in "idf idbf c2f c2bf adab mod mod1p".split())
        S.dma("sp", id_f[:], ident, writes=[b_id_f])
        S.dma("pool", id_bf[:], ident, writes=[b_id_bf])
        S.dma("sp", c2_f[:], c2, writes=[b_c2f])
        S.dma("sp", adab[:], ada_b, writes=[b_adab])
        S.op("act", lambda e: e.activation(c2_bf[:], c2_f[:], AF.Silu), reads=[b_c2f], writes=[b_c2bf])
        st0 = ExitStack()
        aw = [st0.enter_context(nc.sbuf_tensor("aw%d" % i, [128, KC * 512], BF16)) for i in range(2)]
        b_aw = [Buf("aw0"), Buf("aw1")]
        p_mod = pbf(0, 96); b_pmod = b_pb[0]
        def mod_chunk_load(l, g, buf, bb):
            src = ada_w[l].rearrange("(k p) n -> p k n", p=128)[:, :, g * 512:(g + 1) * 512]
            S.dma("pool", buf[:].rearrange("p (k n) -> p k n", k=KC), src, writes=[bb])

        def mod_chunk_mm(l, g, buf, bb, pm_ap, b_pm):
            for jj in range(4):
                j = g * 4 + jj
                for k in range(KC):
                    last = (k == KC - 1)
                    S.op("pe", lambda e, buf=buf, jj=jj, k=k, j=j, last=last: e.matmul(
                        pm_ap[:, 2 * j:2 * j + 2], buf[:, k * 512 + jj * 128:k * 512 + (jj + 1) * 128],
                        c2_bf[:, 2 * k:2 * k + 2], start=(k == 0), stop=last),
                        reads=[bb, b_c2bf], writes=[b_pm], inc=(last and jj == 3))

        def mod_finish(l, pm_ap, b_pm):
            pm = pm_ap.rearrange("p (j s) -> p j s", s=2)
            mv = mod[:, l * 96:(l + 1) * 96].rearrange("p (j s) -> p j s", s=2)
            for s_ in range(2):
                S.op("dve", lambda e, pm=pm, mv=mv, s_=s_, l=l: e.tensor_tensor(
                    mv[:, :, s_], pm[:, :, s_], adab[:, l * 48:(l + 1) * 48], ALU.add),
                    reads=[b_pm, b_adab], writes=[b_mod])
            S.op("dve", lambda e, l=l: e.tensor_scalar_add(mod1p[:, l * 96:(l + 1) * 96], mod[:, l * 96:(l + 1) * 96], 1.0), reads=[b_mod], writes=[b_mod1p])

        for g in range(12):
            mod_chunk_load(0, g, aw[g % 2], b_aw[g % 2])
            mod_chunk_mm(0, g, aw[g % 2], b_aw[g % 2], p_mod, b_pmod)
        mod_finish(0, p_mod, b_pmod)
        S.barrier(); st0.close()
        if debug == "p1":
            S.dma("sp", outs["dbg_mod"], mod[:], reads=[b_mod])

        def modv(l, slot, kc, src, one_plus=False):
            t = mod1p if one_plus else mod
            col = l * 96 + (slot * 8 + kc) * 2 + src
            return t[:, col:col + 1]

        xmT = sb("xmT", [128, KC * N], BF16)
        b_xmT = [Buf("xmT%d" % i) for i in range(NT)]
        if debug in ("x1", "x2"):
            xt = [sb("xt%d" % i, [128, D], F32) for i in range(2)]; b_xt = [Buf("xt0"), Buf("xt1")]

        def mk_sets(alloc, n, tag, with_y=False, with_h=False, plan=None):
            sets = []
            for g in range(n):
                B_ = dict(xt=alloc("xt%s%d" % (tag, g), [128, D], F32), xn=alloc("xn%s%d" % (tag, g), [128, D], BF16), st=alloc("st%s%d" % (tag, g), [128, 16], F32),
                          b_xt=Buf("xt%d" % g), b_xn=Buf("xn%d" % g), b_st=Buf("st%d" % g),
                          tr=((1, 2) if g % 2 == 0 else (7, 0)), mm=((3, 4) if g % 2 == 0 else (5, 6)), tr1=None)
                if plan is not None:
                    p_ = plan[g % len(plan)]
                    B_.update(mm=(p_[0], p_[1]), tr1=p_[2])
                if with_y:
                    B_.update(xq=alloc("xq%s%d" % (tag, g), [128, D], F32), yt=alloc("yt%s%d" % (tag, g), [128, D], F32), b_xq=Buf("xq%d" % g), b_yt=Buf("yt%d" % g))
                if with_h:
                    B_.update(ht=alloc("ht%s%d" % (tag, g), [128, NFC * 128], BF16), b_ht=Buf("ht%d" % g))
                sets.append(B_)
            return sets

        def ln_stats_gen(src, b_src, B_):
            st_t, b_s = B_["st"], B_["b_st"]
            yield
            S.op("dve", lambda e: e.bn_stats(st_t[:, 0:6], src[:, 0:512]), reads=[b_src], writes=[b_s])
            yield
            S.op("dve", lambda e: e.bn_stats(st_t[:, 6:12], src[:, 512:1024]), reads=[b_src], writes=[b_s])
            yield
            S.op("dve", lambda e: e.bn_aggr(st_t[:, 12:14], st_t[:, 0:12]), reads=[b_s], writes=[b_s])
            yield
            S.op("act", lambda e: e.activation(st_t[:, 14:15], st_t[:, 13:14], AF.Sqrt, bias=EPS), reads=[b_s], writes=[b_s])
            yield
            S.op("dve", lambda e: e.reciprocal(st_t[:, 15:16], st_t[:, 14:15]), reads=[b_s], writes=[b_s])

        def ln_mod_T_gen(l, i, slot_shift, slot_scale, B_):
            srcm = 1 if i < 2 else 0
            xt_t, xn_t, st_t = B_["xt"], B_["xn"], B_["st"]
            yield from ln_stats_gen(xt_t, B_["b_xt"], B_)
            yield
            S.op("dve", lambda e: e.tensor_scalar(xn_t[:], xt_t[:], st_t[:, 12:13], st_t[:, 15:16], ALU.subtract, ALU.mult),
                 reads=[B_["b_xt"], B_["b_st"]], writes=[B_["b_xn"]])
            for kc in range(KC):
                if B_["tr1"] is not None:
                    pbk = B_["tr1"]; pv_ = pbank[pbk][:].bitcast(BF16)[:, kc * 128:(kc + 1) * 128]
                else:
                    pbk = B_["tr"][kc % 2]; pv_ = pbb(pbk, 128)
                yield
                S.op("pe", lambda e, kc=kc, pv_=pv_: e.transpose(pv_, xn_t[:, kc * 128:(kc + 1) * 128], id_bf[:]),
                     reads=[B_["b_xn"], b_id_bf], writes=[b_pb[pbk]])
                dst = xmT[:, kc * N + i * 128: kc * N + (i + 1) * 128]
                yield
                S.op("act", lambda e, dst=dst, pv_=pv_, kc=kc: e.activation(
                    dst, pv_, AF.Identity, bias=modv(l, slot_shift, kc, srcm), scale=modv(l, slot_scale, kc, srcm, True)),
                    reads=[b_pb[pbk], b_mod, b_mod1p], writes=[b_xmT[i]])

        GRP = 2

        def run_groups(tiles, sets, load, chain, GRP=GRP):
            tl_ = list(tiles)

            def stream(k):
                mine = tl_[k::GRP]
                for n_, i in enumerate(mine):
                    if n_ == 0:
                        load(i, sets[k])
                    if n_ + 1 < len(mine):
                        load(mine[n_ + 1], sets[k + GRP * ((n_ + 1) % 2)])
                    yield from chain(i, sets[k + GRP * (n_ % 2)])

            run_rr([stream(k) for k in range(GRP)])

        def phase_ln_mod_T(l, src_ap, tiles, slot_shift, slot_scale, src_bufs=None):
            p1 = ExitStack()
            sets = mk_sets(lambda n, s_, dt: p1.enter_context(nc.sbuf_tensor("%s_p1L%d" % (n, l), s_, dt)), 2 * GRP, "a")
            load = lambda i, B_: S.dma("sp", B_["xt"][:], src_ap[i * 128:(i + 1) * 128, :], reads=([src_bufs[i]] if src_bufs else []), writes=[B_["b_xt"]])
            run_groups(tiles, sets, load, lambda i, B_: ln_mod_T_gen(l, i, slot_shift, slot_scale, B_))
            S.barrier(); p1.close()

        phase_ln_mod_T(0, xin, range(NT), 0, 1)
        cut = debug[2] if (debug and debug.startswith("hg") and len(debug) == 3) else None
        dmp = sb("dmp", [128, 512], F32); b_dmp = Buf("dmp")

        def cut_dump(src_ap, src_bufs):
            S.barrier()
            S.op("act", lambda e: e.copy(dmp[:], src_ap), reads=src_bufs, writes=[b_dmp])
            S.dma("sp", outs["dbg_cut"][:, 0:512], dmp[:], reads=[b_dmp])
            S.drain_all("sp"); S.emit(); build.ninstr = S.ninstr

        if cut == "0":
            cut_dump(xmT[:, 0:512], b_xmT); return nc
        if debug == "p1":
            xm_f = sb("xm_f", [128, N], F32); b_xmf = Buf("xmf")
            for kc in range(KC):
                S.op("act", lambda e, kc=kc: e.copy(xm_f[:], xmT[:, kc * N:(kc + 1) * N]), reads=b_xmT, writes=[b_xmf])
                S.dma("sp", outs["dbg_xmT"][:, kc * N:(kc + 1) * N], xm_f[:], reads=[b_xmf])

        mixT = nc.dram_tensor("mixT", [D, N], BF16, kind="Internal").ap()
        b_mixT = [Buf("mixT%d" % i) for i in range(8)]

        def mixer(l, scan_tiles, out_tiles):
            ms = ExitStack()
            sb = lambda n, s_, dt: ms.enter_context(nc.sbuf_tensor("%s_L%d" % (n, l), s_, dt))
            DK = 128; QS = DK ** -0.5; NCH = N // 64
            BLKS = [(0, 512), (512, 512), (1024, 512), (1536, 512), (2048, 256)]
            lb_f = sb("lb_f", [128, 16], F32); lbv = sb("lbv", [128, 16], F32); oml = sb("oml", [128, 16], F32)
            b_lb = Buf("lb")
            cm = sb("cm", [128, 256], F32); rmk = sb("rmk", [128, 512], F32); ngb = sb("ngb", [128, DEPTH * 128], F32)
            b_cm, b_rmk, b_ngb = Buf("cm"), Buf("rmk"), Buf("ngb")
            cm_u = sb("cm_u", [128, 256], mybir.dt.uint32); b_cmu = Buf("cmu")
            S.dma("sp", lb_f[:], lbl, writes=[b_lb]); S.dma("sp", cm[:], cmask, writes=[b_cm]); S.dma("sp", rmk[:], rmask, writes=[b_rmk])
            for l2 in range(DEPTH):
                S.dma("sp", ngb[:, l2 * 128:(l2 + 1) * 128], hgng[l2:l2 + 1, :].partition_broadcast(128), writes=[b_ngb])
            S.op("dve", lambda e: e.tensor_copy(cm_u[:], cm[:]), reads=[b_cm], writes=[b_cmu])
            lt = sb("lt", [128, 32], F32)
            S.op("dve", lambda e: e.memset(lbv[:, 0:8], 0.0), writes=[b_lb], reads=[b_lb])
            S.op("dve", lambda e: e.tensor_max(lt[:, 0:8], lb_f[:, 0:8], lb_f[:, 8:16]), reads=[b_lb], writes=[b_lb])
            S.op("dve", lambda e: e.tensor_sub(lt[:, 8:16], lb_f[:, 0:8], lt[:, 0:8]), reads=[b_lb], writes=[b_lb])
            S.op("dve", lambda e: e.tensor_sub(lt[:, 16:24], lb_f[:, 8:16], lt[:, 0:8]), reads=[b_lb], writes=[b_lb])
            S.op("act", lambda e: e.activation(lt[:, 8:24], lt[:, 8:24], AF.Exp), reads=[b_lb], writes=[b_lb])
            S.op("dve", lambda e: e.tensor_add(lt[:, 24:32], lt[:, 8:16], lt[:, 16:24]), reads=[b_lb], writes=[b_lb])
            S.op("dve", lambda e: e.reciprocal(lt[:, 24:32], lt[:, 24:32]), reads=[b_lb], writes=[b_lb])
            S.op("dve", lambda e: e.tensor_mul(lbv[:, 8:16], lt[:, 16:24], lt[:, 24:32]), reads=[b_lb], writes=[b_lb])
            S.op("dve", lambda e: e.tensor_scalar(oml[:], lbv[:], -1.0, 1.0, ALU.mult, ALU.add), reads=[b_lb], writes=[b_lb])

            wvg = [sb("wvg%d" % i, [128, KC * 256], BF16) for i in range(2)]; b_wvg = [Buf("wvg0"), Buf("wvg1")]
            wzq = [sb("wzq%d" % i, [128, KC * 384], BF16) for i in range(2)]; b_wzq = [Buf("wzq0"), Buf("wzq1")]
            VG = [dict(V=sb("Vh%d" % p_, [128, N], BF16), G=sb("Gh%d" % p_, [128, N], BF16), bV=Buf("V%d" % p_), bG=Buf("G%d" % p_)) for p_ in range(2)]
            Vh, Gh, b_V, b_G = VG[0]["V"], VG[0]["G"], VG[0]["bV"], VG[0]["bG"]
            QT = [sb("QT%d" % d, [128, N], BF16) for d in range(2)]; KT = [sb("KT%d" % d, [128, N], BF16) for d in range(2)]
            QS_T = [sb("QST%d" % d, [128, N], BF16) for d in range(2)]; b_QST = [Buf("QST0"), Buf("QST1")]
            KH = [sb("KH%d" % d, [128, N], BF16) for d in range(2)]; KHt = [sb("KHt%d" % d, [128, N], BF16) for d in range(2)]
            EB = [sb("EB%d" % d, [128, NCH], F32) for d in range(2)]
            EM = [sb("EM%d" % d, [128, NCH], F32) for d in range(2)]; b_EM = [Buf("EM0"), Buf("EM1")]
            b_QT = [Buf("QT0"), Buf("QT1")]; b_KT = [Buf("KT0"), Buf("KT1")]; b_KH = [Buf("KH0"), Buf("KH1")]
            b_KHt = [Buf("KHt0"), Buf("KHt1")]; b_EB = [Buf("EB0"), Buf("EB1")]
            Oacc = [sb("Oacc%d" % d, [128, N], F32) for d in range(2)]; b_O = [Buf("O0"), Buf("O1")]
            HGT = sb("HGT", [128, N], BF16); b_HGT = Buf("HGT")
            NTMP = 9
            tmp = [[sb("tm%d_%d" % (a, i), [128, 512], F32) for i in range(NTMP)] for a in range(2)]
            b_tmp = [[Buf("tm%d_%d" % (a, i)) for i in range(NTMP)] for a in range(2)]
            qs_ = [sb("qs%d" % a, [128, 512], F32) for a in range(2)]; b_qs = [Buf("qs0"), Buf("qs1")]
            Sst = [[sb("S%d_%d" % (d, i), [128, 128], F32) for i in range(2)] for d in range(2)]
            b_S = [[Buf("S%d_%d" % (d, i)) for i in range(2)] for d in range(2)]
            Sbf_f32 = [sb("S3_%d" % d, [128, 128], F32) for d in range(2)]; b_Sx = [Buf("S3_0"), Buf("S3_1")]
            AT = [sb("AT%d" % d, [128, 128], BF16) for d in range(2)]; b_AT = [Buf("AT0"), Buf("AT1")]
            fin = [sb("fin%d" % i, [128, 128], F32) for i in range(2)]; fsq = sb("fsq", [128, 128], F32)
            b_fin = [Buf("fin0"), Buf("fin1")]
            finW = [dict(fsq=(fsq if p_ == 0 else sb("fsq1", [128, 128], F32)), fst=sb("fst%d" % p_, [128, 8], F32), ngg=sb("ngg%d" % p_, [128, 128], F32), hgt=sb("hgt%d" % p_, [128, 128], BF16),
                         b_fsq=Buf("fsq%d" % p_), b_fst=Buf("fst%d" % p_), b_ngg=Buf("ngg%d" % p_), b_hgt=Buf("hgt%d" % p_)) for p_ in range(2)]

            if cut == "L":
                cut_dump(oml[:, 0:16].to_broadcast([128, 16]) if False else xmT[:, 0:512], b_xmT + [b_lb, b_cm, b_rmk, b_ngb]); return "cut"

            e_, one_e, sig, kk, lf, bb, cc_, E1, dd = range(9)

            def hgrn_layer(l, scan_tiles, out_tiles):
                wi = w_in[l].rearrange("(k p) n -> p k n", p=128)
                def load_vg_w(h_):
                    a_ = h_ % 2
                    for ci, c0 in enumerate((h_ * 128, 2048 + h_ * 128)):
                        S.dma("pool", wvg[a_][:].rearrange("p (k n) -> p k n", k=KC)[:, :, ci * 128:(ci + 1) * 128], wi[:, :, c0:c0 + 128], writes=[b_wvg[a_]])

                def load_zq_w(h_):
                    a_ = h_ % 2
                    for ci, c0 in enumerate((512 + h_ * 128, 1024 + h_ * 128, 1536 + h_ * 128)):
                        S.dma("pool", wzq[a_][:].rearrange("p (k n) -> p k n", k=KC)[:, :, ci * 128:(ci + 1) * 128], wi[:, :, c0:c0 + 128], writes=[b_wzq[a_]])

                def A_gen(h_):
                    a_ = h_ % 2; W_ = VG[a_]
                    for i in scan_tiles:
                        pbi = 3 + (i % 2)
                        yield
                        for kc in range(KC):
                            S.op("pe", lambda e, i=i, kc=kc, pbi=pbi: e.matmul(pbf(pbi, 256), xmT[:, kc * N + i * 128:kc * N + (i + 1) * 128],
                                 wvg[a_][:, kc * 256:(kc + 1) * 256], start=(kc == 0), stop=(kc == KC - 1)),
                                 reads=[b_xmT[i], b_wvg[a_]], writes=[b_pb[pbi]], inc=(kc == KC - 1))
                        yield
                        S.op("dve", lambda e, i=i, pbi=pbi: e.tensor_copy(W_["V"][:, i * 128:(i + 1) * 128], pbank[pbi][:, 0:128]), reads=[b_pb[pbi]], writes=[W_["bV"]])
                        yield
                        S.op("act", lambda e, i=i, pbi=pbi: e.activation(W_["G"][:, i * 128:(i + 1) * 128], pbank[pbi][:, 128:256], AF.Silu), reads=[b_pb[pbi]], writes=[W_["bG"]])

                nheads = 1 if cut else 4
                load_vg_w(0)
                if cut == "W":
                    cut_dump(wvg[0][:, 0:512], [b_wvg[0]]); return "cut"
                run_rr([A_gen(0)])
                for h in range(nheads):
                    a = h % 2
                    Vh, Gh, b_V, b_G = VG[a]["V"], VG[a]["G"], VG[a]["bV"], VG[a]["bG"]
                    if h == 0:
                        load_zq_w(0)
                    if h + 1 < nheads:
                        load_zq_w(h + 1)
                        load_vg_w(h + 1)
                    if cut == "A":
                        cut_dump(Vh[:, 0:512], [b_V, b_G]); return "cut"
                    for bi, (t0, nb) in enumerate(BLKS):
                        if t0 // 128 not in scan_tiles:
                            continue
                        tiles_in = [i for i in range(t0 // 128, (t0 + nb) // 128)]
                        for ci in range(3):
                            for kc in range(KC):
                                S.op("pe", lambda e, ci=ci, kc=kc, t0=t0, nb=nb, a=a: e.matmul(pbf(5 + ci, nb), wzq[a][:, kc * 384 + ci * 128:kc * 384 + (ci + 1) * 128],
                                     xmT[:, kc * N + t0:kc * N + t0 + nb], start=(kc == 0), stop=(kc == KC - 1)),
                                     reads=[b_wzq[a]] + [b_xmT[i] for i in tiles_in], writes=[b_pb[5 + ci]], inc=(kc == KC - 1))
                        qa = bi % 2
                        S.op("act", lambda e, nb=nb, qa=qa: e.activation(qs_[qa][:, 0:nb], pbf(7, nb), AF.Silu), reads=[b_pb[7]], writes=[b_qs[qa]])
                        nck = nb // 64; c0 = t0 // 64
                        def gate_chain(d, bi=bi, t0=t0, nb=nb, qa=qa, nck=nck, c0=c0):
                            ta = (bi * 2 + d) % 2
                            T = [t[:, 0:nb] for t in tmp[ta]]; bT = b_tmp[ta]
                            col = l * 8 + d * 4 + h
                            lbc, omc = lbv[:, col:col + 1], oml[:, col:col + 1]
                            yield
                            S.op("act", lambda e, T=T, d=d, nb=nb: e.activation(T[e_], pbf(5 + d, nb), AF.Exp, scale=-1.0), reads=[b_pb[5 + d]], writes=[bT[e_]])
                            yield
                            S.op("act", lambda e, T=T: e.activation(T[one_e], T[e_], AF.Ln, bias=1.0), reads=[bT[e_]], writes=[bT[one_e]])
                            yield
                            S.op("act", lambda e, T=T: e.activation(T[sig], T[one_e], AF.Exp, scale=-1.0), reads=[bT[one_e]], writes=[bT[sig]])
                            yield
                            S.op("dve", lambda e, T=T, omc=omc: e.scalar_tensor_tensor(T[kk], T[e_], omc, T[sig], ALU.mult, ALU.mult), reads=[bT[e_], bT[sig], b_lb], writes=[bT[kk]])
                            yield
                            S.op("act", lambda e, T=T, omc=omc, lbc=lbc: e.activation(T[lf], T[sig], AF.Ln, bias=lbc, scale=omc), reads=[bT[sig], b_lb], writes=[bT[lf]])
                            yield
                            S.op("dve", lambda e, T=T, nb=nb: e.tensor_tensor_scan(T[bb], rmk[:, 0:nb], T[lf], 0.0, ALU.mult, ALU.add), reads=[b_rmk, bT[lf]], writes=[bT[bb]])
                            b3 = T[bb].rearrange("p (c t) -> p c t", t=64)
                            btot = b3[:, :, 63:64]
                            if d == 0:
                                cview, bc_ = T[bb], bT[bb]
                            else:
                                yield
                                S.op("dve", lambda e, T=T: e.tensor_sub(T[cc_], T[lf], T[bb]), reads=[bT[lf], bT[bb]], writes=[bT[cc_]])
                                c3 = T[cc_].rearrange("p (c t) -> p c t", t=64)
                                yield
                                S.op("dve", lambda e, c3=c3, btot=btot, nck=nck: e.tensor_add(c3, c3, btot.to_broadcast([128, nck, 64])), reads=[bT[cc_], bT[bb]], writes=[bT[cc_]])
                                cview, bc_ = T[cc_], bT[cc_]
                            sl = slice(t0, t0 + nb)
                            def tail_scores():
                                MID = 31 if d == 0 else 32
                                cm3 = T[one_e].rearrange("p (c t) -> p c t", t=64); cv3m = cview.rearrange("p (c t) -> p c t", t=64)
                                yield
                                S.op("dve", lambda e, cm3=cm3, cv3m=cv3m, nck=nck, MID=MID: e.tensor_sub(cm3, cv3m, cv3m[:, :, MID:MID + 1].to_broadcast([128, nck, 64])),
                                     reads=[bc_, bT[E1]], writes=[bT[one_e]])
                                yield
                                S.op("act", lambda e, T=T: e.activation(T[E1], T[one_e], AF.Exp), reads=[bT[one_e]], writes=[bT[E1]])
                                yield
                                S.op("dve", lambda e, T=T, qa=qa, d=d, sl=sl, nb=nb: e.scalar_tensor_tensor(QS_T[d][:, sl], qs_[qa][:, 0:nb], QS, T[E1], ALU.mult, ALU.mult),
                                     reads=[b_qs[qa], bT[E1]], writes=[b_QST[d]])
                                yield
                                S.op("act", lambda e, T=T: e.activation(T[E1], T[one_e], AF.Exp, scale=-1.0), reads=[bT[one_e], b_QST[d]], writes=[bT[E1]])
                                yield
                                S.op("dve", lambda e, T=T, d=d, sl=sl: e.tensor_tensor(KT[d][:, sl], T[kk], T[E1], ALU.mult), reads=[bT[kk], bT[E1]], writes=[b_KT[d]])
                                yield
                                S.op("act", lambda e, d=d, c0=c0, nck=nck, cv3m=cv3m, MID=MID: e.activation(EM[d][:, c0:c0 + nck], cv3m[:, :, MID:MID + 1].rearrange("p c o -> p (c o)"), AF.Exp), reads=[bc_], writes=[b_EM[d]])

                            def tail_state():
                                d3 = T[dd].rearrange("p (c t) -> p c t", t=64); cv3 = cview.rearrange("p (c t) -> p c t", t=64)
                                yield
                                S.op("dve", lambda e, d3=d3, cv3=cv3, btot=btot, nck=nck: e.tensor_sub(d3, btot.to_broadcast([128, nck, 64]), cv3), reads=[bc_, bT[bb]], writes=[bT[dd]])
                                yield
                                S.op("act", lambda e, T=T: e.activation(T[dd], T[dd], AF.Exp), reads=[bT[dd]], writes=[bT[dd]])
                                yield
                                S.op("dve", lambda e, T=T, d=d, sl=sl: e.tensor_tensor(KH[d][:, sl], T[kk], T[dd], ALU.mult), reads=[bT[kk], bT[dd]], writes=[b_KH[d]])
                                yield
                                S.op("act", lambda e, d=d, c0=c0, nck=nck, btot=btot: e.activation(EB[d][:, c0:c0 + nck], btot.rearrange("p c o -> p (c o)"), AF.Exp), reads=[bT[bb]], writes=[b_EB[d]])

                            ga_, gb_ = tail_scores(), tail_state()
                            live_ = [ga_, gb_]
                            while live_:
                                for g_ in list(live_):
                                    try:
                                        next(g_)
                                    except StopIteration:
                                        live_.remove(g_)
                                yield
                        run_rr([gate_chain(0), gate_chain(1)])
                    if cut == "B":
                        return
                    for d in range(2):
                        for i in scan_tiles:
                            pbi = 1 + (i % 2)
                            S.op("pe", lambda e, d=d, i=i, pbi=pbi: e.transpose(pbb(pbi, 128), KH[d][:, i * 128:(i + 1) * 128], id_bf[:]), reads=[b_KH[d], b_id_bf], writes=[b_pb[pbi]])
                            S.op("act", lambda e, d=d, i=i, pbi=pbi: e.copy(KHt[d][:, i * 128:(i + 1) * 128], pbb(pbi, 128)), reads=[b_pb[pbi]], writes=[b_KHt[d]])
                    if cut == "C":
                        return
                    order = [list(scan_tiles), [t for t in (1, 0) if t in scan_tiles] + [t for t in range(NT - 1, 1, -1) if t in scan_tiles]]
                    TB = [tmp[a_][j_] for a_ in range(2) for j_ in range(NTMP)]; bTB = [b_tmp[a_][j_] for a_ in range(2) for j_ in range(NTMP)]
                    import os as _os2
                    d_stage = int(_os2.environ.get("HG_D_STAGE", "0")) if cut else 0
                    seqs = []
                    for d in range(1 if d_stage else 2):
                        seq = [(i, cpos) for i in order[d] for cpos in ((0, 1) if d == 0 else (1, 0))]
                        seqs.append(seq)
                        nstep = len(order[d])
                        for cpos in (0, 1):
                            base = cpos * nstep
                            s_ = base
                            while s_ < base + nstep:
                                g_end = min(base + nstep, (s_ // 4 + 1) * 4)
                                pbk = 3 + 2 * cpos + ((s_ // 4) % 2)
                                for sl_ in range(s_, g_end):
                                    i = order[d][sl_ - base]; q = sl_ % 4
                                    ps_ = slice(cpos * 64, cpos * 64 + 64); ts_ = slice(i * 128, (i + 1) * 128)
                                    S.op("pe", lambda e, d=d, ts_=ts_, ps_=ps_, pbk=pbk, q=q, Vh=Vh: e.matmul(pbank[pbk][:, q * 128:(q + 1) * 128], KHt[d][ps_, ts_], Vh[ps_, ts_], start=True, stop=True),
                                         reads=[b_KHt[d], b_V], writes=[b_pb[pbk]], inc=(sl_ == g_end - 1))
                                tb = 9 * d + s_ // 4; c_lo, c_hi = (s_ % 4) * 128, ((g_end - 1) % 4 + 1) * 128
                                S.op("act", lambda e, tb=tb, pbk=pbk, c_lo=c_lo, c_hi=c_hi, TB=TB: e.copy(TB[tb][:, c_lo:c_hi], pbank[pbk][:, c_lo:c_hi]), reads=[b_pb[pbk]], writes=[bTB[tb]])
                                s_ = g_end
                    if d_stage == 1:
                        cut_dump(TB[0][:, 0:512], bTB[0:9]); return "cut"
                    RING = 18
                    b_slot = [[Buf("st%d_%d" % (d_, r_)) for r_ in range(RING)] for d_ in range(2)]
                    sslot = lambda d_, k: (KH[d_][:, (k % RING) * 128:(k % RING + 1) * 128], b_slot[d_][k % RING])
                    S3 = [Sst[d_] + [Sbf_f32[d_]] for d_ in range(2)]; b_S3 = [b_S[d_] + [b_Sx[d_]] for d_ in range(2)]

                    produced = [0, 0]
                    consumed = [0, 0]

                    def chain_gen(d):
                        seq = seqs[d]; nstep = len(order[d])
                        yield
                        S.op("dve", lambda e, d=d, S3=S3: e.memset(S3[d][0][:], 0.0), writes=[b_S3[d][0]])
                        for k, (i, cpos) in enumerate(seq[:-1]):
                            sl_ = cpos * nstep + k // 2
                            ch = i * 2 + cpos; tb = 9 * d + sl_ // 4; q = sl_ % 4
                            si, so = k % 3, (k + 1) % 3
                            yield
                            S.op("dve", lambda e, d=d, si=si, so=so, ch=ch, tb=tb, q=q, TB=TB, S3=S3: e.scalar_tensor_tensor(S3[d][so][:], S3[d][si][:], EB[d][:, ch:ch + 1], TB[tb][:, q * 128:(q + 1) * 128], ALU.mult, ALU.add),
                                 reads=[b_S3[d][si], b_EB[d], bTB[tb]], writes=[b_S3[d][so]])
                            dst, bdst = sslot(d, k + 1)
                            yield
                            while (k + 1) - consumed[d] >= RING:
                                yield
                            i2, cp2 = seq[k + 1]; ch2 = i2 * 2 + cp2
                            S.op("act", lambda e, d=d, so=so, dst=dst, S3=S3, ch2=ch2: e.activation(dst, S3[d][so][:], AF.Identity, scale=EM[d][:, ch2:ch2 + 1]), reads=[b_S3[d][so], b_EM[d]], writes=[bdst])
                            produced[d] = k + 1

                    def out_gen(d):
                        yield
                        S.op("dve", lambda e, d=d: e.memset(AT[d][:], 0.0), writes=[b_AT[d]])
                        for step, i in enumerate(order[d]):
                            if i not in out_tiles:
                                consumed[d] = 2 * (step + 1)
                                continue
                            ts_ = slice(i * 128, (i + 1) * 128); par = step % 2
                            p_sc, p_o = (5, 6)[d], ((7, 0)[d])
                            yield
                            S.op("pe", lambda e, d=d, ts_=ts_, p_sc=p_sc: e.matmul(pbf(p_sc, 128), KT[d][:, ts_], QS_T[d][:, ts_], start=True, stop=True),
                                 reads=[b_KT[d], b_QST[d]], writes=[b_pb[p_sc]])
                            yield
                            S.op("dve", lambda e, d=d, p_sc=p_sc: e.copy_predicated(AT[d][:], cm_u[:, d * 128:(d + 1) * 128], pbf(p_sc, 128)),
                                 reads=[b_pb[p_sc], b_cmu, b_AT[d]], writes=[b_AT[d]])
                            cps = (0, 1) if d == 0 else (1, 0)
                            need = [(cp, k) for cp, k in zip(cps, (2 * step, 2 * step + 1)) if k > 0]
                            yield
                            while need and produced[d] < max(k_ for _, k_ in need):
                                yield
                            S.op("pe", lambda e, d=d, ts_=ts_, p_o=p_o, nn=len(need), Vh=Vh: e.matmul(pbf(p_o, 128), AT[d][:], Vh[:, ts_], start=True, stop=(nn == 0)),
                                 reads=[b_AT[d], b_V], writes=[b_pb[p_o]], inc=(len(need) == 0))
                            for j, (cp, k) in enumerate(need):
                                ps_ = slice(cp * 64, cp * 64 + 64); tsc = slice(i * 128 + cp * 64, i * 128 + cp * 64 + 64)
                                src, bsrc = sslot(d, k)
                                lastj = j == len(need) - 1
                                if not lastj:
                                    pass
                                S.op("pe", lambda e, d=d, tsc=tsc, ps_=ps_, p_o=p_o, src=src, lastj=lastj: e.matmul(pbank[p_o][ps_, 0:128], QS_T[d][:, tsc], src, start=False, stop=lastj),
                                     reads=[b_QST[d], bsrc], writes=[b_pb[p_o]], inc=lastj)
                            consumed[d] = 2 * (step + 1)
                            yield
                            S.op("act", lambda e, d=d, ts_=ts_, p_o=p_o: e.copy(Oacc[d][:, ts_], pbf(p_o, 128)), reads=[b_pb[p_o]], writes=[b_O[d]])

                    if h == nheads - 1 and not cut and debug != "hg":
                        sc_load(l, 0); sc_load(l, 1); sg_load(l)
                    if d_stage:
                        run_rr([chain_gen(0), out_gen(0)])
                    else:
                        run_rr([chain_gen(0), chain_gen(1), out_gen(0), out_gen(1)] + ([A_gen(h + 1)] if h + 1 < nheads else []))
                    if d_stage == 3:
                        cut_dump(Oacc[0][:, 0:512], [b_O[0]]); return "cut"
                    if cut == "D":
                        return
                    def fin_chain(i, fa):
                        ts_ = slice(i * 128, (i + 1) * 128); pbi = 1 + fa
                        W = finW[fa]
                        yield
                        S.op("dve", lambda e: e.tensor_add(fin[fa][:], Oacc[0][:, ts_], Oacc[1][:, ts_]), reads=[b_O[0], b_O[1]], writes=[b_fin[fa]])
                        yield
                        S.op("act", lambda e: e.activation(W["fsq"][:], fin[fa][:], AF.Square, accum_out=W["fst"][:, 0:1]), reads=[b_fin[fa]], writes=[W["b_fsq"], W["b_fst"]])
                        yield
                        S.op("act", lambda e: e.activation(W["fst"][:, 1:2], W["fst"][:, 0:1], AF.Sqrt, bias=EPS, scale=1.0 / 128), reads=[W["b_fst"]], writes=[W["b_fst"]])
                        yield
                        S.op("dve", lambda e: e.reciprocal(W["fst"][:, 2:3], W["fst"][:, 1:2]), reads=[W["b_fst"]], writes=[W["b_fst"]])
                        yield
                        S.op("dve", lambda e, Gcur=Gcur: e.tensor_tensor(W["ngg"][:], ngb[:, l * 128:(l + 1) * 128], Gcur[:, ts_], ALU.mult), reads=[b_ngb, b_Gcur], writes=[W["b_ngg"]])
                        yield
                        S.op("dve", lambda e: e.scalar_tensor_tensor(W["hgt"][:], fin[fa][:], W["fst"][:, 2:3], W["ngg"][:], ALU.mult, ALU.mult), reads=[b_fin[fa], W["b_fst"], W["b_ngg"]], writes=[W["b_hgt"]])
                        yield
                        S.op("pe", lambda e: e.transpose(pbb(pbi, 128), W["hgt"][:], id_bf[:]), reads=[W["b_hgt"], b_id_bf], writes=[b_pb[pbi]])
                        yield
                        S.op("act", lambda e: e.copy(HGT[:, ts_], pbb(pbi, 128)), reads=[b_pb[pbi]], writes=[b_HGT])

                    Gcur, b_Gcur = Gh, b_G
                    ot_ = list(out_tiles)

                    def fin_stream(k, ot_=ot_):
                        for i in ot_[k::2]:
                            yield from fin_chain(i, k)

                    run_rr([fin_stream(0), fin_stream(1)])
                    S.dma("sp", mixT[h * 128:(h + 1) * 128, :], HGT[:], reads=[b_HGT], writes=[b_mixT[h]])

            if cut:
                S.barrier()
                S.op("act", lambda e, Vh=Vh: e.copy(Oacc[1][:], Vh[:]), reads=[b_V], writes=[b_O[1]])
                S.dma("sp", outs["dbg_cut"][:, 0:N], Oacc[1][:], reads=[b_O[1]])
                if cut in "DE":
                    S.dma("sp", outs["dbg_cut"][:, N:2 * N], Oacc[0][:], reads=[b_O[0]])

            scw_s = sb("scw_s", [128, DEPTH * 6], F32); b_scw = Buf("scw")
            S.dma("sp", scw_s[:], scw, writes=[b_scw])
            lngb = sb("lngb", [128, 512], F32); b_lngb = Buf("lngb")
            WsT = sb("WsT", [128, 4 * 128], BF16); b_WsT = Buf("WsT")
            wsn = sb("wsn", [128, 128], BF16); b_wsn = Buf("wsn")
            BS = sb("BS", [128, 2 * 128], F32); b_BS = Buf("BS")
            SEQS = [(0, 256), (256, N)]
            SEGS = [(0, 256), (256, 512), (512, 1024), (1024, 1536), (1536, 2048), (2048, N)]

            pre_done = {}

            def sc_load(l, cc):
                wi_ = w_in[l].rearrange("(k p) n -> p k n", p=128); a_ = cc % 2
                for ci, c0 in enumerate((2560 + cc * 128, 2816 + cc * 128, 3072 + cc * 128)):
                    S.dma("pool", wzq[a_][:].rearrange("p (k n) -> p k n", k=KC)[:, :, ci * 128:(ci + 1) * 128], wi_[:, :, c0:c0 + 128], writes=[b_wzq[a_]])
                pre_done[("sc", l, cc)] = True

            def sg_load(l):
                wi_ = w_in[l].rearrange("(k p) n -> p k n", p=128)
                S.dma("pool", wvg[0][:].rearrange("p (k n) -> p k n", k=KC), wi_[:, :, 3328:3584], writes=[b_wvg[0]])
                S.dma("pool", wvg[1][:].rearrange("p (k n) -> p k n", k=KC), wi_[:, :, 3584:3840], writes=[b_wvg[1]])
                S.dma("sp", lngb[:, 0:256], sgln[2 * l:2 * l + 1, :].partition_broadcast(128), writes=[b_lngb])
                S.dma("sp", lngb[:, 256:512], sgln[2 * l + 1:2 * l + 2, :].partition_broadcast(128), writes=[b_lngb])
                for cc in range(2):
                    S.dma("sp", BS[:, cc * 128:(cc + 1) * 128], sgb[2 * l + cc], writes=[b_BS])
                pre_done[("sg", l)] = True

            def sc_layer(l, tiles):
                wi = w_in[l].rearrange("(k p) n -> p k n", p=128)
                tmax = (max(tiles) + 1) * 128; tmin = min(tiles) * 128
                Pf, GBf, b_P, b_GB = Oacc[0], Oacc[1], b_O[0], b_O[1]
                for cc in range(2):
                    a = cc % 2
                    if not pre_done.get(("sc", l, cc)):
                        sc_load(l, cc)
                    sc_blks = [(t0, min(512, tmax - t0)) for t0 in range(tmin, tmax, 512)]
                    for (t0, nb) in sc_blks:
                        tiles_in = list(range(t0 // 128, (t0 + nb) // 128))
                        for ci in range(3):
                            for kc in range(KC):
                                S.op("pe", lambda e, ci=ci, kc=kc, t0=t0, nb=nb, a=a: e.matmul(pbf(5 + ci, nb), wzq[a][:, kc * 384 + ci * 128:kc * 384 + (ci + 1) * 128],
                                     xmT[:, kc * N + t0:kc * N + t0 + nb], start=(kc == 0), stop=(kc == KC - 1)),
                                     reads=[b_wzq[a]] + [b_xmT[i] for i in tiles_in], writes=[b_pb[5 + ci]], inc=(kc == KC - 1))
                        T0 = tmp[0][0][:, 0:nb]
                        S.op("act", lambda e, t0=t0, nb=nb: e.copy(GBf[:, t0:t0 + nb], pbf(5, nb)), reads=[b_pb[5]], writes=[b_GB])
                        S.op("act", lambda e, T0=T0, nb=nb: e.copy(T0, pbf(6, nb)), reads=[b_pb[6]], writes=[b_tmp[0][0]])
                        S.op("dve", lambda e, T0=T0, t0=t0, nb=nb: e.tensor_tensor(Pf[:, t0:t0 + nb], T0, pbf(7, nb), ALU.mult), reads=[b_tmp[0][0], b_pb[7]], writes=[b_P])
                    wb = l * 6 + cc * 3
                    w0_, w1_, w2_ = scw_s[:, wb:wb + 1], scw_s[:, wb + 1:wb + 2], scw_s[:, wb + 2:wb + 3]
                    for (a0, a1) in SEGS:
                        if a0 < tmin or a0 >= tmax:
                            continue
                        s0, s1 = [sq for sq in SEQS if sq[0] <= a0 < sq[1]][0]
                        Y = tmp[1][0]; bY = b_tmp[1][0]; n_ = a1 - a0
                        S.op("dve", lambda e, Y=Y, a0=a0, a1=a1, n_=n_, w1_=w1_: e.tensor_scalar(Y[:, 0:n_], Pf[:, a0:a1], w1_, None, ALU.mult), reads=[b_P, b_scw], writes=[bY])
                        lo = max(a0, s0 + 1)
                        S.op("dve", lambda e, Y=Y, a0=a0, a1=a1, lo=lo, w0_=w0_: e.scalar_tensor_tensor(Y[:, lo - a0:a1 - a0], Pf[:, lo - 1:a1 - 1], w0_, Y[:, lo - a0:a1 - a0], ALU.mult, ALU.add),
                             reads=[b_P, b_scw, bY], writes=[bY])
                        hi = min(a1, s1 - 1)
                        S.op("dve", lambda e, Y=Y, a0=a0, hi=hi, w2_=w2_: e.scalar_tensor_tensor(Y[:, 0:hi - a0], Pf[:, a0 + 1:hi + 1], w2_, Y[:, 0:hi - a0], ALU.mult, ALU.add),
                             reads=[b_P, b_scw, bY], writes=[bY])
                        S.op("dve", lambda e, Y=Y, a0=a0, a1=a1, n_=n_: e.tensor_tensor(HGT[:, a0:a1], GBf[:, a0:a1], Y[:, 0:n_], ALU.mult), reads=[b_GB, bY], writes=[b_HGT])
                    S.dma("sp", mixT[(4 + cc) * 128:(5 + cc) * 128, tmin:tmax], HGT[:, tmin:tmax], reads=[b_HGT], writes=[b_mixT[4 + cc]])

            def sg_layer(l, tiles):
                wi = w_in[l].rearrange("(k p) n -> p k n", p=128)
                tmax = (max(tiles) + 1) * 128; tmin = min(tiles) * 128
                if not pre_done.get(("sg", l)):
                    sg_load(l)
                for g in range(4):
                    S.dma("pool", wsn[:], sgw[4 * l + g], writes=[b_wsn])
                    S.op("pe", lambda e: e.transpose(pbb(1, 128), wsn[:], id_bf[:]), reads=[b_wsn, b_id_bf], writes=[b_pb[1]])
                    S.op("act", lambda e, g=g: e.copy(WsT[:, g * 128:(g + 1) * 128], pbb(1, 128)), reads=[b_pb[1]], writes=[b_WsT])
                SGT, b_SGT = KH, b_KH
                sgW = [dict(vn=sb("sgvn%d" % p_, [128, 256], F32), vhb=sb("sgvh%d" % p_, [128, 256], BF16), st=sb("sgst%d" % p_, [128, 40], F32),
                            b_vn=Buf("sgvn%d" % p_), b_vhb=Buf("sgvh%d" % p_), b_st=Buf("sgst%d" % p_), banks=((3, 4, 5), (6, 7, 0))[p_], par=p_) for p_ in range(2)]

                def sg_chain(i, W):
                    ts_ = slice(i * 128, (i + 1) * 128); pv, pu, pm = W["banks"]; par = W["par"]
                    vn_t, vhb_t, st_t = W["vn"], W["vhb"], W["st"]
                    yield
                    for kc in range(KC):
                        S.op("pe", lambda e, kc=kc: e.matmul(pbf(pv, 256), xmT[:, kc * N + i * 128:kc * N + (i + 1) * 128], wvg[1][:, kc * 256:(kc + 1) * 256],
                             start=(kc == 0), stop=(kc == KC - 1)), reads=[b_xmT[i], b_wvg[1]], writes=[b_pb[pv]], inc=(kc == KC - 1))
                    for g in range(4):
                        yield
                        S.op("dve", lambda e, g=g: e.bn_stats(st_t[:, g * 6:(g + 1) * 6], pbank[pv][:, g * 64:(g + 1) * 64]), reads=[b_pb[pv]], writes=[W["b_st"]])
                    for g in range(4):
                        yield
                        S.op("dve", lambda e, g=g: e.bn_aggr(st_t[:, 24 + 2 * g:26 + 2 * g], st_t[:, g * 6:(g + 1) * 6]), reads=[W["b_st"]], writes=[W["b_st"]])
                    mv = st_t[:, 24:32].rearrange("p (g two) -> p g two", two=2)
                    yield
                    S.op("act", lambda e: e.activation(st_t[:, 32:36], mv[:, :, 1], AF.Sqrt, bias=EPS), reads=[W["b_st"]], writes=[W["b_st"]])
                    yield
                    S.op("dve", lambda e: e.reciprocal(st_t[:, 36:40], st_t[:, 32:36]), reads=[W["b_st"]], writes=[W["b_st"]])
                    for g in range(4):
                        yield
                        S.op("dve", lambda e, g=g: e.tensor_scalar(vn_t[:, g * 64:(g + 1) * 64], pbank[pv][:, g * 64:(g + 1) * 64], st_t[:, 24 + 2 * g:25 + 2 * g], st_t[:, 36 + g:37 + g],
                             ALU.subtract, ALU.mult), reads=[b_pb[pv], W["b_st"]], writes=[W["b_vn"]])
                    yield
                    S.op("dve", lambda e: e.tensor_mul(vn_t[:], vn_t[:], lngb[:, 0:256]), reads=[W["b_vn"], b_lngb], writes=[W["b_vn"]])
                    yield
                    S.op("dve", lambda e: e.tensor_add(vhb_t[:], vn_t[:], lngb[:, 256:512]), reads=[W["b_vn"], b_lngb], writes=[W["b_vhb"]])
                    yield
                    for cc in range(2):
                        for kc in range(KC):
                            S.op("pe", lambda e, cc=cc, kc=kc: e.matmul(pbank[pu][:, cc * 128:(cc + 1) * 128], wvg[0][:, kc * 256 + cc * 128:kc * 256 + (cc + 1) * 128], xmT[:, kc * N + i * 128:kc * N + (i + 1) * 128],
                                 start=(kc == 0), stop=(kc == KC - 1)), reads=[b_wvg[0], b_xmT[i]], writes=[b_pb[pu]], inc=(kc == KC - 1 and cc == 1))
                    yield
                    for cc in range(2):
                        for gg in range(2):
                            g = 2 * cc + gg
                            S.op("pe", lambda e, g=g, gg=gg, cc=cc: e.matmul(pbank[pm][gg * 64:(gg + 1) * 64, cc * 128:(cc + 1) * 128], vhb_t[:, g * 64:(g + 1) * 64], WsT[:, g * 128:(g + 1) * 128], start=True, stop=True),
                                 reads=[W["b_vhb"], b_WsT], writes=[b_pb[pm]], inc=(gg == 1 and cc == 1))
                    for cc in range(2):
                        T1, T2 = tmp[cc][1 + 2 * par][:, 0:128], tmp[cc][2 + 2 * par][:, 0:128]
                        bT1, bT2 = b_tmp[cc][1 + 2 * par], b_tmp[cc][2 + 2 * par]
                        yield
                        S.op("dve", lambda e, T1=T1, cc=cc: e.tensor_tensor(T1, pbank[pm][:, cc * 128:(cc + 1) * 128], BS[:, cc * 128:(cc + 1) * 128], ALU.add), reads=[b_pb[pm], b_BS], writes=[bT1])
                        yield
                        S.op("act", lambda e, T2=T2, cc=cc: e.copy(T2, pbank[pu][:, cc * 128:(cc + 1) * 128]), reads=[b_pb[pu]], writes=[bT2])
                        yield
                        S.op("dve", lambda e, T1=T1, T2=T2, cc=cc: e.tensor_tensor(SGT[cc][:, ts_], T1, T2, ALU.mult), reads=[bT1, bT2], writes=[b_SGT[cc]])

                tl_ = list(tiles)

                def sg_stream(k):
                    for i in tl_[k::2]:
                        yield from sg_chain(i, sgW[k])

                run_rr([sg_stream(0), sg_stream(1)])
                for cc in range(2):
                    S.dma("sp", mixT[(6 + cc) * 128:(7 + cc) * 128, tmin:tmax], SGT[cc][:, tmin:tmax], reads=[b_SGT[cc]], writes=[b_mixT[6 + cc]])

            r = hgrn_layer(l, scan_tiles, out_tiles)
            if r == "cut":
                st.enter_context(ms)
                return "cut"
            if debug != "hg":
                sc_layer(l, out_tiles)
                sg_layer(l, out_tiles)
            if debug in ("hg", "mix"):
                mixer.dbg = (Oacc[0], b_O[0], HGT, b_HGT)
                st.enter_context(ms)
                return None
            S.barrier(); ms.close()
            return None

        if mixer(0, list(range(NT)), list(range(NT))) == "cut":
            return nc

        ALPHA = (2 * DEPTH) ** 0.25
        x1d = nc.dram_tensor("x1d", [N, D], F32, kind="Internal").ap()
        b_x1d = [Buf("x1d%d" % i) for i in range(NT)]

        def gate_bcast(dst, b_dst, l, slot, srcm, scr, b_scr):
            for kc in range(KC):
                g = modv(l, slot, kc, srcm)
                S.op("dve", lambda e, g=g: e.tensor_scalar(scr[:], id_f[:], 0.0, g, ALU.mult, ALU.add), reads=[b_id_f, b_mod], writes=[b_scr])
                S.op("pe", lambda e: e.matmul(pbf(0, 128), scr[:], id_f[:], start=True, stop=True), reads=[b_scr, b_id_f], writes=[b_pb[0]])
                S.op("act", lambda e, kc=kc: e.copy(dst[:, kc * 128:(kc + 1) * 128], pbf(0, 128)), reads=[b_pb[0]], writes=[b_dst])

        def phase_wout_ln1(l, src_ap, tiles, src_bufs=None):
            ps4 = ExitStack()
            sb4 = lambda n, s_, dt: ps4.enter_context(nc.sbuf_tensor("%s_p4L%d" % (n, l), s_, dt))
            mixS = sb4("mixS", [128, KC * N], BF16); b_mixS = [Buf("mixS%d" % k) for k in range(KC)]
            wo = sb4("wo", [128, KC * D], BF16); b_wo = Buf("wo")
            gbc = [sb4("gbc%d" % i, [128, D], F32) for i in range(2)]; b_gbc = [Buf("gbc0"), Buf("gbc1")]
            lg = sb4("lg", [128, D], F32); lb_ = sb4("lb_", [128, D], F32); b_lg, b_lbb = Buf("lg"), Buf("lbb")
            scr = sb4("scr", [128, 128], F32); b_scr = Buf("scr")
            for k in range(KC):
                S.dma("sp", mixS[:, k * N:(k + 1) * N], mixT[k * 128:(k + 1) * 128, :], reads=[b_mixT[k]], writes=[b_mixS[k]])
            S.dma("pool", wo[:].rearrange("p (k n) -> p k n", k=KC), w_out[l].rearrange("(k p) n -> p k n", p=128), writes=[b_wo])
            S.dma("sp", lg[:], lnp[4 * l:4 * l + 1, :].partition_broadcast(128), writes=[b_lg])
            S.dma("sp", lb_[:], lnp[4 * l + 1:4 * l + 2, :].partition_broadcast(128), writes=[b_lbb])
            gate_bcast(gbc[0], b_gbc[0], l, 2, 0, scr, b_scr)
            if any(i < 2 for i in tiles):
                gate_bcast(gbc[1], b_gbc[1], l, 2, 1, scr, b_scr)
            G4 = 3
            sets = mk_sets(sb4, 2 * G4, "b", with_y=True, plan=[(3, 3, 1), (4, 4, 2), (5, 5, 6)])
            load = lambda i, B_: S.dma("sp", B_["xq"][:], src_ap[i * 128:(i + 1) * 128, :], reads=([src_bufs[i]] if src_bufs else []), writes=[B_["b_xq"]])

            def chain4(i, B_):
                srcm = 1 if i < 2 else 0
                xq_t, yt_t, xt_t, st_t = B_["xq"], B_["yt"], B_["xt"], B_["st"]
                for hf in range(2):
                    pbi = B_["mm"][hf]; hs = slice(hf * 512, (hf + 1) * 512)
                    yield
                    for kc in range(KC):
                        S.op("pe", lambda e, kc=kc, hf=hf, pbi=pbi: e.matmul(pbf(pbi, 512), mixS[:, kc * N + i * 128:kc * N + (i + 1) * 128],
                             wo[:, kc * D + hf * 512:kc * D + (hf + 1) * 512], start=(kc == 0), stop=(kc == KC - 1)),
                             reads=[b_mixS[kc], b_wo], writes=[b_pb[pbi]], inc=(kc == KC - 1))
                    yield
                    S.op("dve", lambda e, hs=hs, pbi=pbi: e.tensor_tensor(yt_t[:, hs], pbf(pbi, 512), gbc[srcm][:, hs], ALU.mult),
                         reads=[b_pb[pbi], b_gbc[srcm]], writes=[B_["b_yt"]])
                    yield
                    S.op("dve", lambda e, hs=hs: e.scalar_tensor_tensor(yt_t[:, hs], xq_t[:, hs], ALPHA, yt_t[:, hs], ALU.mult, ALU.add),
                         reads=[B_["b_xq"], B_["b_yt"]], writes=[B_["b_yt"]])
                yield from ln_stats_gen(yt_t, B_["b_yt"], B_)
                yield
                S.op("dve", lambda e: e.scalar_tensor_tensor(yt_t[:], yt_t[:], st_t[:, 12:13], lg[:], ALU.subtract, ALU.mult),
                     reads=[B_["b_yt"], B_["b_st"], b_lg], writes=[B_["b_yt"]])
                yield
                S.op("dve", lambda e: e.scalar_tensor_tensor(xt_t[:], yt_t[:], st_t[:, 15:16], lb_[:], ALU.mult, ALU.add),
                     reads=[B_["b_yt"], B_["b_st"], b_lbb, B_["b_xt"]], writes=[B_["b_xt"]])
                yield
                S.dma("sp", x1d[i * 128:(i + 1) * 128, :], xt_t[:], reads=[B_["b_xt"]], writes=[b_x1d[i]])
                yield from ln_mod_T_gen(l, i, 3, 4, B_)

            run_groups(tiles, sets, load, chain4, GRP=G4)
            S.barrier(); ps4.close()

        if debug in ("x1", "h", "x2", None):
            phase_wout_ln1(0, xin, list(range(NT)))
        if debug == "x1":
            for i in range(NT):
                a = i % 2
                S.dma("sp", xt[a][:], x1d[i * 128:(i + 1) * 128, :], reads=[b_x1d[i]], writes=[b_xt[a]])
                S.dma("sp", outs["dbg_x1"][i * 128:(i + 1) * 128, :], xt[a][:], reads=[b_xt[a]])
            xm_f = sb("xm2_f", [128, N], F32); b_xmf = Buf("xm2f")
            for kc in range(KC):
                S.op("act", lambda e, kc=kc: e.copy(xm_f[:], xmT[:, kc * N:(kc + 1) * N]), reads=b_xmT, writes=[b_xmf])
                S.dma("sp", outs["dbg_xm2T"][:, kc * N:(kc + 1) * N], xm_f[:], reads=[b_xmf])

        hTd = nc.dram_tensor("hTd", [FF, N], BF16, kind="Internal").ap()
        b_hTd = [Buf("hTd%d" % i) for i in range(NFC)]
        x2d = [nc.dram_tensor("x2d%d" % i, [N, D], F32, kind="Internal").ap() for i in range(DEPTH - 1)]
        b_x2d = [Buf("x2d%d" % i) for i in range(NT)]
        GW = 64

        def phase_ffn_up(l, with_ctx):
            p5 = ExitStack()
            sb5 = lambda n, s_, dt: p5.enter_context(nc.sbuf_tensor("%s_p5L%d" % (n, l), s_, dt))
            fcw_s = sb5("fcw_s", [128, NFC * 9], F32); fcb_s = sb5("fcb_s", [128, NFC], F32); b_fcw, b_fcb = Buf("fcw"), Buf("fcb")
            S.dma("sp", fcw_s[:], fcw[:, l * NFC * 9:(l + 1) * NFC * 9], writes=[b_fcw])
            S.dma("sp", fcb_s[:], fcb[:, l * NFC:(l + 1) * NFC], writes=[b_fcb])
            wag = [sb5("wag%d" % i, [128, KC * 256], BF16) for i in range(2)]; b_wag = [Buf("wag0"), Buf("wag1")]
            apx = [sb5("apx%d" % i, [128, 34 * 66], BF16) for i in range(2)]; apc = [sb5("apc%d" % i, [128, 258], BF16) for i in range(2)]
            b_ap = [Buf("ap0"), Buf("ap1")]
            dg = [sb5("dg%d" % i, [128, 9 * 128], BF16) for i in range(2)]; b_dg = [Buf("dg0"), Buf("dg1")]
            gel = [sb5("gel%d" % i, [128, 512], F32) for i in range(2)]; b_gel = [Buf("gel0"), Buf("gel1")]
            htc = [sb5("htc%d" % i, [128, N], BF16) for i in range(2)]; b_htc = [Buf("htc0"), Buf("htc1")]
            for i in range(2):
                S.op("pool", lambda e, i=i: e.memset(apx[i][:], 0.0), writes=[b_ap[i]])
                S.op("pool", lambda e, i=i: e.memset(apc[i][:], 0.0), writes=[b_ap[i]])
            FB = ([("c", 0, 256, 0)] if with_ctx else []) + [("x", 256 + 512 * j, 512, j) for j in range(4)]
            wu = ffn_up[l].rearrange("(k p) n -> p k n", p=128)
            defer_l = l + 1 if (l + 1 < DEPTH) else None
            if defer_l is not None:
                awd = [sb5("awd%d" % i, [128, KC * 512], BF16) for i in range(2)]; b_awd = [Buf("awd0"), Buf("awd1")]
                pm_d, b_pmd = pbf(1, 96), b_pb[1]
            for fc in range(NFC):
                a = fc % 2
                if defer_l is not None:
                    if fc < 12:
                        mod_chunk_load(defer_l, fc, awd[fc % 2], b_awd[fc % 2])
                    if 1 <= fc <= 12:
                        mod_chunk_mm(defer_l, fc - 1, awd[(fc - 1) % 2], b_awd[(fc - 1) % 2], pm_d, b_pmd)
                    if fc == 13:
                        mod_finish(defer_l, pm_d, b_pmd)
                for ci, c0 in enumerate((fc * 128, FF + fc * 128)):
                    S.dma("pool", wag[a][:].rearrange("p (k n) -> p k n", k=KC)[:, :, ci * 128:(ci + 1) * 128], wu[:, :, c0:c0 + 128], writes=[b_wag[a]])
                for tap in range(9):
                    wcol = fcw_s[:, fc * 9 + tap:fc * 9 + tap + 1]
                    S.op("dve", lambda e, a=a, tap=tap, wcol=wcol: e.tensor_scalar(dg[a][:, tap * 128:(tap + 1) * 128], id_f[:], wcol, None, ALU.mult),
                         reads=[b_id_f, b_fcw], writes=[b_dg[a]])
                apx3 = apx[a][:].rearrange("p (r c) -> p r c", c=66)
                for bn, (kind, t0, nb, j) in enumerate(FB):
                    pa = (3, 6)[bn % 2]
                    tl = list(range(t0 // 128, (t0 + nb) // 128))
                    for kc in range(KC):
                        S.op("pe", lambda e, a=a, kc=kc, t0=t0, nb=nb, pa=pa: e.matmul(pbf(pa, nb), wag[a][:, kc * 256:kc * 256 + 128], xmT[:, kc * N + t0:kc * N + t0 + nb],
                             start=(kc == 0), stop=(kc == KC - 1)), reads=[b_wag[a]] + [b_xmT[i] for i in tl], writes=[b_pb[pa]], inc=(kc == KC - 1))
                    if kind == "c":
                        S.op("act", lambda e, a=a, pa=pa: e.copy(apc[a][:, 1:257], pbf(pa, 256)), reads=[b_pb[pa]], writes=[b_ap[a]])
                    else:
                        S.op("act", lambda e, a=a, pa=pa, j=j, apx3=apx3: e.copy(apx3[:, 1 + 8 * j:9 + 8 * j, 1:65], pbf(pa, 512).rearrange("p (r c) -> p r c", c=GW)),
                             reads=[b_pb[pa]], writes=[b_ap[a]])
                for bn, (kind, t0, nb, j) in enumerate(FB):
                    pc, pg = (4, 7)[bn % 2], (5, 0)[bn % 2]
                    ga = bn % 2
                    tl = list(range(t0 // 128, (t0 + nb) // 128))
                    if kind == "c":
                        for n_, dj in enumerate(range(3)):
                            tap = 3 + dj
                            S.op("pe", lambda e, a=a, tap=tap, dj=dj, pc=pc, n_=n_: e.matmul(pbf(pc, 256), dg[a][:, tap * 128:(tap + 1) * 128], apc[a][:, dj:dj + 256], start=(n_ == 0), stop=(n_ == 2)),
                                 reads=[b_dg[a], b_ap[a]], writes=[b_pb[pc]], inc=(n_ == 2))
                    else:
                        for tap in range(9):
                            di, dj = tap // 3, tap % 3
                            mv_ = apx3[:, di + 8 * j:di + 8 * j + 8, dj:dj + GW]
                            S.op("pe", lambda e, a=a, tap=tap, mv_=mv_, pc=pc: e.matmul(pbf(pc, 512), dg[a][:, tap * 128:(tap + 1) * 128], mv_, start=(tap == 0), stop=(tap == 8)),
                                 reads=[b_dg[a], b_ap[a]], writes=[b_pb[pc]], inc=(tap == 8))
                    for kc in range(KC):
                        S.op("pe", lambda e, a=a, kc=kc, t0=t0, nb=nb, pg=pg: e.matmul(pbf(pg, nb), wag[a][:, kc * 256 + 128:kc * 256 + 256], xmT[:, kc * N + t0:kc * N + t0 + nb],
                             start=(kc == 0), stop=(kc == KC - 1)), reads=[b_wag[a]] + [b_xmT[i] for i in tl], writes=[b_pb[pg]], inc=(kc == KC - 1))
                    bcol = fcb_s[:, fc:fc + 1]
                    S.op("act", lambda e, ga=ga, nb=nb, pc=pc, bcol=bcol: e.activation(gel[ga][:, 0:nb], pbf(pc, nb), AF.Gelu, bias=bcol), reads=[b_pb[pc], b_fcb], writes=[b_gel[ga]])
                    S.op("dve", lambda e, a=a, ga=ga, t0=t0, nb=nb, pg=pg: e.tensor_tensor(htc[a][:, t0:t0 + nb], gel[ga][:, 0:nb], pbf(pg, nb), ALU.mult),
                         reads=[b_gel[ga], b_pb[pg]], writes=[b_htc[a]])
                lo = FB[0][1]
                S.dma("sp", hTd[fc * 128:(fc + 1) * 128, lo:N], htc[a][:, lo:N], reads=[b_htc[a]], writes=[b_hTd[fc]])
            S.barrier(); p5.close()

        def phase_ffn_down(l, tiles, dst_ap, dst_row0):
            p6 = ExitStack()
            sb6 = lambda n, s_, dt: p6.enter_context(nc.sbuf_tensor("%s_p6L%d" % (n, l), s_, dt))
            wd = sb6("wd", [128, NFC * D], BF16); b_wdp = {(hf_, q_): Buf("wd%d%d" % (hf_, q_)) for hf_ in range(2) for q_ in range(2)}
            gbc = [sb6("gbc%d" % i, [128, D], F32) for i in range(2)]; b_gbc = [Buf("gbc0"), Buf("gbc1")]
            lg = sb6("lg", [128, D], F32); lb_ = sb6("lb_", [128, D], F32); b_lg, b_lbb = Buf("lg"), Buf("lbb")
            scr = sb6("scr", [128, 128], F32); b_scr = Buf("scr")
            wdv = ffn_down[l].rearrange("(f p) n -> p f n", p=128)
            for hf_ in range(2):
                for q_ in range(2):
                    S.dma("pool", wd[:].rearrange("p (f n) -> p f n", f=NFC)[:, q_ * 11:(q_ + 1) * 11, hf_ * 512:(hf_ + 1) * 512],
                          wdv[:, q_ * 11:(q_ + 1) * 11, hf_ * 512:(hf_ + 1) * 512], writes=[b_wdp[(hf_, q_)]])
            S.dma("sp", lg[:], lnp[4 * l + 2:4 * l + 3, :].partition_broadcast(128), writes=[b_lg])
            S.dma("sp", lb_[:], lnp[4 * l + 3:4 * l + 4, :].partition_broadcast(128), writes=[b_lbb])
            gate_bcast(gbc[0], b_gbc[0], l, 5, 0, scr, b_scr)
            if any(i < 2 for i in tiles):
                gate_bcast(gbc[1], b_gbc[1], l, 5, 1, scr, b_scr)
            hv = hTd.rearrange("(f p) n -> p f n", p=128)
            sets = mk_sets(sb6, 2 * GRP, "c", with_y=True, with_h=True)

            def load(i, B_):
                S.dma("sp", B_["ht"][:].rearrange("p (f n) -> p f n", f=NFC), hv[:, :, i * 128:(i + 1) * 128], reads=b_hTd, writes=[B_["b_ht"]])
                S.dma("sp", B_["xq"][:], x1d[i * 128:(i + 1) * 128, :], reads=[b_x1d[i]], writes=[B_["b_xq"]])

            def chain6(i, B_):
                srcm = 1 if i < 2 else 0
                xq_t, yt_t, xt_t, st_t, ht_t = B_["xq"], B_["yt"], B_["xt"], B_["st"], B_["ht"]
                for hf in range(2):
                    pbi = B_["mm"][hf]; hs = slice(hf * 512, (hf + 1) * 512)
                    yield
                    for f_ in range(NFC):
                        S.op("pe", lambda e, f_=f_, hf=hf, pbi=pbi: e.matmul(pbf(pbi, 512), ht_t[:, f_ * 128:(f_ + 1) * 128], wd[:, f_ * D + hf * 512:f_ * D + (hf + 1) * 512],
                             start=(f_ == 0), stop=(f_ == NFC - 1)), reads=[B_["b_ht"], b_wdp[(hf, f_ // 11)]], writes=[b_pb[pbi]], inc=(f_ == NFC - 1))
                    yield
                    S.op("dve", lambda e, hs=hs, pbi=pbi: e.tensor_tensor(yt_t[:, hs], pbf(pbi, 512), gbc[srcm][:, hs], ALU.mult),
                         reads=[b_pb[pbi], b_gbc[srcm]], writes=[B_["b_yt"]])
                    yield
                    S.op("dve", lambda e, hs=hs: e.scalar_tensor_tensor(yt_t[:, hs], xq_t[:, hs], ALPHA, yt_t[:, hs], ALU.mult, ALU.add),
                         reads=[B_["b_xq"], B_["b_yt"]], writes=[B_["b_yt"]])
                yield from ln_stats_gen(yt_t, B_["b_yt"], B_)
                yield
                S.op("dve", lambda e: e.scalar_tensor_tensor(yt_t[:], yt_t[:], st_t[:, 12:13], lg[:], ALU.subtract, ALU.mult),
                     reads=[B_["b_yt"], B_["b_st"], b_lg], writes=[B_["b_yt"]])
                yield
                S.op("dve", lambda e: e.scalar_tensor_tensor(xt_t[:], yt_t[:], st_t[:, 15:16], lb_[:], ALU.mult, ALU.add),
                     reads=[B_["b_yt"], B_["b_st"], b_lbb, B_["b_xt"]], writes=[B_["b_xt"]])
                r0 = i * 128 - dst_row0
                yield
                S.dma("sp", dst_ap[r0:r0 + 128, :], xt_t[:], reads=[B_["b_xt"]], writes=[b_x2d[i]])

            run_groups(tiles, sets, load, chain6)
            S.barrier(); p6.close()

        if debug in ("h", "x2", None):
            phase_ffn_up(0, True)
        if debug == "h":
            hst_b = sb("hst_b", [128, N], BF16); hst_f = sb("hst_f", [128, N], F32); b_hsb, b_hsf = Buf("hsb"), Buf("hsf")
            for fc in range(NFC):
                S.dma("sp", hst_b[:], hTd[fc * 128:(fc + 1) * 128, :], reads=[b_hTd[fc]], writes=[b_hsb])
                S.op("act", lambda e: e.copy(hst_f[:], hst_b[:]), reads=[b_hsb], writes=[b_hsf])
                S.dma("sp", outs["dbg_hT"][fc * 128:(fc + 1) * 128, :], hst_f[:], reads=[b_hsf])
        if debug in ("x2", None):
            phase_ffn_down(0, list(range(NT)), x2d[0], 0)
        if debug == "x2":
            for i in range(NT):
                a = i % 2
                S.dma("sp", xt[a][:], x2d[0][i * 128:(i + 1) * 128, :], reads=[b_x2d[i]], writes=[b_xt[a]])
                S.dma("sp", outs["dbg_x2"][i * 128:(i + 1) * 128, :], xt[a][:], reads=[b_xt[a]])

        if debug is None:
            XT = list(range(2, NT))
            phase_ln_mod_T(1, x2d[0], range(NT), 0, 1, src_bufs=b_x2d)
            mixer(1, list(range(NT)), XT)
            phase_wout_ln1(1, x2d[0], XT, src_bufs=b_x2d)
            phase_ffn_up(1, False)
            phase_ffn_down(1, XT, outs["out"], 256)
        if debug in ("hg", "mix"):
            hg_f, b_hgf, hg_b, b_hgb = mixer.dbg
            for h in range(8 if debug == "mix" else 4):
                S.dma("sp", hg_b[:], mixT[h * 128:(h + 1) * 128, :], reads=[b_mixT[h]], writes=[b_hgb])
                S.op("act", lambda e: e.copy(hg_f[:], hg_b[:]), reads=[b_hgb], writes=[b_hgf])
                S.dma("sp", outs["dbg_hgT"][h * 128:(h + 1) * 128, :], hg_f[:], reads=[b_hgf])
        S.drain_all("sp")
        S.emit()
        build.ninstr = S.ninstr
    return nc


def _prep(inputs, b):
    f = lambda a: np.ascontiguousarray(np.asarray(a, dtype=np.float32))
    m = {}
    m["xin"] = f(np.concatenate([inputs["ctx"][b], inputs["x"][b]], axis=0))
    cc = np.stack([np.asarray(inputs["c"][b]).reshape(KC, 128).T, np.asarray(inputs["c_ctx"]).reshape(KC, 128).T], axis=-1)
    m["c2"] = f(cc.reshape(128, KC * 2))
    m["ada_w"] = f(inputs["ada_w"])
    m["ada_b_fm"] = f(np.asarray(inputs["ada_b"]).reshape(DEPTH, 48, 128).transpose(2, 0, 1).reshape(128, DEPTH * 48))
    m["ident"] = np.eye(128, dtype=np.float32)
    m["w_in"] = f(inputs["w_in"])
    m["w_out"] = f(inputs["w_out"])
    m["ffn_up"] = f(inputs["ffn_up"]); m["ffn_down"] = f(inputs["ffn_down"])
    m["fcw_fm"] = f(np.asarray(inputs["ffn_conv_w"]).reshape(DEPTH, 9, NFC, 128).transpose(3, 0, 2, 1).reshape(128, DEPTH * NFC * 9))
    m["fcb_fm"] = f(np.asarray(inputs["ffn_conv_b"]).reshape(DEPTH, NFC, 128).transpose(2, 0, 1).reshape(128, DEPTH * NFC))
    m["lnp"] = f(np.stack([np.asarray(inputs[k]) for k in ("ln1_g", "ln1_b", "ln2_g", "ln2_b")], axis=1).reshape(DEPTH * 4, D))
    m["scw_fm"] = f(np.asarray(inputs["sc_conv_w"]).reshape(DEPTH, 3, 2, 128).transpose(3, 0, 2, 1).reshape(128, DEPTH * 6))
    m["sgln"] = f(np.stack([np.asarray(inputs["sg_ln_g"]), np.asarray(inputs["sg_ln_b"])], axis=1).reshape(DEPTH * 2, 256))
    m["sg_w"] = f(np.asarray(inputs["sg_w"]).reshape(DEPTH * 4, 128, 128))
    sb_ = np.asarray(inputs["sg_b"]).reshape(DEPTH, 2, 2, 1, 128)
    m["sgb_fm"] = f(np.broadcast_to(sb_, (DEPTH, 2, 2, 64, 128)).reshape(DEPTH * 2, 128, 128))
    m["lbl"] = f(np.asarray(inputs["hg_lb"]).reshape(DEPTH, 2, 4, 128).transpose(3, 0, 1, 2).reshape(128, 16))
    m["hg_norm_g"] = f(inputs["hg_norm_g"])
    ii = np.arange(128)
    same = (ii[:, None] // 64) == (ii[None, :] // 64)
    m["cmask"] = f(np.concatenate([same & (ii[:, None] <= ii[None, :]), same & (ii[:, None] >= ii[None, :])], axis=1))
    rm = np.ones((128, 512), np.float32); rm[:, ::64] = 0.0
    m["rmask"] = rm
    return m


_NC = None


def kernel(**inputs):
    global _NC
    if _NC is None:
        _NC = build()
    shared = None
    maps = []
    for b in range(8):
        m = _prep(inputs, b)
        if shared is None:
            shared = {k: m[k] for k in m if k not in ("xin", "c2")}
        else:
            m.update(shared)
        maps.append(m)
    res = run_bass_kernel_spmd(_NC, maps, core_ids=list(range(8)))
    return np.stack([np.asarray(r["out"], dtype=np.float32) for r in res.results], axis=0)
```

```python
import numpy as np
import concourse.bass as bass
import concourse.mybir as mybir

F32 = mybir.dt.float32
BF16 = mybir.dt.bfloat16
ALU = mybir.AluOpType
AF = mybir.ActivationFunctionType


class Buf:
    __slots__ = ("name", "w", "r", "excl")

    def __init__(self, name="", excl=False):
        self.name = name
        self.excl = excl
        self.w = None
        self.r = []


class Sync:
    ENGS = ("pe", "act", "dve", "pool", "sp")

    SEM_LIMIT = 1900

    def __init__(self, nc, stack, n_dma_sems=20):
        self.nc = nc
        self.stack = stack
        self.owner = {}
        self.cur = {}
        self.nsem = 0
        self.q = {e: [] for e in self.ENGS}
        self.sems = {}
        self.cnt = {}
        self.known = {e: {} for e in self.ENGS}
        for e in ("pe", "act", "dve", "pool"):
            self.cur[e] = None
            self._new_sem(e)
        self.dpool = {}
        self.dk = {}
        for e in ("sp", "act", "pool"):
            self.dpool[e] = []
            for i in range(n_dma_sems):
                key = "d_%s_%d" % (e, i)
                self.sems[key] = stack.enter_context(nc.semaphore(key)); self.nsem += 1
                self.cnt[key] = 0
                self.owner[key] = "dma_" + e
                self.dpool[e].append(key)
            self.dk[e] = 0
        self.pe_pending = False
        self.ninstr = 0

    def _new_sem(self, eng):
        n = sum(1 for k in self.owner if self.owner[k] == eng)
        key = "%s#%d" % (eng, n)
        self.sems[key] = self.stack.enter_context(self.nc.semaphore("s_%s_%d" % (eng, n))); self.nsem += 1
        self.cnt[key] = 0
        self.owner[key] = eng
        self.prev = getattr(self, "prev", {})
        self.prev[eng] = self.cur[eng]
        self.cur[eng] = key

    def _latest(self, eng):
        k = self.cur[eng]
        if self.cnt[k]:
            return (k, self.cnt[k])
        p = self.prev.get(eng)
        return (p, self.cnt[p]) if p and self.cnt[p] else None

    def _wait(self, eng, tok):
        if tok is None:
            return
        key, val = tok
        if self.known[eng].get(key, 0) >= val:
            return
        self.known[eng][key] = val
        sem = self.sems[key]
        self.q[eng].append(lambda e, sem=sem, val=val: e.wait_ge(sem, val))
        self.ninstr += 1

    def _deps(self, eng, reads, writes, skip_self=False):
        for b in reads:
            if b.w is not None and not (skip_self and self.owner[b.w[0]] == eng):
                self._wait(eng, b.w)
            if b.excl:
                for t in b.r:
                    if self.owner[t[0]] != eng:
                        self._wait(eng, t)
        for b in writes:
            if b.w is not None and not (skip_self and self.owner[b.w[0]] == eng):
                self._wait(eng, b.w)
            for t in b.r:
                if not (skip_self and self.owner[t[0]] == eng):
                    self._wait(eng, t)

    def _commit(self, tok, reads, writes):
        for b in reads:
            b.r.append(tok)
            if len(b.r) > 64:
                best = {}
                for k, v in b.r:
                    if best.get(k, 0) < v:
                        best[k] = v
                b.r = list(best.items())
        for b in writes:
            b.w = tok
            b.r = []

    def op(self, eng, fn, reads=(), writes=(), inc=True):
        pe = eng == "pe"
        self._deps(eng, reads, writes, skip_self=pe)
        if self.cnt[self.cur[eng]] >= self.SEM_LIMIT and not (pe and self.pe_pending):
            self._new_sem(eng)
        key = self.cur[eng]
        if inc:
            self.cnt[key] += 1
            tok = (key, self.cnt[key])
            sem = self.sems[key]
            self.q[eng].append(lambda e, fn=fn, sem=sem: fn(e).then_inc(sem, 1))
            if pe:
                self.pe_pending = False
        else:
            assert pe
            tok = (key, self.cnt[key] + 1)
            self.q[eng].append(lambda e, fn=fn: fn(e))
            self.pe_pending = True
        self.ninstr += 1
        self._commit(tok, reads, writes)
        return tok

    def dma(self, eng, out, in_, reads=(), writes=(), **kw):
        self._deps(eng, reads, writes)
        pool = self.dpool[eng]
        slot = self.dk[eng] % len(pool)
        key = pool[slot]
        self.dk[eng] += 1
        if self.cnt[key] + 16 > self.SEM_LIMIT:
            self._wait(eng, (key, self.cnt[key]))
            n = sum(1 for k in self.owner if self.owner[k] == "dma_" + eng)
            nk = "d_%s_%d" % (eng, n)
            self.sems[nk] = self.stack.enter_context(self.nc.semaphore(nk)); self.nsem += 1
            self.cnt[nk] = 0; self.owner[nk] = "dma_" + eng
            pool[slot] = nk; key = nk
            self.retired = getattr(self, "retired", []) + [key]
        if self.cnt[key] > 0:
            self._wait(eng, (key, self.cnt[key]))
        self.cnt[key] += 16
        tok = (key, self.cnt[key])
        sem = self.sems[key]
        self.q[eng].append(
            lambda e, out=out, in_=in_, sem=sem, kw=kw: e.dma_start(out=out, in_=in_, **kw).then_inc(sem, 16))
        self.ninstr += 1
        self._commit(tok, reads, writes)
        return tok

    def wait_all(self, eng, toks):
        for t in toks:
            self._wait(eng, t)

    def barrier(self):
        toks = [t for t in (self._latest(e) for e in ("pe", "act", "dve", "pool")) if t]
        assert not self.pe_pending
        for q in self.dpool:
            for key in self.dpool[q]:
                if self.cnt[key]:
                    toks.append((key, self.cnt[key]))
        for eng in self.ENGS:
            for t in toks:
                if self.owner[t[0]] != eng:
                    self._wait(eng, t)

    def drain_all(self, eng="sp"):
        for q in self.dpool:
            for key in self.dpool[q]:
                if self.cnt[key]:
                    self._wait(eng, (key, self.cnt[key]))

    def emit(self):
        assert not self.pe_pending, "last PE op must carry inc"
        nc = self.nc
        q = self.q
        with nc.Block() as block:
            @block.sync
            def _(e):
                for f in q["sp"]:
                    f(e)

            @block.tensor
            def _(e):
                for f in q["pe"]:
                    f(e)

            @block.scalar
            def _(e):
                for f in q["act"]:
                    f(e)

            @block.vector
            def _(e):
                for f in q["dve"]:
                    f(e)

            @block.gpsimd
            def _(e):
                for f in q["pool"]:
                    f(e)


def run_rr(chains):
    live = list(chains)
    while live:
        for g in list(live):
            try:
                next(g)
            except StopIteration:
                live.remove(g)


from contextlib import ExitStack
from concourse.bass_utils import run_bass_kernel_spmd

FF = 2816; NFC = FF // 128
D = 1024; NT = 18; N = NT * 128; DEPTH = 2; NMOD = 6; EPS = 1e-6
KC = D // 128


def build(debug=None):
    nc = bass.Bass("TRN2", target_bir_lowering=False)
    dram = lambda n, s, dt, kind: nc.dram_tensor(n, s, dt, kind=kind).ap()
    xin = dram("xin", [N, D], F32, "ExternalInput")
    c2 = dram("c2", [128, KC * 2], F32, "ExternalInput")
    ada_w = dram("ada_w", [DEPTH, D, NMOD * D], F32, "ExternalInput")
    ada_b = dram("ada_b_fm", [128, DEPTH * 48], F32, "ExternalInput")
    ident = dram("ident", [128, 128], F32, "ExternalInput")
    w_in = dram("w_in", [DEPTH, D, 3840], F32, "ExternalInput")
    lbl = dram("lbl", [128, 16], F32, "ExternalInput")
    hgng = dram("hg_norm_g", [DEPTH, 128], F32, "ExternalInput")
    scw = dram("scw_fm", [128, DEPTH * 6], F32, "ExternalInput")
    sgln = dram("sgln", [DEPTH * 2, 256], F32, "ExternalInput")
    sgw = dram("sg_w", [DEPTH * 4, 128, 128], F32, "ExternalInput")
    sgb = dram("sgb_fm", [DEPTH * 2, 128, 128], F32, "ExternalInput")
    w_out = dram("w_out", [DEPTH, D, D], F32, "ExternalInput")
    ffn_up = dram("ffn_up", [DEPTH, D, 2 * FF], F32, "ExternalInput")
    ffn_down = dram("ffn_down", [DEPTH, FF, D], F32, "ExternalInput")
    fcw = dram("fcw_fm", [128, DEPTH * NFC * 9], F32, "ExternalInput")
    fcb = dram("fcb_fm", [128, DEPTH * NFC], F32, "ExternalInput")
    lnp = dram("lnp", [DEPTH * 4, D], F32, "ExternalInput")
    cmask = dram("cmask", [128, 256], F32, "ExternalInput")
    rmask = dram("rmask", [128, 512], F32, "ExternalInput")
    outs = {}
    if debug is None:
        outs["out"] = dram("out", [N - 256, D], F32, "ExternalOutput")
    if debug and debug.startswith("hg") and len(debug) == 3:
        outs["dbg_cut"] = dram("dbg_cut", [128, 2 * N], F32, "ExternalOutput")
    if debug == "h":
        outs["dbg_hT"] = dram("dbg_hT", [FF, N], F32, "ExternalOutput")
    if debug == "x2":
        outs["dbg_x2"] = dram("dbg_x2", [N, D], F32, "ExternalOutput")
    if debug == "x1":
        outs["dbg_x1"] = dram("dbg_x1", [N, D], F32, "ExternalOutput")
        outs["dbg_xm2T"] = dram("dbg_xm2T", [128, KC * N], F32, "ExternalOutput")
    if debug in ("hg", "mix"):
        outs["dbg_hgT"] = dram("dbg_hgT", [D if debug == "mix" else 512, N], F32, "ExternalOutput")
    if debug == "p1":
        outs["dbg_mod"] = dram("dbg_mod", [128, DEPTH * 96], F32, "ExternalOutput")
        outs["dbg_xmT"] = dram("dbg_xmT", [128, KC * N], F32, "ExternalOutput")
    with ExitStack() as st:
        S = Sync(nc, st)
        sb = lambda n, s, dt: st.enter_context(nc.sbuf_tensor(n, s, dt))
        pbank = [st.enter_context(nc.psum_tensor("pb%d" % i, [128, 512], F32)) for i in range(8)]
        b_pb = [Buf("pb%d" % i, excl=True) for i in range(8)]
        pbf = lambda i, n: pbank[i][:, 0:n]
        pbb = lambda i, n: pbank[i][:].bitcast(BF16)[:, 0:n]
        id_f = sb("id_f", [128, 128], F32); id_bf = sb("id_bf", [128, 128], BF16)
        c2_f = sb("c2_f", [128, KC * 2], F32); c2_bf = sb("c2_bf", [128, KC * 2], BF16)
        adab = sb("adab", [128, DEPTH * 48], F32)
        mod = sb("mod", [128, DEPTH * 96], F32)
        mod1p = sb("mod1p", [128, DEPTH * 96], F32)
        b_id_f, b_id_bf, b_c2f, b_c2bf, b_adab, b_mod, b_mod1p = (Buf(n) for n in "idf idbf c2f c2bf adab mod mod1p".split())
        S.dma("sp", id_f[:], ident, writes=[b_id_f])
        S.dma("pool", id_bf[:], ident, writes=[b_id_bf])
        S.dma("sp", c2_f[:], c2, writes=[b_c2f])
        S.dma("sp", adab[:], ada_b, writes=[b_adab])
        S.op("act", lambda e: e.activation(c2_bf[:], c2_f[:], AF.Silu), reads=[b_c2f], writes=[b_c2bf])
        st0 = ExitStack()
        aw = [st0.enter_context(nc.sbuf_tensor("aw%d" % i, [128, KC * 512], BF16)) for i in range(2)]
        b_aw = [Buf("aw0"), Buf("aw1")]
        p_mod = pbf(0, 96); b_pmod = b_pb[0]
        def mod_chunk_load(l, g, buf, bb):
            src = ada_w[l].rearrange("(k p) n -> p k n", p=128)[:, :, g * 512:(g + 1) * 512]
            S.dma("pool", buf[:].rearrange("p (k n) -> p k n", k=KC), src, writes=[bb])

        def mod_chunk_mm(l, g, buf, bb, pm_ap, b_pm):
            for jj in range(4):
                j = g * 4 + jj
                for k in range(KC):
                    last = (k == KC - 1)
                    S.op("pe", lambda e, buf=buf, jj=jj, k=k, j=j, last=last: e.matmul(
                        pm_ap[:, 2 * j:2 * j + 2], buf[:, k * 512 + jj * 128:k * 512 + (jj + 1) * 128],
                        c2_bf[:, 2 * k:2 * k + 2], start=(k == 0), stop=last),
                        reads=[bb, b_c2bf], writes=[b_pm], inc=(last and jj == 3))

        def mod_finish(l, pm_ap, b_pm):
            pm = pm_ap.rearrange("p (j s) -> p j s", s=2)
            mv = mod[:, l * 96:(l + 1) * 96].rearrange("p (j s) -> p j s", s=2)
            for s_ in range(2):
                S.op("dve", lambda e, pm=pm, mv=mv, s_=s_, l=l: e.tensor_tensor(
                    mv[:, :, s_], pm[:, :, s_], adab[:, l * 48:(l + 1) * 48], ALU.add),
                    reads=[b_pm, b_adab], writes=[b_mod])
            S.op("dve", lambda e, l=l: e.tensor_scalar_add(mod1p[:, l * 96:(l + 1) * 96], mod[:, l * 96:(l + 1) * 96], 1.0), reads=[b_mod], writes=[b_mod1p])

        for g in range(12):
            mod_chunk_load(0, g, aw[g % 2], b_aw[g % 2])
            mod_chunk_mm(0, g, aw[g % 2], b_aw[g % 2], p_mod, b_pmod)
        mod_finish(0, p_mod, b_pmod)
        S.barrier(); st0.close()
        if debug == "p1":
            S.dma("sp", outs["dbg_mod"], mod[:], reads=[b_mod])

        def modv(l, slot, kc, src, one_plus=False):
            t = mod1p if one_plus else mod
            col = l * 96 + (slot * 8 + kc) * 2 + src
            return t[:, col:col + 1]

        xmT = sb("xmT", [128, KC * N], BF16)
        b_xmT = [Buf("xmT%d" % i) for i in range(NT)]
        if debug in ("x1", "x2"):
            xt = [sb("xt%d" % i, [128, D], F32) for i in range(2)]; b_xt = [Buf("xt0"), Buf("xt1")]

        def mk_sets(alloc, n, tag, with_y=False, with_h=False, plan=None):
            sets = []
            for g in range(n):
                B_ = dict(xt=alloc("xt%s%d" % (tag, g), [128, D], F32), xn=alloc("xn%s%d" % (tag, g), [128, D], BF16), st=alloc("st%s%d" % (tag, g), [128, 16], F32),
                          b_xt=Buf("xt%d" % g), b_xn=Buf("xn%d" % g), b_st=Buf("st%d" % g),
                          tr=((1, 2) if g % 2 == 0 else (7, 0)), mm=((3, 4) if g % 2 == 0 else (5, 6)), tr1=None)
                if plan is not None:
                    p_ = plan[g % len(plan)]
                    B_.update(mm=(p_[0], p_[1]), tr1=p_[2])
                if with_y:
                    B_.update(xq=alloc("xq%s%d" % (tag, g), [128, D], F32), yt=alloc("yt%s%d" % (tag, g), [128, D], F32), b_xq=Buf("xq%d" % g), b_yt=Buf("yt%d" % g))
                if with_h:
                    B_.update(ht=alloc("ht%s%d" % (tag, g), [128, NFC * 128], BF16), b_ht=Buf("ht%d" % g))
                sets.append(B_)
            return sets

        def ln_stats_gen(src, b_src, B_):
            st_t, b_s = B_["st"], B_["b_st"]
            yield
            S.op("dve", lambda e: e.bn_stats(st_t[:, 0:6], src[:, 0:512]), reads=[b_src], writes=[b_s])
            yield
            S.op("dve", lambda e: e.bn_stats(st_t[:, 6:12], src[:, 512:1024]), reads=[b_src], writes=[b_s])
            yield
            S.op("dve", lambda e: e.bn_aggr(st_t[:, 12:14], st_t[:, 0:12]), reads=[b_s], writes=[b_s])
            yield
            S.op("act", lambda e: e.activation(st_t[:, 14:15], st_t[:, 13:14], AF.Sqrt, bias=EPS), reads=[b_s], writes=[b_s])
            yield
            S.op("dve", lambda e: e.reciprocal(st_t[:, 15:16], st_t[:, 14:15]), reads=[b_s], writes=[b_s])

        def ln_mod_T_gen(l, i, slot_shift, slot_scale, B_):
            srcm = 1 if i < 2 else 0
            xt_t, xn_t, st_t = B_["xt"], B_["xn"], B_["st"]
            yield from ln_stats_gen(xt_t, B_["b_xt"], B_)
            yield
            S.op("dve", lambda e: e.tensor_scalar(xn_t[:], xt_t[:], st_t[:, 12:13], st_t[:, 15:16], ALU.subtract, ALU.mult),
                 reads=[B_["b_xt"], B_["b_st"]], writes=[B_["b_xn"]])
            for kc in range(KC):
                if B_["tr1"] is not None:
                    pbk = B_["tr1"]; pv_ = pbank[pbk][:].bitcast(BF16)[:, kc * 128:(kc + 1) * 128]
                else:
                    pbk = B_["tr"][kc % 2]; pv_ = pbb(pbk, 128)
                yield
                S.op("pe", lambda e, kc=kc, pv_=pv_: e.transpose(pv_, xn_t[:, kc * 128:(kc + 1) * 128], id_bf[:]),
                     reads=[B_["b_xn"], b_id_bf], writes=[b_pb[pbk]])
                dst = xmT[:, kc * N + i * 128: kc * N + (i + 1) * 128]
                yield
                S.op("act", lambda e, dst=dst, pv_=pv_, kc=kc: e.activation(
                    dst, pv_, AF.Identity, bias=modv(l, slot_shift, kc, srcm), scale=modv(l, slot_scale, kc, srcm, True)),
                    reads=[b_pb[pbk], b_mod, b_mod1p], writes=[b_xmT[i]])

        GRP = 2

        def run_groups(tiles, sets, load, chain, GRP=GRP):
            tl_ = list(tiles)

            def stream(k):
                mine = tl_[k::GRP]
                for n_, i in enumerate(mine):
                    if n_ == 0:
                        load(i, sets[k])
                    if n_ + 1 < len(mine):
                        load(mine[n_ + 1], sets[k + GRP * ((n_ + 1) % 2)])
                    yield from chain(i, sets[k + GRP * (n_ % 2)])

            run_rr([stream(k) for k in range(GRP)])

        def phase_ln_mod_T(l, src_ap, tiles, slot_shift, slot_scale, src_bufs=None):
            p1 = ExitStack()
            sets = mk_sets(lambda n, s_, dt: p1.enter_context(nc.sbuf_tensor("%s_p1L%d" % (n, l), s_, dt)), 2 * GRP, "a")
            load = lambda i, B_: S.dma("sp", B_["xt"][:], src_ap[i * 128:(i + 1) * 128, :], reads=([src_bufs[i]] if src_bufs else []), writes=[B_["b_xt"]])
            run_groups(tiles, sets, load, lambda i, B_: ln_mod_T_gen(l, i, slot_shift, slot_scale, B_))
            S.barrier(); p1.close()

        phase_ln_mod_T(0, xin, range(NT), 0, 1)
        cut = debug[2] if (debug and debug.startswith("hg") and len(debug) == 3) else None
        dmp = sb("dmp", [128, 512], F32); b_dmp = Buf("dmp")

        def cut_dump(src_ap, src_bufs):
            S.barrier()
            S.op("act", lambda e: e.copy(dmp[:], src_ap), reads=src_bufs, writes=[b_dmp])
            S.dma("sp", outs["dbg_cut"][:, 0:512], dmp[:], reads=[b_dmp])
            S.drain_all("sp"); S.emit(); build.ninstr = S.ninstr

        if cut == "0":
            cut_dump(xmT[:, 0:512], b_xmT); return nc
        if debug == "p1":
            xm_f = sb("xm_f", [128, N], F32); b_xmf = Buf("xmf")
            for kc in range(KC):
                S.op("act", lambda e, kc=kc: e.copy(xm_f[:], xmT[:, kc * N:(kc + 1) * N]), reads=b_xmT, writes=[b_xmf])
                S.dma("sp", outs["dbg_xmT"][:, kc * N:(kc + 1) * N], xm_f[:], reads=[b_xmf])

        mixT = nc.dram_tensor("mixT", [D, N], BF16, kind="Internal").ap()
        b_mixT = [Buf("mixT%d" % i) for i in range(8)]

        def mixer(l, scan_tiles, out_tiles):
            ms = ExitStack()
            sb = lambda n, s_, dt: ms.enter_context(nc.sbuf_tensor("%s_L%d" % (n, l), s_, dt))
            DK = 128; QS = DK ** -0.5; NCH = N // 64
            BLKS = [(0, 512), (512, 512), (1024, 512), (1536, 512), (2048, 256)]
            lb_f = sb("lb_f", [128, 16], F32); lbv = sb("lbv", [128, 16], F32); oml = sb("oml", [128, 16], F32)
            b_lb = Buf("lb")
            cm = sb("cm", [128, 256], F32); rmk = sb("rmk", [128, 512], F32); ngb = sb("ngb", [128, DEPTH * 128], F32)
            b_cm, b_rmk, b_ngb = Buf("cm"), Buf("rmk"), Buf("ngb")
            cm_u = sb("cm_u", [128, 256], mybir.dt.uint32); b_cmu = Buf("cmu")
            S.dma("sp", lb_f[:], lbl, writes=[b_lb]); S.dma("sp", cm[:], cmask, writes=[b_cm]); S.dma("sp", rmk[:], rmask, writes=[b_rmk])
            for l2 in range(DEPTH):
                S.dma("sp", ngb[:, l2 * 128:(l2 + 1) * 128], hgng[l2:l2 + 1, :].partition_broadcast(128), writes=[b_ngb])
            S.op("dve", lambda e: e.tensor_copy(cm_u[:], cm[:]), reads=[b_cm], writes=[b_cmu])
            lt = sb("lt", [128, 32], F32)
            S.op("dve", lambda e: e.memset(lbv[:, 0:8], 0.0), writes=[b_lb], reads=[b_lb])
            S.op("dve", lambda e: e.tensor_max(lt[:, 0:8], lb_f[:, 0:8], lb_f[:, 8:16]), reads=[b_lb], writes=[b_lb])
            S.op("dve", lambda e: e.tensor_sub(lt[:, 8:16], lb_f[:, 0:8], lt[:, 0:8]), reads=[b_lb], writes=[b_lb])
            S.op("dve", lambda e: e.tensor_sub(lt[:, 16:24], lb_f[:, 8:16], lt[:, 0:8]), reads=[b_lb], writes=[b_lb])
            S.op("act", lambda e: e.activation(lt[:, 8:24], lt[:, 8:24], AF.Exp), reads=[b_lb], writes=[b_lb])
            S.op("dve", lambda e: e.tensor_add(lt[:, 24:32], lt[:, 8:16], lt[:, 16:24]), reads=[b_lb], writes=[b_lb])
            S.op("dve", lambda e: e.reciprocal(lt[:, 24:32], lt[:, 24:32]), reads=[b_lb], writes=[b_lb])
            S.op("dve", lambda e: e.tensor_mul(lbv[:, 8:16], lt[:, 16:24], lt[:, 24:32]), reads=[b_lb], writes=[b_lb])
            S.op("dve", lambda e: e.tensor_scalar(oml[:], lbv[:], -1.0, 1.0, ALU.mult, ALU.add), reads=[b_lb], writes=[b_lb])

            wvg = [sb("wvg%d" % i, [128, KC * 256], BF16) for i in range(2)]; b_wvg = [Buf("wvg0"), Buf("wvg1")]
            wzq = [sb("wzq%d" % i, [128, KC * 384], BF16) for i in range(2)]; b_wzq = [Buf("wzq0"), Buf("wzq1")]
            VG = [dict(V=sb("Vh%d" % p_, [128, N], BF16), G=sb("Gh%d" % p_, [128, N], BF16), bV=Buf("V%d" % p_), bG=Buf("G%d" % p_)) for p_ in range(2)]
            Vh, Gh, b_V, b_G = VG[0]["V"], VG[0]["G"], VG[0]["bV"], VG[0]["bG"]
            QT = [sb("QT%d" % d, [128, N], BF16) for d in range(2)]; KT = [sb("KT%d" % d, [128, N], BF16) for d in range(2)]
            QS_T = [sb("QST%d" % d, [128, N], BF16) for d in range(2)]; b_QST = [Buf("QST0"), Buf("QST1")]
            KH = [sb("KH%d" % d, [128, N], BF16) for d in range(2)]; KHt = [sb("KHt%d" % d, [128, N], BF16) for d in range(2)]
            EB = [sb("EB%d" % d, [128, NCH], F32) for d in range(2)]
            EM = [sb("EM%d" % d, [128, NCH], F32) for d in range(2)]; b_EM = [Buf("EM0"), Buf("EM1")]
            b_QT = [Buf("QT0"), Buf("QT1")]; b_KT = [Buf("KT0"), Buf("KT1")]; b_KH = [Buf("KH0"), Buf("KH1")]
            b_KHt = [Buf("KHt0"), Buf("KHt1")]; b_EB = [Buf("EB0"), Buf("EB1")]
            Oacc = [sb("Oacc%d" % d, [128, N], F32) for d in range(2)]; b_O = [Buf("O0"), Buf("O1")]
            HGT = sb("HGT", [128, N], BF16); b_HGT = Buf("HGT")
            NTMP = 9
            tmp = [[sb("tm%d_%d" % (a, i), [128, 512], F32) for i in range(NTMP)] for a in range(2)]
            b_tmp = [[Buf("tm%d_%d" % (a, i)) for i in range(NTMP)] for a in range(2)]
            qs_ = [sb("qs%d" % a, [128, 512], F32) for a in range(2)]; b_qs = [Buf("qs0"), Buf("qs1")]
            Sst = [[sb("S%d_%d" % (d, i), [128, 128], F32) for i in range(2)] for d in range(2)]
            b_S = [[Buf("S%d_%d" % (d, i)) for i in range(2)] for d in range(2)]
            Sbf_f32 = [sb("S3_%d" % d, [128, 128], F32) for d in range(2)]; b_Sx = [Buf("S3_0"), Buf("S3_1")]
            AT = [sb("AT%d" % d, [128, 128], BF16) for d in range(2)]; b_AT = [Buf("AT0"), Buf("AT1")]
            fin = [sb("fin%d" % i, [128, 128], F32) for i in range(2)]; fsq = sb("fsq", [128, 128], F32)
            b_fin = [Buf("fin0"), Buf("fin1")]
            finW = [dict(fsq=(fsq if p_ == 0 else sb("fsq1", [128, 128], F32)), fst=sb("fst%d" % p_, [128, 8], F32), ngg=sb("ngg%d" % p_, [128, 128], F32), hgt=sb("hgt%d" % p_, [128, 128], BF16),
                         b_fsq=Buf("fsq%d" % p_), b_fst=Buf("fst%d" % p_), b_ngg=Buf("ngg%d" % p_), b_hgt=Buf("hgt%d" % p_)) for p_ in range(2)]

            if cut == "L":
                cut_dump(oml[:, 0:16].to_broadcast([128, 16]) if False else xmT[:, 0:512], b_xmT + [b_lb, b_cm, b_rmk, b_ngb]); return "cut"

            e_, one_e, sig, kk, lf, bb, cc_, E1, dd = range(9)

            def hgrn_layer(l, scan_tiles, out_tiles):
                wi = w_in[l].rearrange("(k p) n -> p k n", p=128)
                def load_vg_w(h_):
                    a_ = h_ % 2
                    for ci, c0 in enumerate((h_ * 128, 2048 + h_ * 128)):
                        S.dma("pool", wvg[a_][:].rearrange("p (k n) -> p k n", k=KC)[:, :, ci * 128:(ci + 1) * 128], wi[:, :, c0:c0 + 128], writes=[b_wvg[a_]])

                def load_zq_w(h_):
                    a_ = h_ % 2
                    for ci, c0 in enumerate((512 + h_ * 128, 1024 + h_ * 128, 1536 + h_ * 128)):
                        S.dma("pool", wzq[a_][:].rearrange("p (k n) -> p k n", k=KC)[:, :, ci * 128:(ci + 1) * 128], wi[:, :, c0:c0 + 128], writes=[b_wzq[a_]])

                def A_gen(h_):
                    a_ = h_ % 2; W_ = VG[a_]
                    for i in scan_tiles:
                        pbi = 3 + (i % 2)
                        yield
                        for kc in range(KC):
                            S.op("pe", lambda e, i=i, kc=kc, pbi=pbi: e.matmul(pbf(pbi, 256), xmT[:, kc * N + i * 128:kc * N + (i + 1) * 128],
                                 wvg[a_][:, kc * 256:(kc + 1) * 256], start=(kc == 0), stop=(kc == KC - 1)),
                                 reads=[b_xmT[i], b_wvg[a_]], writes=[b_pb[pbi]], inc=(kc == KC - 1))
                        yield
                        S.op("dve", lambda e, i=i, pbi=pbi: e.tensor_copy(W_["V"][:, i * 128:(i + 1) * 128], pbank[pbi][:, 0:128]), reads=[b_pb[pbi]], writes=[W_["bV"]])
                        yield
                        S.op("act", lambda e, i=i, pbi=pbi: e.activation(W_["G"][:, i * 128:(i + 1) * 128], pbank[pbi][:, 128:256], AF.Silu), reads=[b_pb[pbi]], writes=[W_["bG"]])

                nheads = 1 if cut else 4
                load_vg_w(0)
                if cut == "W":
                    cut_dump(wvg[0][:, 0:512], [b_wvg[0]]); return "cut"
                run_rr([A_gen(0)])
                for h in range(nheads):
                    a = h % 2
                    Vh, Gh, b_V, b_G = VG[a]["V"], VG[a]["G"], VG[a]["bV"], VG[a]["bG"]
                    if h == 0:
                        load_zq_w(0)
                    if h + 1 < nheads:
                        load_zq_w(h + 1)
                        load_vg_w(h + 1)
                    if cut == "A":
                        cut_dump(Vh[:, 0:512], [b_V, b_G]); return "cut"
                    for bi, (t0, nb) in enumerate(BLKS):
                        if t0 // 128 not in scan_tiles:
                            continue
                        tiles_in = [i for i in range(t0 // 128, (t0 + nb) // 128)]
                        for ci in range(3):
                            for kc in range(KC):
                                S.op("pe", lambda e, ci=ci, kc=kc, t0=t0, nb=nb, a=a: e.matmul(pbf(5 + ci, nb), wzq[a][:, kc * 384 + ci * 128:kc * 384 + (ci + 1) * 128],
                                     xmT[:, kc * N + t0:kc * N + t0 + nb], start=(kc == 0), stop=(kc == KC - 1)),
                                     reads=[b_wzq[a]] + [b_xmT[i] for i in tiles_in], writes=[b_pb[5 + ci]], inc=(kc == KC - 1))
                        qa = bi % 2
                        S.op("act", lambda e, nb=nb, qa=qa: e.activation(qs_[qa][:, 0:nb], pbf(7, nb), AF.Silu), reads=[b_pb[7]], writes=[b_qs[qa]])
                        nck = nb // 64; c0 = t0 // 64
                        def gate_chain(d, bi=bi, t0=t0, nb=nb, qa=qa, nck=nck, c0=c0):
                            ta = (bi * 2 + d) % 2
                            T = [t[:, 0:nb] for t in tmp[ta]]; bT = b_tmp[ta]
                            col = l * 8 + d * 4 + h
                            lbc, omc = lbv[:, col:col + 1], oml[:, col:col + 1]
                            yield
                            S.op("act", lambda e, T=T, d=d, nb=nb: e.activation(T[e_], pbf(5 + d, nb), AF.Exp, scale=-1.0), reads=[b_pb[5 + d]], writes=[bT[e_]])
                            yield
                            S.op("act", lambda e, T=T: e.activation(T[one_e], T[e_], AF.Ln, bias=1.0), reads=[bT[e_]], writes=[bT[one_e]])
                            yield
                            S.op("act", lambda e, T=T: e.activation(T[sig], T[one_e], AF.Exp, scale=-1.0), reads=[bT[one_e]], writes=[bT[sig]])
                            yield
                            S.op("act", lambda e, T=T, omc=omc, lbc=lbc: e.activation(T[lf], T[sig], AF.Ln, bias=lbc, scale=omc), reads=[bT[sig], b_lb], writes=[bT[lf]])
                            yield
                            S.op("dve", lambda e, T=T, nb=nb: e.tensor_tensor_scan(T[bb], rmk[:, 0:nb], T[lf], 0.0, ALU.mult, ALU.add), reads=[b_rmk, bT[lf]], writes=[bT[bb]])
                            b3 = T[bb].rearrange("p (c t) -> p c t", t=64)
                            btot = b3[:, :, 63:64]
                            if d == 0:
                                cview, bc_ = T[bb], bT[bb]
                            else:
                                yield
                                S.op("dve", lambda e, T=T: e.tensor_sub(T[cc_], T[lf], T[bb]), reads=[bT[lf], bT[bb]], writes=[bT[cc_]])
                                c3 = T[cc_].rearrange("p (c t) -> p c t", t=64)
                                yield
                                S.op("dve", lambda e, c3=c3, btot=btot, nck=nck: e.tensor_add(c3, c3, btot.to_broadcast([128, nck, 64])), reads=[bT[cc_], bT[bb]], writes=[bT[cc_]])
                                cview, bc_ = T[cc_], bT[cc_]
                            sl = slice(t0, t0 + nb)
                            def tail_scores():
                                MID = 31 if d == 0 else 32
                                cm3 = T[one_e].rearrange("p (c t) -> p c t", t=64); cv3m = cview.rearrange("p (c t) -> p c t", t=64)
                                yield
                                S.op("dve", lambda e, cm3=cm3, cv3m=cv3m, nck=nck, MID=MID: e.tensor_sub(cm3, cv3m, cv3m[:, :, MID:MID + 1].to_broadcast([128, nck, 64])),
                                     reads=[bc_, bT[E1]], writes=[bT[one_e]])
                                yield
                                S.op("act", lambda e, T=T: e.activation(T[E1], T[one_e], AF.Exp), reads=[bT[one_e]], writes=[bT[E1]])
                                yield
                                S.op("dve", lambda e, T=T, qa=qa, d=d, sl=sl, nb=nb: e.scalar_tensor_tensor(QS_T[d][:, sl], qs_[qa][:, 0:nb], QS, T[E1], ALU.mult, ALU.mult),
                                     reads=[b_qs[qa], bT[E1]], writes=[b_QST[d]])
                                yield
                                S.op("act", lambda e, T=T: e.activation(T[E1], T[one_e], AF.Exp, scale=-1.0), reads=[bT[one_e], b_QST[d]], writes=[bT[E1]])
                                yield
                                S.op("dve", lambda e, T=T, d=d, sl=sl: e.tensor_tensor(KT[d][:, sl], T[kk], T[E1], ALU.mult), reads=[bT[kk], bT[E1]], writes=[b_KT[d]])
                                yield
                                S.op("act", lambda e, d=d, c0=c0, nck=nck, cv3m=cv3m, MID=MID: e.activation(EM[d][:, c0:c0 + nck], cv3m[:, :, MID:MID + 1].rearrange("p c o -> p (c o)"), AF.Exp), reads=[bc_], writes=[b_EM[d]])

                            def tail_state():
                                yield
                                S.op("dve", lambda e, T=T, omc=omc: e.scalar_tensor_tensor(T[kk], T[e_], omc, T[sig], ALU.mult, ALU.mult), reads=[bT[e_], bT[sig], b_lb], writes=[bT[kk]])
                                d3 = T[dd].rearrange("p (c t) -> p c t", t=64); cv3 = cview.rearrange("p (c t) -> p c t", t=64)
                                yield
                                S.op("dve", lambda e, d3=d3, cv3=cv3, btot=btot, nck=nck: e.tensor_sub(d3, btot.to_broadcast([128, nck, 64]), cv3), reads=[bc_, bT[bb]], writes=[bT[dd]])
                                yield
                                S.op("act", lambda e, T=T: e.activation(T[dd], T[dd], AF.Exp), reads=[bT[dd]], writes=[bT[dd]])
                                yield
                                S.op("dve", lambda e, T=T, d=d, sl=sl: e.tensor_tensor(KH[d][:, sl], T[kk], T[dd], ALU.mult), reads=[bT[kk], bT[dd]], writes=[b_KH[d]])
                                yield
                                S.op("act", lambda e, d=d, c0=c0, nck=nck, btot=btot: e.activation(EB[d][:, c0:c0 + nck], btot.rearrange("p c o -> p (c o)"), AF.Exp), reads=[bT[bb]], writes=[b_EB[d]])

                            ga_, gb_ = tail_scores(), tail_state()
                            live_ = [ga_, gb_]
                            while live_:
                                for g_ in list(live_):
                                    try:
                                        next(g_)
                                    except StopIteration:
                                        live_.remove(g_)
                                yield
                        run_rr([gate_chain(0), gate_chain(1)])
                    if cut == "B":
                        return
                    for d in range(2):
                        for i in scan_tiles:
                            pbi = 1 + (i % 2)
                            S.op("pe", lambda e, d=d, i=i, pbi=pbi: e.transpose(pbb(pbi, 128), KH[d][:, i * 128:(i + 1) * 128], id_bf[:]), reads=[b_KH[d], b_id_bf], writes=[b_pb[pbi]])
                            S.op("act", lambda e, d=d, i=i, pbi=pbi: e.copy(KHt[d][:, i * 128:(i + 1) * 128], pbb(pbi, 128)), reads=[b_pb[pbi]], writes=[b_KHt[d]])
                    if cut == "C":
                        return
                    order = [list(scan_tiles), [t for t in (1, 0) if t in scan_tiles] + [t for t in range(NT - 1, 1, -1) if t in scan_tiles]]
                    TB = [tmp[a_][j_] for a_ in range(2) for j_ in range(NTMP)]; bTB = [b_tmp[a_][j_] for a_ in range(2) for j_ in range(NTMP)]
                    import os as _os2
                    d_stage = int(_os2.environ.get("HG_D_STAGE", "0")) if cut else 0
                    seqs = []
                    for d in range(1 if d_stage else 2):
                        seq = [(i, cpos) for i in order[d] for cpos in ((0, 1) if d == 0 else (1, 0))]
                        seqs.append(seq)
                        nstep = len(order[d])
                        for cpos in (0, 1):
                            base = cpos * nstep
                            s_ = base
                            while s_ < base + nstep:
                                g_end = min(base + nstep, (s_ // 4 + 1) * 4)
                                pbk = 3 + 2 * cpos + ((s_ // 4) % 2)
                                for sl_ in range(s_, g_end):
                                    i = order[d][sl_ - base]; q = sl_ % 4
                                    ps_ = slice(cpos * 64, cpos * 64 + 64); ts_ = slice(i * 128, (i + 1) * 128)
                                    S.op("pe", lambda e, d=d, ts_=ts_, ps_=ps_, pbk=pbk, q=q, Vh=Vh: e.matmul(pbank[pbk][:, q * 128:(q + 1) * 128], KHt[d][ps_, ts_], Vh[ps_, ts_], start=True, stop=True),
                                         reads=[b_KHt[d], b_V], writes=[b_pb[pbk]], inc=(sl_ == g_end - 1))
                                tb = 9 * d + s_ // 4; c_lo, c_hi = (s_ % 4) * 128, ((g_end - 1) % 4 + 1) * 128
                                S.op("act", lambda e, tb=tb, pbk=pbk, c_lo=c_lo, c_hi=c_hi, TB=TB: e.copy(TB[tb][:, c_lo:c_hi], pbank[pbk][:, c_lo:c_hi]), reads=[b_pb[pbk]], writes=[bTB[tb]])
                                s_ = g_end
                    if d_stage == 1:
                        cut_dump(TB[0][:, 0:512], bTB[0:9]); return "cut"
                    RING = 18
                    b_slot = [[Buf("st%d_%d" % (d_, r_)) for r_ in range(RING)] for d_ in range(2)]
                    sslot = lambda d_, k: (KH[d_][:, (k % RING) * 128:(k % RING + 1) * 128], b_slot[d_][k % RING])
                    S3 = [Sst[d_] + [Sbf_f32[d_]] for d_ in range(2)]; b_S3 = [b_S[d_] + [b_Sx[d_]] for d_ in range(2)]

                    produced = [0, 0]
                    consumed = [0, 0]

                    def chain_gen(d):
                        seq = seqs[d]; nstep = len(order[d])
                        yield
                        S.op("dve", lambda e, d=d, S3=S3: e.memset(S3[d][0][:], 0.0), writes=[b_S3[d][0]])
                        for k, (i, cpos) in enumerate(seq[:-1]):
                            sl_ = cpos * nstep + k // 2
                            ch = i * 2 + cpos; tb = 9 * d + sl_ // 4; q = sl_ % 4
                            si, so = k % 3, (k + 1) % 3
                            yield
                            S.op("dve", lambda e, d=d, si=si, so=so, ch=ch, tb=tb, q=q, TB=TB, S3=S3: e.scalar_tensor_tensor(S3[d][so][:], S3[d][si][:], EB[d][:, ch:ch + 1], TB[tb][:, q * 128:(q + 1) * 128], ALU.mult, ALU.add),
                                 reads=[b_S3[d][si], b_EB[d], bTB[tb]], writes=[b_S3[d][so]])
                            dst, bdst = sslot(d, k + 1)
                            yield
                            while (k + 1) - consumed[d] >= RING:
                                yield
                            i2, cp2 = seq[k + 1]; ch2 = i2 * 2 + cp2
                            S.op("act", lambda e, d=d, so=so, dst=dst, S3=S3, ch2=ch2: e.activation(dst, S3[d][so][:], AF.Identity, scale=EM[d][:, ch2:ch2 + 1]), reads=[b_S3[d][so], b_EM[d]], writes=[bdst])
                            produced[d] = k + 1

                    def out_gen(d):
                        yield
                        S.op("dve", lambda e, d=d: e.memset(AT[d][:], 0.0), writes=[b_AT[d]])
                        for step, i in enumerate(order[d]):
                            if i not in out_tiles:
                                consumed[d] = 2 * (step + 1)
                                continue
                            ts_ = slice(i * 128, (i + 1) * 128); par = step % 2
                            p_sc, p_o = (5, 6)[d], ((7, 0)[d])
                            yield
                            S.op("pe", lambda e, d=d, ts_=ts_, p_sc=p_sc: e.matmul(pbf(p_sc, 128), KT[d][:, ts_], QS_T[d][:, ts_], start=True, stop=True),
                                 reads=[b_KT[d], b_QST[d]], writes=[b_pb[p_sc]])
                            yield
                            S.op("dve", lambda e, d=d, p_sc=p_sc: e.copy_predicated(AT[d][:], cm_u[:, d * 128:(d + 1) * 128], pbf(p_sc, 128)),
                                 reads=[b_pb[p_sc], b_cmu, b_AT[d]], writes=[b_AT[d]])
                            cps = (0, 1) if d == 0 else (1, 0)
                            need = [(cp, k) for cp, k in zip(cps, (2 * step, 2 * step + 1)) if k > 0]
                            yield
                            while need and produced[d] < max(k_ for _, k_ in need):
                                yield
                            S.op("pe", lambda e, d=d, ts_=ts_, p_o=p_o, nn=len(need), Vh=Vh: e.matmul(pbf(p_o, 128), AT[d][:], Vh[:, ts_], start=True, stop=(nn == 0)),
                                 reads=[b_AT[d], b_V], writes=[b_pb[p_o]], inc=(len(need) == 0))
                            for j, (cp, k) in enumerate(need):
                                ps_ = slice(cp * 64, cp * 64 + 64); tsc = slice(i * 128 + cp * 64, i * 128 + cp * 64 + 64)
                                src, bsrc = sslot(d, k)
                                lastj = j == len(need) - 1
                                if not lastj:
                                    pass
                                S.op("pe", lambda e, d=d, tsc=tsc, ps_=ps_, p_o=p_o, src=src, lastj=lastj: e.matmul(pbank[p_o][ps_, 0:128], QS_T[d][:, tsc], src, start=False, stop=lastj),
                                     reads=[b_QST[d], bsrc], writes=[b_pb[p_o]], inc=lastj)
                            consumed[d] = 2 * (step + 1)
                            yield
                            S.op("act", lambda e, d=d, ts_=ts_, p_o=p_o: e.copy(Oacc[d][:, ts_], pbf(p_o, 128)), reads=[b_pb[p_o]], writes=[b_O[d]])

                    if h == nheads - 1 and not cut and debug != "hg":
                        sc_load(l, 0); sc_load(l, 1); sg_load(l)
                    if d_stage:
                        run_rr([chain_gen(0), out_gen(0)])
                    else:
                        run_rr([chain_gen(0), chain_gen(1), out_gen(0), out_gen(1)] + ([A_gen(h + 1)] if h + 1 < nheads else []))
                    if d_stage == 3:
                        cut_dump(Oacc[0][:, 0:512], [b_O[0]]); return "cut"
                    if cut == "D":
                        return
                    def fin_chain(i, fa):
                        ts_ = slice(i * 128, (i + 1) * 128); pbi = 1 + fa
                        W = finW[fa]
                        yield
                        S.op("dve", lambda e, Gcur=Gcur: e.tensor_tensor(W["ngg"][:], ngb[:, l * 128:(l + 1) * 128], Gcur[:, ts_], ALU.mult), reads=[b_ngb, b_Gcur], writes=[W["b_ngg"]])
                        yield
                        S.op("dve", lambda e: e.tensor_add(fin[fa][:], Oacc[0][:, ts_], Oacc[1][:, ts_]), reads=[b_O[0], b_O[1]], writes=[b_fin[fa]])
                        yield
                        S.op("act", lambda e: e.activation(W["fsq"][:], fin[fa][:], AF.Square, accum_out=W["fst"][:, 0:1]), reads=[b_fin[fa]], writes=[W["b_fsq"], W["b_fst"]])
                        yield
                        S.op("act", lambda e: e.activation(W["fst"][:, 1:2], W["fst"][:, 0:1], AF.Sqrt, bias=EPS, scale=1.0 / 128), reads=[W["b_fst"]], writes=[W["b_fst"]])
                        yield
                        S.op("dve", lambda e: e.reciprocal(W["fst"][:, 2:3], W["fst"][:, 1:2]), reads=[W["b_fst"]], writes=[W["b_fst"]])
                        yield
                        S.op("dve", lambda e: e.scalar_tensor_tensor(W["hgt"][:], fin[fa][:], W["fst"][:, 2:3], W["ngg"][:], ALU.mult, ALU.mult), reads=[b_fin[fa], W["b_fst"], W["b_ngg"]], writes=[W["b_hgt"]])
                        yield
                        S.op("pe", lambda e: e.transpose(pbb(pbi, 128), W["hgt"][:], id_bf[:]), reads=[W["b_hgt"], b_id_bf], writes=[b_pb[pbi]])
                        yield
                        S.op("act", lambda e: e.copy(HGT[:, ts_], pbb(pbi, 128)), reads=[b_pb[pbi]], writes=[b_HGT])

                    Gcur, b_Gcur = Gh, b_G
                    ot_ = list(out_tiles)

                    def fin_stream(k, ot_=ot_):
                        for i in ot_[k::2]:
                            yield from fin_chain(i, k)

                    run_rr([fin_stream(0), fin_stream(1)])
                    S.dma("sp", mixT[h * 128:(h + 1) * 128, :], HGT[:], reads=[b_HGT], writes=[b_mixT[h]])

            if cut:
                S.barrier()
                S.op("act", lambda e, Vh=Vh: e.copy(Oacc[1][:], Vh[:]), reads=[b_V], writes=[b_O[1]])
                S.dma("sp", outs["dbg_cut"][:, 0:N], Oacc[1][:], reads=[b_O[1]])
                if cut in "DE":
                    S.dma("sp", outs["dbg_cut"][:, N:2 * N], Oacc[0][:], reads=[b_O[0]])

            scw_s = sb("scw_s", [128, DEPTH * 6], F32); b_scw = Buf("scw")
            S.dma("sp", scw_s[:], scw, writes=[b_scw])
            lngb = sb("lngb", [128, 512], F32); b_lngb = Buf("lngb")
            WsT = sb("WsT", [128, 4 * 128], BF16); b_WsT = Buf("WsT")
            wsn = sb("wsn", [128, 128], BF16); b_wsn = Buf("wsn")
            BS = sb("BS", [128, 2 * 128], F32); b_BS = Buf("BS")
            SEQS = [(0, 256), (256, N)]
            SEGS = [(0, 256), (256, 512), (512, 1024), (1024, 1536), (1536, 2048), (2048, N)]

            pre_done = {}

            def sc_load(l, cc):
                wi_ = w_in[l].rearrange("(k p) n -> p k n", p=128); a_ = cc % 2
                for ci, c0 in enumerate((2560 + cc * 128, 2816 + cc * 128, 3072 + cc * 128)):
                    S.dma("pool", wzq[a_][:].rearrange("p (k n) -> p k n", k=KC)[:, :, ci * 128:(ci + 1) * 128], wi_[:, :, c0:c0 + 128], writes=[b_wzq[a_]])
                pre_done[("sc", l, cc)] = True

            def sg_load(l):
                wi_ = w_in[l].rearrange("(k p) n -> p k n", p=128)
                S.dma("pool", wvg[0][:].rearrange("p (k n) -> p k n", k=KC), wi_[:, :, 3328:3584], writes=[b_wvg[0]])
                S.dma("pool", wvg[1][:].rearrange("p (k n) -> p k n", k=KC), wi_[:, :, 3584:3840], writes=[b_wvg[1]])
                S.dma("sp", lngb[:, 0:256], sgln[2 * l:2 * l + 1, :].partition_broadcast(128), writes=[b_lngb])
                S.dma("sp", lngb[:, 256:512], sgln[2 * l + 1:2 * l + 2, :].partition_broadcast(128), writes=[b_lngb])
                for cc in range(2):
                    S.dma("sp", BS[:, cc * 128:(cc + 1) * 128], sgb[2 * l + cc], writes=[b_BS])
                pre_done[("sg", l)] = True

            def sc_layer(l, tiles):
                wi = w_in[l].rearrange("(k p) n -> p k n", p=128)
                tmax = (max(tiles) + 1) * 128; tmin = min(tiles) * 128
                Pf, GBf, b_P, b_GB = Oacc[0], Oacc[1], b_O[0], b_O[1]
                for cc in range(2):
                    a = cc % 2
                    if not pre_done.get(("sc", l, cc)):
                        sc_load(l, cc)
                    sc_blks = [(t0, min(512, tmax - t0)) for t0 in range(tmin, tmax, 512)]
                    for (t0, nb) in sc_blks:
                        tiles_in = list(range(t0 // 128, (t0 + nb) // 128))
                        for ci in range(3):
                            for kc in range(KC):
                                S.op("pe", lambda e, ci=ci, kc=kc, t0=t0, nb=nb, a=a: e.matmul(pbf(5 + ci, nb), wzq[a][:, kc * 384 + ci * 128:kc * 384 + (ci + 1) * 128],
                                     xmT[:, kc * N + t0:kc * N + t0 + nb], start=(kc == 0), stop=(kc == KC - 1)),
                                     reads=[b_wzq[a]] + [b_xmT[i] for i in tiles_in], writes=[b_pb[5 + ci]], inc=(kc == KC - 1))
                        T0 = tmp[0][0][:, 0:nb]
                        S.op("act", lambda e, t0=t0, nb=nb: e.copy(GBf[:, t0:t0 + nb], pbf(5, nb)), reads=[b_pb[5]], writes=[b_GB])
                        S.op("act", lambda e, T0=T0, nb=nb: e.copy(T0, pbf(6, nb)), reads=[b_pb[6]], writes=[b_tmp[0][0]])
                        S.op("dve", lambda e, T0=T0, t0=t0, nb=nb: e.tensor_tensor(Pf[:, t0:t0 + nb], T0, pbf(7, nb), ALU.mult), reads=[b_tmp[0][0], b_pb[7]], writes=[b_P])
                    wb = l * 6 + cc * 3
                    w0_, w1_, w2_ = scw_s[:, wb:wb + 1], scw_s[:, wb + 1:wb + 2], scw_s[:, wb + 2:wb + 3]
                    for (a0, a1) in SEGS:
                        if a0 < tmin or a0 >= tmax:
                            continue
                        s0, s1 = [sq for sq in SEQS if sq[0] <= a0 < sq[1]][0]
                        Y = tmp[1][0]; bY = b_tmp[1][0]; n_ = a1 - a0
                        S.op("dve", lambda e, Y=Y, a0=a0, a1=a1, n_=n_, w1_=w1_: e.tensor_scalar(Y[:, 0:n_], Pf[:, a0:a1], w1_, None, ALU.mult), reads=[b_P, b_scw], writes=[bY])
                        lo = max(a0, s0 + 1)
                        S.op("dve", lambda e, Y=Y, a0=a0, a1=a1, lo=lo, w0_=w0_: e.scalar_tensor_tensor(Y[:, lo - a0:a1 - a0], Pf[:, lo - 1:a1 - 1], w0_, Y[:, lo - a0:a1 - a0], ALU.mult, ALU.add),
                             reads=[b_P, b_scw, bY], writes=[bY])
                        hi = min(a1, s1 - 1)
                        S.op("dve", lambda e, Y=Y, a0=a0, hi=hi, w2_=w2_: e.scalar_tensor_tensor(Y[:, 0:hi - a0], Pf[:, a0 + 1:hi + 1], w2_, Y[:, 0:hi - a0], ALU.mult, ALU.add),
                             reads=[b_P, b_scw, bY], writes=[bY])
                        S.op("dve", lambda e, Y=Y, a0=a0, a1=a1, n_=n_: e.tensor_tensor(HGT[:, a0:a1], GBf[:, a0:a1], Y[:, 0:n_], ALU.mult), reads=[b_GB, bY], writes=[b_HGT])
                    S.dma("sp", mixT[(4 + cc) * 128:(5 + cc) * 128, tmin:tmax], HGT[:, tmin:tmax], reads=[b_HGT], writes=[b_mixT[4 + cc]])

            def sg_layer(l, tiles):
                wi = w_in[l].rearrange("(k p) n -> p k n", p=128)
                tmax = (max(tiles) + 1) * 128; tmin = min(tiles) * 128
                if not pre_done.get(("sg", l)):
                    sg_load(l)
                for g in range(4):
                    S.dma("pool", wsn[:], sgw[4 * l + g], writes=[b_wsn])
                    S.op("pe", lambda e: e.transpose(pbb(1, 128), wsn[:], id_bf[:]), reads=[b_wsn, b_id_bf], writes=[b_pb[1]])
                    S.op("act", lambda e, g=g: e.copy(WsT[:, g * 128:(g + 1) * 128], pbb(1, 128)), reads=[b_pb[1]], writes=[b_WsT])
                SGT, b_SGT = KH, b_KH
                sgW = [dict(vn=sb("sgvn%d" % p_, [128, 256], F32), vhb=sb("sgvh%d" % p_, [128, 256], BF16), st=sb("sgst%d" % p_, [128, 40], F32),
                            b_vn=Buf("sgvn%d" % p_), b_vhb=Buf("sgvh%d" % p_), b_st=Buf("sgst%d" % p_), banks=((3, 4, 5), (6, 7, 0))[p_], par=p_) for p_ in range(2)]

                def sg_chain(i, W):
                    ts_ = slice(i * 128, (i + 1) * 128); pv, pu, pm = W["banks"]; par = W["par"]
                    vn_t, vhb_t, st_t = W["vn"], W["vhb"], W["st"]
                    yield
                    for kc in range(KC):
                        S.op("pe", lambda e, kc=kc: e.matmul(pbf(pv, 256), xmT[:, kc * N + i * 128:kc * N + (i + 1) * 128], wvg[1][:, kc * 256:(kc + 1) * 256],
                             start=(kc == 0), stop=(kc == KC - 1)), reads=[b_xmT[i], b_wvg[1]], writes=[b_pb[pv]], inc=(kc == KC - 1))
                    for g in range(4):
                        yield
                        S.op("dve", lambda e, g=g: e.bn_stats(st_t[:, g * 6:(g + 1) * 6], pbank[pv][:, g * 64:(g + 1) * 64]), reads=[b_pb[pv]], writes=[W["b_st"]])
                    for g in range(4):
                        yield
                        S.op("dve", lambda e, g=g: e.bn_aggr(st_t[:, 24 + 2 * g:26 + 2 * g], st_t[:, g * 6:(g + 1) * 6]), reads=[W["b_st"]], writes=[W["b_st"]])
                    mv = st_t[:, 24:32].rearrange("p (g two) -> p g two", two=2)
                    yield
                    S.op("act", lambda e: e.activation(st_t[:, 32:36], mv[:, :, 1], AF.Sqrt, bias=EPS), reads=[W["b_st"]], writes=[W["b_st"]])
                    yield
                    S.op("dve", lambda e: e.reciprocal(st_t[:, 36:40], st_t[:, 32:36]), reads=[W["b_st"]], writes=[W["b_st"]])
                    for g in range(4):
                        yield
                        S.op("dve", lambda e, g=g: e.tensor_scalar(vn_t[:, g * 64:(g + 1) * 64], pbank[pv][:, g * 64:(g + 1) * 64], st_t[:, 24 + 2 * g:25 + 2 * g], st_t[:, 36 + g:37 + g],
                             ALU.subtract, ALU.mult), reads=[b_pb[pv], W["b_st"]], writes=[W["b_vn"]])
                    yield
                    S.op("dve", lambda e: e.tensor_mul(vn_t[:], vn_t[:], lngb[:, 0:256]), reads=[W["b_vn"], b_lngb], writes=[W["b_vn"]])
                    yield
                    S.op("dve", lambda e: e.tensor_add(vhb_t[:], vn_t[:], lngb[:, 256:512]), reads=[W["b_vn"], b_lngb], writes=[W["b_vhb"]])
                    yield
                    for cc in range(2):
                        for kc in range(KC):
                            S.op("pe", lambda e, cc=cc, kc=kc: e.matmul(pbank[pu][:, cc * 128:(cc + 1) * 128], wvg[0][:, kc * 256 + cc * 128:kc * 256 + (cc + 1) * 128], xmT[:, kc * N + i * 128:kc * N + (i + 1) * 128],
                                 start=(kc == 0), stop=(kc == KC - 1)), reads=[b_wvg[0], b_xmT[i]], writes=[b_pb[pu]], inc=(kc == KC - 1 and cc == 1))
                    yield
                    for cc in range(2):
                        for gg in range(2):
                            g = 2 * cc + gg
                            S.op("pe", lambda e, g=g, gg=gg, cc=cc: e.matmul(pbank[pm][gg * 64:(gg + 1) * 64, cc * 128:(cc + 1) * 128], vhb_t[:, g * 64:(g + 1) * 64], WsT[:, g * 128:(g + 1) * 128], start=True, stop=True),
                                 reads=[W["b_vhb"], b_WsT], writes=[b_pb[pm]], inc=(gg == 1 and cc == 1))
                    for cc in range(2):
                        T1, T2 = tmp[cc][1 + 2 * par][:, 0:128], tmp[cc][2 + 2 * par][:, 0:128]
                        bT1, bT2 = b_tmp[cc][1 + 2 * par], b_tmp[cc][2 + 2 * par]
                        yield
                        S.op("dve", lambda e, T1=T1, cc=cc: e.tensor_tensor(T1, pbank[pm][:, cc * 128:(cc + 1) * 128], BS[:, cc * 128:(cc + 1) * 128], ALU.add), reads=[b_pb[pm], b_BS], writes=[bT1])
                        yield
                        S.op("act", lambda e, T2=T2, cc=cc: e.copy(T2, pbank[pu][:, cc * 128:(cc + 1) * 128]), reads=[b_pb[pu]], writes=[bT2])
                        yield
                        S.op("dve", lambda e, T1=T1, T2=T2, cc=cc: e.tensor_tensor(SGT[cc][:, ts_], T1, T2, ALU.mult), reads=[bT1, bT2], writes=[b_SGT[cc]])

                tl_ = list(tiles)

                def sg_stream(k):
                    for i in tl_[k::2]:
                        yield from sg_chain(i, sgW[k])

                run_rr([sg_stream(0), sg_stream(1)])
                for cc in range(2):
                    S.dma("sp", mixT[(6 + cc) * 128:(7 + cc) * 128, tmin:tmax], SGT[cc][:, tmin:tmax], reads=[b_SGT[cc]], writes=[b_mixT[6 + cc]])

            r = hgrn_layer(l, scan_tiles, out_tiles)
            if r == "cut":
                st.enter_context(ms)
                return "cut"
            if debug != "hg":
                sc_layer(l, out_tiles)
                sg_layer(l, out_tiles)
            if debug in ("hg", "mix"):
                mixer.dbg = (Oacc[0], b_O[0], HGT, b_HGT)
                st.enter_context(ms)
                return None
            S.barrier(); ms.close()
            return None

        if mixer(0, list(range(NT)), list(range(NT))) == "cut":
            return nc

        ALPHA = (2 * DEPTH) ** 0.25
        x1d = nc.dram_tensor("x1d", [N, D], F32, kind="Internal").ap()
        b_x1d = [Buf("x1d%d" % i) for i in range(NT)]

        def gate_bcast(dst, b_dst, l, slot, srcm, scr, b_scr):
            for kc in range(KC):
                g = modv(l, slot, kc, srcm)
                S.op("dve", lambda e, g=g: e.tensor_scalar(scr[:], id_f[:], 0.0, g, ALU.mult, ALU.add), reads=[b_id_f, b_mod], writes=[b_scr])
                S.op("pe", lambda e: e.matmul(pbf(0, 128), scr[:], id_f[:], start=True, stop=True), reads=[b_scr, b_id_f], writes=[b_pb[0]])
                S.op("act", lambda e, kc=kc: e.copy(dst[:, kc * 128:(kc + 1) * 128], pbf(0, 128)), reads=[b_pb[0]], writes=[b_dst])

        def phase_wout_ln1(l, src_ap, tiles, src_bufs=None):
            ps4 = ExitStack()
            sb4 = lambda n, s_, dt: ps4.enter_context(nc.sbuf_tensor("%s_p4L%d" % (n, l), s_, dt))
            mixS = sb4("mixS", [128, KC * N], BF16); b_mixS = [Buf("mixS%d" % k) for k in range(KC)]
            wo = sb4("wo", [128, KC * D], BF16); b_wo = Buf("wo")
            gbc = [sb4("gbc%d" % i, [128, D], F32) for i in range(2)]; b_gbc = [Buf("gbc0"), Buf("gbc1")]
            lg = sb4("lg", [128, D], F32); lb_ = sb4("lb_", [128, D], F32); b_lg, b_lbb = Buf("lg"), Buf("lbb")
            scr = sb4("scr", [128, 128], F32); b_scr = Buf("scr")
            for k in range(KC):
                S.dma("sp", mixS[:, k * N:(k + 1) * N], mixT[k * 128:(k + 1) * 128, :], reads=[b_mixT[k]], writes=[b_mixS[k]])
            S.dma("pool", wo[:].rearrange("p (k n) -> p k n", k=KC), w_out[l].rearrange("(k p) n -> p k n", p=128), writes=[b_wo])
            S.dma("sp", lg[:], lnp[4 * l:4 * l + 1, :].partition_broadcast(128), writes=[b_lg])
            S.dma("sp", lb_[:], lnp[4 * l + 1:4 * l + 2, :].partition_broadcast(128), writes=[b_lbb])
            gate_bcast(gbc[0], b_gbc[0], l, 2, 0, scr, b_scr)
            if any(i < 2 for i in tiles):
                gate_bcast(gbc[1], b_gbc[1], l, 2, 1, scr, b_scr)
            G4 = 3
            sets = mk_sets(sb4, 2 * G4, "b", with_y=True, plan=[(3, 3, 1), (4, 4, 2), (5, 5, 6)])
            load = lambda i, B_: S.dma("sp", B_["xq"][:], src_ap[i * 128:(i + 1) * 128, :], reads=([src_bufs[i]] if src_bufs else []), writes=[B_["b_xq"]])

            def chain4(i, B_):
                srcm = 1 if i < 2 else 0
                xq_t, yt_t, xt_t, st_t = B_["xq"], B_["yt"], B_["xt"], B_["st"]
                for hf in range(2):
                    pbi = B_["mm"][hf]; hs = slice(hf * 512, (hf + 1) * 512)
                    yield
                    for kc in range(KC):
                        S.op("pe", lambda e, kc=kc, hf=hf, pbi=pbi: e.matmul(pbf(pbi, 512), mixS[:, kc * N + i * 128:kc * N + (i + 1) * 128],
                             wo[:, kc * D + hf * 512:kc * D + (hf + 1) * 512], start=(kc == 0), stop=(kc == KC - 1)),
                             reads=[b_mixS[kc], b_wo], writes=[b_pb[pbi]], inc=(kc == KC - 1))
                    yield
                    S.op("dve", lambda e, hs=hs, pbi=pbi: e.tensor_tensor(yt_t[:, hs], pbf(pbi, 512), gbc[srcm][:, hs], ALU.mult),
                         reads=[b_pb[pbi], b_gbc[srcm]], writes=[B_["b_yt"]])
                    yield
                    S.op("dve", lambda e, hs=hs: e.scalar_tensor_tensor(yt_t[:, hs], xq_t[:, hs], ALPHA, yt_t[:, hs], ALU.mult, ALU.add),
                         reads=[B_["b_xq"], B_["b_yt"]], writes=[B_["b_yt"]])
                yield from ln_stats_gen(yt_t, B_["b_yt"], B_)
                yield
                S.op("dve", lambda e: e.scalar_tensor_tensor(yt_t[:], yt_t[:], st_t[:, 12:13], lg[:], ALU.subtract, ALU.mult),
                     reads=[B_["b_yt"], B_["b_st"], b_lg], writes=[B_["b_yt"]])
                yield
                S.op("dve", lambda e: e.scalar_tensor_tensor(xt_t[:], yt_t[:], st_t[:, 15:16], lb_[:], ALU.mult, ALU.add),
                     reads=[B_["b_yt"], B_["b_st"], b_lbb, B_["b_xt"]], writes=[B_["b_xt"]])
                yield
                S.dma("sp", x1d[i * 128:(i + 1) * 128, :], xt_t[:], reads=[B_["b_xt"]], writes=[b_x1d[i]])
                yield from ln_mod_T_gen(l, i, 3, 4, B_)

            run_groups(tiles, sets, load, chain4, GRP=G4)
            S.barrier(); ps4.close()

        if debug in ("x1", "h", "x2", None):
            phase_wout_ln1(0, xin, list(range(NT)))
        if debug == "x1":
            for i in range(NT):
                a = i % 2
                S.dma("sp", xt[a][:], x1d[i * 128:(i + 1) * 128, :], reads=[b_x1d[i]], writes=[b_xt[a]])
                S.dma("sp", outs["dbg_x1"][i * 128:(i + 1) * 128, :], xt[a][:], reads=[b_xt[a]])
            xm_f = sb("xm2_f", [128, N], F32); b_xmf = Buf("xm2f")
            for kc in range(KC):
                S.op("act", lambda e, kc=kc: e.copy(xm_f[:], xmT[:, kc * N:(kc + 1) * N]), reads=b_xmT, writes=[b_xmf])
                S.dma("sp", outs["dbg_xm2T"][:, kc * N:(kc + 1) * N], xm_f[:], reads=[b_xmf])

        hTd = nc.dram_tensor("hTd", [FF, N], BF16, kind="Internal").ap()
        b_hTd = [Buf("hTd%d" % i) for i in range(NFC)]
        x2d = [nc.dram_tensor("x2d%d" % i, [N, D], F32, kind="Internal").ap() for i in range(DEPTH - 1)]
        b_x2d = [Buf("x2d%d" % i) for i in range(NT)]
        GW = 64

        def phase_ffn_up(l, with_ctx):
            p5 = ExitStack()
            sb5 = lambda n, s_, dt: p5.enter_context(nc.sbuf_tensor("%s_p5L%d" % (n, l), s_, dt))
            fcw_s = sb5("fcw_s", [128, NFC * 9], F32); fcb_s = sb5("fcb_s", [128, NFC], F32); b_fcw, b_fcb = Buf("fcw"), Buf("fcb")
            S.dma("sp", fcw_s[:], fcw[:, l * NFC * 9:(l + 1) * NFC * 9], writes=[b_fcw])
            S.dma("sp", fcb_s[:], fcb[:, l * NFC:(l + 1) * NFC], writes=[b_fcb])
            wag = [sb5("wag%d" % i, [128, KC * 256], BF16) for i in range(2)]; b_wag = [Buf("wag0"), Buf("wag1")]
            apx = [sb5("apx%d" % i, [128, 34 * 66], BF16) for i in range(2)]; apc = [sb5("apc%d" % i, [128, 258], BF16) for i in range(2)]
            b_ap = [Buf("ap0"), Buf("ap1")]
            dg = [sb5("dg%d" % i, [128, 9 * 128], BF16) for i in range(2)]; b_dg = [Buf("dg0"), Buf("dg1")]
            gel = [sb5("gel%d" % i, [128, 512], F32) for i in range(2)]; b_gel = [Buf("gel0"), Buf("gel1")]
            htc = [sb5("htc%d" % i, [128, N], BF16) for i in range(2)]; b_htc = [Buf("htc0"), Buf("htc1")]
            for i in range(2):
                S.op("pool", lambda e, i=i: e.memset(apx[i][:], 0.0), writes=[b_ap[i]])
                S.op("pool", lambda e, i=i: e.memset(apc[i][:], 0.0), writes=[b_ap[i]])
            FB = ([("c", 0, 256, 0)] if with_ctx else []) + [("x", 256 + 512 * j, 512, j) for j in range(4)]
            wu = ffn_up[l].rearrange("(k p) n -> p k n", p=128)
            defer_l = l + 1 if (l + 1 < DEPTH) else None
            if defer_l is not None:
                awd = [sb5("awd%d" % i, [128, KC * 512], BF16) for i in range(2)]; b_awd = [Buf("awd0"), Buf("awd1")]
                pm_d, b_pmd = pbf(1, 96), b_pb[1]
            for fc in range(NFC):
                a = fc % 2
                if defer_l is not None:
                    if fc < 12:
                        mod_chunk_load(defer_l, fc, awd[fc % 2], b_awd[fc % 2])
                    if 1 <= fc <= 12:
                        mod_chunk_mm(defer_l, fc - 1, awd[(fc - 1) % 2], b_awd[(fc - 1) % 2], pm_d, b_pmd)
                    if fc == 13:
                        mod_finish(defer_l, pm_d, b_pmd)
                for ci, c0 in enumerate((fc * 128, FF + fc * 128)):
                    S.dma("pool", wag[a][:].rearrange("p (k n) -> p k n", k=KC)[:, :, ci * 128:(ci + 1) * 128], wu[:, :, c0:c0 + 128], writes=[b_wag[a]])
                for tap in range(9):
                    wcol = fcw_s[:, fc * 9 + tap:fc * 9 + tap + 1]
                    S.op("dve", lambda e, a=a, tap=tap, wcol=wcol: e.tensor_scalar(dg[a][:, tap * 128:(tap + 1) * 128], id_f[:], wcol, None, ALU.mult),
                         reads=[b_id_f, b_fcw], writes=[b_dg[a]])
                apx3 = apx[a][:].rearrange("p (r c) -> p r c", c=66)
                for bn, (kind, t0, nb, j) in enumerate(FB):
                    pa = (3, 6)[bn % 2]
                    tl = list(range(t0 // 128, (t0 + nb) // 128))
                    for kc in range(KC):
                        S.op("pe", lambda e, a=a, kc=kc, t0=t0, nb=nb, pa=pa: e.matmul(pbf(pa, nb), wag[a][:, kc * 256:kc * 256 + 128], xmT[:, kc * N + t0:kc * N + t0 + nb],
                             start=(kc == 0), stop=(kc == KC - 1)), reads=[b_wag[a]] + [b_xmT[i] for i in tl], writes=[b_pb[pa]], inc=(kc == KC - 1))
                    if kind == "c":
                        S.op("act", lambda e, a=a, pa=pa: e.copy(apc[a][:, 1:257], pbf(pa, 256)), reads=[b_pb[pa]], writes=[b_ap[a]])
                    else:
                        S.op("act", lambda e, a=a, pa=pa, j=j, apx3=apx3: e.copy(apx3[:, 1 + 8 * j:9 + 8 * j, 1:65], pbf(pa, 512).rearrange("p (r c) -> p r c", c=GW)),
                             reads=[b_pb[pa]], writes=[b_ap[a]])
                for bn, (kind, t0, nb, j) in enumerate(FB):
                    pc, pg = (4, 7)[bn % 2], (5, 0)[bn % 2]
                    ga = bn % 2
                    tl = list(range(t0 // 128, (t0 + nb) // 128))
                    if kind == "c":
                        for n_, dj in enumerate(range(3)):
                            tap = 3 + dj
                            S.op("pe", lambda e, a=a, tap=tap, dj=dj, pc=pc, n_=n_: e.matmul(pbf(pc, 256), dg[a][:, tap * 128:(tap + 1) * 128], apc[a][:, dj:dj + 256], start=(n_ == 0), stop=(n_ == 2)),
                                 reads=[b_dg[a], b_ap[a]], writes=[b_pb[pc]], inc=(n_ == 2))
                    else:
                        for tap in range(9):
                            di, dj = tap // 3, tap % 3
                            mv_ = apx3[:, di + 8 * j:di + 8 * j + 8, dj:dj + GW]
                            S.op("pe", lambda e, a=a, tap=tap, mv_=mv_, pc=pc: e.matmul(pbf(pc, 512), dg[a][:, tap * 128:(tap + 1) * 128], mv_, start=(tap == 0), stop=(tap == 8)),
                                 reads=[b_dg[a], b_ap[a]], writes=[b_pb[pc]], inc=(tap == 8))
                    for kc in range(KC):
                        S.op("pe", lambda e, a=a, kc=kc, t0=t0, nb=nb, pg=pg: e.matmul(pbf(pg, nb), wag[a][:, kc * 256 + 128:kc * 256 + 256], xmT[:, kc * N + t0:kc * N + t0 + nb],
                             start=(kc == 0), stop=(kc == KC - 1)), reads=[b_wag[a]] + [b_xmT[i] for i in tl], writes=[b_pb[pg]], inc=(kc == KC - 1))
                    bcol = fcb_s[:, fc:fc + 1]
                    S.op("act", lambda e, ga=ga, nb=nb, pc=pc, bcol=bcol: e.activation(gel[ga][:, 0:nb], pbf(pc, nb), AF.Gelu, bias=bcol), reads=[b_pb[pc], b_fcb], writes=[b_gel[ga]])
                    S.op("dve", lambda e, a=a, ga=ga, t0=t0, nb=nb, pg=pg: e.tensor_tensor(htc[a][:, t0:t0 + nb], gel[ga][:, 0:nb], pbf(pg, nb), ALU.mult),
                         reads=[b_gel[ga], b_pb[pg]], writes=[b_htc[a]])
                lo = FB[0][1]
                S.dma("sp", hTd[fc * 128:(fc + 1) * 128, lo:N], htc[a][:, lo:N], reads=[b_htc[a]], writes=[b_hTd[fc]])
            S.barrier(); p5.close()

        def phase_ffn_down(l, tiles, dst_ap, dst_row0):
            p6 = ExitStack()
            sb6 = lambda n, s_, dt: p6.enter_context(nc.sbuf_tensor("%s_p6L%d" % (n, l), s_, dt))
            wd = sb6("wd", [128, NFC * D], BF16); b_wdp = {(hf_, q_): Buf("wd%d%d" % (hf_, q_)) for hf_ in range(2) for q_ in range(2)}
            gbc = [sb6("gbc%d" % i, [128, D], F32) for i in range(2)]; b_gbc = [Buf("gbc0"), Buf("gbc1")]
            lg = sb6("lg", [128, D], F32); lb_ = sb6("lb_", [128, D], F32); b_lg, b_lbb = Buf("lg"), Buf("lbb")
            scr = sb6("scr", [128, 128], F32); b_scr = Buf("scr")
            wdv = ffn_down[l].rearrange("(f p) n -> p f n", p=128)
            for hf_ in range(2):
                for q_ in range(2):
                    S.dma("pool", wd[:].rearrange("p (f n) -> p f n", f=NFC)[:, q_ * 11:(q_ + 1) * 11, hf_ * 512:(hf_ + 1) * 512],
                          wdv[:, q_ * 11:(q_ + 1) * 11, hf_ * 512:(hf_ + 1) * 512], writes=[b_wdp[(hf_, q_)]])
            S.dma("sp", lg[:], lnp[4 * l + 2:4 * l + 3, :].partition_broadcast(128), writes=[b_lg])
            S.dma("sp", lb_[:], lnp[4 * l + 3:4 * l + 4, :].partition_broadcast(128), writes=[b_lbb])
            gate_bcast(gbc[0], b_gbc[0], l, 5, 0, scr, b_scr)
            if any(i < 2 for i in tiles):
                gate_bcast(gbc[1], b_gbc[1], l, 5, 1, scr, b_scr)
            hv = hTd.rearrange("(f p) n -> p f n", p=128)
            sets = mk_sets(sb6, 2 * GRP, "c", with_y=True, with_h=True)

            def load(i, B_):
                S.dma("sp", B_["ht"][:].rearrange("p (f n) -> p f n", f=NFC), hv[:, :, i * 128:(i + 1) * 128], reads=b_hTd, writes=[B_["b_ht"]])
                S.dma("sp", B_["xq"][:], x1d[i * 128:(i + 1) * 128, :], reads=[b_x1d[i]], writes=[B_["b_xq"]])

            def chain6(i, B_):
                srcm = 1 if i < 2 else 0
                xq_t, yt_t, xt_t, st_t, ht_t = B_["xq"], B_["yt"], B_["xt"], B_["st"], B_["ht"]
                for hf in range(2):
                    pbi = B_["mm"][hf]; hs = slice(hf * 512, (hf + 1) * 512)
                    yield
                    for f_ in range(NFC):
                        S.op("pe", lambda e, f_=f_, hf=hf, pbi=pbi: e.matmul(pbf(pbi, 512), ht_t[:, f_ * 128:(f_ + 1) * 128], wd[:, f_ * D + hf * 512:f_ * D + (hf + 1) * 512],
                             start=(f_ == 0), stop=(f_ == NFC - 1)), reads=[B_["b_ht"], b_wdp[(hf, f_ // 11)]], writes=[b_pb[pbi]], inc=(f_ == NFC - 1))
                    yield
                    S.op("dve", lambda e, hs=hs, pbi=pbi: e.tensor_tensor(yt_t[:, hs], pbf(pbi, 512), gbc[srcm][:, hs], ALU.mult),
                         reads=[b_pb[pbi], b_gbc[srcm]], writes=[B_["b_yt"]])
                    yield
                    S.op("dve", lambda e, hs=hs: e.scalar_tensor_tensor(yt_t[:, hs], xq_t[:, hs], ALPHA, yt_t[:, hs], ALU.mult, ALU.add),
                         reads=[B_["b_xq"], B_["b_yt"]], writes=[B_["b_yt"]])
                yield from ln_stats_gen(yt_t, B_["b_yt"], B_)
                yield
                S.op("dve", lambda e: e.scalar_tensor_tensor(yt_t[:], yt_t[:], st_t[:, 12:13], lg[:], ALU.subtract, ALU.mult),
                     reads=[B_["b_yt"], B_["b_st"], b_lg], writes=[B_["b_yt"]])
                yield
                S.op("dve", lambda e: e.scalar_tensor_tensor(xt_t[:], yt_t[:], st_t[:, 15:16], lb_[:], ALU.mult, ALU.add),
                     reads=[B_["b_yt"], B_["b_st"], b_lbb, B_["b_xt"]], writes=[B_["b_xt"]])
                r0 = i * 128 - dst_row0
                yield
                S.dma("sp", dst_ap[r0:r0 + 128, :], xt_t[:], reads=[B_["b_xt"]], writes=[b_x2d[i]])

            run_groups(tiles, sets, load, chain6)
            S.barrier(); p6.close()

        if debug in ("h", "x2", None):
            phase_ffn_up(0, True)
        if debug == "h":
            hst_b = sb("hst_b", [128, N], BF16); hst_f = sb("hst_f", [128, N], F32); b_hsb, b_hsf = Buf("hsb"), Buf("hsf")
            for fc in range(NFC):
                S.dma("sp", hst_b[:], hTd[fc * 128:(fc + 1) * 128, :], reads=[b_hTd[fc]], writes=[b_hsb])
                S.op("act", lambda e: e.copy(hst_f[:], hst_b[:]), reads=[b_hsb], writes=[b_hsf])
                S.dma("sp", outs["dbg_hT"][fc * 128:(fc + 1) * 128, :], hst_f[:], reads=[b_hsf])
        if debug in ("x2", None):
            phase_ffn_down(0, list(range(NT)), x2d[0], 0)
        if debug == "x2":
            for i in range(NT):
                a = i % 2
                S.dma("sp", xt[a][:], x2d[0][i * 128:(i + 1) * 128, :], reads=[b_x2d[i]], writes=[b_xt[a]])
                S.dma("sp", outs["dbg_x2"][i * 128:(i + 1) * 128, :], xt[a][:], reads=[b_xt[a]])

        if debug is None:
            XT = list(range(2, NT))
            phase_ln_mod_T(1, x2d[0], range(NT), 0, 1, src_bufs=b_x2d)
            mixer(1, list(range(NT)), XT)
            phase_wout_ln1(1, x2d[0], XT, src_bufs=b_x2d)
            phase_ffn_up(1, False)
            phase_ffn_down(1, XT, outs["out"], 256)
        if debug in ("hg", "mix"):
            hg_f, b_hgf, hg_b, b_hgb = mixer.dbg
            for h in range(8 if debug == "mix" else 4):
                S.dma("sp", hg_b[:], mixT[h * 128:(h + 1) * 128, :], reads=[b_mixT[h]], writes=[b_hgb])
                S.op("act", lambda e: e.copy(hg_f[:], hg_b[:]), reads=[b_hgb], writes=[b_hgf])
                S.dma("sp", outs["dbg_hgT"][h * 128:(h + 1) * 128, :], hg_f[:], reads=[b_hgf])
        S.drain_all("sp")
        S.emit()
        build.ninstr = S.ninstr
    return nc


def _prep(inputs, b):
    f = lambda a: np.ascontiguousarray(np.asarray(a, dtype=np.float32))
    m = {}
    m["xin"] = f(np.concatenate([inputs["ctx"][b], inputs["x"][b]], axis=0))
    cc = np.stack([np.asarray(inputs["c"][b]).reshape(KC, 128).T, np.asarray(inputs["c_ctx"]).reshape(KC, 128).T], axis=-1)
    m["c2"] = f(cc.reshape(128, KC * 2))
    m["ada_w"] = f(inputs["ada_w"])
    m["ada_b_fm"] = f(np.asarray(inputs["ada_b"]).reshape(DEPTH, 48, 128).transpose(2, 0, 1).reshape(128, DEPTH * 48))
    m["ident"] = np.eye(128, dtype=np.float32)
    m["w_in"] = f(inputs["w_in"])
    m["w_out"] = f(inputs["w_out"])
    m["ffn_up"] = f(inputs["ffn_up"]); m["ffn_down"] = f(inputs["ffn_down"])
    m["fcw_fm"] = f(np.asarray(inputs["ffn_conv_w"]).reshape(DEPTH, 9, NFC, 128).transpose(3, 0, 2, 1).reshape(128, DEPTH * NFC * 9))
    m["fcb_fm"] = f(np.asarray(inputs["ffn_conv_b"]).reshape(DEPTH, NFC, 128).transpose(2, 0, 1).reshape(128, DEPTH * NFC))
    m["lnp"] = f(np.stack([np.asarray(inputs[k]) for k in ("ln1_g", "ln1_b", "ln2_g", "ln2_b")], axis=1).reshape(DEPTH * 4, D))
    m["scw_fm"] = f(np.asarray(inputs["sc_conv_w"]).reshape(DEPTH, 3, 2, 128).transpose(3, 0, 2, 1).reshape(128, DEPTH * 6))
    m["sgln"] = f(np.stack([np.asarray(inputs["sg_ln_g"]), np.asarray(inputs["sg_ln_b"])], axis=1).reshape(DEPTH * 2, 256))
    m["sg_w"] = f(np.asarray(inputs["sg_w"]).reshape(DEPTH * 4, 128, 128))
    sb_ = np.asarray(inputs["sg_b"]).reshape(DEPTH, 2, 2, 1, 128)
    m["sgb_fm"] = f(np.broadcast_to(sb_, (DEPTH, 2, 2, 64, 128)).reshape(DEPTH * 2, 128, 128))
    m["lbl"] = f(np.asarray(inputs["hg_lb"]).reshape(DEPTH, 2, 4, 128).transpose(3, 0, 1, 2).reshape(128, 16))
    m["hg_norm_g"] = f(inputs["hg_norm_g"])
    ii = np.arange(128)
    same = (ii[:, None] // 64) == (ii[None, :] // 64)
    m["cmask"] = f(np.concatenate([same & (ii[:, None] <= ii[None, :]), same & (ii[:, None] >= ii[None, :])], axis=1))
    rm = np.ones((128, 512), np.float32); rm[:, ::64] = 0.0
    m["rmask"] = rm
    return m


_NC = None


def kernel(**inputs):
    global _NC
    if _NC is None:
        _NC = build()
    shared = None
    maps = []
    for b in range(8):
        m = _prep(inputs, b)
        if shared is None:
            shared = {k: m[k] for k in m if k not in ("xin", "c2")}
        else:
            m.update(shared)
        maps.append(m)
    res = run_bass_kernel_spmd(_NC, maps, core_ids=list(range(8)))
    return np.stack([np.asarray(r["out"], dtype=np.float32) for r in res.results], axis=0)
```

```python
import numpy as np
import concourse.bass as bass
import concourse.mybir as mybir

F32 = mybir.dt.float32
BF16 = mybir.dt.bfloat16
ALU = mybir.AluOpType
AF = mybir.ActivationFunctionType


class Buf:
    __slots__ = ("name", "w", "r", "excl")

    def __init__(self, name="", excl=False):
        self.name = name
        self.excl = excl
        self.w = None
        self.r = []


class Sync:
    ENGS = ("pe", "act", "dve", "pool", "sp")

    SEM_LIMIT = 1900

    def __init__(self, nc, stack, n_dma_sems=20):
        self.nc = nc
        self.stack = stack
        self.owner = {}
        self.cur = {}
        self.nsem = 0
        self.q = {e: [] for e in self.ENGS}
        self.sems = {}
        self.cnt = {}
        self.known = {e: {} for e in self.ENGS}
        for e in ("pe", "act", "dve", "pool"):
            self.cur[e] = None
            self._new_sem(e)
        self.dpool = {}
        self.dk = {}
        for e in ("sp", "act", "pool"):
            self.dpool[e] = []
            for i in range(n_dma_sems):
                key = "d_%s_%d" % (e, i)
                self.sems[key] = stack.enter_context(nc.semaphore(key)); self.nsem += 1
                self.cnt[key] = 0
                self.owner[key] = "dma_" + e
                self.dpool[e].append(key)
            self.dk[e] = 0
        self.pe_pending = False
        self.ninstr = 0

    def _new_sem(self, eng):
        n = sum(1 for k in self.owner if self.owner[k] == eng)
        key = "%s#%d" % (eng, n)
        self.sems[key] = self.stack.enter_context(self.nc.semaphore("s_%s_%d" % (eng, n))); self.nsem += 1
        self.cnt[key] = 0
        self.owner[key] = eng
        self.prev = getattr(self, "prev", {})
        self.prev[eng] = self.cur[eng]
        self.cur[eng] = key

    def _latest(self, eng):
        k = self.cur[eng]
        if self.cnt[k]:
            return (k, self.cnt[k])
        p = self.prev.get(eng)
        return (p, self.cnt[p]) if p and self.cnt[p] else None

    def _wait(self, eng, tok):
        if tok is None:
            return
        key, val = tok
        if self.known[eng].get(key, 0) >= val:
            return
        self.known[eng][key] = val
        sem = self.sems[key]
        self.q[eng].append(lambda e, sem=sem, val=val: e.wait_ge(sem, val))
        self.ninstr += 1

    def _deps(self, eng, reads, writes, skip_self=False):
        for b in reads:
            if b.w is not None and not (skip_self and self.owner[b.w[0]] == eng):
                self._wait(eng, b.w)
            if b.excl:
                for t in b.r:
                    if self.owner[t[0]] != eng:
                        self._wait(eng, t)
        for b in writes:
            if b.w is not None and not (skip_self and self.owner[b.w[0]] == eng):
                self._wait(eng, b.w)
            for t in b.r:
                if not (skip_self and self.owner[t[0]] == eng):
                    self._wait(eng, t)

    def _commit(self, tok, reads, writes):
        for b in reads:
            b.r.append(tok)
            if len(b.r) > 64:
                best = {}
                for k, v in b.r:
                    if best.get(k, 0) < v:
                        best[k] = v
                b.r = list(best.items())
        for b in writes:
            b.w = tok
            b.r = []

    def op(self, eng, fn, reads=(), writes=(), inc=True):
        pe = eng == "pe"
        self._deps(eng, reads, writes, skip_self=pe)
        if self.cnt[self.cur[eng]] >= self.SEM_LIMIT and not (pe and self.pe_pending):
            self._new_sem(eng)
        key = self.cur[eng]
        if inc:
            self.cnt[key] += 1
            tok = (key, self.cnt[key])
            sem = self.sems[key]
            self.q[eng].append(lambda e, fn=fn, sem=sem: fn(e).then_inc(sem, 1))
            if pe:
                self.pe_pending = False
        else:
            assert pe
            tok = (key, self.cnt[key] + 1)
            self.q[eng].append(lambda e, fn=fn: fn(e))
            self.pe_pending = True
        self.ninstr += 1
        self._commit(tok, reads, writes)
        return tok

    def dma(self, eng, out, in_, reads=(), writes=(), **kw):
        self._deps(eng, reads, writes)
        pool = self.dpool[eng]
        slot = self.dk[eng] % len(pool)
        key = pool[slot]
        self.dk[eng] += 1
        if self.cnt[key] + 16 > self.SEM_LIMIT:
            self._wait(eng, (key, self.cnt[key]))
            n = sum(1 for k in self.owner if self.owner[k] == "dma_" + eng)
            nk = "d_%s_%d" % (eng, n)
            self.sems[nk] = self.stack.enter_context(self.nc.semaphore(nk)); self.nsem += 1
            self.cnt[nk] = 0; self.owner[nk] = "dma_" + eng
            pool[slot] = nk; key = nk
            self.retired = getattr(self, "retired", []) + [key]
        if self.cnt[key] > 0:
            self._wait(eng, (key, self.cnt[key]))
        self.cnt[key] += 16
        tok = (key, self.cnt[key])
        sem = self.sems[key]
        self.q[eng].append(
            lambda e, out=out, in_=in_, sem=sem, kw=kw: e.dma_start(out=out, in_=in_, **kw).then_inc(sem, 16))
        self.ninstr += 1
        self._commit(tok, reads, writes)
        return tok

    def wait_all(self, eng, toks):
        for t in toks:
            self._wait(eng, t)

    def barrier(self):
        toks = [t for t in (self._latest(e) for e in ("pe", "act", "dve", "pool")) if t]
        assert not self.pe_pending
        for q in self.dpool:
            for key in self.dpool[q]:
                if self.cnt[key]:
                    toks.append((key, self.cnt[key]))
        for eng in self.ENGS:
            for t in toks:
                if self.owner[t[0]] != eng:
                    self._wait(eng, t)

    def drain_all(self, eng="sp"):
        for q in self.dpool:
            for key in self.dpool[q]:
                if self.cnt[key]:
                    self._wait(eng, (key, self.cnt[key]))

    def emit(self):
        assert not self.pe_pending, "last PE op must carry inc"
        nc = self.nc
        q = self.q
        with nc.Block() as block:
            @block.sync
            def _(e):
                for f in q["sp"]:
                    f(e)

            @block.tensor
            def _(e):
                for f in q["pe"]:
                    f(e)

            @block.scalar
            def _(e):
                for f in q["act"]:
                    f(e)

            @block.vector
            def _(e):
                for f in q["dve"]:
                    f(e)

            @block.gpsimd
            def _(e):
                for f in q["pool"]:
                    f(e)


def run_rr(chains):
    live = list(chains)
    while live:
        for g in list(live):
            try:
                next(g)
            except StopIteration:
                live.remove(g)


from contextlib import ExitStack
from concourse.bass_utils import run_bass_kernel_spmd

FF = 2816; NFC = FF // 128
D = 1024; NT = 18; N = NT * 128; DEPTH = 2; NMOD = 6; EPS = 1e-6
KC = D // 128


def build(debug=None):
    nc = bass.Bass("TRN2", target_bir_lowering=False)
    dram = lambda n, s, dt, kind: nc.dram_tensor(n, s, dt, kind=kind).ap()
    xin = dram("xin", [N, D], F32, "ExternalInput")
    c2 = dram("c2", [128, KC * 2], F32, "ExternalInput")
    ada_w = dram("ada_w", [DEPTH, D, NMOD * D], F32, "ExternalInput")
    ada_b = dram("ada_b_fm", [128, DEPTH * 48], F32, "ExternalInput")
    ident = dram("ident", [128, 128], F32, "ExternalInput")
    w_in = dram("w_in", [DEPTH, D, 3840], F32, "ExternalInput")
    lbl = dram("lbl", [128, 16], F32, "ExternalInput")
    hgng = dram("hg_norm_g", [DEPTH, 128], F32, "ExternalInput")
    scw = dram("scw_fm", [128, DEPTH * 6], F32, "ExternalInput")
    sgln = dram("sgln", [DEPTH * 2, 256], F32, "ExternalInput")
    sgw = dram("sg_w", [DEPTH * 4, 128, 128], F32, "ExternalInput")
    sgb = dram("sgb_fm", [DEPTH * 2, 128, 128], F32, "ExternalInput")
    w_out = dram("w_out", [DEPTH, D, D], F32, "ExternalInput")
    ffn_up = dram("ffn_up", [DEPTH, D, 2 * FF], F32, "ExternalInput")
    ffn_down = dram("ffn_down", [DEPTH, FF, D], F32, "ExternalInput")
    fcw = dram("fcw_fm", [128, DEPTH * NFC * 9], F32, "ExternalInput")
    fcb = dram("fcb_fm", [128, DEPTH * NFC], F32, "ExternalInput")
    lnp = dram("lnp", [DEPTH * 4, D], F32, "ExternalInput")
    cmask = dram("cmask", [128, 256], F32, "ExternalInput")
    rmask = dram("rmask", [128, 512], F32, "ExternalInput")
    outs = {}
    if debug is None:
        outs["out"] = dram("out", [N - 256, D], F32, "ExternalOutput")
    if debug and debug.startswith("hg") and len(debug) == 3:
        outs["dbg_cut"] = dram("dbg_cut", [128, 2 * N], F32, "ExternalOutput")
    if debug == "h":
        outs["dbg_hT"] = dram("dbg_hT", [FF, N], F32, "ExternalOutput")
    if debug == "x2":
        outs["dbg_x2"] = dram("dbg_x2", [N, D], F32, "ExternalOutput")
    if debug == "x1":
        outs["dbg_x1"] = dram("dbg_x1", [N, D], F32, "ExternalOutput")
        outs["dbg_xm2T"] = dram("dbg_xm2T", [128, KC * N], F32, "ExternalOutput")
    if debug in ("hg", "mix"):
        outs["dbg_hgT"] = dram("dbg_hgT", [D if debug == "mix" else 512, N], F32, "ExternalOutput")
    if debug == "p1":
        outs["dbg_mod"] = dram("dbg_mod", [128, DEPTH * 96], F32, "ExternalOutput")
        outs["dbg_xmT"] = dram("dbg_xmT", [128, KC * N], F32, "ExternalOutput")
    with ExitStack() as st:
        S = Sync(nc, st)
        sb = lambda n, s, dt: st.enter_context(nc.sbuf_tensor(n, s, dt))
        pbank = [st.enter_context(nc.psum_tensor("pb%d" % i, [128, 512], F32)) for i in range(8)]
        b_pb = [Buf("pb%d" % i, excl=True) for i in range(8)]
        pbf = lambda i, n: pbank[i][:, 0:n]
        pbb = lambda i, n: pbank[i][:].bitcast(BF16)[:, 0:n]
        id_f = sb("id_f", [128, 128], F32); id_bf = sb("id_bf", [128, 128], BF16)
        c2_f = sb("c2_f", [128, KC * 2], F32); c2_bf = sb("c2_bf", [128, KC * 2], BF16)
        adab = sb("adab", [128, DEPTH * 48], F32)
        mod = sb("mod", [128, DEPTH * 96], F32)
        mod1p = sb("mod1p", [128, DEPTH * 96], F32)
        b_id_f, b_id_bf, b_c2f, b_c2bf, b_adab, b_mod, b_mod1p = (Buf(n) for n in "idf idbf c2f c2bf adab mod mod1p".split())
        S.dma("sp", id_f[:], ident, writes=[b_id_f])
        S.dma("pool", id_bf[:], ident, writes=[b_id_bf])
        S.dma("sp", c2_f[:], c2, writes=[b_c2f])
        S.dma("sp", adab[:], ada_b, writes=[b_adab])
        S.op("act", lambda e: e.activation(c2_bf[:], c2_f[:], AF.Silu), reads=[b_c2f], writes=[b_c2bf])
        st0 = ExitStack()
        aw = [st0.enter_context(nc.sbuf_tensor("aw%d" % i, [128, KC * 512], BF16)) for i in range(2)]
        b_aw = [Buf("aw0"), Buf("aw1")]
        p_mod = pbf(0, 96); b_pmod = b_pb[0]
        def mod_chunk_load(l, g, buf, bb):
            src = ada_w[l].rearrange("(k p) n -> p k n", p=128)[:, :, g * 512:(g + 1) * 512]
            S.dma("pool", buf[:].rearrange("p (k n) -> p k n", k=KC), src, writes=[bb])

        def mod_chunk_mm(l, g, buf, bb, pm_ap, b_pm):
            for jj in range(4):
                j = g * 4 + jj
                for k in range(KC):
                    last = (k == KC - 1)
                    S.op("pe", lambda e, buf=buf, jj=jj, k=k, j=j, last=last: e.matmul(
                        pm_ap[:, 2 * j:2 * j + 2], buf[:, k * 512 + jj * 128:k * 512 + (jj + 1) * 128],
                        c2_bf[:, 2 * k:2 * k + 2], start=(k == 0), stop=last),
                        reads=[bb, b_c2bf], writes=[b_pm], inc=(last and jj == 3))

        def mod_finish(l, pm_ap, b_pm):
            pm = pm_ap.rearrange("p (j s) -> p j s", s=2)
            mv = mod[:, l * 96:(l + 1) * 96].rearrange("p (j s) -> p j s", s=2)
            for s_ in range(2):
                S.op("dve", lambda e, pm=pm, mv=mv, s_=s_, l=l: e.tensor_tensor(
                    mv[:, :, s_], pm[:, :, s_], adab[:, l * 48:(l + 1) * 48], ALU.add),
                    reads=[b_pm, b_adab], writes=[b_mod])
            S.op("dve", lambda e, l=l: e.tensor_scalar_add(mod1p[:, l * 96:(l + 1) * 96], mod[:, l * 96:(l + 1) * 96], 1.0), reads=[b_mod], writes=[b_mod1p])

        for g in range(12):
            mod_chunk_load(0, g, aw[g % 2], b_aw[g % 2])
            mod_chunk_mm(0, g, aw[g % 2], b_aw[g % 2], p_mod, b_pmod)
        mod_finish(0, p_mod, b_pmod)
        S.barrier(); st0.close()
        if debug == "p1":
            S.dma("sp", outs["dbg_mod"], mod[:], reads=[b_mod])

        def modv(l, slot, kc, src, one_plus=False):
            t = mod1p if one_plus else mod
            col = l * 96 + (slot * 8 + kc) * 2 + src
            return t[:, col:col + 1]

        xmT = sb("xmT", [128, KC * N], BF16)
        b_xmT = [Buf("xmT%d" % i) for i in range(NT)]
        if debug in ("x1", "x2"):
            xt = [sb("xt%d" % i, [128, D], F32) for i in range(2)]; b_xt = [Buf("xt0"), Buf("xt1")]

        def mk_sets(alloc, n, tag, with_y=False, with_h=False, plan=None):
            sets = []
            for g in range(n):
                B_ = dict(xt=alloc("xt%s%d" % (tag, g), [128, D], F32), xn=alloc("xn%s%d" % (tag, g), [128, D], BF16), st=alloc("st%s%d" % (tag, g), [128, 16], F32),
                          b_xt=Buf("xt%d" % g), b_xn=Buf("xn%d" % g), b_st=Buf("st%d" % g),
                          tr=((1, 2) if g % 2 == 0 else (7, 0)), mm=((3, 4) if g % 2 == 0 else (5, 6)), tr1=None)
                if plan is not None:
                    p_ = plan[g % len(plan)]
                    if len(p_) == 4:
                        B_.update(mm=(p_[0], p_[1]), tr=(p_[2], p_[3]), tr1=None)
                    else:
                        B_.update(mm=(p_[0], p_[1]), tr1=p_[2])
                if with_y:
                    B_.update(xq=alloc("xq%s%d" % (tag, g), [128, D], F32), yt=alloc("yt%s%d" % (tag, g), [128, D], F32), b_xq=Buf("xq%d" % g), b_yt=Buf("yt%d" % g))
                if with_h:
                    B_.update(ht=alloc("ht%s%d" % (tag, g), [128, NFC * 128], BF16), b_ht=Buf("ht%d" % g))
                sets.append(B_)
            return sets

        def ln_stats_gen(src, b_src, B_):
            st_t, b_s = B_["st"], B_["b_st"]
            yield
            S.op("dve", lambda e: e.bn_stats(st_t[:, 0:6], src[:, 0:512]), reads=[b_src], writes=[b_s])
            yield
            S.op("dve", lambda e: e.bn_stats(st_t[:, 6:12], src[:, 512:1024]), reads=[b_src], writes=[b_s])
            yield
            S.op("dve", lambda e: e.bn_aggr(st_t[:, 12:14], st_t[:, 0:12]), reads=[b_s], writes=[b_s])
            yield
            S.op("act", lambda e: e.activation(st_t[:, 14:15], st_t[:, 13:14], AF.Sqrt, bias=EPS), reads=[b_s], writes=[b_s])
            yield
            S.op("dve", lambda e: e.reciprocal(st_t[:, 15:16], st_t[:, 14:15]), reads=[b_s], writes=[b_s])

        def ln_mod_T_gen(l, i, slot_shift, slot_scale, B_):
            srcm = 1 if i < 2 else 0
            xt_t, xn_t, st_t = B_["xt"], B_["xn"], B_["st"]
            yield from ln_stats_gen(xt_t, B_["b_xt"], B_)
            yield
            S.op("dve", lambda e: e.tensor_scalar(xn_t[:], xt_t[:], st_t[:, 12:13], st_t[:, 15:16], ALU.subtract, ALU.mult),
                 reads=[B_["b_xt"], B_["b_st"]], writes=[B_["b_xn"]])
            for kc in range(KC):
                if B_["tr1"] is not None:
                    pbk = B_["tr1"]; pv_ = pbank[pbk][:].bitcast(BF16)[:, kc * 128:(kc + 1) * 128]
                else:
                    pbk = B_["tr"][kc % 2]; pv_ = pbb(pbk, 128)
                yield
                S.op("pe", lambda e, kc=kc, pv_=pv_: e.transpose(pv_, xn_t[:, kc * 128:(kc + 1) * 128], id_bf[:]),
                     reads=[B_["b_xn"], b_id_bf], writes=[b_pb[pbk]])
                dst = xmT[:, kc * N + i * 128: kc * N + (i + 1) * 128]
                yield
                S.op("act", lambda e, dst=dst, pv_=pv_, kc=kc: e.activation(
                    dst, pv_, AF.Identity, bias=modv(l, slot_shift, kc, srcm), scale=modv(l, slot_scale, kc, srcm, True)),
                    reads=[b_pb[pbk], b_mod, b_mod1p], writes=[b_xmT[i]])

        GRP = 2

        def run_groups(tiles, sets, load, chain, GRP=GRP):
            tl_ = list(tiles)

            def stream(k):
                mine = tl_[k::GRP]
                for n_, i in enumerate(mine):
                    if n_ == 0:
                        load(i, sets[k])
                    if n_ + 1 < len(mine):
                        load(mine[n_ + 1], sets[k + GRP * ((n_ + 1) % 2)])
                    yield from chain(i, sets[k + GRP * (n_ % 2)])

            run_rr([stream(k) for k in range(GRP)])

        def phase_ln_mod_T(l, src_ap, tiles, slot_shift, slot_scale, src_bufs=None):
            p1 = ExitStack()
            G1 = 4
            sets = mk_sets(lambda n, s_, dt: p1.enter_context(nc.sbuf_tensor("%s_p1L%d" % (n, l), s_, dt)), 2 * G1, "a",
                           plan=[(None, None, 1, 2), (None, None, 3, 4), (None, None, 5, 6), (None, None, 7, 0)])
            load = lambda i, B_: S.dma("sp", B_["xt"][:], src_ap[i * 128:(i + 1) * 128, :], reads=([src_bufs[i]] if src_bufs else []), writes=[B_["b_xt"]])
            run_groups(tiles, sets, load, lambda i, B_: ln_mod_T_gen(l, i, slot_shift, slot_scale, B_), GRP=G1)
            S.barrier(); p1.close()

        phase_ln_mod_T(0, xin, range(NT), 0, 1)
        cut = debug[2] if (debug and debug.startswith("hg") and len(debug) == 3) else None
        dmp = sb("dmp", [128, 512], F32); b_dmp = Buf("dmp")

        def cut_dump(src_ap, src_bufs):
            S.barrier()
            S.op("act", lambda e: e.copy(dmp[:], src_ap), reads=src_bufs, writes=[b_dmp])
            S.dma("sp", outs["dbg_cut"][:, 0:512], dmp[:], reads=[b_dmp])
            S.drain_all("sp"); S.emit(); build.ninstr = S.ninstr

        if cut == "0":
            cut_dump(xmT[:, 0:512], b_xmT); return nc
        if debug == "p1":
            xm_f = sb("xm_f", [128, N], F32); b_xmf = Buf("xmf")
            for kc in range(KC):
                S.op("act", lambda e, kc=kc: e.copy(xm_f[:], xmT[:, kc * N:(kc + 1) * N]), reads=b_xmT, writes=[b_xmf])
                S.dma("sp", outs["dbg_xmT"][:, kc * N:(kc + 1) * N], xm_f[:], reads=[b_xmf])

        mixT = nc.dram_tensor("mixT", [D, N], BF16, kind="Internal").ap()
        b_mixT = [Buf("mixT%d" % i) for i in range(8)]

        def mixer(l, scan_tiles, out_tiles):
            ms = ExitStack()
            sb = lambda n, s_, dt: ms.enter_context(nc.sbuf_tensor("%s_L%d" % (n, l), s_, dt))
            DK = 128; QS = DK ** -0.5; NCH = N // 64
            BLKS = [(0, 512), (512, 512), (1024, 512), (1536, 512), (2048, 256)]
            lb_f = sb("lb_f", [128, 16], F32); lbv = sb("lbv", [128, 16], F32); oml = sb("oml", [128, 16], F32)
            b_lb = Buf("lb")
            cm = sb("cm", [128, 256], F32); rmk = sb("rmk", [128, 512], F32); ngb = sb("ngb", [128, DEPTH * 128], F32)
            b_cm, b_rmk, b_ngb = Buf("cm"), Buf("rmk"), Buf("ngb")
            cm_u = sb("cm_u", [128, 256], mybir.dt.uint32); b_cmu = Buf("cmu")
            S.dma("sp", lb_f[:], lbl, writes=[b_lb]); S.dma("sp", cm[:], cmask, writes=[b_cm]); S.dma("sp", rmk[:], rmask, writes=[b_rmk])
            for l2 in range(DEPTH):
                S.dma("sp", ngb[:, l2 * 128:(l2 + 1) * 128], hgng[l2:l2 + 1, :].partition_broadcast(128), writes=[b_ngb])
            S.op("dve", lambda e: e.tensor_copy(cm_u[:], cm[:]), reads=[b_cm], writes=[b_cmu])
            lt = sb("lt", [128, 32], F32)
            S.op("dve", lambda e: e.memset(lbv[:, 0:8], 0.0), writes=[b_lb], reads=[b_lb])
            S.op("dve", lambda e: e.tensor_max(lt[:, 0:8], lb_f[:, 0:8], lb_f[:, 8:16]), reads=[b_lb], writes=[b_lb])
            S.op("dve", lambda e: e.tensor_sub(lt[:, 8:16], lb_f[:, 0:8], lt[:, 0:8]), reads=[b_lb], writes=[b_lb])
            S.op("dve", lambda e: e.tensor_sub(lt[:, 16:24], lb_f[:, 8:16], lt[:, 0:8]), reads=[b_lb], writes=[b_lb])
            S.op("act", lambda e: e.activation(lt[:, 8:24], lt[:, 8:24], AF.Exp), reads=[b_lb], writes=[b_lb])
            S.op("dve", lambda e: e.tensor_add(lt[:, 24:32], lt[:, 8:16], lt[:, 16:24]), reads=[b_lb], writes=[b_lb])
            S.op("dve", lambda e: e.reciprocal(lt[:, 24:32], lt[:, 24:32]), reads=[b_lb], writes=[b_lb])
            S.op("dve", lambda e: e.tensor_mul(lbv[:, 8:16], lt[:, 16:24], lt[:, 24:32]), reads=[b_lb], writes=[b_lb])
            S.op("dve", lambda e: e.tensor_scalar(oml[:], lbv[:], -1.0, 1.0, ALU.mult, ALU.add), reads=[b_lb], writes=[b_lb])

            wvg = [sb("wvg%d" % i, [128, KC * 256], BF16) for i in range(2)]; b_wvg = [Buf("wvg0"), Buf("wvg1")]
            wzq = [sb("wzq%d" % i, [128, KC * 384], BF16) for i in range(2)]; b_wzq = [Buf("wzq0"), Buf("wzq1")]
            VG = [dict(V=sb("Vh%d" % p_, [128, N], BF16), G=sb("Gh%d" % p_, [128, N], BF16), bV=Buf("V%d" % p_), bG=Buf("G%d" % p_)) for p_ in range(2)]
            Vh, Gh, b_V, b_G = VG[0]["V"], VG[0]["G"], VG[0]["bV"], VG[0]["bG"]
            QT = [sb("QT%d" % d, [128, N], BF16) for d in range(2)]; KT = [sb("KT%d" % d, [128, N], BF16) for d in range(2)]
            QS_T = [sb("QST%d" % d, [128, N], BF16) for d in range(2)]; b_QST = [Buf("QST0"), Buf("QST1")]
            KH = [sb("KH%d" % d, [128, N], BF16) for d in range(2)]; KHt = [sb("KHt%d" % d, [128, N], BF16) for d in range(2)]
            EB = [sb("EB%d" % d, [128, NCH], F32) for d in range(2)]
            EM = [sb("EM%d" % d, [128, NCH], F32) for d in range(2)]; b_EM = [Buf("EM0"), Buf("EM1")]
            b_QT = [Buf("QT0"), Buf("QT1")]; b_KT = [Buf("KT0"), Buf("KT1")]; b_KH = [Buf("KH0"), Buf("KH1")]
            b_KHt = [Buf("KHt0"), Buf("KHt1")]; b_EB = [Buf("EB0"), Buf("EB1")]
            Oacc = [sb("Oacc%d" % d, [128, N], F32) for d in range(2)]; b_O = [Buf("O0"), Buf("O1")]
            HGT = sb("HGT", [128, N], BF16); b_HGT = Buf("HGT")
            NTMP = 9
            tmp = [[sb("tm%d_%d" % (a, i), [128, 512], F32) for i in range(NTMP)] for a in range(2)]
            b_tmp = [[Buf("tm%d_%d" % (a, i)) for i in range(NTMP)] for a in range(2)]
            qs_ = [sb("qs%d" % a, [128, 512], F32) for a in range(2)]; b_qs = [Buf("qs0"), Buf("qs1")]
            Sst = [[sb("S%d_%d" % (d, i), [128, 128], F32) for i in range(2)] for d in range(2)]
            b_S = [[Buf("S%d_%d" % (d, i)) for i in range(2)] for d in range(2)]
            Sbf_f32 = [sb("S3_%d" % d, [128, 128], F32) for d in range(2)]; b_Sx = [Buf("S3_0"), Buf("S3_1")]
            AT = [sb("AT%d" % d, [128, 128], BF16) for d in range(2)]; b_AT = [Buf("AT0"), Buf("AT1")]
            fin = [sb("fin%d" % i, [128, 128], F32) for i in range(2)]; fsq = sb("fsq", [128, 128], F32)
            b_fin = [Buf("fin0"), Buf("fin1")]
            finW = [dict(fsq=(fsq if p_ == 0 else sb("fsq1", [128, 128], F32)), fst=sb("fst%d" % p_, [128, 8], F32), ngg=sb("ngg%d" % p_, [128, 128], F32), hgt=sb("hgt%d" % p_, [128, 128], BF16),
                         b_fsq=Buf("fsq%d" % p_), b_fst=Buf("fst%d" % p_), b_ngg=Buf("ngg%d" % p_), b_hgt=Buf("hgt%d" % p_)) for p_ in range(2)]

            if cut == "L":
                cut_dump(oml[:, 0:16].to_broadcast([128, 16]) if False else xmT[:, 0:512], b_xmT + [b_lb, b_cm, b_rmk, b_ngb]); return "cut"

            e_, one_e, sig, kk, lf, bb, cc_, E1, dd = range(9)

            def hgrn_layer(l, scan_tiles, out_tiles):
                wi = w_in[l].rearrange("(k p) n -> p k n", p=128)
                def load_vg_w(h_):
                    a_ = h_ % 2
                    for ci, c0 in enumerate((h_ * 128, 2048 + h_ * 128)):
                        S.dma("pool", wvg[a_][:].rearrange("p (k n) -> p k n", k=KC)[:, :, ci * 128:(ci + 1) * 128], wi[:, :, c0:c0 + 128], writes=[b_wvg[a_]])

                def load_zq_w(h_):
                    a_ = h_ % 2
                    for ci, c0 in enumerate((512 + h_ * 128, 1024 + h_ * 128, 1536 + h_ * 128)):
                        S.dma("pool", wzq[a_][:].rearrange("p (k n) -> p k n", k=KC)[:, :, ci * 128:(ci + 1) * 128], wi[:, :, c0:c0 + 128], writes=[b_wzq[a_]])

                def A_gen(h_):
                    a_ = h_ % 2; W_ = VG[a_]
                    for i in scan_tiles:
                        pbi = 3 + (i % 2)
                        yield
                        for kc in range(KC):
                            S.op("pe", lambda e, i=i, kc=kc, pbi=pbi: e.matmul(pbf(pbi, 256), xmT[:, kc * N + i * 128:kc * N + (i + 1) * 128],
                                 wvg[a_][:, kc * 256:(kc + 1) * 256], start=(kc == 0), stop=(kc == KC - 1)),
                                 reads=[b_xmT[i], b_wvg[a_]], writes=[b_pb[pbi]], inc=(kc == KC - 1))
                        yield
                        S.op("dve", lambda e, i=i, pbi=pbi: e.tensor_copy(W_["V"][:, i * 128:(i + 1) * 128], pbank[pbi][:, 0:128]), reads=[b_pb[pbi]], writes=[W_["bV"]])
                        yield
                        S.op("act", lambda e, i=i, pbi=pbi: e.activation(W_["G"][:, i * 128:(i + 1) * 128], pbank[pbi][:, 128:256], AF.Silu), reads=[b_pb[pbi]], writes=[W_["bG"]])

                nheads = 1 if cut else 4
                load_vg_w(0)
                if cut == "W":
                    cut_dump(wvg[0][:, 0:512], [b_wvg[0]]); return "cut"
                run_rr([A_gen(0)])
                for h in range(nheads):
                    a = h % 2
                    Vh, Gh, b_V, b_G = VG[a]["V"], VG[a]["G"], VG[a]["bV"], VG[a]["bG"]
                    if h == 0:
                        load_zq_w(0)
                    if h + 1 < nheads:
                        load_zq_w(h + 1)
                        load_vg_w(h + 1)
                    if cut == "A":
                        cut_dump(Vh[:, 0:512], [b_V, b_G]); return "cut"
                    for bi, (t0, nb) in enumerate(BLKS):
                        if t0 // 128 not in scan_tiles:
                            continue
                        tiles_in = [i for i in range(t0 // 128, (t0 + nb) // 128)]
                        for ci in range(3):
                            for kc in range(KC):
                                S.op("pe", lambda e, ci=ci, kc=kc, t0=t0, nb=nb, a=a: e.matmul(pbf(5 + ci, nb), wzq[a][:, kc * 384 + ci * 128:kc * 384 + (ci + 1) * 128],
                                     xmT[:, kc * N + t0:kc * N + t0 + nb], start=(kc == 0), stop=(kc == KC - 1)),
                                     reads=[b_wzq[a]] + [b_xmT[i] for i in tiles_in], writes=[b_pb[5 + ci]], inc=(kc == KC - 1))
                        qa = bi % 2
                        S.op("act", lambda e, nb=nb, qa=qa: e.activation(qs_[qa][:, 0:nb], pbf(7, nb), AF.Silu), reads=[b_pb[7]], writes=[b_qs[qa]])
                        nck = nb // 64; c0 = t0 // 64
                        def gate_chain(d, bi=bi, t0=t0, nb=nb, qa=qa, nck=nck, c0=c0):
                            ta = (bi * 2 + d) % 2
                            T = [t[:, 0:nb] for t in tmp[ta]]; bT = b_tmp[ta]
                            col = l * 8 + d * 4 + h
                            lbc, omc = lbv[:, col:col + 1], oml[:, col:col + 1]
                            yield
                            S.op("act", lambda e, T=T, d=d, nb=nb: e.activation(T[e_], pbf(5 + d, nb), AF.Exp, scale=-1.0), reads=[b_pb[5 + d]], writes=[bT[e_]])
                            yield
                            S.op("act", lambda e, T=T: e.activation(T[one_e], T[e_], AF.Ln, bias=1.0), reads=[bT[e_]], writes=[bT[one_e]])
                            yield
                            S.op("act", lambda e, T=T: e.activation(T[sig], T[one_e], AF.Exp, scale=-1.0), reads=[bT[one_e]], writes=[bT[sig]])
                            yield
                            S.op("dve", lambda e, T=T, omc=omc: e.scalar_tensor_tensor(T[kk], T[e_], omc, T[sig], ALU.mult, ALU.mult), reads=[bT[e_], bT[sig], b_lb], writes=[bT[kk]])
                            yield
                            S.op("act", lambda e, T=T, omc=omc, lbc=lbc: e.activation(T[lf], T[sig], AF.Ln, bias=lbc, scale=omc), reads=[bT[sig], b_lb], writes=[bT[lf]])
                            yield
                            S.op("dve", lambda e, T=T, nb=nb: e.tensor_tensor_scan(T[bb], rmk[:, 0:nb], T[lf], 0.0, ALU.mult, ALU.add), reads=[b_rmk, bT[lf]], writes=[bT[bb]])
                            b3 = T[bb].rearrange("p (c t) -> p c t", t=64)
                            btot = b3[:, :, 63:64]
                            if d == 0:
                                cview, bc_ = T[bb], bT[bb]
                            else:
                                yield
                                S.op("dve", lambda e, T=T: e.tensor_sub(T[cc_], T[lf], T[bb]), reads=[bT[lf], bT[bb]], writes=[bT[cc_]])
                                c3 = T[cc_].rearrange("p (c t) -> p c t", t=64)
                                yield
                                S.op("dve", lambda e, c3=c3, btot=btot, nck=nck: e.tensor_add(c3, c3, btot.to_broadcast([128, nck, 64])), reads=[bT[cc_], bT[bb]], writes=[bT[cc_]])
                                cview, bc_ = T[cc_], bT[cc_]
                            sl = slice(t0, t0 + nb)
                            def tail_scores():
                                MID = 31 if d == 0 else 32
                                cm3 = T[one_e].rearrange("p (c t) -> p c t", t=64); cv3m = cview.rearrange("p (c t) -> p c t", t=64)
                                yield
                                S.op("dve", lambda e, cm3=cm3, cv3m=cv3m, nck=nck, MID=MID: e.tensor_sub(cm3, cv3m, cv3m[:, :, MID:MID + 1].to_broadcast([128, nck, 64])),
                                     reads=[bc_, bT[E1]], writes=[bT[one_e]])
                                yield
                                S.op("act", lambda e, T=T: e.activation(T[E1], T[one_e], AF.Exp), reads=[bT[one_e]], writes=[bT[E1]])
                                yield
                                S.op("dve", lambda e, T=T, qa=qa, d=d, sl=sl, nb=nb: e.scalar_tensor_tensor(QS_T[d][:, sl], qs_[qa][:, 0:nb], QS, T[E1], ALU.mult, ALU.mult),
                                     reads=[b_qs[qa], bT[E1]], writes=[b_QST[d]])
                                yield
                                S.op("act", lambda e, T=T: e.activation(T[E1], T[one_e], AF.Exp, scale=-1.0), reads=[bT[one_e], b_QST[d]], writes=[bT[E1]])
                                yield
                                S.op("dve", lambda e, T=T, d=d, sl=sl: e.tensor_tensor(KT[d][:, sl], T[kk], T[E1], ALU.mult), reads=[bT[kk], bT[E1]], writes=[b_KT[d]])
                                yield
                                S.op("act", lambda e, d=d, c0=c0, nck=nck, cv3m=cv3m, MID=MID: e.activation(EM[d][:, c0:c0 + nck], cv3m[:, :, MID:MID + 1].rearrange("p c o -> p (c o)"), AF.Exp), reads=[bc_], writes=[b_EM[d]])

                            def tail_state():
                                d3 = T[dd].rearrange("p (c t) -> p c t", t=64); cv3 = cview.rearrange("p (c t) -> p c t", t=64)
                                yield
                                S.op("dve", lambda e, d3=d3, cv3=cv3, btot=btot, nck=nck: e.tensor_sub(d3, btot.to_broadcast([128, nck, 64]), cv3), reads=[bc_, bT[bb]], writes=[bT[dd]])
                                yield
                                S.op("act", lambda e, T=T: e.activation(T[dd], T[dd], AF.Exp), reads=[bT[dd]], writes=[bT[dd]])
                                yield
                                S.op("dve", lambda e, T=T, d=d, sl=sl: e.tensor_tensor(KH[d][:, sl], T[kk], T[dd], ALU.mult), reads=[bT[kk], bT[dd]], writes=[b_KH[d]])
                                yield
                                S.op("act", lambda e, d=d, c0=c0, nck=nck, btot=btot: e.activation(EB[d][:, c0:c0 + nck], btot.rearrange("p c o -> p (c o)"), AF.Exp), reads=[bT[bb]], writes=[b_EB[d]])

                            ga_, gb_ = tail_scores(), tail_state()
                            live_ = [ga_, gb_]
                            while live_:
                                for g_ in list(live_):
                                    try:
                                        next(g_)
                                    except StopIteration:
                                        live_.remove(g_)
                                yield
                        run_rr([gate_chain(0), gate_chain(1)])
                    if cut == "B":
                        return
                    for d in range(2):
                        for i in scan_tiles:
                            pbi = 1 + (i % 2)
                            S.op("pe", lambda e, d=d, i=i, pbi=pbi: e.transpose(pbb(pbi, 128), KH[d][:, i * 128:(i + 1) * 128], id_bf[:]), reads=[b_KH[d], b_id_bf], writes=[b_pb[pbi]])
                            S.op("act", lambda e, d=d, i=i, pbi=pbi: e.copy(KHt[d][:, i * 128:(i + 1) * 128], pbb(pbi, 128)), reads=[b_pb[pbi]], writes=[b_KHt[d]])
                    if cut == "C":
                        return
                    order = [list(scan_tiles), [t for t in (1, 0) if t in scan_tiles] + [t for t in range(NT - 1, 1, -1) if t in scan_tiles]]
                    TB = [tmp[a_][j_] for a_ in range(2) for j_ in range(NTMP)]; bTB = [b_tmp[a_][j_] for a_ in range(2) for j_ in range(NTMP)]
                    import os as _os2
                    d_stage = int(_os2.environ.get("HG_D_STAGE", "0")) if cut else 0
                    seqs = []
                    for d in range(1 if d_stage else 2):
                        seq = [(i, cpos) for i in order[d] for cpos in ((0, 1) if d == 0 else (1, 0))]
                        seqs.append(seq)
                        nstep = len(order[d])
                        for cpos in (0, 1):
                            base = cpos * nstep
                            s_ = base
                            while s_ < base + nstep:
                                g_end = min(base + nstep, (s_ // 4 + 1) * 4)
                                pbk = 3 + 2 * cpos + ((s_ // 4) % 2)
                                for sl_ in range(s_, g_end):
                                    i = order[d][sl_ - base]; q = sl_ % 4
                                    ps_ = slice(cpos * 64, cpos * 64 + 64); ts_ = slice(i * 128, (i + 1) * 128)
                                    S.op("pe", lambda e, d=d, ts_=ts_, ps_=ps_, pbk=pbk, q=q, Vh=Vh: e.matmul(pbank[pbk][:, q * 128:(q + 1) * 128], KHt[d][ps_, ts_], Vh[ps_, ts_], start=True, stop=True),
                                         reads=[b_KHt[d], b_V], writes=[b_pb[pbk]], inc=(sl_ == g_end - 1))
                                tb = 9 * d + s_ // 4; c_lo, c_hi = (s_ % 4) * 128, ((g_end - 1) % 4 + 1) * 128
                                S.op("act", lambda e, tb=tb, pbk=pbk, c_lo=c_lo, c_hi=c_hi, TB=TB: e.copy(TB[tb][:, c_lo:c_hi], pbank[pbk][:, c_lo:c_hi]), reads=[b_pb[pbk]], writes=[bTB[tb]])
                                s_ = g_end
                    if d_stage == 1:
                        cut_dump(TB[0][:, 0:512], bTB[0:9]); return "cut"
                    RING = 18
                    b_slot = [[Buf("st%d_%d" % (d_, r_)) for r_ in range(RING)] for d_ in range(2)]
                    sslot = lambda d_, k: (KH[d_][:, (k % RING) * 128:(k % RING + 1) * 128], b_slot[d_][k % RING])
                    S3 = [Sst[d_] + [Sbf_f32[d_]] for d_ in range(2)]; b_S3 = [b_S[d_] + [b_Sx[d_]] for d_ in range(2)]

                    produced = [0, 0]
                    consumed = [0, 0]

                    def chain_gen(d):
                        seq = seqs[d]; nstep = len(order[d])
                        yield
                        S.op("dve", lambda e, d=d, S3=S3: e.memset(S3[d][0][:], 0.0), writes=[b_S3[d][0]])
                        for k, (i, cpos) in enumerate(seq[:-1]):
                            sl_ = cpos * nstep + k // 2
                            ch = i * 2 + cpos; tb = 9 * d + sl_ // 4; q = sl_ % 4
                            si, so = k % 3, (k + 1) % 3
                            yield
                            S.op("dve", lambda e, d=d, si=si, so=so, ch=ch, tb=tb, q=q, TB=TB, S3=S3: e.scalar_tensor_tensor(S3[d][so][:], S3[d][si][:], EB[d][:, ch:ch + 1], TB[tb][:, q * 128:(q + 1) * 128], ALU.mult, ALU.add),
                                 reads=[b_S3[d][si], b_EB[d], bTB[tb]], writes=[b_S3[d][so]])
                            dst, bdst = sslot(d, k + 1)
                            yield
                            while (k + 1) - consumed[d] >= RING:
                                yield
                            i2, cp2 = seq[k + 1]; ch2 = i2 * 2 + cp2
                            S.op("act", lambda e, d=d, so=so, dst=dst, S3=S3, ch2=ch2: e.activation(dst, S3[d][so][:], AF.Identity, scale=EM[d][:, ch2:ch2 + 1]), reads=[b_S3[d][so], b_EM[d]], writes=[bdst])
                            produced[d] = k + 1

                    def out_gen(d):
                        yield
                        S.op("dve", lambda e, d=d: e.memset(AT[d][:], 0.0), writes=[b_AT[d]])
                        for step, i in enumerate(order[d]):
                            if i not in out_tiles:
                                consumed[d] = 2 * (step + 1)
                                continue
                            ts_ = slice(i * 128, (i + 1) * 128); par = step % 2
                            p_sc, p_o = (5, 6)[d], ((7, 0)[d])
                            yield
                            S.op("pe", lambda e, d=d, ts_=ts_, p_sc=p_sc: e.matmul(pbf(p_sc, 128), KT[d][:, ts_], QS_T[d][:, ts_], start=True, stop=True),
                                 reads=[b_KT[d], b_QST[d]], writes=[b_pb[p_sc]])
                            yield
                            S.op("dve", lambda e, d=d, p_sc=p_sc: e.copy_predicated(AT[d][:], cm_u[:, d * 128:(d + 1) * 128], pbf(p_sc, 128)),
                                 reads=[b_pb[p_sc], b_cmu, b_AT[d]], writes=[b_AT[d]])
                            cps = (0, 1) if d == 0 else (1, 0)
                            need = [(cp, k) for cp, k in zip(cps, (2 * step, 2 * step + 1)) if k > 0]
                            yield
                            while need and produced[d] < max(k_ for _, k_ in need):
                                yield
                            S.op("pe", lambda e, d=d, ts_=ts_, p_o=p_o, nn=len(need), Vh=Vh: e.matmul(pbf(p_o, 128), AT[d][:], Vh[:, ts_], start=True, stop=(nn == 0)),
                                 reads=[b_AT[d], b_V], writes=[b_pb[p_o]], inc=(len(need) == 0))
                            for j, (cp, k) in enumerate(need):
                                ps_ = slice(cp * 64, cp * 64 + 64); tsc = slice(i * 128 + cp * 64, i * 128 + cp * 64 + 64)
                                src, bsrc = sslot(d, k)
                                lastj = j == len(need) - 1
                                if not lastj:
                                    pass
                                S.op("pe", lambda e, d=d, tsc=tsc, ps_=ps_, p_o=p_o, src=src, lastj=lastj: e.matmul(pbank[p_o][ps_, 0:128], QS_T[d][:, tsc], src, start=False, stop=lastj),
                                     reads=[b_QST[d], bsrc], writes=[b_pb[p_o]], inc=lastj)
                            consumed[d] = 2 * (step + 1)
                            yield
                            S.op("act", lambda e, d=d, ts_=ts_, p_o=p_o: e.copy(Oacc[d][:, ts_], pbf(p_o, 128)), reads=[b_pb[p_o]], writes=[b_O[d]])

                    if h == nheads - 1 and not cut and debug != "hg":
                        sc_load(l, 0); sc_load(l, 1); sg_load(l)
                    if d_stage:
                        run_rr([chain_gen(0), out_gen(0)])
                    else:
                        run_rr([chain_gen(0), chain_gen(1), out_gen(0), out_gen(1)] + ([A_gen(h + 1)] if h + 1 < nheads else []))
                    if d_stage == 3:
                        cut_dump(Oacc[0][:, 0:512], [b_O[0]]); return "cut"
                    if cut == "D":
                        return
                    def fin_chain(i, fa):
                        ts_ = slice(i * 128, (i + 1) * 128); pbi = 1 + fa
                        W = finW[fa]
                        yield
                        S.op("dve", lambda e: e.tensor_add(fin[fa][:], Oacc[0][:, ts_], Oacc[1][:, ts_]), reads=[b_O[0], b_O[1]], writes=[b_fin[fa]])
                        yield
                        S.op("act", lambda e: e.activation(W["fsq"][:], fin[fa][:], AF.Square, accum_out=W["fst"][:, 0:1]), reads=[b_fin[fa]], writes=[W["b_fsq"], W["b_fst"]])
                        yield
                        S.op("act", lambda e: e.activation(W["fst"][:, 1:2], W["fst"][:, 0:1], AF.Sqrt, bias=EPS, scale=1.0 / 128), reads=[W["b_fst"]], writes=[W["b_fst"]])
                        yield
                        S.op("dve", lambda e: e.reciprocal(W["fst"][:, 2:3], W["fst"][:, 1:2]), reads=[W["b_fst"]], writes=[W["b_fst"]])
                        yield
                        S.op("dve", lambda e, Gcur=Gcur: e.tensor_tensor(W["ngg"][:], ngb[:, l * 128:(l + 1) * 128], Gcur[:, ts_], ALU.mult), reads=[b_ngb, b_Gcur], writes=[W["b_ngg"]])
                        yield
                        S.op("dve", lambda e: e.scalar_tensor_tensor(W["hgt"][:], fin[fa][:], W["fst"][:, 2:3], W["ngg"][:], ALU.mult, ALU.mult), reads=[b_fin[fa], W["b_fst"], W["b_ngg"]], writes=[W["b_hgt"]])
                        yield
                        S.op("pe", lambda e: e.transpose(pbb(pbi, 128), W["hgt"][:], id_bf[:]), reads=[W["b_hgt"], b_id_bf], writes=[b_pb[pbi]])
                        yield
                        S.op("act", lambda e: e.copy(HGT[:, ts_], pbb(pbi, 128)), reads=[b_pb[pbi]], writes=[b_HGT])

                    Gcur, b_Gcur = Gh, b_G
                    ot_ = list(out_tiles)

                    def fin_stream(k, ot_=ot_):
                        for i in ot_[k::2]:
                            yield from fin_chain(i, k)

                    run_rr([fin_stream(0), fin_stream(1)])
                    S.dma("sp", mixT[h * 128:(h + 1) * 128, :], HGT[:], reads=[b_HGT], writes=[b_mixT[h]])

            if cut:
                S.barrier()
                S.op("act", lambda e, Vh=Vh: e.copy(Oacc[1][:], Vh[:]), reads=[b_V], writes=[b_O[1]])
                S.dma("sp", outs["dbg_cut"][:, 0:N], Oacc[1][:], reads=[b_O[1]])
                if cut in "DE":
                    S.dma("sp", outs["dbg_cut"][:, N:2 * N], Oacc[0][:], reads=[b_O[0]])

            scw_s = sb("scw_s", [128, DEPTH * 6], F32); b_scw = Buf("scw")
            S.dma("sp", scw_s[:], scw, writes=[b_scw])
            lngb = sb("lngb", [128, 512], F32); b_lngb = Buf("lngb")
            WsT = sb("WsT", [128, 4 * 128], BF16); b_WsT = Buf("WsT")
            wsn = sb("wsn", [128, 128], BF16); b_wsn = Buf("wsn")
            BS = sb("BS", [128, 2 * 128], F32); b_BS = Buf("BS")
            SEQS = [(0, 256), (256, N)]
            SEGS = [(0, 256), (256, 512), (512, 1024), (1024, 1536), (1536, 2048), (2048, N)]

            pre_done = {}

            def sc_load(l, cc):
                wi_ = w_in[l].rearrange("(k p) n -> p k n", p=128); a_ = cc % 2
                for ci, c0 in enumerate((2560 + cc * 128, 2816 + cc * 128, 3072 + cc * 128)):
                    S.dma("pool", wzq[a_][:].rearrange("p (k n) -> p k n", k=KC)[:, :, ci * 128:(ci + 1) * 128], wi_[:, :, c0:c0 + 128], writes=[b_wzq[a_]])
                pre_done[("sc", l, cc)] = True

            def sg_load(l):
                wi_ = w_in[l].rearrange("(k p) n -> p k n", p=128)
                S.dma("pool", wvg[0][:].rearrange("p (k n) -> p k n", k=KC), wi_[:, :, 3328:3584], writes=[b_wvg[0]])
                S.dma("pool", wvg[1][:].rearrange("p (k n) -> p k n", k=KC), wi_[:, :, 3584:3840], writes=[b_wvg[1]])
                S.dma("sp", lngb[:, 0:256], sgln[2 * l:2 * l + 1, :].partition_broadcast(128), writes=[b_lngb])
                S.dma("sp", lngb[:, 256:512], sgln[2 * l + 1:2 * l + 2, :].partition_broadcast(128), writes=[b_lngb])
                for cc in range(2):
                    S.dma("sp", BS[:, cc * 128:(cc + 1) * 128], sgb[2 * l + cc], writes=[b_BS])
                pre_done[("sg", l)] = True

            def sc_layer(l, tiles):
                wi = w_in[l].rearrange("(k p) n -> p k n", p=128)
                tmax = (max(tiles) + 1) * 128; tmin = min(tiles) * 128
                Pf, GBf, b_P, b_GB = Oacc[0], Oacc[1], b_O[0], b_O[1]
                for cc in range(2):
                    a = cc % 2
                    if not pre_done.get(("sc", l, cc)):
                        sc_load(l, cc)
                    sc_blks = [(t0, min(512, tmax - t0)) for t0 in range(tmin, tmax, 512)]
                    for (t0, nb) in sc_blks:
                        tiles_in = list(range(t0 // 128, (t0 + nb) // 128))
                        for ci in range(3):
                            for kc in range(KC):
                                S.op("pe", lambda e, ci=ci, kc=kc, t0=t0, nb=nb, a=a: e.matmul(pbf(5 + ci, nb), wzq[a][:, kc * 384 + ci * 128:kc * 384 + (ci + 1) * 128],
                                     xmT[:, kc * N + t0:kc * N + t0 + nb], start=(kc == 0), stop=(kc == KC - 1)),
                                     reads=[b_wzq[a]] + [b_xmT[i] for i in tiles_in], writes=[b_pb[5 + ci]], inc=(kc == KC - 1))
                        T0 = tmp[0][0][:, 0:nb]
                        S.op("act", lambda e, t0=t0, nb=nb: e.copy(GBf[:, t0:t0 + nb], pbf(5, nb)), reads=[b_pb[5]], writes=[b_GB])
                        S.op("act", lambda e, T0=T0, nb=nb: e.copy(T0, pbf(6, nb)), reads=[b_pb[6]], writes=[b_tmp[0][0]])
                        S.op("dve", lambda e, T0=T0, t0=t0, nb=nb: e.tensor_tensor(Pf[:, t0:t0 + nb], T0, pbf(7, nb), ALU.mult), reads=[b_tmp[0][0], b_pb[7]], writes=[b_P])
                    wb = l * 6 + cc * 3
                    w0_, w1_, w2_ = scw_s[:, wb:wb + 1], scw_s[:, wb + 1:wb + 2], scw_s[:, wb + 2:wb + 3]
                    for (a0, a1) in SEGS:
                        if a0 < tmin or a0 >= tmax:
                            continue
                        s0, s1 = [sq for sq in SEQS if sq[0] <= a0 < sq[1]][0]
                        Y = tmp[1][0]; bY = b_tmp[1][0]; n_ = a1 - a0
                        S.op("dve", lambda e, Y=Y, a0=a0, a1=a1, n_=n_, w1_=w1_: e.tensor_scalar(Y[:, 0:n_], Pf[:, a0:a1], w1_, None, ALU.mult), reads=[b_P, b_scw], writes=[bY])
                        lo = max(a0, s0 + 1)
                        S.op("dve", lambda e, Y=Y, a0=a0, a1=a1, lo=lo, w0_=w0_: e.scalar_tensor_tensor(Y[:, lo - a0:a1 - a0], Pf[:, lo - 1:a1 - 1], w0_, Y[:, lo - a0:a1 - a0], ALU.mult, ALU.add),
                             reads=[b_P, b_scw, bY], writes=[bY])
                        hi = min(a1, s1 - 1)
                        S.op("dve", lambda e, Y=Y, a0=a0, hi=hi, w2_=w2_: e.scalar_tensor_tensor(Y[:, 0:hi - a0], Pf[:, a0 + 1:hi + 1], w2_, Y[:, 0:hi - a0], ALU.mult, ALU.add),
                             reads=[b_P, b_scw, bY], writes=[bY])
                        S.op("dve", lambda e, Y=Y, a0=a0, a1=a1, n_=n_: e.tensor_tensor(HGT[:, a0:a1], GBf[:, a0:a1], Y[:, 0:n_], ALU.mult), reads=[b_GB, bY], writes=[b_HGT])
                    S.dma("sp", mixT[(4 + cc) * 128:(5 + cc) * 128, tmin:tmax], HGT[:, tmin:tmax], reads=[b_HGT], writes=[b_mixT[4 + cc]])

            def sg_layer(l, tiles):
                wi = w_in[l].rearrange("(k p) n -> p k n", p=128)
                tmax = (max(tiles) + 1) * 128; tmin = min(tiles) * 128
                if not pre_done.get(("sg", l)):
                    sg_load(l)
                for g in range(4):
                    S.dma("pool", wsn[:], sgw[4 * l + g], writes=[b_wsn])
                    S.op("pe", lambda e: e.transpose(pbb(1, 128), wsn[:], id_bf[:]), reads=[b_wsn, b_id_bf], writes=[b_pb[1]])
                    S.op("act", lambda e, g=g: e.copy(WsT[:, g * 128:(g + 1) * 128], pbb(1, 128)), reads=[b_pb[1]], writes=[b_WsT])
                SGT, b_SGT = KH, b_KH
                sgW = [dict(vn=sb("sgvn%d" % p_, [128, 256], F32), vhb=sb("sgvh%d" % p_, [128, 256], BF16), st=sb("sgst%d" % p_, [128, 40], F32),
                            b_vn=Buf("sgvn%d" % p_), b_vhb=Buf("sgvh%d" % p_), b_st=Buf("sgst%d" % p_), banks=((3, 4, 5), (6, 7, 0))[p_], par=p_) for p_ in range(2)]

                def sg_chain(i, W):
                    ts_ = slice(i * 128, (i + 1) * 128); pv, pu, pm = W["banks"]; par = W["par"]
                    vn_t, vhb_t, st_t = W["vn"], W["vhb"], W["st"]
                    yield
                    for kc in range(KC):
                        S.op("pe", lambda e, kc=kc: e.matmul(pbf(pv, 256), xmT[:, kc * N + i * 128:kc * N + (i + 1) * 128], wvg[1][:, kc * 256:(kc + 1) * 256],
                             start=(kc == 0), stop=(kc == KC - 1)), reads=[b_xmT[i], b_wvg[1]], writes=[b_pb[pv]], inc=(kc == KC - 1))
                    for g in range(4):
                        yield
                        S.op("dve", lambda e, g=g: e.bn_stats(st_t[:, g * 6:(g + 1) * 6], pbank[pv][:, g * 64:(g + 1) * 64]), reads=[b_pb[pv]], writes=[W["b_st"]])
                    for g in range(4):
                        yield
                        S.op("dve", lambda e, g=g: e.bn_aggr(st_t[:, 24 + 2 * g:26 + 2 * g], st_t[:, g * 6:(g + 1) * 6]), reads=[W["b_st"]], writes=[W["b_st"]])
                    mv = st_t[:, 24:32].rearrange("p (g two) -> p g two", two=2)
                    yield
                    S.op("act", lambda e: e.activation(st_t[:, 32:36], mv[:, :, 1], AF.Sqrt, bias=EPS), reads=[W["b_st"]], writes=[W["b_st"]])
                    yield
                    S.op("dve", lambda e: e.reciprocal(st_t[:, 36:40], st_t[:, 32:36]), reads=[W["b_st"]], writes=[W["b_st"]])
                    for g in range(4):
                        yield
                        S.op("dve", lambda e, g=g: e.tensor_scalar(vn_t[:, g * 64:(g + 1) * 64], pbank[pv][:, g * 64:(g + 1) * 64], st_t[:, 24 + 2 * g:25 + 2 * g], st_t[:, 36 + g:37 + g],
                             ALU.subtract, ALU.mult), reads=[b_pb[pv], W["b_st"]], writes=[W["b_vn"]])
                    yield
                    S.op("dve", lambda e: e.tensor_mul(vn_t[:], vn_t[:], lngb[:, 0:256]), reads=[W["b_vn"], b_lngb], writes=[W["b_vn"]])
                    yield
                    S.op("dve", lambda e: e.tensor_add(vhb_t[:], vn_t[:], lngb[:, 256:512]), reads=[W["b_vn"], b_lngb], writes=[W["b_vhb"]])
                    yield
                    for cc in range(2):
                        for kc in range(KC):
                            S.op("pe", lambda e, cc=cc, kc=kc: e.matmul(pbank[pu][:, cc * 128:(cc + 1) * 128], wvg[0][:, kc * 256 + cc * 128:kc * 256 + (cc + 1) * 128], xmT[:, kc * N + i * 128:kc * N + (i + 1) * 128],
                                 start=(kc == 0), stop=(kc == KC - 1)), reads=[b_wvg[0], b_xmT[i]], writes=[b_pb[pu]], inc=(kc == KC - 1 and cc == 1))
                    yield
                    for cc in range(2):
                        for gg in range(2):
                            g = 2 * cc + gg
                            S.op("pe", lambda e, g=g, gg=gg, cc=cc: e.matmul(pbank[pm][gg * 64:(gg + 1) * 64, cc * 128:(cc + 1) * 128], vhb_t[:, g * 64:(g + 1) * 64], WsT[:, g * 128:(g + 1) * 128], start=True, stop=True),
                                 reads=[W["b_vhb"], b_WsT], writes=[b_pb[pm]], inc=(gg == 1 and cc == 1))
                    for cc in range(2):
                        T1, T2 = tmp[cc][1 + 2 * par][:, 0:128], tmp[cc][2 + 2 * par][:, 0:128]
                        bT1, bT2 = b_tmp[cc][1 + 2 * par], b_tmp[cc][2 + 2 * par]
                        yield
                        S.op("dve", lambda e, T1=T1, cc=cc: e.tensor_tensor(T1, pbank[pm][:, cc * 128:(cc + 1) * 128], BS[:, cc * 128:(cc + 1) * 128], ALU.add), reads=[b_pb[pm], b_BS], writes=[bT1])
                        yield
                        S.op("act", lambda e, T2=T2, cc=cc: e.copy(T2, pbank[pu][:, cc * 128:(cc + 1) * 128]), reads=[b_pb[pu]], writes=[bT2])
                        yield
                        S.op("dve", lambda e, T1=T1, T2=T2, cc=cc: e.tensor_tensor(SGT[cc][:, ts_], T1, T2, ALU.mult), reads=[bT1, bT2], writes=[b_SGT[cc]])

                tl_ = list(tiles)

                def sg_stream(k):
                    for i in tl_[k::2]:
                        yield from sg_chain(i, sgW[k])

                run_rr([sg_stream(0), sg_stream(1)])
                for cc in range(2):
                    S.dma("sp", mixT[(6 + cc) * 128:(7 + cc) * 128, tmin:tmax], SGT[cc][:, tmin:tmax], reads=[b_SGT[cc]], writes=[b_mixT[6 + cc]])

            r = hgrn_layer(l, scan_tiles, out_tiles)
            if r == "cut":
                st.enter_context(ms)
                return "cut"
            if debug != "hg":
                sc_layer(l, out_tiles)
                sg_layer(l, out_tiles)
            if debug in ("hg", "mix"):
                mixer.dbg = (Oacc[0], b_O[0], HGT, b_HGT)
                st.enter_context(ms)
                return None
            S.barrier(); ms.close()
            return None

        if mixer(0, list(range(NT)), list(range(NT))) == "cut":
            return nc

        ALPHA = (2 * DEPTH) ** 0.25
        x1d = nc.dram_tensor("x1d", [N, D], F32, kind="Internal").ap()
        b_x1d = [Buf("x1d%d" % i) for i in range(NT)]

        def gate_bcast(dst, b_dst, l, slot, srcm, scr, b_scr):
            for kc in range(KC):
                g = modv(l, slot, kc, srcm)
                S.op("dve", lambda e, g=g: e.tensor_scalar(scr[:], id_f[:], 0.0, g, ALU.mult, ALU.add), reads=[b_id_f, b_mod], writes=[b_scr])
                S.op("pe", lambda e: e.matmul(pbf(0, 128), scr[:], id_f[:], start=True, stop=True), reads=[b_scr, b_id_f], writes=[b_pb[0]])
                S.op("act", lambda e, kc=kc: e.copy(dst[:, kc * 128:(kc + 1) * 128], pbf(0, 128)), reads=[b_pb[0]], writes=[b_dst])

        def phase_wout_ln1(l, src_ap, tiles, src_bufs=None):
            ps4 = ExitStack()
            sb4 = lambda n, s_, dt: ps4.enter_context(nc.sbuf_tensor("%s_p4L%d" % (n, l), s_, dt))
            mixS = sb4("mixS", [128, KC * N], BF16); b_mixS = [Buf("mixS%d" % k) for k in range(KC)]
            wo = sb4("wo", [128, KC * D], BF16); b_wo = Buf("wo")
            gbc = [sb4("gbc%d" % i, [128, D], F32) for i in range(2)]; b_gbc = [Buf("gbc0"), Buf("gbc1")]
            lg = sb4("lg", [128, D], F32); lb_ = sb4("lb_", [128, D], F32); b_lg, b_lbb = Buf("lg"), Buf("lbb")
            scr = sb4("scr", [128, 128], F32); b_scr = Buf("scr")
            for k in range(KC):
                S.dma("sp", mixS[:, k * N:(k + 1) * N], mixT[k * 128:(k + 1) * 128, :], reads=[b_mixT[k]], writes=[b_mixS[k]])
            S.dma("pool", wo[:].rearrange("p (k n) -> p k n", k=KC), w_out[l].rearrange("(k p) n -> p k n", p=128), writes=[b_wo])
            S.dma("sp", lg[:], lnp[4 * l:4 * l + 1, :].partition_broadcast(128), writes=[b_lg])
            S.dma("sp", lb_[:], lnp[4 * l + 1:4 * l + 2, :].partition_broadcast(128), writes=[b_lbb])
            gate_bcast(gbc[0], b_gbc[0], l, 2, 0, scr, b_scr)
            if any(i < 2 for i in tiles):
                gate_bcast(gbc[1], b_gbc[1], l, 2, 1, scr, b_scr)
            G4 = 3
            sets = mk_sets(sb4, 2 * G4, "b", with_y=True, plan=[(3, 3, 1), (4, 4, 2), (5, 5, 6)])
            load = lambda i, B_: S.dma("sp", B_["xq"][:], src_ap[i * 128:(i + 1) * 128, :], reads=([src_bufs[i]] if src_bufs else []), writes=[B_["b_xq"]])

            def chain4(i, B_):
                srcm = 1 if i < 2 else 0
                xq_t, yt_t, xt_t, st_t = B_["xq"], B_["yt"], B_["xt"], B_["st"]
                for hf in range(2):
                    pbi = B_["mm"][hf]; hs = slice(hf * 512, (hf + 1) * 512)
                    yield
                    for kc in range(KC):
                        S.op("pe", lambda e, kc=kc, hf=hf, pbi=pbi: e.matmul(pbf(pbi, 512), mixS[:, kc * N + i * 128:kc * N + (i + 1) * 128],
                             wo[:, kc * D + hf * 512:kc * D + (hf + 1) * 512], start=(kc == 0), stop=(kc == KC - 1)),
                             reads=[b_mixS[kc], b_wo], writes=[b_pb[pbi]], inc=(kc == KC - 1))
                    yield
                    S.op("dve", lambda e, hs=hs, pbi=pbi: e.tensor_tensor(yt_t[:, hs], pbf(pbi, 512), gbc[srcm][:, hs], ALU.mult),
                         reads=[b_pb[pbi], b_gbc[srcm]], writes=[B_["b_yt"]])
                    yield
                    S.op("dve", lambda e, hs=hs: e.scalar_tensor_tensor(yt_t[:, hs], xq_t[:, hs], ALPHA, yt_t[:, hs], ALU.mult, ALU.add),
                         reads=[B_["b_xq"], B_["b_yt"]], writes=[B_["b_yt"]])
                yield from ln_stats_gen(yt_t, B_["b_yt"], B_)
                yield
                S.op("dve", lambda e: e.scalar_tensor_tensor(yt_t[:], yt_t[:], st_t[:, 12:13], lg[:], ALU.subtract, ALU.mult),
                     reads=[B_["b_yt"], B_["b_st"], b_lg], writes=[B_["b_yt"]])
                yield
                S.op("dve", lambda e: e.scalar_tensor_tensor(xt_t[:], yt_t[:], st_t[:, 15:16], lb_[:], ALU.mult, ALU.add),
                     reads=[B_["b_yt"], B_["b_st"], b_lbb, B_["b_xt"]], writes=[B_["b_xt"]])
                yield
                S.dma("sp", x1d[i * 128:(i + 1) * 128, :], xt_t[:], reads=[B_["b_xt"]], writes=[b_x1d[i]])
                yield from ln_mod_T_gen(l, i, 3, 4, B_)

            run_groups(tiles, sets, load, chain4, GRP=G4)
            S.barrier(); ps4.close()

        if debug in ("x1", "h", "x2", None):
            phase_wout_ln1(0, xin, list(range(NT)))
        if debug == "x1":
            for i in range(NT):
                a = i % 2
                S.dma("sp", xt[a][:], x1d[i * 128:(i + 1) * 128, :], reads=[b_x1d[i]], writes=[b_xt[a]])
                S.dma("sp", outs["dbg_x1"][i * 128:(i + 1) * 128, :], xt[a][:], reads=[b_xt[a]])
            xm_f = sb("xm2_f", [128, N], F32); b_xmf = Buf("xm2f")
            for kc in range(KC):
                S.op("act", lambda e, kc=kc: e.copy(xm_f[:], xmT[:, kc * N:(kc + 1) * N]), reads=b_xmT, writes=[b_xmf])
                S.dma("sp", outs["dbg_xm2T"][:, kc * N:(kc + 1) * N], xm_f[:], reads=[b_xmf])

        hTd = nc.dram_tensor("hTd", [FF, N], BF16, kind="Internal").ap()
        b_hTd = [Buf("hTd%d" % i) for i in range(NFC)]
        x2d = [nc.dram_tensor("x2d%d" % i, [N, D], F32, kind="Internal").ap() for i in range(DEPTH - 1)]
        b_x2d = [Buf("x2d%d" % i) for i in range(NT)]
        GW = 64

        def phase_ffn_up(l, with_ctx):
            p5 = ExitStack()
            sb5 = lambda n, s_, dt: p5.enter_context(nc.sbuf_tensor("%s_p5L%d" % (n, l), s_, dt))
            fcw_s = sb5("fcw_s", [128, NFC * 9], F32); fcb_s = sb5("fcb_s", [128, NFC], F32); b_fcw, b_fcb = Buf("fcw"), Buf("fcb")
            S.dma("sp", fcw_s[:], fcw[:, l * NFC * 9:(l + 1) * NFC * 9], writes=[b_fcw])
            S.dma("sp", fcb_s[:], fcb[:, l * NFC:(l + 1) * NFC], writes=[b_fcb])
            wag = [sb5("wag%d" % i, [128, KC * 256], BF16) for i in range(2)]; b_wag = [Buf("wag0"), Buf("wag1")]
            apx = [sb5("apx%d" % i, [128, 34 * 66], BF16) for i in range(2)]; apc = [sb5("apc%d" % i, [128, 258], BF16) for i in range(2)]
            b_ap = [Buf("ap0"), Buf("ap1")]
            dg = [sb5("dg%d" % i, [128, 9 * 128], BF16) for i in range(2)]; b_dg = [Buf("dg0"), Buf("dg1")]
            gel = [sb5("gel%d" % i, [128, 512], F32) for i in range(2)]; b_gel = [Buf("gel0"), Buf("gel1")]
            htc = [sb5("htc%d" % i, [128, N], BF16) for i in range(2)]; b_htc = [Buf("htc0"), Buf("htc1")]
            for i in range(2):
                S.op("pool", lambda e, i=i: e.memset(apx[i][:], 0.0), writes=[b_ap[i]])
                S.op("pool", lambda e, i=i: e.memset(apc[i][:], 0.0), writes=[b_ap[i]])
            FB = ([("c", 0, 256, 0)] if with_ctx else []) + [("x", 256 + 512 * j, 512, j) for j in range(4)]
            wu = ffn_up[l].rearrange("(k p) n -> p k n", p=128)
            defer_l = l + 1 if (l + 1 < DEPTH) else None
            if defer_l is not None:
                awd = [sb5("awd%d" % i, [128, KC * 512], BF16) for i in range(2)]; b_awd = [Buf("awd0"), Buf("awd1")]
                pm_d, b_pmd = pbf(1, 96), b_pb[1]
            for fc in range(NFC):
                a = fc % 2
                if defer_l is not None:
                    if fc < 12:
                        mod_chunk_load(defer_l, fc, awd[fc % 2], b_awd[fc % 2])
                    if 1 <= fc <= 12:
                        mod_chunk_mm(defer_l, fc - 1, awd[(fc - 1) % 2], b_awd[(fc - 1) % 2], pm_d, b_pmd)
                    if fc == 13:
                        mod_finish(defer_l, pm_d, b_pmd)
                for ci, c0 in enumerate((fc * 128, FF + fc * 128)):
                    S.dma("pool", wag[a][:].rearrange("p (k n) -> p k n", k=KC)[:, :, ci * 128:(ci + 1) * 128], wu[:, :, c0:c0 + 128], writes=[b_wag[a]])
                for tap in range(9):
                    wcol = fcw_s[:, fc * 9 + tap:fc * 9 + tap + 1]
                    S.op("dve", lambda e, a=a, tap=tap, wcol=wcol: e.tensor_scalar(dg[a][:, tap * 128:(tap + 1) * 128], id_f[:], wcol, None, ALU.mult),
                         reads=[b_id_f, b_fcw], writes=[b_dg[a]])
                apx3 = apx[a][:].rearrange("p (r c) -> p r c", c=66)
                for bn, (kind, t0, nb, j) in enumerate(FB):
                    pa = (3, 6)[bn % 2]
                    tl = list(range(t0 // 128, (t0 + nb) // 128))
                    for kc in range(KC):
                        S.op("pe", lambda e, a=a, kc=kc, t0=t0, nb=nb, pa=pa: e.matmul(pbf(pa, nb), wag[a][:, kc * 256:kc * 256 + 128], xmT[:, kc * N + t0:kc * N + t0 + nb],
                             start=(kc == 0), stop=(kc == KC - 1)), reads=[b_wag[a]] + [b_xmT[i] for i in tl], writes=[b_pb[pa]], inc=(kc == KC - 1))
                    if kind == "c":
                        S.op("act", lambda e, a=a, pa=pa: e.copy(apc[a][:, 1:257], pbf(pa, 256)), reads=[b_pb[pa]], writes=[b_ap[a]])
                    else:
                        S.op("act", lambda e, a=a, pa=pa, j=j, apx3=apx3: e.copy(apx3[:, 1 + 8 * j:9 + 8 * j, 1:65], pbf(pa, 512).rearrange("p (r c) -> p r c", c=GW)),
                             reads=[b_pb[pa]], writes=[b_ap[a]])
                for bn, (kind, t0, nb, j) in enumerate(FB):
                    pc, pg = (4, 7)[bn % 2], (5, 0)[bn % 2]
                    ga = bn % 2
                    tl = list(range(t0 // 128, (t0 + nb) // 128))
                    if kind == "c":
                        for n_, dj in enumerate(range(3)):
                            tap = 3 + dj
                            S.op("pe", lambda e, a=a, tap=tap, dj=dj, pc=pc, n_=n_: e.matmul(pbf(pc, 256), dg[a][:, tap * 128:(tap + 1) * 128], apc[a][:, dj:dj + 256], start=(n_ == 0), stop=(n_ == 2)),
                                 reads=[b_dg[a], b_ap[a]], writes=[b_pb[pc]], inc=(n_ == 2))
                    else:
                        for tap in range(9):
                            di, dj = tap // 3, tap % 3
                            mv_ = apx3[:, di + 8 * j:di + 8 * j + 8, dj:dj + GW]
                            S.op("pe", lambda e, a=a, tap=tap, mv_=mv_, pc=pc: e.matmul(pbf(pc, 512), dg[a][:, tap * 128:(tap + 1) * 128], mv_, start=(tap == 0), stop=(tap == 8)),
                                 reads=[b_dg[a], b_ap[a]], writes=[b_pb[pc]], inc=(tap == 8))
                    for kc in range(KC):
                        S.op("pe", lambda e, a=a, kc=kc, t0=t0, nb=nb, pg=pg: e.matmul(pbf(pg, nb), wag[a][:, kc * 256 + 128:kc * 256 + 256], xmT[:, kc * N + t0:kc * N + t0 + nb],
                             start=(kc == 0), stop=(kc == KC - 1)), reads=[b_wag[a]] + [b_xmT[i] for i in tl], writes=[b_pb[pg]], inc=(kc == KC - 1))
                    bcol = fcb_s[:, fc:fc + 1]
                    S.op("act", lambda e, ga=ga, nb=nb, pc=pc, bcol=bcol: e.activation(gel[ga][:, 0:nb], pbf(pc, nb), AF.Gelu, bias=bcol), reads=[b_pb[pc], b_fcb], writes=[b_gel[ga]])
                    S.op("dve", lambda e, a=a, ga=ga, t0=t0, nb=nb, pg=pg: e.tensor_tensor(htc[a][:, t0:t0 + nb], gel[ga][:, 0:nb], pbf(pg, nb), ALU.mult),
                         reads=[b_gel[ga], b_pb[pg]], writes=[b_htc[a]])
                lo = FB[0][1]
                S.dma("sp", hTd[fc * 128:(fc + 1) * 128, lo:N], htc[a][:, lo:N], reads=[b_htc[a]], writes=[b_hTd[fc]])
            S.barrier(); p5.close()

        def phase_ffn_down(l, tiles, dst_ap, dst_row0):
            p6 = ExitStack()
            sb6 = lambda n, s_, dt: p6.enter_context(nc.sbuf_tensor("%s_p6L%d" % (n, l), s_, dt))
            wd = sb6("wd", [128, NFC * D], BF16); b_wdp = {(hf_, q_): Buf("wd%d%d" % (hf_, q_)) for hf_ in range(2) for q_ in range(2)}
            gbc = [sb6("gbc%d" % i, [128, D], F32) for i in range(2)]; b_gbc = [Buf("gbc0"), Buf("gbc1")]
            lg = sb6("lg", [128, D], F32); lb_ = sb6("lb_", [128, D], F32); b_lg, b_lbb = Buf("lg"), Buf("lbb")
            scr = sb6("scr", [128, 128], F32); b_scr = Buf("scr")
            wdv = ffn_down[l].rearrange("(f p) n -> p f n", p=128)
            for hf_ in range(2):
                for q_ in range(2):
                    S.dma("pool", wd[:].rearrange("p (f n) -> p f n", f=NFC)[:, q_ * 11:(q_ + 1) * 11, hf_ * 512:(hf_ + 1) * 512],
                          wdv[:, q_ * 11:(q_ + 1) * 11, hf_ * 512:(hf_ + 1) * 512], writes=[b_wdp[(hf_, q_)]])
            S.dma("sp", lg[:], lnp[4 * l + 2:4 * l + 3, :].partition_broadcast(128), writes=[b_lg])
            S.dma("sp", lb_[:], lnp[4 * l + 3:4 * l + 4, :].partition_broadcast(128), writes=[b_lbb])
            gate_bcast(gbc[0], b_gbc[0], l, 5, 0, scr, b_scr)
            if any(i < 2 for i in tiles):
                gate_bcast(gbc[1], b_gbc[1], l, 5, 1, scr, b_scr)
            hv = hTd.rearrange("(f p) n -> p f n", p=128)
            sets = mk_sets(sb6, 2 * GRP, "c", with_y=True, with_h=True)

            def load(i, B_):
                S.dma("sp", B_["ht"][:].rearrange("p (f n) -> p f n", f=NFC), hv[:, :, i * 128:(i + 1) * 128], reads=b_hTd, writes=[B_["b_ht"]])
                S.dma("sp", B_["xq"][:], x1d[i * 128:(i + 1) * 128, :], reads=[b_x1d[i]], writes=[B_["b_xq"]])

            def chain6(i, B_):
                srcm = 1 if i < 2 else 0
                xq_t, yt_t, xt_t, st_t, ht_t = B_["xq"], B_["yt"], B_["xt"], B_["st"], B_["ht"]
                for hf in range(2):
                    pbi = B_["mm"][hf]; hs = slice(hf * 512, (hf + 1) * 512)
                    yield
                    for f_ in range(NFC):
                        S.op("pe", lambda e, f_=f_, hf=hf, pbi=pbi: e.matmul(pbf(pbi, 512), ht_t[:, f_ * 128:(f_ + 1) * 128], wd[:, f_ * D + hf * 512:f_ * D + (hf + 1) * 512],
                             start=(f_ == 0), stop=(f_ == NFC - 1)), reads=[B_["b_ht"], b_wdp[(hf, f_ // 11)]], writes=[b_pb[pbi]], inc=(f_ == NFC - 1))
                    yield
                    S.op("dve", lambda e, hs=hs, pbi=pbi: e.tensor_tensor(yt_t[:, hs], pbf(pbi, 512), gbc[srcm][:, hs], ALU.mult),
                         reads=[b_pb[pbi], b_gbc[srcm]], writes=[B_["b_yt"]])
                    yield
                    S.op("dve", lambda e, hs=hs: e.scalar_tensor_tensor(yt_t[:, hs], xq_t[:, hs], ALPHA, yt_t[:, hs], ALU.mult, ALU.add),
                         reads=[B_["b_xq"], B_["b_yt"]], writes=[B_["b_yt"]])
                yield from ln_stats_gen(yt_t, B_["b_yt"], B_)
                yield
                S.op("dve", lambda e: e.scalar_tensor_tensor(yt_t[:], yt_t[:], st_t[:, 12:13], lg[:], ALU.subtract, ALU.mult),
                     reads=[B_["b_yt"], B_["b_st"], b_lg], writes=[B_["b_yt"]])
                yield
                S.op("dve", lambda e: e.scalar_tensor_tensor(xt_t[:], yt_t[:], st_t[:, 15:16], lb_[:], ALU.mult, ALU.add),
                     reads=[B_["b_yt"], B_["b_st"], b_lbb, B_["b_xt"]], writes=[B_["b_xt"]])
                r0 = i * 128 - dst_row0
                yield
                S.dma("sp", dst_ap[r0:r0 + 128, :], xt_t[:], reads=[B_["b_xt"]], writes=[b_x2d[i]])

            run_groups(tiles, sets, load, chain6)
            S.barrier(); p6.close()

        if debug in ("h", "x2", None):
            phase_ffn_up(0, True)
        if debug == "h":
            hst_b = sb("hst_b", [128, N], BF16); hst_f = sb("hst_f", [128, N], F32); b_hsb, b_hsf = Buf("hsb"), Buf("hsf")
            for fc in range(NFC):
                S.dma("sp", hst_b[:], hTd[fc * 128:(fc + 1) * 128, :], reads=[b_hTd[fc]], writes=[b_hsb])
                S.op("act", lambda e: e.copy(hst_f[:], hst_b[:]), reads=[b_hsb], writes=[b_hsf])
                S.dma("sp", outs["dbg_hT"][fc * 128:(fc + 1) * 128, :], hst_f[:], reads=[b_hsf])
        if debug in ("x2", None):
            phase_ffn_down(0, list(range(NT)), x2d[0], 0)
        if debug == "x2":
            for i in range(NT):
                a = i % 2
                S.dma("sp", xt[a][:], x2d[0][i * 128:(i + 1) * 128, :], reads=[b_x2d[i]], writes=[b_xt[a]])
                S.dma("sp", outs["dbg_x2"][i * 128:(i + 1) * 128, :], xt[a][:], reads=[b_xt[a]])

        if debug is None:
            XT = list(range(2, NT))
            phase_ln_mod_T(1, x2d[0], range(NT), 0, 1, src_bufs=b_x2d)
            mixer(1, list(range(NT)), XT)
            phase_wout_ln1(1, x2d[0], XT, src_bufs=b_x2d)
            phase_ffn_up(1, False)
            phase_ffn_down(1, XT, outs["out"], 256)
        if debug in ("hg", "mix"):
            hg_f, b_hgf, hg_b, b_hgb = mixer.dbg
            for h in range(8 if debug == "mix" else 4):
                S.dma("sp", hg_b[:], mixT[h * 128:(h + 1) * 128, :], reads=[b_mixT[h]], writes=[b_hgb])
                S.op("act", lambda e: e.copy(hg_f[:], hg_b[:]), reads=[b_hgb], writes=[b_hgf])
                S.dma("sp", outs["dbg_hgT"][h * 128:(h + 1) * 128, :], hg_f[:], reads=[b_hgf])
        S.drain_all("sp")
        S.emit()
        build.ninstr = S.ninstr
    return nc


def _prep(inputs, b):
    f = lambda a: np.ascontiguousarray(np.asarray(a, dtype=np.float32))
    m = {}
    m["xin"] = f(np.concatenate([inputs["ctx"][b], inputs["x"][b]], axis=0))
    cc = np.stack([np.asarray(inputs["c"][b]).reshape(KC, 128).T, np.asarray(inputs["c_ctx"]).reshape(KC, 128).T], axis=-1)
    m["c2"] = f(cc.reshape(128, KC * 2))
    m["ada_w"] = f(inputs["ada_w"])
    m["ada_b_fm"] = f(np.asarray(inputs["ada_b"]).reshape(DEPTH, 48, 128).transpose(2, 0, 1).reshape(128, DEPTH * 48))
    m["ident"] = np.eye(128, dtype=np.float32)
    m["w_in"] = f(inputs["w_in"])
    m["w_out"] = f(inputs["w_out"])
    m["ffn_up"] = f(inputs["ffn_up"]); m["ffn_down"] = f(inputs["ffn_down"])
    m["fcw_fm"] = f(np.asarray(inputs["ffn_conv_w"]).reshape(DEPTH, 9, NFC, 128).transpose(3, 0, 2, 1).reshape(128, DEPTH * NFC * 9))
    m["fcb_fm"] = f(np.asarray(inputs["ffn_conv_b"]).reshape(DEPTH, NFC, 128).transpose(2, 0, 1).reshape(128, DEPTH * NFC))
    m["lnp"] = f(np.stack([np.asarray(inputs[k]) for k in ("ln1_g", "ln1_b", "ln2_g", "ln2_b")], axis=1).reshape(DEPTH * 4, D))
    m["scw_fm"] = f(np.asarray(inputs["sc_conv_w"]).reshape(DEPTH, 3, 2, 128).transpose(3, 0, 2, 1).reshape(128, DEPTH * 6))
    m["sgln"] = f(np.stack([np.asarray(inputs["sg_ln_g"]), np.asarray(inputs["sg_ln_b"])], axis=1).reshape(DEPTH * 2, 256))
    m["sg_w"] = f(np.asarray(inputs["sg_w"]).reshape(DEPTH * 4, 128, 128))
    sb_ = np.asarray(inputs["sg_b"]).reshape(DEPTH, 2, 2, 1, 128)
    m["sgb_fm"] = f(np.broadcast_to(sb_, (DEPTH, 2, 2, 64, 128)).reshape(DEPTH * 2, 128, 128))
    m["lbl"] = f(np.asarray(inputs["hg_lb"]).reshape(DEPTH, 2, 4, 128).transpose(3, 0, 1, 2).reshape(128, 16))
    m["hg_norm_g"] = f(inputs["hg_norm_g"])
    ii = np.arange(128)
    same = (ii[:, None] // 64) == (ii[None, :] // 64)
    m["cmask"] = f(np.concatenate([same & (ii[:, None] <= ii[None, :]), same & (ii[:, None] >= ii[None, :])], axis=1))
    rm = np.ones((128, 512), np.float32); rm[:, ::64] = 0.0
    m["rmask"] = rm
    return m


_NC = None


def kernel(**inputs):
    global _NC
    if _NC is None:
        _NC = build()
    shared = None
    maps = []
    for b in range(8):
        m = _prep(inputs, b)
        if shared is None:
            shared = {k: m[k] for k in m if k not in ("xin", "c2")}
        else:
            m.update(shared)
        maps.append(m)
    res = run_bass_kernel_spmd(_NC, maps, core_ids=list(range(8)))
    return np.stack([np.asarray(r["out"], dtype=np.float32) for r in res.results], axis=0)
```
